# Optimizing a Trainium2 kernel written in Bass

```python
import math
import jax, jax.numpy as jnp
from jax import lax
import numpy as np

D_MODEL = 1024
BATCH = 8
SEQ = 2048
DEPTH = 4

HEAD_DIM = 64
ATT_HEADS = D_MODEL // HEAD_DIM
ATT_WIDTH = ATT_HEADS * HEAD_DIM
ROT_DIM = HEAD_DIM // 4
ROPE_THETA = 500000.0
DILATED_PATTERNS = ((128, 1), (512, 4), (2048, 16))
MAX_REACH = 2048
Q_BLOCK = 128
SSM_WIDTH = D_MODEL // 2
SSM_GROUP = 16
SSM_GROUPS = SSM_WIDTH // SSM_GROUP
SSM_STATE = 64
POOL_WIDTH = 2 * D_MODEL
POOL_WINDOWS = (2, 4, 8, 16)
POOL_GROUP = POOL_WIDTH // len(POOL_WINDOWS)
EVEN_IN = 4 * ATT_WIDTH + 2 * SSM_WIDTH
EVEN_OUT = ATT_WIDTH + SSM_WIDTH
ODD_IN = 2 * POOL_WIDTH
N_EVEN = (DEPTH + 1) // 2
N_ODD = DEPTH // 2
RMS_EPS = 1e-6

kernel_name = "hybrid_dilated_attn_s5_pool_block"


def rmsnorm(x, gain):
    xf = x.astype(jnp.float32)
    y = xf * lax.rsqrt(jnp.mean(xf * xf, axis=-1, keepdims=True) + RMS_EPS)
    return (y * gain.astype(jnp.float32)).astype(x.dtype)


def apply_partial_rope(x, pos):
    half = ROT_DIM // 2
    inv_freq = ROPE_THETA ** (-jnp.arange(0, ROT_DIM, 2, dtype=jnp.float32) / ROT_DIM)
    ang = pos.astype(jnp.float32)[:, None] * inv_freq[None, :]
    cos = jnp.cos(ang)[None, :, None, :]
    sin = jnp.sin(ang)[None, :, None, :]
    xr = x[..., :ROT_DIM].astype(jnp.float32)
    x1, x2 = xr[..., :half], xr[..., half:]
    rot = jnp.concatenate([x1 * cos - x2 * sin, x2 * cos + x1 * sin], axis=-1)
    return jnp.concatenate([rot.astype(x.dtype), x[..., ROT_DIM:]], axis=-1)


def dilated_offsets():
    return np.concatenate([np.arange(w // d + 1) * d for w, d in DILATED_PATTERNS]).astype(np.int32)


def dilated_attention(q, k, v):
    b, s, h, dh = q.shape
    offsets = jnp.asarray(dilated_offsets())
    qh = q.transpose(0, 2, 1, 3)
    pad = ((0, 0), (0, 0), (MAX_REACH, 0), (0, 0))
    k_pad = jnp.pad(k.transpose(0, 2, 1, 3), pad)
    v_pad = jnp.pad(v.transpose(0, 2, 1, 3), pad)

    def block(bi):
        start = bi * Q_BLOCK
        q_blk = lax.dynamic_slice_in_dim(qh, start, Q_BLOCK, axis=2)
        rel = start + jnp.arange(Q_BLOCK, dtype=jnp.int32)[:, None] - offsets[None, :]
        k_g = k_pad[:, :, rel + MAX_REACH]
        v_g = v_pad[:, :, rel + MAX_REACH]
        scores = jnp.einsum('bhqd,bhqkd->bhqk', q_blk, k_g).astype(jnp.float32)
        scores = jnp.where(rel >= 0, scores, -jnp.inf)
        p = jax.nn.softmax(scores, axis=-1)
        return jnp.einsum('bhqk,bhqkd->bhqd', p.astype(v.dtype), v_g)

    out = lax.map(block, jnp.arange(s // Q_BLOCK, dtype=jnp.int32))
    return out.transpose(1, 0, 3, 2, 4).reshape(b, s, h * dh)


def s5_ssm(u, a_re, a_im, log_dt, b_re, b_im, c_re, c_im, d_skip, glu_w, glu_b):
    bsz, s, _ = u.shape
    f32 = jnp.float32
    uf = u.astype(f32)
    ug = uf.reshape(bsz, s, SSM_GROUPS, SSM_GROUP)
    lam = lax.complex(a_re.astype(f32), a_im.astype(f32))
    dt = jnp.exp(log_dt.astype(f32))[:, None]
    lam_bar = jnp.exp(lam * dt)
    b_mat = lax.complex(b_re.astype(f32), b_im.astype(f32))
    b_bar = ((lam_bar - 1.0) / lam)[..., None] * b_mat
    bu = jnp.einsum('bsgh,gph->bsgp', ug.astype(jnp.complex64), b_bar)
    a_seq = jnp.broadcast_to(lam_bar, bu.shape)

    def combine(left, right):
        a_l, s_l = left
        a_r, s_r = right
        return a_r * a_l, a_r * s_l + s_r

    _, states = lax.associative_scan(combine, (a_seq, bu), axis=1)
    c_mat = lax.complex(c_re.astype(f32), c_im.astype(f32))
    y = jnp.real(jnp.einsum('bsgp,ghp->bsgh', states, c_mat)).reshape(bsz, s, SSM_WIDTH)
    y = jax.nn.gelu(y + d_skip.astype(f32) * uf)
    y = y * jax.nn.sigmoid(y @ glu_w.astype(f32) + glu_b.astype(f32))
    return y.astype(u.dtype)


def multiscale_pool(u, pool_w, pool_scale):
    bsz, s, _ = u.shape
    ug = u.astype(jnp.float32).reshape(bsz, s, len(POOL_WINDOWS), POOL_GROUP)
    count_base = jnp.arange(1, s + 1, dtype=jnp.float32)[None, :, None]
    outs = []
    for g, w in enumerate(POOL_WINDOWS):
        ch = ug[:, :, g]
        cs = jnp.cumsum(ch, axis=1)
        lagged = jnp.pad(cs, ((0, 0), (w, 0), (0, 0)))[:, :s]
        mean = (cs - lagged) / jnp.minimum(count_base, float(w))
        outs.append(mean - ch)
    mixed = jnp.stack(outs, axis=2)
    y = jnp.einsum('bsgc,gcd->bsgd', mixed, pool_w.astype(jnp.float32)).reshape(bsz, s, POOL_WIDTH)
    return (y * pool_scale.astype(jnp.float32)).astype(u.dtype)


def even_mixer(h, w_in, w_out, a_re, a_im, log_dt, b_re, b_im, c_re, c_im, d_skip, glu_w, glu_b):
    bsz, s, _ = h.shape
    proj = h @ w_in
    cuts = [ATT_WIDTH, 2 * ATT_WIDTH, 3 * ATT_WIDTH, 4 * ATT_WIDTH, 4 * ATT_WIDTH + SSM_WIDTH]
    q, k, v, g_att, u_ssm, g_ssm = jnp.split(proj, cuts, axis=-1)
    pos = jnp.arange(s, dtype=jnp.int32)
    q = apply_partial_rope(q.reshape(bsz, s, ATT_HEADS, HEAD_DIM), pos) * (HEAD_DIM ** -0.5)
    k = apply_partial_rope(k.reshape(bsz, s, ATT_HEADS, HEAD_DIM), pos)
    v = v.reshape(bsz, s, ATT_HEADS, HEAD_DIM)
    att = dilated_attention(q, k, v)
    ssm = s5_ssm(u_ssm, a_re, a_im, log_dt, b_re, b_im, c_re, c_im, d_skip, glu_w, glu_b)
    merged = jnp.concatenate([att * jax.nn.silu(g_att), ssm * jax.nn.silu(g_ssm)], axis=-1)
    return merged @ w_out


def odd_mixer(h, w_in, pool_w, pool_scale, w_out):
    u, gate = jnp.split(h @ w_in, [POOL_WIDTH], axis=-1)
    y = multiscale_pool(u, pool_w, pool_scale)
    return (y * jax.nn.silu(gate)) @ w_out


def setup_inputs(seed: int = 0) -> dict:
    key = jax.random.key(seed)
    ks = jax.random.split(key, 24)
    f32 = jnp.float32

    def nrm(k, shape, scale):
        return jax.random.normal(k, shape, f32) * scale

    n_idx = jnp.arange(SSM_STATE, dtype=f32)
    return {
        "x": jax.random.normal(ks[0], (BATCH, SEQ, D_MODEL), f32),
        "pre_norm": 1.0 + nrm(ks[1], (DEPTH, D_MODEL), 0.1),
        "post_norm": 1.0 + nrm(ks[2], (DEPTH, D_MODEL), 0.1),
        "even_w_in": nrm(ks[3], (N_EVEN, D_MODEL, EVEN_IN), D_MODEL ** -0.5),
        "even_w_out": nrm(ks[4], (N_EVEN, EVEN_OUT, D_MODEL), EVEN_OUT ** -0.5),
        "ssm_a_re": -0.5 + nrm(ks[5], (N_EVEN, SSM_GROUPS, SSM_STATE), 0.01),
        "ssm_a_im": math.pi * n_idx + nrm(ks[6], (N_EVEN, SSM_GROUPS, SSM_STATE), 0.01),
        "ssm_log_dt": jax.random.uniform(ks[7], (N_EVEN, SSM_GROUPS), f32, math.log(1e-3), math.log(1e-1)),
        "ssm_b_re": nrm(ks[8], (N_EVEN, SSM_GROUPS, SSM_STATE, SSM_GROUP), (2 * SSM_GROUP) ** -0.5),
        "ssm_b_im": nrm(ks[9], (N_EVEN, SSM_GROUPS, SSM_STATE, SSM_GROUP), (2 * SSM_GROUP) ** -0.5),
        "ssm_c_re": nrm(ks[10], (N_EVEN, SSM_GROUPS, SSM_GROUP, SSM_STATE), (2 * SSM_STATE) ** -0.5),
        "ssm_c_im": nrm(ks[11], (N_EVEN, SSM_GROUPS, SSM_GROUP, SSM_STATE), (2 * SSM_STATE) ** -0.5),
        "ssm_d": nrm(ks[12], (N_EVEN, SSM_WIDTH), 1.0),
        "ssm_glu_w": nrm(ks[13], (N_EVEN, SSM_WIDTH, SSM_WIDTH), SSM_WIDTH ** -0.5),
        "ssm_glu_b": nrm(ks[14], (N_EVEN, SSM_WIDTH), 0.02),
        "odd_w_in": nrm(ks[15], (N_ODD, D_MODEL, ODD_IN), D_MODEL ** -0.5),
        "pool_w": nrm(ks[16], (N_ODD, len(POOL_WINDOWS), POOL_GROUP, POOL_GROUP), POOL_GROUP ** -0.5),
        "pool_scale": 1.0 + nrm(ks[17], (N_ODD, POOL_WIDTH), 0.1),
        "odd_w_out": nrm(ks[18], (N_ODD, POOL_WIDTH, D_MODEL), POOL_WIDTH ** -0.5),
    }


def reference(x, pre_norm, post_norm, even_w_in, even_w_out, ssm_a_re, ssm_a_im, ssm_log_dt,
              ssm_b_re, ssm_b_im, ssm_c_re, ssm_c_im, ssm_d, ssm_glu_w, ssm_glu_b,
              odd_w_in, pool_w, pool_scale, odd_w_out):
    for layer in range(DEPTH):
        h = rmsnorm(x, pre_norm[layer])
        i = layer // 2
        if layer % 2 == 0:
            y = even_mixer(h, even_w_in[i], even_w_out[i], ssm_a_re[i], ssm_a_im[i], ssm_log_dt[i],
                           ssm_b_re[i], ssm_b_im[i], ssm_c_re[i], ssm_c_im[i], ssm_d[i],
                           ssm_glu_w[i], ssm_glu_b[i])
        else:
            y = odd_mixer(h, odd_w_in[i], pool_w[i], pool_scale[i], odd_w_out[i])
        x = x + rmsnorm(y, post_norm[layer])
    return x
```

```python
import math
import numpy as np
import concourse.bass as bass
import concourse.mybir as mybir
from concourse.bass_utils import run_bass_kernel_spmd

F32 = mybir.dt.float32
BF16 = mybir.dt.bfloat16
I32 = mybir.dt.int32
ALU = mybir.AluOpType
AF = mybir.ActivationFunctionType
AX = mybir.AxisListType

S = 2048
D = 1024
NB = 16
EPS = 1e-6
TWO_PI = 2.0 * math.pi


class FW:
    def __init__(self, nc):
        self.nc = nc
        self.eng = {"pe": nc.tensor, "act": nc.scalar, "dve": nc.vector,
                    "pool": nc.gpsimd, "sp": nc.sync}
        self.sem = {}
        self.cnt = {}
        for e in self.eng:
            self.sem[e] = nc.alloc_semaphore("s_" + e)
            self.cnt[e] = 0
        self.waited = {}
        self.lastw = {}
        self.rd = {}
        self.dsem = {}

    def _deps(self, reads, writes):
        deps = {}

        def add(t):
            if t is not None and deps.get(t[0], 0) < t[1]:
                deps[t[0]] = t[1]
        for k in reads:
            add(self.lastw.get(k))
        for k in writes:
            add(self.lastw.get(k))
            for sk, v in self.rd.get(k, {}).items():
                add((sk, v))
        return deps

    def _semof(self, sk):
        if isinstance(sk, tuple):
            return self.dsem[sk[1]][0]
        return self.sem[sk]

    def _emit_waits(self, e, deps):
        for sk, v in deps.items():
            if sk == e and v > self.cnt[e]:
                continue
            if self.waited.get((e, sk), 0) < v:
                self.eng[e].wait_ge(self._semof(sk), v)
                self.waited[(e, sk)] = v

    def _record(self, t, reads, writes):
        for k in writes:
            self.lastw[k] = t
            self.rd[k] = {}
        for k in reads:
            d = self.rd.setdefault(k, {})
            if d.get(t[0], 0) < t[1]:
                d[t[0]] = t[1]

    def op(self, e, fn, reads=(), writes=(), inc=True):
        self._emit_waits(e, self._deps(reads, writes))
        ins = fn(self.eng[e])
        if inc:
            ins.then_inc(self.sem[e], 1)
            self.cnt[e] += 1
            t = (e, self.cnt[e])
        else:
            t = (e, self.cnt[e] + 1)
        self._record(t, reads, writes)
        return ins

    def dma(self, q, out, in_, reads=(), writes=(), key=None, **kw):
        if key is None:
            key = writes[0] if writes else reads[0]
        if key not in self.dsem:
            self.dsem[key] = [self.nc.alloc_semaphore("d%d" % len(self.dsem)), 0]
        self._emit_waits(q, self._deps(reads, writes))
        ins = self.eng[q].dma_start(out=out, in_=in_, **kw)
        ins.then_inc(self.dsem[key][0], 16)
        self.dsem[key][1] += 16
        t = (("dma", key), self.dsem[key][1])
        self._record(t, reads, writes)
        return ins

    def barrier(self):
        for e in self.eng:
            deps = {}
            for o in self.eng:
                if o != e and self.cnt[o] > 0:
                    deps[o] = self.cnt[o]
            for key, (s, c) in self.dsem.items():
                if c > 0:
                    deps[("dma", key)] = c
            self._emit_waits(e, deps)


def _mult(d):
    d = np.asarray(d)
    m = ((d >= 0) & (d <= 128)).astype(np.float32)
    m += ((d >= 0) & (d <= 512) & (d % 4 == 0)).astype(np.float32)
    m += ((d >= 0) & (d % 16 == 0)).astype(np.float32)
    return m


def host_consts():
    c = {}
    c["c_ident"] = np.eye(128, dtype=np.float32)
    j = np.arange(128)[:, None]
    t = np.arange(128)[None, :]
    c["c_mask"] = np.concatenate([_mult(128 * dl + t - j) for dl in range(-3, 16)], axis=1).astype(np.float32)
    bands = np.zeros((128, 4, 3, 128), np.float32)
    tp = np.arange(128)[:, None]
    tt = np.arange(128)[None, :]
    for wi, w in enumerate((2, 4, 8, 16)):
        dcur = tt - tp
        bands[:, wi, 0, :] = ((dcur >= 0) & (dcur < w)) / float(w) - (dcur == 0)
        dprev = tt + 128 - tp
        bands[:, wi, 1, :] = ((dprev >= 0) & (dprev < w)) / float(w)
        cnt = np.minimum(tt + 1, w).astype(np.float32)
        bands[:, wi, 2, :] = ((dcur >= 0) & (dcur < w)) / cnt - (dcur == 0)
    c["c_bands"] = bands.reshape(128, 4 * 3 * 128)
    half = 8
    inv = 500000.0 ** (-np.arange(0, 16, 2, dtype=np.float32) / 16.0)
    ang = np.arange(S, dtype=np.float32)[None, :] * inv[:, None]
    C = np.ones((64, S), np.float32)
    Sg = np.zeros((64, S), np.float32)
    C[0:8] = np.cos(ang)
    C[8:16] = np.cos(ang)
    Sg[0:8] = -np.sin(ang)
    Sg[8:16] = np.sin(ang)
    c["c_ropec"] = np.concatenate([C, C], 0)
    c["c_ropes"] = np.concatenate([Sg, Sg], 0)
    gm = np.zeros((128, 8), np.float32)
    for g in range(8):
        gm[g * 16:(g + 1) * 16, g] = 1.0
    c["c_gmask"] = gm
    return c


def swap_perm():
    perm = np.arange(2048)
    for blk in range(2048 // 64):
        b = blk * 64
        perm[b:b + 8] = np.arange(b + 8, b + 16)
        perm[b + 8:b + 16] = np.arange(b, b + 8)
    return perm


def build(layers=(0, 1, 2, 3)):
    nc = bass.Bass("TRN2", target_bir_lowering=False)
    dt_ = {}

    def din(name, shape):
        h = nc.dram_tensor(name, list(shape), F32, kind="ExternalInput")
        dt_[name] = h
        return h.ap()

    x_d = din("x", [S, D])
    pre_d = din("pre_norm", [4, D])
    post_d = din("post_norm", [4, D])
    ewin_d = din("even_w_in", [2, D, 5120])
    ewsw_d = din("even_w_sw", [2, D, 2048])
    ewout_d = din("even_w_out", [2, 1536, D])
    are_d = din("ssm_a_re", [2, 32, 64])
    aim_d = din("ssm_a_im", [2, 32, 64])
    ldt_d = din("ssm_log_dt", [2, 32])
    bre_d = din("ssm_b_re", [2, 32, 64, 16])
    bim_d = din("ssm_b_im", [2, 32, 64, 16])
    cre_d = din("ssm_c_re", [2, 32, 16, 64])
    cim_d = din("ssm_c_im", [2, 32, 16, 64])
    sd_d = din("ssm_d", [2, 512])
    glw_d = din("ssm_glu_w", [2, 512, 512])
    glb_d = din("ssm_glu_b", [2, 512])
    owin_d = din("odd_w_in", [2, D, 4096])
    pw_d = din("pool_w", [2, 4, 512, 512])
    psc_d = din("pool_scale", [2, 2048])
    owout_d = din("odd_w_out", [2, 2048, D])
    cid_d = din("c_ident", [128, 128])
    cmask_d = din("c_mask", [128, 19 * 128])
    cband_d = din("c_bands", [128, 12 * 128])
    cropec_d = din("c_ropec", [128, S])
    cropes_d = din("c_ropes", [128, S])
    cgm_d = din("c_gmask", [128, 8])
    out_d = nc.dram_tensor("out", [S, D], F32, kind="ExternalOutput").ap()

    f = FW(nc)
    uid = [0]

    def nk(p):
        uid[0] += 1
        return "%s%d" % (p, uid[0])

    def bcast(name, off, n):
        return bass.AP(dt_[name], off, [[0, 128], [1, n]])

    from contextlib import ExitStack
    with ExitStack() as es:
        def sb(name, shape, dtype):
            return es.enter_context(nc.sbuf_tensor(name, list(shape), dtype))

        def psb(name, shape, dtype):
            return es.enter_context(nc.psum_tensor(name, list(shape), dtype))

        xres = sb("xres", [128, NB, D], F32)
        ps = [psb("ps%d" % i, [128, 512], F32) for i in range(6)]
        pst = [psb("pst%d" % i, [128, 1024], BF16) for i in range(2)]
        PS = ["ps%d" % i for i in range(6)]
        PST = ["pst0", "pst1"]
        ident = sb("ident", [128, 128], BF16)
        identf = sb("identf", [128, 128], F32)
        gain = sb("gain", [128, D], F32)
        junk = sb("junk", [128, D], BF16)
        stat = sb("stat", [128, 64], F32)
        gmask = sb("gmask", [128, 8], F32)
        wst = [sb("wst%d" % i, [128, 8, 128], F32) for i in range(2)]
        wbf = [sb("wbf%d" % i, [128, 8, 128], BF16) for i in range(2)]
        wctr = [0, 0]

        f.dma("sp", identf[:], cid_d, writes=["identf"])
        f.op("dve", lambda e: e.tensor_copy(out=ident[:], in_=identf[:]), reads=["identf"], writes=["ident"])
        f.dma("sp", gmask[:], cgm_d, writes=["gmask"])
        for b in range(NB):
            f.dma("sp", xres[:, b, :], x_d[b * 128:(b + 1) * 128, :], writes=[("x", b)])

        cast_rr = [0]

        def cast(out, in_, reads, writes):
            e = ("pool", "act")[cast_rr[0] % 2]
            cast_rr[0] += 1
            if e == "pool":
                f.op("pool", lambda g: g.tensor_copy(out=out, in_=in_), reads=reads, writes=writes)
            else:
                f.op("act", lambda g: g.copy(out=out, in_=in_), reads=reads, writes=writes)

        def load_chunk(src_ap, kc=8):
            si = wctr[0] % 2
            wctr[0] += 1
            bi = wctr[1] % 2
            wctr[1] += 1
            f.dma("sp", wst[si][:, 0:kc, :], src_ap.rearrange("(k p) n -> p k n", p=128), writes=["wst%d" % si])
            cast(wbf[bi][:, 0:kc, :], wst[si][:, 0:kc, :], ["wst%d" % si], ["wbf%d" % bi])
            return wbf[bi], "wbf%d" % bi

        def load_big(dst, dkey, src_ap, kc, ncols, stg, skey):
            for k0 in range(0, kc, 2):
                f.dma("sp", stg[:, :, :], src_ap[k0 * 128:(k0 + 2) * 128, :].rearrange("(k p) n -> p k n", p=128),
                      writes=[skey])
                cast(dst[:, k0:k0 + 2, :], stg[:, :, :], [skey], [dkey])

        scol = [0]

        def newcol():
            scol[0] = (scol[0] + 1) % 64
            return scol[0]

        def rstd_from_ss(c_ss):
            c1 = newcol()
            c2 = newcol()
            f.op("act", lambda e: e.activation(out=stat[:, c1:c1 + 1], in_=stat[:, c_ss:c_ss + 1], func=AF.Sqrt,
                                               scale=1.0 / D, bias=EPS),
                 reads=[("st", c_ss)], writes=[("st", c1)])
            f.op("dve", lambda e: e.reciprocal(out=stat[:, c2:c2 + 1], in_=stat[:, c1:c1 + 1]),
                 reads=[("st", c1)], writes=[("st", c2)])
            return c2

        def prenorm_block(b, dst, dkey):
            c = newcol()
            f.op("act", lambda e: e.activation(out=junk[:], in_=xres[:, b, :], func=AF.Square,
                                               accum_out=stat[:, c:c + 1]),
                 reads=[("x", b)], writes=["junk", ("st", c)])
            c2 = rstd_from_ss(c)
            f.op("dve", lambda e: e.scalar_tensor_tensor(out=dst, in0=xres[:, b, :], scalar=stat[:, c2:c2 + 1],
                                                         in1=gain[:], op0=ALU.mult, op1=ALU.mult),
                 reads=[("x", b), ("st", c2), "gain"], writes=[dkey])

        def transpose_block(src, skey, dst_ap, dkey, pi):
            for c in range(8):
                f.op("pe", lambda e, c=c: e.transpose(pst[pi][:, c * 128:(c + 1) * 128], src[:, c * 128:(c + 1) * 128],
                                                      ident[:]),
                     reads=[skey, "ident"], writes=[PST[pi]], inc=(c == 7))
            f.op("act", lambda e: e.copy(out=dst_ap, in_=pst[pi][:].rearrange("p (c t) -> p c t", c=8)),
                 reads=[PST[pi]], writes=[dkey])

        def outproj_block(b, mT_fn, mkeys, KC, wout, wkey, l):
            pp = (b % 2) * 2
            for fh in range(2):
                for kc in range(KC):
                    f.op("pe", lambda e, kc=kc, fh=fh: e.matmul(ps[pp + fh][:], lhsT=mT_fn(kc),
                                                                rhs=wout[:, kc, fh * 512:(fh + 1) * 512],
                                                                start=(kc == 0), stop=(kc == KC - 1)),
                         reads=list(mkeys) + [wkey], writes=[PS[pp + fh]], inc=(kc == KC - 1))
            ca = newcol()
            cb = newcol()
            f.op("act", lambda e: e.activation(out=junk[:, 0:512], in_=ps[pp][:], func=AF.Square,
                                               accum_out=stat[:, ca:ca + 1]),
                 reads=[PS[pp]], writes=["junk", ("st", ca)])
            f.op("act", lambda e: e.activation(out=junk[:, 512:1024], in_=ps[pp + 1][:], func=AF.Square,
                                               accum_out=stat[:, cb:cb + 1]),
                 reads=[PS[pp + 1]], writes=["junk", ("st", cb)])
            cs = newcol()
            f.op("dve", lambda e: e.tensor_tensor(out=stat[:, cs:cs + 1], in0=stat[:, ca:ca + 1],
                                                  in1=stat[:, cb:cb + 1], op=ALU.add),
                 reads=[("st", ca), ("st", cb)], writes=[("st", cs)])
            c2 = rstd_from_ss(cs)
            for fh in range(2):
                tk = "ytmp%d" % fh
                f.op("dve", lambda e, fh=fh: e.scalar_tensor_tensor(out=ytmp_box[0][:, fh, :], in0=ps[pp + fh][:],
                                                                    scalar=stat[:, c2:c2 + 1],
                                                                    in1=gain[:, fh * 512:(fh + 1) * 512],
                                                                    op0=ALU.mult, op1=ALU.mult),
                     reads=[PS[pp + fh], ("st", c2), "gain"], writes=[tk])
                f.op("pool", lambda e, fh=fh: e.tensor_tensor(out=xres[:, b, fh * 512:(fh + 1) * 512],
                                                              in0=xres[:, b, fh * 512:(fh + 1) * 512],
                                                              in1=ytmp_box[0][:, fh, :], op=ALU.add),
                     reads=[("x", b), tk], writes=[("x", b)])

        ytmp_box = [None]

        def odd_layer(l):
            i = l // 2
            with ExitStack() as ls:
                def lsb(name, shape, dtype):
                    return ls.enter_context(nc.sbuf_tensor(nk(name), list(shape), dtype))
                hN = lsb("hN", [128, 5, D], BF16)
                hT = lsb("hTq", [128, 8, 512], BF16)
                phT = lsb("phT", [128, 8, 512], BF16)
                mixed = lsb("mixed", [128, 4, 512], BF16)
                mT = lsb("mTq", [128, 16, 512], BF16)
                sg = lsb("sg", [128, 512], F32)
                bandf = lsb("bandf", [128, 12, 128], F32)
                band = lsb("band", [128, 4, 4, 128], BF16)
                btmp = lsb("btmp", [128, 4, 128], F32)
                pwst = lsb("pwst", [128, 2, 512], F32)
                pwbf = lsb("pwbf", [128, 4, 512], BF16)
                wout = lsb("wout", [128, 16, D], BF16)
                wost = lsb("wost", [128, 2, D], F32)
                pscale = lsb("pscale", [128, 16], F32)
                ytmp_box[0] = lsb("ytmp", [128, 2, 512], F32)
                f.dma("sp", bandf[:], cband_d.rearrange("p (a t) -> p a t", t=128), writes=["bandf"])
                bv = bandf[:].rearrange("p (w k) t -> p w k t", k=3)
                f.op("dve", lambda e: e.tensor_copy(out=band[:, :, 0:3, :], in_=bv), reads=["bandf"], writes=["band"])
                f.op("dve", lambda e: e.tensor_copy(out=btmp[:], in_=band[:, :, 2, :]), reads=["band"], writes=["btmp"])
                f.op("dve", lambda e: e.tensor_tensor(out=btmp[:], in0=bv[:, :, 2, :], in1=btmp[:], op=ALU.subtract),
                     reads=["bandf", "btmp"], writes=["btmp"])
                f.op("dve", lambda e: e.tensor_copy(out=band[:, :, 3, :], in_=btmp[:]), reads=["btmp"], writes=["band"])
                f.dma("sp", pscale[:], psc_d[i].rearrange("(c p) -> p c", p=128), writes=["pscale"],
                      allow_slow_non_contiguous=True)
                for tq in range(4):
                    b0 = tq * 4
                    f.dma("sp", gain[:], bcast("pre_norm", l * D, D), writes=["gain"])
                    if tq > 0:
                        f.op("pool", lambda e: e.tensor_copy(out=hN[:, 0, :], in_=hN[:, 4, :]),
                             reads=[("hN", 4)], writes=[("hN", 0)])
                    for s_, b in enumerate(range(b0 - 1, b0 + 4)):
                        if s_ == 0:
                            continue
                        prenorm_block(b, hN[:, s_, :], ("hN", s_))
                    for s_ in range(1, 5):
                        transpose_block(hN[:, s_, :], ("hN", s_), hT[:, :, (s_ - 1) * 128:s_ * 128], "hT", s_ % 2)
                    for g in range(4):
                        for s_ in range(1, 5):
                            b = b0 + s_ - 1
                            for hf in range(2):
                                pk = 4 + hf
                                for cc in range(4):
                                    c = hf * 4 + cc
                                    o = ps[pk][:, cc * 128:(cc + 1) * 128]
                                    lh = hN[:, s_, c * 128:(c + 1) * 128]
                                    if b == 0:
                                        f.op("pe", lambda e, o=o, lh=lh: e.matmul(o, lhsT=lh, rhs=band[:, g, 2, :],
                                                                                  start=True, stop=False),
                                             reads=[("hN", s_), "band"], writes=[PS[pk]], inc=False)
                                        f.op("pe", lambda e, o=o, lh=lh: e.matmul(o, lhsT=lh, rhs=band[:, g, 3, :],
                                                                                  start=False, stop=True),
                                             reads=[("hN", s_), "band"], writes=[PS[pk]], inc=(cc == 3))
                                    else:
                                        lp = hN[:, s_ - 1, c * 128:(c + 1) * 128]
                                        f.op("pe", lambda e, o=o, lh=lh: e.matmul(o, lhsT=lh, rhs=band[:, g, 0, :],
                                                                                  start=True, stop=False),
                                             reads=[("hN", s_), "band"], writes=[PS[pk]], inc=False)
                                        f.op("pe", lambda e, o=o, lp=lp: e.matmul(o, lhsT=lp, rhs=band[:, g, 1, :],
                                                                                  start=False, stop=True),
                                             reads=[("hN", s_ - 1), "band"], writes=[PS[pk]], inc=(cc == 3))
                                f.op("act", lambda e, hf=hf, pk=pk: e.copy(
                                    out=phT[:, hf * 4:(hf + 1) * 4, (s_ - 1) * 128:s_ * 128],
                                    in_=ps[pk][:].rearrange("p (c t) -> p c t", c=4)),
                                    reads=[PS[pk]], writes=["phT"])
                        for j in range(4):
                            col = g * 512 + j * 128
                            wb, wk = load_chunk(owin_d[i][:, col:col + 128])
                            pk = j % 2
                            for kc in range(8):
                                f.op("pe", lambda e, kc=kc: e.matmul(ps[pk][:], lhsT=wb[:, kc, :], rhs=phT[:, kc, :],
                                                                     start=(kc == 0), stop=(kc == 7)),
                                     reads=[wk, "phT"], writes=[PS[pk]], inc=(kc == 7))
                            f.op("act", lambda e, j=j, pk=pk: e.copy(out=mixed[:, j, :], in_=ps[pk][:]),
                                 reads=[PS[pk]], writes=[("mixed", j)])
                        load_big(pwbf, "pwbf", pw_d[i, g], 4, 512, pwst, "pwst")
                        for dj in range(4):
                            col = 2048 + g * 512 + dj * 128
                            wb, wk = load_chunk(owin_d[i][:, col:col + 128])
                            pg = 2
                            for kc in range(8):
                                f.op("pe", lambda e, kc=kc: e.matmul(ps[pg][:], lhsT=wb[:, kc, :], rhs=hT[:, kc, :],
                                                                     start=(kc == 0), stop=(kc == 7)),
                                     reads=[wk, "hT"], writes=[PS[pg]], inc=(kc == 7))
                            f.op("act", lambda e: e.activation(out=sg[:], in_=ps[pg][:], func=AF.Silu),
                                 reads=[PS[pg]], writes=["sg"])
                            py = 3
                            for j in range(4):
                                f.op("pe", lambda e, j=j, dj=dj: e.matmul(ps[py][:], lhsT=pwbf[:, j, dj * 128:(dj + 1) * 128],
                                                                          rhs=mixed[:, j, :], start=(j == 0), stop=(j == 3)),
                                     reads=["pwbf", ("mixed", j)], writes=[PS[py]], inc=(j == 3))
                            ch = g * 4 + dj
                            f.op("dve", lambda e, ch=ch: e.scalar_tensor_tensor(out=mT[:, ch, :], in0=ps[py][:],
                                                                                scalar=pscale[:, ch:ch + 1], in1=sg[:],
                                                                                op0=ALU.mult, op1=ALU.mult),
                                 reads=[PS[py], "pscale", "sg"], writes=[("mT", ch)])
                    load_big(wout, "wout", owout_d[i], 16, D, wost, "wost")
                    f.dma("sp", gain[:], bcast("post_norm", l * D, D), writes=["gain"])
                    for bb in range(4):
                        outproj_block(b0 + bb, lambda kc, bb=bb: mT[:, kc, bb * 128:(bb + 1) * 128],
                                      [("mT", ch) for ch in range(16)], 16, wout, "wout", l)
                f.barrier()

        def even_layer(l):
            i = l // 2
            with ExitStack() as ls:
                def lsb(name, shape, dtype):
                    return ls.enter_context(nc.sbuf_tensor(nk(name), list(shape), dtype))
                hT = lsb("hT", [128, 8, S], BF16)
                mT = lsb("mT", [128, 12, S], BF16)
                f.dma("sp", gain[:], bcast("pre_norm", l * D, D), writes=["gain"])
                with ExitStack() as hs_:
                    hNb = [hs_.enter_context(nc.sbuf_tensor(nk("hNb"), [128, D], BF16)) for k in range(2)]
                    for b in range(NB):
                        prenorm_block(b, hNb[b % 2][:], "hNb%d" % (b % 2))
                        transpose_block(hNb[b % 2], "hNb%d" % (b % 2), hT[:, :, b * 128:(b + 1) * 128], "hT", b % 2)
                    f.barrier()

                def proj_fm(wb, wk, pk, tg):
                    for kc in range(8):
                        f.op("pe", lambda e, kc=kc: e.matmul(ps[pk][:], lhsT=wb[:, kc, :],
                                                             rhs=hT[:, kc, tg * 512:(tg + 1) * 512],
                                                             start=(kc == 0), stop=(kc == 7)),
                             reads=[wk, "hT"], writes=[PS[pk]], inc=(kc == 7))

                with ExitStack() as as_:
                    def asb(name, shape, dtype):
                        return as_.enter_context(nc.sbuf_tensor(nk(name), list(shape), dtype))
                    qT = asb("qT", [128, S], BF16)
                    kT = asb("kT", [128, S], BF16)
                    V = asb("V", [128, NB, 2, 128], BF16)
                    ropec = asb("ropec", [128, S], BF16)
                    ropes = asb("ropes", [128, S], BF16)
                    maskT = asb("maskT", [128, 19 * 128], BF16)
                    Pt = [asb("Pt%d" % k, [128, 512], BF16) for k in range(4)]
                    rtmp = [asb("rtmp%d" % k, [128, 512], F32) for k in range(2)]
                    rc = rtmp[0]
                    atmp = rtmp[1]
                    for (dstc, dkc, srcc, ncol) in ((ropec, "ropec", cropec_d, S), (ropes, "ropes", cropes_d, S),
                                                    (maskT, "maskT", cmask_d, 19 * 128)):
                        for c0_ in range(0, ncol, 1024):
                            n_ = min(1024, ncol - c0_)
                            si = wctr[0] % 2
                            wctr[0] += 1
                            stv = wst[si][:].rearrange("p a b -> p (a b)")
                            f.dma("sp", stv[:, 0:n_], srcc[:, c0_:c0_ + n_], writes=["wst%d" % si])
                            f.op("dve", lambda e: e.tensor_copy(out=dstc[:, c0_:c0_ + n_], in_=stv[:, 0:n_]),
                                 reads=["wst%d" % si], writes=[dkc])
                    f.op("pool", lambda e: e.memset(V[:, :, :, 64:128], 1.0), writes=["V"])
                    mrr = [0]
                    for hp in range(8):
                        c0 = hp * 128
                        for (dst, dk, base, swb) in ((qT, "qT", 0, 0), (kT, "kT", 1024, 1024)):
                            wb, wk = load_chunk(ewin_d[i][:, base + c0:base + c0 + 128])
                            wb2, wk2 = load_chunk(ewsw_d[i][:, swb + c0:swb + c0 + 128])
                            for tg in range(4):
                                sl = slice(tg * 512, (tg + 1) * 512)
                                proj_fm(wb, wk, 0, tg)
                                proj_fm(wb2, wk2, 1, tg)
                                r0 = "rtmp0"
                                r1 = "rtmp1"
                                f.op("dve", lambda e, sl=sl: e.tensor_tensor(out=rtmp[0][:], in0=ps[0][:], in1=ropec[:, sl],
                                                                             op=ALU.mult),
                                     reads=[PS[0], "ropec"], writes=[r0])
                                f.op("dve", lambda e, sl=sl: e.tensor_tensor(out=rtmp[1][:], in0=ps[1][:], in1=ropes[:, sl],
                                                                             op=ALU.mult),
                                     reads=[PS[1], "ropes"], writes=[r1])
                                f.op("pool", lambda e, sl=sl, dst=dst: e.tensor_tensor(out=dst[:, sl], in0=rtmp[0][:],
                                                                                      in1=rtmp[1][:], op=ALU.add),
                                     reads=[r0, r1], writes=[dk])
                        wb, wk = load_chunk(ewin_d[i][:, 3072 + c0:3072 + c0 + 128])
                        for tg in range(4):
                            proj_fm(wb, wk, 2, tg)
                            f.op("act", lambda e, tg=tg: e.activation(out=mT[:, hp, tg * 512:(tg + 1) * 512], in_=ps[2][:],
                                                                      func=AF.Silu),
                                 reads=[PS[2]], writes=[("mT", hp)])
                        wb, wk = load_chunk(ewin_d[i][:, 2048 + c0:2048 + c0 + 128])
                        for b4 in range(4):
                            pk = 3
                            for bb in range(4):
                                b = b4 * 4 + bb
                                for kc in range(8):
                                    f.op("pe", lambda e, kc=kc, b=b, bb=bb: e.matmul(
                                        ps[pk][:, bb * 128:(bb + 1) * 128], lhsT=hT[:, kc, b * 128:(b + 1) * 128],
                                        rhs=wb[:, kc, :], start=(kc == 0), stop=(kc == 7)),
                                        reads=[wk, "hT"], writes=[PS[pk]], inc=(kc == 7 and bb == 3))
                            f.op("act", lambda e, b4=b4: e.copy(
                                out=V[:, b4 * 4:(b4 + 1) * 4, :, 0:64],
                                in_=ps[pk][:].rearrange("p (b a d) -> p b a d", b=4, a=2)),
                                reads=[PS[pk]], writes=["V"])
                        items = []
                        for a in range(2):
                            for qg in range(4):
                                nkb = 4 * qg + 4
                                for kb in range(nkb):
                                    items.append((a, qg, kb, nkb))
                        LAG = 2

                        def stage1(idx):
                            a, qg, kb, nkb = items[idx]
                            pr = slice(64 * a, 64 * a + 64)
                            cq = max(128 * kb, 512 * qg)
                            N = 512 * (qg + 1) - cq
                            sk = idx % 4
                            f.op("pe", lambda e: e.matmul(
                                ps[sk][:, 0:N], lhsT=kT[pr, kb * 128:(kb + 1) * 128], rhs=qT[pr, cq:cq + N],
                                start=True, stop=True),
                                reads=["kT", "qT"], writes=[PS[sk]])
                            f.op("act", lambda e: e.activation(out=Pt[sk][:, 0:N], in_=ps[sk][:, 0:N],
                                                               func=AF.Exp, scale=0.125),
                                 reads=[PS[sk]], writes=["Pt%d" % sk])
                            moff = ((cq - 128 * kb) // 128 + 3) * 128
                            me = ("dve", "dve", "pool")[mrr[0] % 3]
                            mrr[0] += 1
                            f.op(me, lambda e: e.tensor_tensor(
                                out=Pt[sk][:, 0:N], in0=Pt[sk][:, 0:N], in1=maskT[:, moff:moff + N], op=ALU.mult),
                                reads=["Pt%d" % sk, "maskT"], writes=["Pt%d" % sk])

                        def stage2(idx):
                            a, qg, kb, nkb = items[idx]
                            pr = slice(64 * a, 64 * a + 64)
                            cq = max(128 * kb, 512 * qg)
                            N = 512 * (qg + 1) - cq
                            sk = idx % 4
                            po = 4 + (qg % 2)
                            oc = cq - 512 * qg
                            f.op("pe", lambda e: e.matmul(
                                ps[po][:, oc:oc + N], lhsT=V[:, kb, a, :], rhs=Pt[sk][:, 0:N],
                                start=(kb == 0), stop=(kb == nkb - 1)),
                                reads=["V", "Pt%d" % sk], writes=[PS[po]])
                            if kb == nkb - 1:
                                qs = slice(qg * 512, (qg + 1) * 512)
                                f.op("act", lambda e: e.activation(out=rc[64:128, :], in_=ps[po][64:128, :], func=AF.Ln),
                                     reads=[PS[po]], writes=["rtmp0"])
                                f.op("act", lambda e: e.activation(out=rc[64:128, :], in_=rc[64:128, :], func=AF.Exp,
                                                                   scale=-1.0),
                                     reads=["rtmp0"], writes=["rtmp0"])
                                f.op("dve", lambda e: e.tensor_tensor(out=atmp[pr, :], in0=ps[po][0:64, :],
                                                                      in1=rc[64:128, :], op=ALU.mult),
                                     reads=[PS[po], "rtmp0"], writes=["rtmp1"])
                                f.op("pool", lambda e: e.tensor_tensor(out=mT[pr, hp, qs], in0=atmp[pr, :],
                                                                       in1=mT[pr, hp, qs], op=ALU.mult),
                                     reads=["rtmp1", ("mT", hp)], writes=[("mT", hp)])

                        for idx in range(len(items) + LAG):
                            if idx < len(items):
                                stage1(idx)
                            if idx - LAG >= 0:
                                stage2(idx - LAG)
                    f.barrier()

                with ExitStack() as ss_:
                    def ssb(name, shape, dtype):
                        return ss_.enter_context(nc.sbuf_tensor(nk(name), list(shape), dtype))
                    NPW = 17 + 7
                    PR = ssb("PR", [128, 16, 3 * 24 + 4], F32)
                    pa = ssb("pa", [128, 16, 12], F32)
                    pi32 = ssb("pi32", [128, 16], I32)
                    XR = ssb("XR", [128, S], F32)
                    XI = ssb("XI", [128, S], F32)
                    cs_ = [ssb("cs%d" % k, [128, 2, 128], F32) for k in range(2)]
                    Xb = [ssb("Xb%d" % k, [128, 512], BF16) for k in range(2)]
                    uT = ssb("uT", [128, S], BF16)
                    bnat = ssb("bnat", [128, 2, 16, 16], F32)
                    cnat = ssb("cnat", [128, 2, 64], F32)
                    padT = ssb("padT", [128, 128], BF16)
                    padTf = ssb("padTf", [128, 2, 128], F32)
                    Bpad = ssb("Bpad", [128, 4, 2, 128], BF16)
                    Cpad = ssb("Cpad", [128, 4, 2, 128], BF16)
                    cf = ssb("cf", [128, 4, 128], F32)
                    dvec = ssb("dvec", [128, 4], F32)
                    glub = ssb("glub", [128, 4], F32)
                    gl = [ssb("gl%d" % k, [128, 512], F32) for k in range(2)] + [XR[:, 0:512]]

                    def ld_gp(dst_col, name):
                        for e_ in range(2):
                            f.dma("sp", pa[e_ * 64:(e_ + 1) * 64, :, dst_col],
                                  bass.AP(dt_[name], i * 2048 + e_ * 64, [[1, 64], [128, 16]]),
                                  writes=["pa"], allow_slow_non_contiguous=True)
                    ld_gp(0, "ssm_a_re")
                    ld_gp(1, "ssm_a_im")
                    for e_ in range(2):
                        f.dma("sp", pa[e_ * 64:(e_ + 1) * 64, :, 2],
                              bass.AP(dt_["ssm_log_dt"], i * 32 + e_, [[0, 64], [2, 16]]),
                              writes=["pa"], allow_slow_non_contiguous=True)
                    f.dma("sp", dvec[:], sd_d[i].rearrange("(c p) -> p c", p=128), writes=["dvec"],
                          allow_slow_non_contiguous=True)
                    f.dma("sp", glub[:], glb_d[i].rearrange("(c p) -> p c", p=128), writes=["glub"],
                          allow_slow_non_contiguous=True)

                    def pop(eng, fn, w=("pa",)):
                        f.op(eng, fn, reads=["pa", "PR"], writes=list(w))
                    A = lambda c: pa[:, :, c]
                    pop("act", lambda e: e.activation(out=A(3), in_=A(2), func=AF.Exp))
                    pop("dve", lambda e: e.tensor_tensor(out=A(4), in0=A(0), in1=A(3), op=ALU.mult))
                    pop("dve", lambda e: e.tensor_tensor(out=A(5), in0=A(1), in1=A(3), op=ALU.mult))
                    pop("act", lambda e: e.activation(out=A(4), in_=A(4), func=AF.Exp))

                    def sin_of(dst, shift):
                        pop("dve", lambda e: e.tensor_scalar(out=A(6), in0=A(5), scalar1=shift, scalar2=1.0 / TWO_PI,
                                                             op0=ALU.add, op1=ALU.mult))
                        f.op("dve", lambda e: e.tensor_copy(out=pi32[:], in_=A(6)), reads=["pa"], writes=["pi32"])
                        f.op("dve", lambda e: e.tensor_copy(out=A(7), in_=pi32[:]), reads=["pi32"], writes=["pa"])
                        pop("dve", lambda e: e.tensor_tensor(out=A(6), in0=A(6), in1=A(7), op=ALU.subtract))
                        pop("dve", lambda e: e.tensor_scalar(out=A(6), in0=A(6), scalar1=TWO_PI, scalar2=math.pi,
                                                             op0=ALU.mult, op1=ALU.min))
                        pop("dve", lambda e: e.tensor_scalar(out=A(6), in0=A(6), scalar1=-math.pi, scalar2=None,
                                                             op0=ALU.max))
                        pop("act", lambda e: e.activation(out=dst, in_=A(6), func=AF.Sin))
                    sin_of(A(8), 0.0)
                    sin_of(A(9), math.pi / 2)
                    P3 = lambda k, c: PR[:, :, 3 * k + c]
                    pop("dve", lambda e: e.tensor_tensor(out=P3(0, 0), in0=A(4), in1=A(9), op=ALU.mult), w=("PR",))
                    pop("dve", lambda e: e.tensor_tensor(out=P3(0, 1), in0=A(4), in1=A(8), op=ALU.mult), w=("PR",))

                    def cmul(dst, a_, b_):
                        pop("dve", lambda e: e.tensor_tensor(out=A(6), in0=P3(a_, 0), in1=P3(b_, 0), op=ALU.mult))
                        pop("dve", lambda e: e.tensor_tensor(out=A(7), in0=P3(a_, 1), in1=P3(b_, 1), op=ALU.mult))
                        pop("dve", lambda e: e.tensor_tensor(out=A(10), in0=P3(a_, 0), in1=P3(b_, 1), op=ALU.mult))
                        pop("dve", lambda e: e.tensor_tensor(out=A(11), in0=P3(a_, 1), in1=P3(b_, 0), op=ALU.mult))
                        pop("dve", lambda e: e.tensor_tensor(out=P3(dst, 0), in0=A(6), in1=A(7), op=ALU.subtract), w=("PR",))
                        pop("dve", lambda e: e.tensor_tensor(out=P3(dst, 1), in0=A(10), in1=A(11), op=ALU.add), w=("PR",))
                    for j in range(1, 16):
                        cmul(j, j - 1, 0)
                    for k in range(16, 16 + 7):
                        cmul(k, k - 1, k - 1)
                    for k in range(23):
                        pop("dve", lambda e, k=k: e.tensor_scalar(out=P3(k, 2), in0=P3(k, 1), scalar1=-1.0, scalar2=None,
                                                                  op0=ALU.mult), w=("PR",))
                    FR = PR[:, :, 72]
                    FI = PR[:, :, 73]
                    pop("dve", lambda e: e.tensor_scalar(out=A(6), in0=P3(0, 0), scalar1=-1.0, scalar2=None, op0=ALU.add))
                    pop("dve", lambda e: e.tensor_tensor(out=A(7), in0=A(0), in1=A(0), op=ALU.mult))
                    pop("dve", lambda e: e.tensor_tensor(out=A(10), in0=A(1), in1=A(1), op=ALU.mult))
                    pop("dve", lambda e: e.tensor_tensor(out=A(7), in0=A(7), in1=A(10), op=ALU.add))
                    pop("dve", lambda e: e.reciprocal(out=A(7), in_=A(7)))
                    pop("dve", lambda e: e.tensor_tensor(out=A(10), in0=A(6), in1=A(0), op=ALU.mult))
                    pop("dve", lambda e: e.tensor_tensor(out=A(11), in0=P3(0, 1), in1=A(1), op=ALU.mult))
                    pop("dve", lambda e: e.tensor_tensor(out=A(10), in0=A(10), in1=A(11), op=ALU.add))
                    pop("dve", lambda e: e.tensor_tensor(out=FR, in0=A(10), in1=A(7), op=ALU.mult), w=("PR",))
                    pop("dve", lambda e: e.tensor_tensor(out=A(10), in0=P3(0, 1), in1=A(0), op=ALU.mult))
                    pop("dve", lambda e: e.tensor_tensor(out=A(11), in0=A(6), in1=A(1), op=ALU.mult))
                    pop("dve", lambda e: e.tensor_tensor(out=A(10), in0=A(10), in1=A(11), op=ALU.subtract))
                    pop("dve", lambda e: e.tensor_tensor(out=FI, in0=A(10), in1=A(7), op=ALU.mult), w=("PR",))
                    for ri, name in enumerate(("ssm_b_re", "ssm_b_im")):
                        for e_ in range(2):
                            f.dma("sp", bnat[e_ * 64:(e_ + 1) * 64, ri, :, :],
                                  bass.AP(dt_[name], i * 32 * 1024 + e_ * 1024,
                                          [[16, 64], [2048, 16], [1, 16]]), writes=["bnat"])

                    XALL = [["X0"] + [("XR", s_) for s_ in range(16)], ["X1"] + [("XI", s_) for s_ in range(16)]]

                    def PSC(q, k, c):
                        return PR[:, q, 3 * k + c:3 * k + c + 1]

                    for j in range(4):
                        for ri, name in enumerate(("ssm_c_re", "ssm_c_im")):
                            f.dma("sp", cnat[:, ri, :],
                                  bass.AP(dt_[name], i * 32 * 1024 + j * 8 * 1024, [[64, 128], [1, 64]]),
                                  writes=["cnat"])
                        for qq in range(4):
                            q = j * 4 + qq
                            for ri in range(2):
                                f.op("pool", lambda e: e.memset(padT[:], 0.0), writes=["padT"])
                                for e_ in range(2):
                                    co = 16 * (2 * qq + e_)
                                    f.op("dve", lambda e, e_=e_, co=co, ri=ri, q=q: e.tensor_copy(
                                        out=padT[e_ * 64:(e_ + 1) * 64, co:co + 16],
                                        in_=bnat[e_ * 64:(e_ + 1) * 64, ri, q, :]),
                                        reads=["bnat"], writes=["padT"])
                                f.op("pe", lambda e: e.transpose(pst[0][:, 0:128], padT[:], ident[:]),
                                     reads=["padT", "ident"], writes=[PST[0]])
                                f.op("act", lambda e, qq=qq, ri=ri: e.copy(out=Bpad[:, qq, ri, :], in_=pst[0][:, 0:128]),
                                     reads=[PST[0]], writes=["Bpad"])
                            for ri in range(2):
                                for e_ in range(2):
                                    g8 = 2 * qq + e_
                                    f.op("dve", lambda e, e_=e_, ri=ri, g8=g8: e.tensor_scalar(
                                        out=padTf[:, ri, e_ * 64:(e_ + 1) * 64], in0=cnat[:, ri, :],
                                        scalar1=gmask[:, g8:g8 + 1], scalar2=None, op0=ALU.mult),
                                        reads=["cnat", "gmask"], writes=["padTf"])
                            for ri in range(2):
                                f.op("pe", lambda e, ri=ri: e.matmul(ps[4][:, ri * 128:(ri + 1) * 128], lhsT=padTf[:, ri, :],
                                                                     rhs=identf[:], start=True, stop=True),
                                     reads=["padTf", "identf"], writes=[PS[4]])
                            fr = PR[:, q, 72:73]
                            fi = PR[:, q, 73:74]
                            f.op("act", lambda e: e.copy(out=cf[:, 0:2, :], in_=ps[4][:, 0:256].rearrange("p (a b) -> p a b", a=2)),
                                 reads=[PS[4]], writes=["cf"])
                            f.op("dve", lambda e, fr=fr: e.tensor_scalar(out=cf[:, 2, :], in0=cf[:, 0, :], scalar1=fr, scalar2=None,
                                                                         op0=ALU.mult), reads=["cf", "PR"], writes=["cf"])
                            f.op("dve", lambda e, fi=fi: e.tensor_scalar(out=cf[:, 3, :], in0=cf[:, 1, :], scalar1=fi, scalar2=None,
                                                                         op0=ALU.mult), reads=["cf", "PR"], writes=["cf"])
                            f.op("dve", lambda e, qq=qq: e.tensor_tensor(out=Cpad[:, qq, 0, :], in0=cf[:, 2, :], in1=cf[:, 3, :],
                                                                         op=ALU.subtract), reads=["cf"], writes=["Cpad"])
                            f.op("dve", lambda e, fi=fi: e.tensor_scalar(out=cf[:, 2, :], in0=cf[:, 0, :], scalar1=fi, scalar2=-1.0,
                                                                         op0=ALU.mult, op1=ALU.mult), reads=["cf", "PR"], writes=["cf"])
                            f.op("dve", lambda e, fr=fr: e.tensor_scalar(out=cf[:, 3, :], in0=cf[:, 1, :], scalar1=fr, scalar2=None,
                                                                         op0=ALU.mult), reads=["cf", "PR"], writes=["cf"])
                            f.op("dve", lambda e, qq=qq: e.tensor_tensor(out=Cpad[:, qq, 1, :], in0=cf[:, 2, :], in1=cf[:, 3, :],
                                                                         op=ALU.subtract), reads=["cf"], writes=["Cpad"])
                        wb, wk = load_chunk(ewin_d[i][:, 4096 + j * 128:4096 + (j + 1) * 128])
                        for tg in range(4):
                            sl = slice(tg * 512, (tg + 1) * 512)
                            proj_fm(wb, wk, 4, tg)
                            f.op("act", lambda e, sl=sl: e.copy(out=uT[:, sl], in_=ps[4][:]), reads=[PS[4]], writes=["uT"])
                        for qq in range(4):
                            q = j * 4 + qq
                            for ri, X in enumerate((XR, XI)):
                                for tg in range(4):
                                    sl = slice(tg * 512, (tg + 1) * 512)
                                    pk = 4 + (tg % 2)
                                    f.op("pe", lambda e, sl=sl, pk=pk, ri=ri, qq=qq: e.matmul(
                                        ps[pk][:], lhsT=Bpad[:, qq, ri, :], rhs=uT[:, sl], start=True, stop=True),
                                        reads=["Bpad", "uT"], writes=[PS[pk]])
                                    f.op("act", lambda e, sl=sl, pk=pk, X=X: e.copy(out=X[:, sl], in_=ps[pk][:]),
                                         reads=[PS[pk]], writes=XALL[ri])
                            XRv = XR[:].rearrange("p (c s) -> p c s", s=16)
                            XIv = XI[:].rearrange("p (c s) -> p c s", s=16)

                            def cstep(oR, oI, iR, iI, k, kOR, kOI, kIR, kII):
                                for (o_, i_, c_, ko, ki) in ((oR, iR, 0, kOR, kIR), (oI, iR, 1, kOI, kIR),
                                                             (oR, iI, 2, kOR, kII), (oI, iI, 0, kOI, kII)):
                                    f.op("dve", lambda e, o_=o_, i_=i_, c_=c_: e.scalar_tensor_tensor(
                                        out=o_, in0=i_, scalar=PSC(q, k, c_), in1=o_, op0=ALU.mult, op1=ALU.add),
                                        reads=[ki, "PR"], writes=[ko])
                            for s_ in range(1, 16):
                                cstep(XRv[:, :, s_], XIv[:, :, s_], XRv[:, :, s_ - 1], XIv[:, :, s_ - 1], 0,
                                      ("XR", s_), ("XI", s_), ("XR", s_ - 1), ("XI", s_ - 1))
                            cur = 0
                            f.op("pool", lambda e: e.tensor_copy(out=cs_[0][:, 0, :], in_=XRv[:, :, 15]),
                                 reads=[("XR", 15)], writes=[("cs", 0, 0)])
                            f.op("pool", lambda e: e.tensor_copy(out=cs_[0][:, 1, :], in_=XIv[:, :, 15]),
                                 reads=[("XI", 15)], writes=[("cs", 0, 1)])
                            for k in range(7):
                                sh = 1 << k
                                src = cs_[cur]
                                dst = cs_[1 - cur]
                                for c_ in range(2):
                                    f.op("pool", lambda e, src=src, dst=dst, c_=c_: e.tensor_copy(out=dst[:, c_, :], in_=src[:, c_, :]),
                                         reads=[("cs", cur, c_)], writes=[("cs", 1 - cur, c_)])
                                cstep(dst[:, 0, sh:128], dst[:, 1, sh:128], src[:, 0, 0:128 - sh], src[:, 1, 0:128 - sh],
                                      15 + k, ("cs", 1 - cur, 0), ("cs", 1 - cur, 1), ("cs", cur, 0), ("cs", cur, 1))
                                cur = 1 - cur
                            fin = cs_[cur]
                            for s_ in range(16):
                                cstep(XRv[:, 1:128, s_], XIv[:, 1:128, s_], fin[:, 0, 0:127], fin[:, 1, 0:127], s_,
                                      ("XR", s_), ("XI", s_), ("cs", cur, 0), ("cs", cur, 1))
                            for ri, X in enumerate((XR, XI)):
                                for tg in range(4):
                                    sl = slice(tg * 512, (tg + 1) * 512)
                                    bi = (ri * 4 + tg) % 2
                                    cast(Xb[bi][:], X[:, sl], XALL[ri], ["Xb%d" % bi])
                                    f.op("pe", lambda e, tg=tg, bi=bi, ri=ri, qq=qq: e.matmul(
                                        ps[tg][:], lhsT=Cpad[:, qq, ri, :], rhs=Xb[bi][:],
                                        start=(qq == 0 and ri == 0), stop=(qq == 3 and ri == 1)),
                                        reads=["Cpad", "Xb%d" % bi], writes=[PS[tg]])
                        for tg in range(4):
                            sl = slice(tg * 512, (tg + 1) * 512)
                            f.op("dve", lambda e, sl=sl, tg=tg, j=j: e.scalar_tensor_tensor(out=gl[0][:], in0=uT[:, sl],
                                                                                            scalar=dvec[:, j:j + 1], in1=ps[tg][:],
                                                                                            op0=ALU.mult, op1=ALU.add),
                                 reads=[PS[tg], "uT", "dvec"], writes=["gl0"])
                            f.op("pool", lambda e: e.tensor_tensor(out=gl[1][:], in0=gl[0][:], in1=gl[0][:], op=ALU.mult),
                                 reads=["gl0"], writes=["gl1"])
                            f.op("pool", lambda e: e.tensor_scalar(out=gl[1][:], in0=gl[1][:], scalar1=0.044715, scalar2=1.0,
                                                                   op0=ALU.mult, op1=ALU.add), reads=["gl1"], writes=["gl1"])
                            f.op("pool", lambda e: e.tensor_tensor(out=gl[1][:], in0=gl[1][:], in1=gl[0][:], op=ALU.mult),
                                 reads=["gl1", "gl0"], writes=["gl1"])
                            f.op("act", lambda e: e.activation(out=gl[2][:], in_=gl[1][:], func=AF.Sigmoid,
                                                               scale=2.0 * math.sqrt(2.0 / math.pi)),
                                 reads=["gl1"], writes=["gl2"] + XALL[0])
                            f.op("dve", lambda e, sl=sl, j=j: e.tensor_tensor(out=mT[:, 8 + j, sl], in0=gl[0][:], in1=gl[2][:],
                                                                              op=ALU.mult),
                                 reads=["gl0", "gl2"] + XALL[0], writes=[("mT", 8 + j)])
                    for tg in range(4):
                        sl = slice(tg * 512, (tg + 1) * 512)
                        for fo in range(4):
                            wb, wk = load_chunk(glw_d[i][:, fo * 128:(fo + 1) * 128], kc=4)
                            for jj in range(4):
                                f.op("pe", lambda e, fo=fo, jj=jj, sl=sl: e.matmul(
                                    ps[fo][:], lhsT=wb[:, jj, :], rhs=mT[:, 8 + jj, sl],
                                    start=(jj == 0), stop=(jj == 3)),
                                    reads=[wk, ("mT", 8 + jj)], writes=[PS[fo]], inc=(jj == 3))
                        for fo in range(4):
                            wb, wk = load_chunk(ewin_d[i][:, 4608 + fo * 128:4608 + (fo + 1) * 128])
                            proj_fm(wb, wk, 4, tg)
                            f.op("act", lambda e: e.activation(out=gl[0][:], in_=ps[4][:], func=AF.Silu),
                                 reads=[PS[4]], writes=["gl0"])
                            f.op("act", lambda e, fo=fo: e.activation(out=gl[2][:], in_=ps[fo][:], func=AF.Sigmoid,
                                                                      bias=glub[:, fo:fo + 1]),
                                 reads=[PS[fo], "glub"], writes=["gl2"] + XALL[0])
                            f.op("pool", lambda e: e.tensor_tensor(out=gl[1][:], in0=gl[2][:], in1=gl[0][:], op=ALU.mult),
                                 reads=["gl2", "gl0"] + XALL[0], writes=["gl1"])
                            f.op("dve", lambda e, fo=fo, sl=sl: e.tensor_tensor(out=mT[:, 8 + fo, sl], in0=mT[:, 8 + fo, sl],
                                                                                in1=gl[1][:], op=ALU.mult),
                                 reads=["gl1", ("mT", 8 + fo)] + [PS[k] for k in range(4)], writes=[("mT", 8 + fo)])
                    f.barrier()

                with ExitStack() as os_:
                    wout = os_.enter_context(nc.sbuf_tensor(nk("ewout"), [128, 12, D], BF16))
                    wost = os_.enter_context(nc.sbuf_tensor(nk("ewost"), [128, 2, D], F32))
                    ytmp_box[0] = os_.enter_context(nc.sbuf_tensor(nk("ytmp"), [128, 2, 512], F32))
                    load_big(wout, "ewout", ewout_d[i], 12, D, wost, "ewost")
                    f.dma("sp", gain[:], bcast("post_norm", l * D, D), writes=["gain"])
                    for b in range(NB):
                        outproj_block(b, lambda kc, b=b: mT[:, kc, b * 128:(b + 1) * 128],
                                      [("mT", ch) for ch in range(12)], 12, wout, "ewout", l)
                    f.barrier()

        for l in layers:
            if l % 2 == 0:
                even_layer(l)
            else:
                odd_layer(l)

        for b in range(NB):
            f.dma("sp", out_d[b * 128:(b + 1) * 128, :], xres[:, b, :], reads=[("x", b)], key="out")
        nc.sync.wait_ge(f.dsem["out"][0], f.dsem["out"][1])
    return nc


_CACHE = {}


def prep_inputs(inputs):
    w = {k: np.ascontiguousarray(np.asarray(v, dtype=np.float32)) for k, v in inputs.items()}
    perm = swap_perm()
    shared = {k: v for k, v in w.items() if k != "x"}
    shared["even_w_sw"] = np.ascontiguousarray(w["even_w_in"][:, :, 0:2048][:, :, perm])
    shared.update(host_consts())
    return w["x"], shared


def kernel(**inputs):
    x, shared = prep_inputs(inputs)
    if "nc" not in _CACHE:
        _CACHE["nc"] = build()
    nc = _CACHE["nc"]
    in_maps = []
    for c in range(8):
        m = dict(shared)
        m["x"] = np.ascontiguousarray(x[c])
        in_maps.append(m)
    res = run_bass_kernel_spmd(nc, in_maps, core_ids=list(range(8)))
    return np.stack([np.asarray(r["out"], dtype=np.float32) for r in res.results], axis=0)
```

```python
import math
import numpy as np
import concourse.bass as bass
import concourse.mybir as mybir
from concourse.bass_utils import run_bass_kernel_spmd

F32 = mybir.dt.float32
BF16 = mybir.dt.bfloat16
I32 = mybir.dt.int32
ALU = mybir.AluOpType
AF = mybir.ActivationFunctionType
AX = mybir.AxisListType

S = 2048
D = 1024
NB = 16
EPS = 1e-6
TWO_PI = 2.0 * math.pi


class FW:
    def __init__(self, nc):
        self.nc = nc
        self.eng = {"pe": nc.tensor, "act": nc.scalar, "dve": nc.vector,
                    "pool": nc.gpsimd, "sp": nc.sync}
        self.sem = {}
        self.cnt = {}
        for e in self.eng:
            self.sem[e] = nc.alloc_semaphore("s_" + e)
            self.cnt[e] = 0
        self.waited = {}
        self.lastw = {}
        self.rd = {}
        self.dsem = {}

    def _deps(self, reads, writes):
        deps = {}

        def add(t):
            if t is not None and deps.get(t[0], 0) < t[1]:
                deps[t[0]] = t[1]
        for k in reads:
            add(self.lastw.get(k))
        for k in writes:
            add(self.lastw.get(k))
            for sk, v in self.rd.get(k, {}).items():
                add((sk, v))
        return deps

    def _semof(self, sk):
        if isinstance(sk, tuple):
            return self.dsem[sk[1]][0]
        return self.sem[sk]

    def _emit_waits(self, e, deps):
        for sk, v in deps.items():
            if sk == e and v > self.cnt[e]:
                continue
            if self.waited.get((e, sk), 0) < v:
                self.eng[e].wait_ge(self._semof(sk), v)
                self.waited[(e, sk)] = v

    def _record(self, t, reads, writes):
        for k in writes:
            self.lastw[k] = t
            self.rd[k] = {}
        for k in reads:
            d = self.rd.setdefault(k, {})
            if d.get(t[0], 0) < t[1]:
                d[t[0]] = t[1]

    def op(self, e, fn, reads=(), writes=(), inc=True):
        self._emit_waits(e, self._deps(reads, writes))
        ins = fn(self.eng[e])
        if inc:
            ins.then_inc(self.sem[e], 1)
            self.cnt[e] += 1
            t = (e, self.cnt[e])
        else:
            t = (e, self.cnt[e] + 1)
        self._record(t, reads, writes)
        return ins

    def dma(self, q, out, in_, reads=(), writes=(), key=None, **kw):
        if key is None:
            key = writes[0] if writes else reads[0]
        if key not in self.dsem:
            self.dsem[key] = [self.nc.alloc_semaphore("d%d" % len(self.dsem)), 0]
        self._emit_waits(q, self._deps(reads, writes))
        ins = self.eng[q].dma_start(out=out, in_=in_, **kw)
        ins.then_inc(self.dsem[key][0], 16)
        self.dsem[key][1] += 16
        t = (("dma", key), self.dsem[key][1])
        self._record(t, reads, writes)
        return ins

    def barrier(self):
        for e in self.eng:
            deps = {}
            for o in self.eng:
                if o != e and self.cnt[o] > 0:
                    deps[o] = self.cnt[o]
            for key, (s, c) in self.dsem.items():
                if c > 0:
                    deps[("dma", key)] = c
            self._emit_waits(e, deps)


def _mult(d):
    d = np.asarray(d)
    m = ((d >= 0) & (d <= 128)).astype(np.float32)
    m += ((d >= 0) & (d <= 512) & (d % 4 == 0)).astype(np.float32)
    m += ((d >= 0) & (d % 16 == 0)).astype(np.float32)
    return m


def host_consts():
    c = {}
    c["c_ident"] = np.eye(128, dtype=np.float32)
    j = np.arange(128)[:, None]
    t = np.arange(128)[None, :]
    c["c_mask"] = np.concatenate([_mult(128 * dl + t - j) for dl in range(-3, 16)], axis=1).astype(np.float32)
    bands = np.zeros((128, 4, 3, 128), np.float32)
    tp = np.arange(128)[:, None]
    tt = np.arange(128)[None, :]
    for wi, w in enumerate((2, 4, 8, 16)):
        dcur = tt - tp
        bands[:, wi, 0, :] = ((dcur >= 0) & (dcur < w)) / float(w) - (dcur == 0)
        dprev = tt + 128 - tp
        bands[:, wi, 1, :] = ((dprev >= 0) & (dprev < w)) / float(w)
        cnt = np.minimum(tt + 1, w).astype(np.float32)
        bands[:, wi, 2, :] = ((dcur >= 0) & (dcur < w)) / cnt - (dcur == 0)
    c["c_bands"] = bands.reshape(128, 4 * 3 * 128)
    half = 8
    inv = 500000.0 ** (-np.arange(0, 16, 2, dtype=np.float32) / 16.0)
    ang = np.arange(S, dtype=np.float32)[None, :] * inv[:, None]
    C = np.ones((64, S), np.float32)
    Sg = np.zeros((64, S), np.float32)
    C[0:8] = np.cos(ang)
    C[8:16] = np.cos(ang)
    Sg[0:8] = -np.sin(ang)
    Sg[8:16] = np.sin(ang)
    c["c_ropec"] = np.concatenate([C, C], 0)
    c["c_ropes"] = np.concatenate([Sg, Sg], 0)
    gm = np.zeros((128, 8), np.float32)
    for g in range(8):
        gm[g * 16:(g + 1) * 16, g] = 1.0
    c["c_gmask"] = gm
    return c


def swap_perm():
    perm = np.arange(2048)
    for blk in range(2048 // 64):
        b = blk * 64
        perm[b:b + 8] = np.arange(b + 8, b + 16)
        perm[b + 8:b + 16] = np.arange(b, b + 8)
    return perm


def build(layers=(0, 1, 2, 3)):
    nc = bass.Bass("TRN2", target_bir_lowering=False)
    dt_ = {}

    def din(name, shape):
        h = nc.dram_tensor(name, list(shape), F32, kind="ExternalInput")
        dt_[name] = h
        return h.ap()

    x_d = din("x", [S, D])
    pre_d = din("pre_norm", [4, D])
    post_d = din("post_norm", [4, D])
    ewin_d = din("even_w_in", [2, D, 5120])
    ewsw_d = din("even_w_sw", [2, D, 2048])
    ewout_d = din("even_w_out", [2, 1536, D])
    are_d = din("ssm_a_re", [2, 32, 64])
    aim_d = din("ssm_a_im", [2, 32, 64])
    ldt_d = din("ssm_log_dt", [2, 32])
    bre_d = din("ssm_b_re", [2, 32, 64, 16])
    bim_d = din("ssm_b_im", [2, 32, 64, 16])
    cre_d = din("ssm_c_re", [2, 32, 16, 64])
    cim_d = din("ssm_c_im", [2, 32, 16, 64])
    sd_d = din("ssm_d", [2, 512])
    glw_d = din("ssm_glu_w", [2, 512, 512])
    glb_d = din("ssm_glu_b", [2, 512])
    owin_d = din("odd_w_in", [2, D, 4096])
    pw_d = din("pool_w", [2, 4, 512, 512])
    psc_d = din("pool_scale", [2, 2048])
    owout_d = din("odd_w_out", [2, 2048, D])
    cid_d = din("c_ident", [128, 128])
    cmask_d = din("c_mask", [128, 19 * 128])
    cband_d = din("c_bands", [128, 12 * 128])
    cropec_d = din("c_ropec", [128, S])
    cropes_d = din("c_ropes", [128, S])
    cgm_d = din("c_gmask", [128, 8])
    out_d = nc.dram_tensor("out", [S, D], F32, kind="ExternalOutput").ap()

    f = FW(nc)
    uid = [0]

    def nk(p):
        uid[0] += 1
        return "%s%d" % (p, uid[0])

    def bcast(name, off, n):
        return bass.AP(dt_[name], off, [[0, 128], [1, n]])

    from contextlib import ExitStack
    with ExitStack() as es:
        def sb(name, shape, dtype):
            return es.enter_context(nc.sbuf_tensor(name, list(shape), dtype))

        def psb(name, shape, dtype):
            return es.enter_context(nc.psum_tensor(name, list(shape), dtype))

        xres = sb("xres", [128, NB, D], F32)
        ps = [psb("ps%d" % i, [128, 512], F32) for i in range(6)]
        pst = [psb("pst%d" % i, [128, 1024], BF16) for i in range(2)]
        PS = ["ps%d" % i for i in range(6)]
        PST = ["pst0", "pst1"]
        ident = sb("ident", [128, 128], BF16)
        identf = sb("identf", [128, 128], F32)
        gain = sb("gain", [128, D], F32)
        junk = sb("junk", [128, D], BF16)
        stat = sb("stat", [128, 64], F32)
        gmask = sb("gmask", [128, 8], F32)
        wbf = [sb("wbf%d" % i, [128, 8, 128], BF16) for i in range(4)]
        wctr = [0, 0]

        f.dma("sp", identf[:], cid_d, writes=["identf"])
        f.op("dve", lambda e: e.tensor_copy(out=ident[:], in_=identf[:]), reads=["identf"], writes=["ident"])
        f.dma("sp", gmask[:], cgm_d, writes=["gmask"])
        for b in range(NB):
            f.dma("sp", xres[:, b, :], x_d[b * 128:(b + 1) * 128, :], writes=[("x", b)])

        cast_rr = [0]

        def cast(out, in_, reads, writes):
            e = "act"
            if e == "pool":
                f.op("pool", lambda g: g.tensor_copy(out=out, in_=in_), reads=reads, writes=writes)
            else:
                f.op("act", lambda g: g.copy(out=out, in_=in_), reads=reads, writes=writes)

        class WS:
            DEPTH = 2

            def __init__(self, srcs):
                self.srcs = srcs
                self.issued = 0
                self.cur = 0
                self.buf = {}

            def get(self):
                while self.issued < min(len(self.srcs), self.cur + 1 + WS.DEPTH):
                    src, kc = self.srcs[self.issued]
                    bi = wctr[1] % 4
                    wctr[1] += 1
                    f.dma("pool", wbf[bi][:, 0:kc, :], src.rearrange("(k p) n -> p k n", p=128),
                          writes=["wbf%d" % bi])
                    self.buf[self.issued] = bi
                    self.issued += 1
                bi = self.buf[self.cur]
                self.cur += 1
                return wbf[bi], "wbf%d" % bi

        def load_big(dst, dkey, src_ap, kc, step=4):
            keys = []
            for k0 in range(0, kc, step):
                k1 = min(kc, k0 + step)
                f.dma("pool", dst[:, k0:k1, :], src_ap[k0 * 128:k1 * 128, :].rearrange("(k p) n -> p k n", p=128),
                      writes=[(dkey, k0)])
                keys.append((dkey, k0))
            return keys

        def load_big_hw(dst, dkey, src_ap, kc, stg, skey):
            keys = []
            for k0 in range(0, kc, 2):
                f.dma("sp", stg[:, :, :], src_ap[k0 * 128:(k0 + 2) * 128, :].rearrange("(k p) n -> p k n", p=128),
                      writes=[skey])
                f.op("act", lambda g: g.copy(out=dst[:, k0:k0 + 2, :], in_=stg[:, :, :]), reads=[skey], writes=[(dkey, k0)])
                keys.append((dkey, k0))
            return keys

        scol = [0]

        def newcol():
            scol[0] = (scol[0] + 1) % 64
            return scol[0]

        def rstd_from_ss(c_ss):
            c1 = newcol()
            c2 = newcol()
            f.op("act", lambda e: e.activation(out=stat[:, c1:c1 + 1], in_=stat[:, c_ss:c_ss + 1], func=AF.Sqrt,
                                               scale=1.0 / D, bias=EPS),
                 reads=[("st", c_ss)], writes=[("st", c1)])
            f.op("dve", lambda e: e.reciprocal(out=stat[:, c2:c2 + 1], in_=stat[:, c1:c1 + 1]),
                 reads=[("st", c1)], writes=[("st", c2)])
            return c2

        def prenorm_block(b, dst, dkey):
            c = newcol()
            f.op("act", lambda e: e.activation(out=junk[:], in_=xres[:, b, :], func=AF.Square,
                                               accum_out=stat[:, c:c + 1]),
                 reads=[("x", b)], writes=["junk", ("st", c)])
            c2 = rstd_from_ss(c)
            f.op("dve", lambda e: e.scalar_tensor_tensor(out=dst, in0=xres[:, b, :], scalar=stat[:, c2:c2 + 1],
                                                         in1=gain[:], op0=ALU.mult, op1=ALU.mult),
                 reads=[("x", b), ("st", c2), "gain"], writes=[dkey])

        def transpose_block(src, skey, dst_ap, dkey, pi):
            for c in range(8):
                f.op("pe", lambda e, c=c: e.transpose(pst[pi][:, c * 128:(c + 1) * 128], src[:, c * 128:(c + 1) * 128],
                                                      ident[:]),
                     reads=[skey, "ident"], writes=[PST[pi]], inc=(c == 7))
            f.op("act", lambda e: e.copy(out=dst_ap, in_=pst[pi][:].rearrange("p (c t) -> p c t", c=8)),
                 reads=[PST[pi]], writes=[dkey])

        def outproj_block(b, mT_fn, mkeys, KC, wout, wkey, l):
            pp = (b % 2) * 2
            for fh in range(2):
                for kc in range(KC):
                    f.op("pe", lambda e, kc=kc, fh=fh: e.matmul(ps[pp + fh][:], lhsT=mT_fn(kc),
                                                                rhs=wout[:, kc, fh * 512:(fh + 1) * 512],
                                                                start=(kc == 0), stop=(kc == KC - 1)),
                         reads=list(mkeys) + list(wkey), writes=[PS[pp + fh]], inc=(kc == KC - 1))
            ca = newcol()
            cb = newcol()
            f.op("act", lambda e: e.activation(out=junk[:, 0:512], in_=ps[pp][:], func=AF.Square,
                                               accum_out=stat[:, ca:ca + 1]),
                 reads=[PS[pp]], writes=["junk", ("st", ca)])
            f.op("act", lambda e: e.activation(out=junk[:, 512:1024], in_=ps[pp + 1][:], func=AF.Square,
                                               accum_out=stat[:, cb:cb + 1]),
                 reads=[PS[pp + 1]], writes=["junk", ("st", cb)])
            cs = newcol()
            f.op("dve", lambda e: e.tensor_tensor(out=stat[:, cs:cs + 1], in0=stat[:, ca:ca + 1],
                                                  in1=stat[:, cb:cb + 1], op=ALU.add),
                 reads=[("st", ca), ("st", cb)], writes=[("st", cs)])
            c2 = rstd_from_ss(cs)
            for fh in range(2):
                tk = "ytmp%d" % fh
                f.op("dve", lambda e, fh=fh: e.scalar_tensor_tensor(out=ytmp_box[0][:, fh, :], in0=ps[pp + fh][:],
                                                                    scalar=stat[:, c2:c2 + 1],
                                                                    in1=gain[:, fh * 512:(fh + 1) * 512],
                                                                    op0=ALU.mult, op1=ALU.mult),
                     reads=[PS[pp + fh], ("st", c2), "gain"], writes=[tk])
                f.op("pool", lambda e, fh=fh: e.tensor_tensor(out=xres[:, b, fh * 512:(fh + 1) * 512],
                                                              in0=xres[:, b, fh * 512:(fh + 1) * 512],
                                                              in1=ytmp_box[0][:, fh, :], op=ALU.add),
                     reads=[("x", b), tk], writes=[("x", b)])

        ytmp_box = [None]

        def odd_layer(l):
            i = l // 2
            with ExitStack() as ls:
                def lsb(name, shape, dtype):
                    return ls.enter_context(nc.sbuf_tensor(nk(name), list(shape), dtype))
                hN = lsb("hN", [128, 5, D], BF16)
                hT = lsb("hTq", [128, 8, 512], BF16)
                phT = lsb("phT", [128, 8, 512], BF16)
                mixed = lsb("mixed", [128, 4, 512], BF16)
                mT = lsb("mTq", [128, 16, 512], BF16)
                sg = lsb("sg", [128, 512], F32)
                bandf = lsb("bandf", [128, 12, 128], F32)
                band = lsb("band", [128, 4, 4, 128], BF16)
                btmp = lsb("btmp", [128, 4, 128], F32)
                pwbf = lsb("pwbf", [128, 4, 512], BF16)
                wout = lsb("wout", [128, 16, D], BF16)
                pscale = lsb("pscale", [128, 16], F32)
                wost = [lsb("wost%d" % k_, [128, 2, D], F32) for k_ in range(2)]
                pwst = lsb("pwst", [128, 4, 512], F32)
                ytmp_box[0] = lsb("ytmp", [128, 2, 512], F32)
                f.dma("sp", bandf[:], cband_d.rearrange("p (a t) -> p a t", t=128), writes=["bandf"])
                bv = bandf[:].rearrange("p (w k) t -> p w k t", k=3)
                f.op("dve", lambda e: e.tensor_copy(out=band[:, :, 0:3, :], in_=bv), reads=["bandf"], writes=["band"])
                f.op("dve", lambda e: e.tensor_copy(out=btmp[:], in_=band[:, :, 2, :]), reads=["band"], writes=["btmp"])
                f.op("dve", lambda e: e.tensor_tensor(out=btmp[:], in0=bv[:, :, 2, :], in1=btmp[:], op=ALU.subtract),
                     reads=["bandf", "btmp"], writes=["btmp"])
                f.op("dve", lambda e: e.tensor_copy(out=band[:, :, 3, :], in_=btmp[:]), reads=["btmp"], writes=["band"])
                f.dma("sp", pscale[:], psc_d[i].rearrange("(c p) -> p c", p=128), writes=["pscale"],
                      allow_slow_non_contiguous=True)
                srcs = []
                for tq in range(4):
                    for g in range(4):
                        for j in range(4):
                            srcs.append((owin_d[i][:, g * 512 + j * 128:g * 512 + (j + 1) * 128], 8))
                        for dj in range(4):
                            col = 2048 + g * 512 + dj * 128
                            srcs.append((owin_d[i][:, col:col + 128], 8))
                ws = WS(srcs)
                for tq in range(4):
                    b0 = tq * 4
                    woutk = [("wout", 2 * k_) for k_ in range(8)]
                    wstep = [0]

                    def wout_step():
                        k_ = wstep[0]
                        wstep[0] += 1
                        if 1 <= k_ <= 8:
                            kk = k_ - 1
                            f.op("act", lambda g_: g_.copy(out=wout[:, 2 * kk:2 * kk + 2, :], in_=wost[kk % 2][:]),
                                 reads=["wost%d" % (kk % 2)], writes=[("wout", 2 * kk)])
                        if k_ < 8:
                            f.dma("sp", wost[k_ % 2][:],
                                  owout_d[i][k_ * 256:(k_ + 1) * 256, :].rearrange("(k p) n -> p k n", p=128),
                                  writes=["wost%d" % (k_ % 2)])
                    wout_step()
                    f.dma("sp", gain[:], bcast("pre_norm", l * D, D), writes=["gain"])
                    if tq > 0:
                        f.op("pool", lambda e: e.tensor_copy(out=hN[:, 0, :], in_=hN[:, 4, :]),
                             reads=[("hN", 4)], writes=[("hN", 0)])
                    for s_, b in enumerate(range(b0 - 1, b0 + 4)):
                        if s_ == 0:
                            continue
                        prenorm_block(b, hN[:, s_, :], ("hN", s_))
                    for s_ in range(1, 5):
                        transpose_block(hN[:, s_, :], ("hN", s_), hT[:, :, (s_ - 1) * 128:s_ * 128], "hT", s_ % 2)
                    for g in range(4):
                        f.dma("sp", pwst[:], pw_d[i, g].rearrange("(k p) n -> p k n", p=128), writes=["pwst"])
                        pwk = [("pwbf", 0), ("pwbf", 2)]
                        for s_ in range(1, 5):
                            b = b0 + s_ - 1
                            for hf in range(2):
                                pk = 4 + hf
                                for cc in range(4):
                                    c = hf * 4 + cc
                                    o = ps[pk][:, cc * 128:(cc + 1) * 128]
                                    lh = hN[:, s_, c * 128:(c + 1) * 128]
                                    if b == 0:
                                        f.op("pe", lambda e, o=o, lh=lh: e.matmul(o, lhsT=lh, rhs=band[:, g, 2, :],
                                                                                  start=True, stop=False),
                                             reads=[("hN", s_), "band"], writes=[PS[pk]], inc=False)
                                        f.op("pe", lambda e, o=o, lh=lh: e.matmul(o, lhsT=lh, rhs=band[:, g, 3, :],
                                                                                  start=False, stop=True),
                                             reads=[("hN", s_), "band"], writes=[PS[pk]], inc=(cc == 3))
                                    else:
                                        lp = hN[:, s_ - 1, c * 128:(c + 1) * 128]
                                        f.op("pe", lambda e, o=o, lh=lh: e.matmul(o, lhsT=lh, rhs=band[:, g, 0, :],
                                                                                  start=True, stop=False),
                                             reads=[("hN", s_), "band"], writes=[PS[pk]], inc=False)
                                        f.op("pe", lambda e, o=o, lp=lp: e.matmul(o, lhsT=lp, rhs=band[:, g, 1, :],
                                                                                  start=False, stop=True),
                                             reads=[("hN", s_ - 1), "band"], writes=[PS[pk]], inc=(cc == 3))
                                f.op("act", lambda e, hf=hf, pk=pk: e.copy(
                                    out=phT[:, hf * 4:(hf + 1) * 4, (s_ - 1) * 128:s_ * 128],
                                    in_=ps[pk][:].rearrange("p (c t) -> p c t", c=4)),
                                    reads=[PS[pk]], writes=["phT"])
                        for j in range(4):
                            wout_step()
                            wb, wk = ws.get()
                            pk = j % 2
                            for kc in range(8):
                                f.op("pe", lambda e, kc=kc: e.matmul(ps[pk][:], lhsT=wb[:, kc, :], rhs=phT[:, kc, :],
                                                                     start=(kc == 0), stop=(kc == 7)),
                                     reads=[wk, "phT"], writes=[PS[pk]], inc=(kc == 7))
                            f.op("act", lambda e, j=j, pk=pk: e.copy(out=mixed[:, j, :], in_=ps[pk][:]),
                                 reads=[PS[pk]], writes=[("mixed", j)])
                        for k0_ in (0, 2):
                            f.op("act", lambda g_, k0_=k0_: g_.copy(out=pwbf[:, k0_:k0_ + 2, :], in_=pwst[:, k0_:k0_ + 2, :]),
                                 reads=["pwst"], writes=[("pwbf", k0_)])
                        for dj in range(4):
                            wb, wk = ws.get()
                            pg = 2
                            for kc in range(8):
                                f.op("pe", lambda e, kc=kc: e.matmul(ps[pg][:], lhsT=wb[:, kc, :], rhs=hT[:, kc, :],
                                                                     start=(kc == 0), stop=(kc == 7)),
                                     reads=[wk, "hT"], writes=[PS[pg]], inc=(kc == 7))
                            f.op("act", lambda e: e.activation(out=sg[:], in_=ps[pg][:], func=AF.Silu),
                                 reads=[PS[pg]], writes=["sg"])
                            py = 3
                            for j in range(4):
                                f.op("pe", lambda e, j=j, dj=dj: e.matmul(ps[py][:], lhsT=pwbf[:, j, dj * 128:(dj + 1) * 128],
                                                                          rhs=mixed[:, j, :], start=(j == 0), stop=(j == 3)),
                                     reads=pwk + [("mixed", j)], writes=[PS[py]], inc=(j == 3))
                            ch = g * 4 + dj
                            f.op("dve", lambda e, ch=ch: e.scalar_tensor_tensor(out=mT[:, ch, :], in0=ps[py][:],
                                                                                scalar=pscale[:, ch:ch + 1], in1=sg[:],
                                                                                op0=ALU.mult, op1=ALU.mult),
                                 reads=[PS[py], "pscale", "sg"], writes=[("mT", ch)])
                    f.dma("sp", gain[:], bcast("post_norm", l * D, D), writes=["gain"])
                    for bb in range(4):
                        outproj_block(b0 + bb, lambda kc, bb=bb: mT[:, kc, bb * 128:(bb + 1) * 128],
                                      [("mT", ch) for ch in range(16)], 16, wout, woutk, l)
                f.barrier()

        def even_layer(l):
            i = l // 2
            with ExitStack() as ls:
                def lsb(name, shape, dtype):
                    return ls.enter_context(nc.sbuf_tensor(nk(name), list(shape), dtype))
                hT = lsb("hT", [128, 8, S], BF16)
                mT = lsb("mT", [128, 12, S], BF16)
                f.dma("sp", gain[:], bcast("pre_norm", l * D, D), writes=["gain"])
                with ExitStack() as hs_:
                    hNb = [hs_.enter_context(nc.sbuf_tensor(nk("hNb"), [128, D], BF16)) for k in range(2)]
                    for b in range(NB):
                        prenorm_block(b, hNb[b % 2][:], "hNb%d" % (b % 2))
                        transpose_block(hNb[b % 2], "hNb%d" % (b % 2), hT[:, :, b * 128:(b + 1) * 128], "hT", b % 2)
                    f.barrier()

                def proj_fm(wb, wk, pk, tg):
                    for kc in range(8):
                        f.op("pe", lambda e, kc=kc: e.matmul(ps[pk][:], lhsT=wb[:, kc, :],
                                                             rhs=hT[:, kc, tg * 512:(tg + 1) * 512],
                                                             start=(kc == 0), stop=(kc == 7)),
                             reads=[wk, "hT"], writes=[PS[pk]], inc=(kc == 7))

                with ExitStack() as as_:
                    def asb(name, shape, dtype):
                        return as_.enter_context(nc.sbuf_tensor(nk(name), list(shape), dtype))
                    qT = asb("qT", [128, S], BF16)
                    kT = asb("kT", [128, 2, S], BF16)
                    V = asb("V", [128, NB, 2, 128], BF16)
                    ropec = asb("ropec", [128, S], BF16)
                    ropes = asb("ropes", [128, S], BF16)
                    maskT = asb("maskT", [128, 19 * 128], BF16)
                    Pt = [asb("Pt%d" % k, [128, 512], BF16) for k in range(4)]
                    rtmp = [asb("rtmp%d" % k, [128, 512], F32) for k in range(2)]
                    rc = rtmp[0]
                    atmp = rtmp[1]
                    f.dma("pool", ropec[:], cropec_d, writes=["ropec"])
                    f.dma("pool", ropes[:], cropes_d, writes=["ropes"])
                    f.dma("pool", maskT[:], cmask_d, writes=["maskT"])
                    srcs = []
                    for hp_ in range(8):
                        c0_ = hp_ * 128
                        for (base_, swb_) in ((0, 0), (1024, 1024)):
                            srcs.append((ewin_d[i][:, base_ + c0_:base_ + c0_ + 128], 8))
                            srcs.append((ewsw_d[i][:, swb_ + c0_:swb_ + c0_ + 128], 8))
                        srcs.append((ewin_d[i][:, 3072 + c0_:3072 + c0_ + 128], 8))
                        srcs.append((ewin_d[i][:, 2048 + c0_:2048 + c0_ + 128], 8))
                    ws = WS(srcs)
                    f.op("pool", lambda e: e.memset(V[:, :, :, 64:128], 1.0), writes=["V"])
                    f.op("pool", lambda e: e.memset(kT[:], 0.0), writes=["kT"])
                    mrr = [0]
                    for hp in range(8):
                        c0 = hp * 128
                        for (dst, dk, base, swb) in ((qT, "qT", 0, 0), (kT, "kT", 1024, 1024)):
                            wb, wk = ws.get()
                            wb2, wk2 = ws.get()
                            for tg in range(4):
                                sl = slice(tg * 512, (tg + 1) * 512)
                                pa0 = 2 * (tg % 2)
                                proj_fm(wb, wk, pa0, tg)
                                proj_fm(wb2, wk2, pa0 + 1, tg)
                                r0 = "rtmp0"
                                r1 = "rtmp1"
                                f.op("dve", lambda e, sl=sl: e.tensor_tensor(out=rtmp[0][:], in0=ps[pa0][:], in1=ropec[:, sl],
                                                                             op=ALU.mult),
                                     reads=[PS[pa0], "ropec"], writes=[r0])
                                f.op("dve", lambda e, sl=sl: e.tensor_tensor(out=rtmp[1][:], in0=ps[pa0 + 1][:], in1=ropes[:, sl],
                                                                             op=ALU.mult),
                                     reads=[PS[pa0 + 1], "ropes"], writes=[r1])
                                if dk == "qT":
                                    f.op("dve", lambda e, sl=sl: e.tensor_tensor(out=qT[:, sl], in0=rtmp[0][:],
                                                                                 in1=rtmp[1][:], op=ALU.add),
                                         reads=[r0, r1], writes=[dk])
                                else:
                                    for a_ in range(2):
                                        pr_ = slice(64 * a_, 64 * a_ + 64)
                                        f.op("dve", lambda e, sl=sl, a_=a_, pr_=pr_: e.tensor_tensor(
                                            out=kT[pr_, a_, sl], in0=rtmp[0][pr_, :], in1=rtmp[1][pr_, :], op=ALU.add),
                                            reads=[r0, r1], writes=[dk])
                        wb, wk = ws.get()
                        for tg in range(4):
                            proj_fm(wb, wk, 2, tg)
                            f.op("act", lambda e, tg=tg: e.activation(out=mT[:, hp, tg * 512:(tg + 1) * 512], in_=ps[2][:],
                                                                      func=AF.Silu),
                                 reads=[PS[2]], writes=[("mT", hp)])
                        wb, wk = ws.get()
                        for b4 in range(4):
                            pk = 3
                            for bb in range(4):
                                b = b4 * 4 + bb
                                for kc in range(8):
                                    f.op("pe", lambda e, kc=kc, b=b, bb=bb: e.matmul(
                                        ps[pk][:, bb * 128:(bb + 1) * 128], lhsT=hT[:, kc, b * 128:(b + 1) * 128],
                                        rhs=wb[:, kc, :], start=(kc == 0), stop=(kc == 7)),
                                        reads=[wk, "hT"], writes=[PS[pk]], inc=(kc == 7 and bb == 3))
                            f.op("act", lambda e, b4=b4: e.copy(
                                out=V[:, b4 * 4:(b4 + 1) * 4, :, 0:64],
                                in_=ps[pk][:].rearrange("p (b a d) -> p b a d", b=4, a=2)),
                                reads=[PS[pk]], writes=["V"])
                        items = []
                        for a in range(2):
                            for qg in range(4):
                                nkb = 4 * qg + 4
                                for kb in range(nkb):
                                    items.append((a, qg, kb, nkb))
                        LAG = 2

                        def stage1(idx):
                            a, qg, kb, nkb = items[idx]
                            pr = slice(64 * a, 64 * a + 64)
                            cq = max(128 * kb, 512 * qg)
                            N = 512 * (qg + 1) - cq
                            sk = idx % 4
                            f.op("pe", lambda e: e.matmul(
                                ps[sk][:, 0:N], lhsT=kT[:, a, kb * 128:(kb + 1) * 128], rhs=qT[:, cq:cq + N],
                                start=True, stop=True),
                                reads=["kT", "qT"], writes=[PS[sk]])
                            f.op("act", lambda e: e.activation(out=Pt[sk][:, 0:N], in_=ps[sk][:, 0:N],
                                                               func=AF.Exp, scale=0.125),
                                 reads=[PS[sk]], writes=["Pt%d" % sk])
                            moff = ((cq - 128 * kb) // 128 + 3) * 128
                            me = ("dve", "dve", "pool")[mrr[0] % 3]
                            mrr[0] += 1
                            f.op(me, lambda e: e.tensor_tensor(
                                out=Pt[sk][:, 0:N], in0=Pt[sk][:, 0:N], in1=maskT[:, moff:moff + N], op=ALU.mult),
                                reads=["Pt%d" % sk, "maskT"], writes=["Pt%d" % sk])

                        def stage2(idx):
                            a, qg, kb, nkb = items[idx]
                            pr = slice(64 * a, 64 * a + 64)
                            cq = max(128 * kb, 512 * qg)
                            N = 512 * (qg + 1) - cq
                            sk = idx % 4
                            po = 4 + (qg % 2)
                            oc = cq - 512 * qg
                            f.op("pe", lambda e: e.matmul(
                                ps[po][:, oc:oc + N], lhsT=V[:, kb, a, :], rhs=Pt[sk][:, 0:N],
                                start=(kb == 0), stop=(kb == nkb - 1)),
                                reads=["V", "Pt%d" % sk], writes=[PS[po]])
                            if kb == nkb - 1:
                                qs = slice(qg * 512, (qg + 1) * 512)
                                f.op("act", lambda e: e.activation(out=rc[64:128, :], in_=ps[po][64:128, :], func=AF.Ln),
                                     reads=[PS[po]], writes=["rtmp0"])
                                f.op("act", lambda e: e.activation(out=rc[64:128, :], in_=rc[64:128, :], func=AF.Exp,
                                                                   scale=-1.0),
                                     reads=["rtmp0"], writes=["rtmp0"])
                                f.op("dve", lambda e: e.tensor_tensor(out=atmp[pr, :], in0=ps[po][0:64, :],
                                                                      in1=rc[64:128, :], op=ALU.mult),
                                     reads=[PS[po], "rtmp0"], writes=["rtmp1"])
                                f.op("pool", lambda e: e.tensor_tensor(out=mT[pr, hp, qs], in0=atmp[pr, :],
                                                                       in1=mT[pr, hp, qs], op=ALU.mult),
                                     reads=["rtmp1", ("mT", hp)], writes=[("mT", hp)])

                        for idx in range(len(items) + LAG):
                            if idx < len(items):
                                stage1(idx)
                            if idx - LAG >= 0:
                                stage2(idx - LAG)
                    f.barrier()

                with ExitStack() as ss_:
                    def ssb(name, shape, dtype):
                        return ss_.enter_context(nc.sbuf_tensor(nk(name), list(shape), dtype))
                    NPW = 17 + 7
                    PR = ssb("PR", [128, 16, 3 * 24 + 4], F32)
                    pa = ssb("pa", [128, 16, 12], F32)
                    pi32 = ssb("pi32", [128, 16], I32)
                    XR = ssb("XR", [128, S], F32)
                    XI = ssb("XI", [128, S], F32)
                    cs_ = [ssb("cs%d" % k, [128, 2, 128], F32) for k in range(2)]
                    Xb = [ssb("Xb%d" % k, [128, 512], BF16) for k in range(2)]
                    uT = ssb("uT", [128, S], BF16)
                    bnat = ssb("bnat", [128, 2, 16, 16], F32)
                    cnat = ssb("cnat", [128, 2, 64], F32)
                    padT = ssb("padT", [128, 128], BF16)
                    padTf = ssb("padTf", [128, 2, 128], F32)
                    Bpad = ssb("Bpad", [128, 4, 2, 128], BF16)
                    Cpad = ssb("Cpad", [128, 4, 2, 128], BF16)
                    cf = ssb("cf", [128, 4, 128], F32)
                    dvec = ssb("dvec", [128, 4], F32)
                    glub = ssb("glub", [128, 4], F32)
                    gl = [ssb("gl%d" % k, [128, 512], F32) for k in range(2)] + [XR[:, 0:512]]

                    def ld_gp(dst_col, name):
                        for e_ in range(2):
                            f.dma("sp", pa[e_ * 64:(e_ + 1) * 64, :, dst_col],
                                  bass.AP(dt_[name], i * 2048 + e_ * 64, [[1, 64], [128, 16]]),
                                  writes=["pa"], allow_slow_non_contiguous=True)
                    ld_gp(0, "ssm_a_re")
                    ld_gp(1, "ssm_a_im")
                    for e_ in range(2):
                        f.dma("sp", pa[e_ * 64:(e_ + 1) * 64, :, 2],
                              bass.AP(dt_["ssm_log_dt"], i * 32 + e_, [[0, 64], [2, 16]]),
                              writes=["pa"], allow_slow_non_contiguous=True)
                    f.dma("sp", dvec[:], sd_d[i].rearrange("(c p) -> p c", p=128), writes=["dvec"],
                          allow_slow_non_contiguous=True)
                    f.dma("sp", glub[:], glb_d[i].rearrange("(c p) -> p c", p=128), writes=["glub"],
                          allow_slow_non_contiguous=True)

                    def pop(eng, fn, w=("pa",)):
                        f.op(eng, fn, reads=["pa", "PR"], writes=list(w))
                    A = lambda c: pa[:, :, c]
                    pop("act", lambda e: e.activation(out=A(3), in_=A(2), func=AF.Exp))
                    pop("dve", lambda e: e.tensor_tensor(out=A(4), in0=A(0), in1=A(3), op=ALU.mult))
                    pop("dve", lambda e: e.tensor_tensor(out=A(5), in0=A(1), in1=A(3), op=ALU.mult))
                    pop("act", lambda e: e.activation(out=A(4), in_=A(4), func=AF.Exp))

                    def sin_of(dst, shift):
                        pop("dve", lambda e: e.tensor_scalar(out=A(6), in0=A(5), scalar1=shift, scalar2=1.0 / TWO_PI,
                                                             op0=ALU.add, op1=ALU.mult))
                        f.op("dve", lambda e: e.tensor_copy(out=pi32[:], in_=A(6)), reads=["pa"], writes=["pi32"])
                        f.op("dve", lambda e: e.tensor_copy(out=A(7), in_=pi32[:]), reads=["pi32"], writes=["pa"])
                        pop("dve", lambda e: e.tensor_tensor(out=A(6), in0=A(6), in1=A(7), op=ALU.subtract))
                        pop("dve", lambda e: e.tensor_scalar(out=A(6), in0=A(6), scalar1=TWO_PI, scalar2=math.pi,
                                                             op0=ALU.mult, op1=ALU.min))
                        pop("dve", lambda e: e.tensor_scalar(out=A(6), in0=A(6), scalar1=-math.pi, scalar2=None,
                                                             op0=ALU.max))
                        pop("act", lambda e: e.activation(out=dst, in_=A(6), func=AF.Sin))
                    sin_of(A(8), 0.0)
                    sin_of(A(9), math.pi / 2)
                    P3 = lambda k, c: PR[:, :, 3 * k + c]
                    pop("dve", lambda e: e.tensor_tensor(out=P3(0, 0), in0=A(4), in1=A(9), op=ALU.mult), w=("PR",))
                    pop("dve", lambda e: e.tensor_tensor(out=P3(0, 1), in0=A(4), in1=A(8), op=ALU.mult), w=("PR",))

                    def cmul(dst, a_, b_):
                        pop("dve", lambda e: e.tensor_tensor(out=A(6), in0=P3(a_, 0), in1=P3(b_, 0), op=ALU.mult))
                        pop("dve", lambda e: e.tensor_tensor(out=A(7), in0=P3(a_, 1), in1=P3(b_, 1), op=ALU.mult))
                        pop("dve", lambda e: e.tensor_tensor(out=A(10), in0=P3(a_, 0), in1=P3(b_, 1), op=ALU.mult))
                        pop("dve", lambda e: e.tensor_tensor(out=A(11), in0=P3(a_, 1), in1=P3(b_, 0), op=ALU.mult))
                        pop("dve", lambda e: e.tensor_tensor(out=P3(dst, 0), in0=A(6), in1=A(7), op=ALU.subtract), w=("PR",))
                        pop("dve", lambda e: e.tensor_tensor(out=P3(dst, 1), in0=A(10), in1=A(11), op=ALU.add), w=("PR",))
                    for j in range(1, 16):
                        cmul(j, j - 1, 0)
                    for k in range(16, 16 + 7):
                        cmul(k, k - 1, k - 1)
                    for k in range(23):
                        pop("dve", lambda e, k=k: e.tensor_scalar(out=P3(k, 2), in0=P3(k, 1), scalar1=-1.0, scalar2=None,
                                                                  op0=ALU.mult), w=("PR",))
                    FR = PR[:, :, 72]
                    FI = PR[:, :, 73]
                    pop("dve", lambda e: e.tensor_scalar(out=A(6), in0=P3(0, 0), scalar1=-1.0, scalar2=None, op0=ALU.add))
                    pop("dve", lambda e: e.tensor_tensor(out=A(7), in0=A(0), in1=A(0), op=ALU.mult))
                    pop("dve", lambda e: e.tensor_tensor(out=A(10), in0=A(1), in1=A(1), op=ALU.mult))
                    pop("dve", lambda e: e.tensor_tensor(out=A(7), in0=A(7), in1=A(10), op=ALU.add))
                    pop("dve", lambda e: e.reciprocal(out=A(7), in_=A(7)))
                    pop("dve", lambda e: e.tensor_tensor(out=A(10), in0=A(6), in1=A(0), op=ALU.mult))
                    pop("dve", lambda e: e.tensor_tensor(out=A(11), in0=P3(0, 1), in1=A(1), op=ALU.mult))
                    pop("dve", lambda e: e.tensor_tensor(out=A(10), in0=A(10), in1=A(11), op=ALU.add))
                    pop("dve", lambda e: e.tensor_tensor(out=FR, in0=A(10), in1=A(7), op=ALU.mult), w=("PR",))
                    pop("dve", lambda e: e.tensor_tensor(out=A(10), in0=P3(0, 1), in1=A(0), op=ALU.mult))
                    pop("dve", lambda e: e.tensor_tensor(out=A(11), in0=A(6), in1=A(1), op=ALU.mult))
                    pop("dve", lambda e: e.tensor_tensor(out=A(10), in0=A(10), in1=A(11), op=ALU.subtract))
                    pop("dve", lambda e: e.tensor_tensor(out=FI, in0=A(10), in1=A(7), op=ALU.mult), w=("PR",))
                    for ri, name in enumerate(("ssm_b_re", "ssm_b_im")):
                        for e_ in range(2):
                            f.dma("sp", bnat[e_ * 64:(e_ + 1) * 64, ri, :, :],
                                  bass.AP(dt_[name], i * 32 * 1024 + e_ * 1024,
                                          [[16, 64], [2048, 16], [1, 16]]), writes=["bnat"])

                    XALL = [["X0"] + [("XR", s_) for s_ in range(16)], ["X1"] + [("XI", s_) for s_ in range(16)]]

                    srcs = [(ewin_d[i][:, 4096 + j_ * 128:4096 + (j_ + 1) * 128], 8) for j_ in range(4)]
                    for tg_ in range(4):
                        srcs += [(glw_d[i][:, fo_ * 128:(fo_ + 1) * 128], 4) for fo_ in range(4)]
                        srcs += [(ewin_d[i][:, 4608 + fo_ * 128:4608 + (fo_ + 1) * 128], 8) for fo_ in range(4)]
                    ws = WS(srcs)

                    def PSC(q, k, c):
                        return PR[:, q, 3 * k + c:3 * k + c + 1]

                    for j in range(4):
                        for ri, name in enumerate(("ssm_c_re", "ssm_c_im")):
                            f.dma("sp", cnat[:, ri, :],
                                  bass.AP(dt_[name], i * 32 * 1024 + j * 8 * 1024, [[64, 128], [1, 64]]),
                                  writes=["cnat"])
                        for qq in range(4):
                            q = j * 4 + qq
                            for ri in range(2):
                                f.op("pool", lambda e: e.memset(padT[:], 0.0), writes=["padT"])
                                for e_ in range(2):
                                    co = 16 * (2 * qq + e_)
                                    f.op("dve", lambda e, e_=e_, co=co, ri=ri, q=q: e.tensor_copy(
                                        out=padT[e_ * 64:(e_ + 1) * 64, co:co + 16],
                                        in_=bnat[e_ * 64:(e_ + 1) * 64, ri, q, :]),
                                        reads=["bnat"], writes=["padT"])
                                f.op("pe", lambda e: e.transpose(pst[0][:, 0:128], padT[:], ident[:]),
                                     reads=["padT", "ident"], writes=[PST[0]])
                                f.op("act", lambda e, qq=qq, ri=ri: e.copy(out=Bpad[:, qq, ri, :], in_=pst[0][:, 0:128]),
                                     reads=[PST[0]], writes=["Bpad"])
                            for ri in range(2):
                                for e_ in range(2):
                                    g8 = 2 * qq + e_
                                    f.op("dve", lambda e, e_=e_, ri=ri, g8=g8: e.tensor_scalar(
                                        out=padTf[:, ri, e_ * 64:(e_ + 1) * 64], in0=cnat[:, ri, :],
                                        scalar1=gmask[:, g8:g8 + 1], scalar2=None, op0=ALU.mult),
                                        reads=["cnat", "gmask"], writes=["padTf"])
                            for ri in range(2):
                                f.op("pe", lambda e, ri=ri: e.matmul(ps[4][:, ri * 128:(ri + 1) * 128], lhsT=padTf[:, ri, :],
                                                                     rhs=identf[:], start=True, stop=True),
                                     reads=["padTf", "identf"], writes=[PS[4]])
                            fr = PR[:, q, 72:73]
                            fi = PR[:, q, 73:74]
                            f.op("act", lambda e: e.copy(out=cf[:, 0:2, :], in_=ps[4][:, 0:256].rearrange("p (a b) -> p a b", a=2)),
                                 reads=[PS[4]], writes=["cf"])
                            f.op("dve", lambda e, fr=fr: e.tensor_scalar(out=cf[:, 2, :], in0=cf[:, 0, :], scalar1=fr, scalar2=None,
                                                                         op0=ALU.mult), reads=["cf", "PR"], writes=["cf"])
                            f.op("dve", lambda e, fi=fi: e.tensor_scalar(out=cf[:, 3, :], in0=cf[:, 1, :], scalar1=fi, scalar2=None,
                                                                         op0=ALU.mult), reads=["cf", "PR"], writes=["cf"])
                            f.op("dve", lambda e, qq=qq: e.tensor_tensor(out=Cpad[:, qq, 0, :], in0=cf[:, 2, :], in1=cf[:, 3, :],
                                                                         op=ALU.subtract), reads=["cf"], writes=["Cpad"])
                            f.op("dve", lambda e, fi=fi: e.tensor_scalar(out=cf[:, 2, :], in0=cf[:, 0, :], scalar1=fi, scalar2=-1.0,
                                                                         op0=ALU.mult, op1=ALU.mult), reads=["cf", "PR"], writes=["cf"])
                            f.op("dve", lambda e, fr=fr: e.tensor_scalar(out=cf[:, 3, :], in0=cf[:, 1, :], scalar1=fr, scalar2=None,
                                                                         op0=ALU.mult), reads=["cf", "PR"], writes=["cf"])
                            f.op("dve", lambda e, qq=qq: e.tensor_tensor(out=Cpad[:, qq, 1, :], in0=cf[:, 2, :], in1=cf[:, 3, :],
                                                                         op=ALU.subtract), reads=["cf"], writes=["Cpad"])
                        wb, wk = ws.get()
                        for tg in range(4):
                            sl = slice(tg * 512, (tg + 1) * 512)
                            proj_fm(wb, wk, 4, tg)
                            f.op("act", lambda e, sl=sl: e.copy(out=uT[:, sl], in_=ps[4][:]), reads=[PS[4]], writes=["uT"])
                        for qq in range(4):
                            q = j * 4 + qq
                            for ri, X in enumerate((XR, XI)):
                                for tg in range(4):
                                    sl = slice(tg * 512, (tg + 1) * 512)
                                    pk = 4 + (tg % 2)
                                    f.op("pe", lambda e, sl=sl, pk=pk, ri=ri, qq=qq: e.matmul(
                                        ps[pk][:], lhsT=Bpad[:, qq, ri, :], rhs=uT[:, sl], start=True, stop=True),
                                        reads=["Bpad", "uT"], writes=[PS[pk]])
                                    f.op("act", lambda e, sl=sl, pk=pk, X=X: e.copy(out=X[:, sl], in_=ps[pk][:]),
                                         reads=[PS[pk]], writes=XALL[ri])
                            XRv = XR[:].rearrange("p (c s) -> p c s", s=16)
                            XIv = XI[:].rearrange("p (c s) -> p c s", s=16)

                            def cstep(oR, oI, iR, iI, k, kOR, kOI, kIR, kII):
                                for (o_, i_, c_, ko, ki) in ((oR, iR, 0, kOR, kIR), (oI, iR, 1, kOI, kIR),
                                                             (oR, iI, 2, kOR, kII), (oI, iI, 0, kOI, kII)):
                                    f.op("dve", lambda e, o_=o_, i_=i_, c_=c_: e.scalar_tensor_tensor(
                                        out=o_, in0=i_, scalar=PSC(q, k, c_), in1=o_, op0=ALU.mult, op1=ALU.add),
                                        reads=[ki, "PR"], writes=[ko])
                            for s_ in range(1, 16):
                                cstep(XRv[:, :, s_], XIv[:, :, s_], XRv[:, :, s_ - 1], XIv[:, :, s_ - 1], 0,
                                      ("XR", s_), ("XI", s_), ("XR", s_ - 1), ("XI", s_ - 1))
                            cur = 0
                            f.op("pool", lambda e: e.tensor_copy(out=cs_[0][:, 0, :], in_=XRv[:, :, 15]),
                                 reads=[("XR", 15)], writes=[("cs", 0, 0)])
                            f.op("pool", lambda e: e.tensor_copy(out=cs_[0][:, 1, :], in_=XIv[:, :, 15]),
                                 reads=[("XI", 15)], writes=[("cs", 0, 1)])
                            for k in range(7):
                                sh = 1 << k
                                src = cs_[cur]
                                dst = cs_[1 - cur]
                                for c_ in range(2):
                                    f.op("pool", lambda e, src=src, dst=dst, c_=c_: e.tensor_copy(out=dst[:, c_, :], in_=src[:, c_, :]),
                                         reads=[("cs", cur, c_)], writes=[("cs", 1 - cur, c_)])
                                cstep(dst[:, 0, sh:128], dst[:, 1, sh:128], src[:, 0, 0:128 - sh], src[:, 1, 0:128 - sh],
                                      15 + k, ("cs", 1 - cur, 0), ("cs", 1 - cur, 1), ("cs", cur, 0), ("cs", cur, 1))
                                cur = 1 - cur
                            fin = cs_[cur]
                            for s_ in range(16):
                                cstep(XRv[:, 1:128, s_], XIv[:, 1:128, s_], fin[:, 0, 0:127], fin[:, 1, 0:127], s_,
                                      ("XR", s_), ("XI", s_), ("cs", cur, 0), ("cs", cur, 1))
                            for ri, X in enumerate((XR, XI)):
                                for tg in range(4):
                                    sl = slice(tg * 512, (tg + 1) * 512)
                                    bi = (ri * 4 + tg) % 2
                                    cast(Xb[bi][:], X[:, sl], XALL[ri], ["Xb%d" % bi])
                                    f.op("pe", lambda e, tg=tg, bi=bi, ri=ri, qq=qq: e.matmul(
                                        ps[tg][:], lhsT=Cpad[:, qq, ri, :], rhs=Xb[bi][:],
                                        start=(qq == 0 and ri == 0), stop=(qq == 3 and ri == 1)),
                                        reads=["Cpad", "Xb%d" % bi], writes=[PS[tg]])
                        for tg in range(4):
                            sl = slice(tg * 512, (tg + 1) * 512)
                            f.op("dve", lambda e, sl=sl, tg=tg, j=j: e.scalar_tensor_tensor(out=gl[0][:], in0=uT[:, sl],
                                                                                            scalar=dvec[:, j:j + 1], in1=ps[tg][:],
                                                                                            op0=ALU.mult, op1=ALU.add),
                                 reads=[PS[tg], "uT", "dvec"], writes=["gl0"])
                            f.op("pool", lambda e: e.tensor_tensor(out=gl[1][:], in0=gl[0][:], in1=gl[0][:], op=ALU.mult),
                                 reads=["gl0"], writes=["gl1"])
                            f.op("pool", lambda e: e.tensor_scalar(out=gl[1][:], in0=gl[1][:], scalar1=0.044715, scalar2=1.0,
                                                                   op0=ALU.mult, op1=ALU.add), reads=["gl1"], writes=["gl1"])
                            f.op("pool", lambda e: e.tensor_tensor(out=gl[1][:], in0=gl[1][:], in1=gl[0][:], op=ALU.mult),
                                 reads=["gl1", "gl0"], writes=["gl1"])
                            f.op("act", lambda e: e.activation(out=gl[2][:], in_=gl[1][:], func=AF.Sigmoid,
                                                               scale=2.0 * math.sqrt(2.0 / math.pi)),
                                 reads=["gl1"], writes=["gl2"] + XALL[0])
                            f.op("dve", lambda e, sl=sl, j=j: e.tensor_tensor(out=mT[:, 8 + j, sl], in0=gl[0][:], in1=gl[2][:],
                                                                              op=ALU.mult),
                                 reads=["gl0", "gl2"] + XALL[0], writes=[("mT", 8 + j)])
                    for tg in range(4):
                        sl = slice(tg * 512, (tg + 1) * 512)
                        for fo in range(4):
                            wb, wk = ws.get()
                            for jj in range(4):
                                f.op("pe", lambda e, fo=fo, jj=jj, sl=sl: e.matmul(
                                    ps[fo][:], lhsT=wb[:, jj, :], rhs=mT[:, 8 + jj, sl],
                                    start=(jj == 0), stop=(jj == 3)),
                                    reads=[wk, ("mT", 8 + jj)], writes=[PS[fo]], inc=(jj == 3))
                        for fo in range(4):
                            wb, wk = ws.get()
                            proj_fm(wb, wk, 4, tg)
                            f.op("act", lambda e: e.activation(out=gl[0][:], in_=ps[4][:], func=AF.Silu),
                                 reads=[PS[4]], writes=["gl0"])
                            f.op("act", lambda e, fo=fo: e.activation(out=gl[2][:], in_=ps[fo][:], func=AF.Sigmoid,
                                                                      bias=glub[:, fo:fo + 1]),
                                 reads=[PS[fo], "glub"], writes=["gl2"] + XALL[0])
                            f.op("pool", lambda e: e.tensor_tensor(out=gl[1][:], in0=gl[2][:], in1=gl[0][:], op=ALU.mult),
                                 reads=["gl2", "gl0"] + XALL[0], writes=["gl1"])
                            f.op("dve", lambda e, fo=fo, sl=sl: e.tensor_tensor(out=mT[:, 8 + fo, sl], in0=mT[:, 8 + fo, sl],
                                                                                in1=gl[1][:], op=ALU.mult),
                                 reads=["gl1", ("mT", 8 + fo)] + [PS[k] for k in range(4)], writes=[("mT", 8 + fo)])
                    f.barrier()

                with ExitStack() as os_:
                    wout = os_.enter_context(nc.sbuf_tensor(nk("ewout"), [128, 12, D], BF16))
                    ytmp_box[0] = os_.enter_context(nc.sbuf_tensor(nk("ytmp"), [128, 2, 512], F32))
                    woutk = load_big(wout, "ewout", ewout_d[i], 12)
                    f.dma("sp", gain[:], bcast("post_norm", l * D, D), writes=["gain"])
                    for b in range(NB):
                        outproj_block(b, lambda kc, b=b: mT[:, kc, b * 128:(b + 1) * 128],
                                      [("mT", ch) for ch in range(12)], 12, wout, woutk, l)
                    f.barrier()

        for l in layers:
            if l % 2 == 0:
                even_layer(l)
            else:
                odd_layer(l)

        for b in range(NB):
            f.dma("sp", out_d[b * 128:(b + 1) * 128, :], xres[:, b, :], reads=[("x", b)], key="out")
        nc.sync.wait_ge(f.dsem["out"][0], f.dsem["out"][1])
    return nc


_CACHE = {}


def prep_inputs(inputs):
    w = {k: np.ascontiguousarray(np.asarray(v, dtype=np.float32)) for k, v in inputs.items()}
    perm = swap_perm()
    shared = {k: v for k, v in w.items() if k != "x"}
    shared["even_w_sw"] = np.ascontiguousarray(w["even_w_in"][:, :, 0:2048][:, :, perm])
    shared.update(host_consts())
    return w["x"], shared


def kernel(**inputs):
    x, shared = prep_inputs(inputs)
    if "nc" not in _CACHE:
        _CACHE["nc"] = build()
    nc = _CACHE["nc"]
    in_maps = []
    for c in range(8):
        m = dict(shared)
        m["x"] = np.ascontiguousarray(x[c])
        in_maps.append(m)
    res = run_bass_kernel_spmd(nc, in_maps, core_ids=list(range(8)))
    return np.stack([np.asarray(r["out"], dtype=np.float32) for r in res.results], axis=0)
```

```python
import math
import numpy as np
import concourse.bass as bass
import concourse.mybir as mybir
from concourse.bass_utils import run_bass_kernel_spmd

F32 = mybir.dt.float32
BF16 = mybir.dt.bfloat16
I32 = mybir.dt.int32
ALU = mybir.AluOpType
AF = mybir.ActivationFunctionType
AX = mybir.AxisListType

S = 2048
D = 1024
NB = 16
EPS = 1e-6
TWO_PI = 2.0 * math.pi


class FW:
    def __init__(self, nc):
        self.nc = nc
        self.eng = {"pe": nc.tensor, "act": nc.scalar, "dve": nc.vector,
                    "pool": nc.gpsimd, "sp": nc.sync}
        self.sem = {}
        self.cnt = {}
        for e in self.eng:
            self.sem[e] = nc.alloc_semaphore("s_" + e)
            self.cnt[e] = 0
        self.waited = {}
        self.lastw = {}
        self.rd = {}
        self.dsem = {}

    def _deps(self, reads, writes):
        deps = {}

        def add(t):
            if t is not None and deps.get(t[0], 0) < t[1]:
                deps[t[0]] = t[1]
        for k in reads:
            add(self.lastw.get(k))
        for k in writes:
            add(self.lastw.get(k))
            for sk, v in self.rd.get(k, {}).items():
                add((sk, v))
        return deps

    def _semof(self, sk):
        if isinstance(sk, tuple):
            return self.dsem[sk[1]][0]
        return self.sem[sk]

    def _emit_waits(self, e, deps):
        for sk, v in deps.items():
            if sk == e and v > self.cnt[e]:
                continue
            if self.waited.get((e, sk), 0) < v:
                self.eng[e].wait_ge(self._semof(sk), v)
                self.waited[(e, sk)] = v

    def _record(self, t, reads, writes):
        for k in writes:
            self.lastw[k] = t
            self.rd[k] = {}
        for k in reads:
            d = self.rd.setdefault(k, {})
            if d.get(t[0], 0) < t[1]:
                d[t[0]] = t[1]

    def op(self, e, fn, reads=(), writes=(), inc=True):
        self._emit_waits(e, self._deps(reads, writes))
        ins = fn(self.eng[e])
        if inc:
            ins.then_inc(self.sem[e], 1)
            self.cnt[e] += 1
            t = (e, self.cnt[e])
        else:
            t = (e, self.cnt[e] + 1)
        self._record(t, reads, writes)
        return ins

    def dma(self, q, out, in_, reads=(), writes=(), key=None, **kw):
        if key is None:
            key = writes[0] if writes else reads[0]
        if key not in self.dsem:
            self.dsem[key] = [self.nc.alloc_semaphore("d%d" % len(self.dsem)), 0]
        self._emit_waits(q, self._deps(reads, writes))
        ins = self.eng[q].dma_start(out=out, in_=in_, **kw)
        ins.then_inc(self.dsem[key][0], 16)
        self.dsem[key][1] += 16
        t = (("dma", key), self.dsem[key][1])
        self._record(t, reads, writes)
        return ins

    def barrier(self):
        for e in self.eng:
            deps = {}
            for o in self.eng:
                if o != e and self.cnt[o] > 0:
                    deps[o] = self.cnt[o]
            for key, (s, c) in self.dsem.items():
                if c > 0:
                    deps[("dma", key)] = c
            self._emit_waits(e, deps)


def _mult(d):
    d = np.asarray(d)
    m = ((d >= 0) & (d <= 128)).astype(np.float32)
    m += ((d >= 0) & (d <= 512) & (d % 4 == 0)).astype(np.float32)
    m += ((d >= 0) & (d % 16 == 0)).astype(np.float32)
    return m


def host_consts():
    c = {}
    c["c_ident"] = np.eye(128, dtype=np.float32)
    j = np.arange(128)[:, None]
    t = np.arange(128)[None, :]
    c["c_mask"] = np.concatenate([_mult(128 * dl + t - j) for dl in range(-3, 16)], axis=1).astype(np.float32)
    bands = np.zeros((128, 4, 3, 128), np.float32)
    tp = np.arange(128)[:, None]
    tt = np.arange(128)[None, :]
    for wi, w in enumerate((2, 4, 8, 16)):
        dcur = tt - tp
        bands[:, wi, 0, :] = ((dcur >= 0) & (dcur < w)) / float(w) - (dcur == 0)
        dprev = tt + 128 - tp
        bands[:, wi, 1, :] = ((dprev >= 0) & (dprev < w)) / float(w)
        cnt = np.minimum(tt + 1, w).astype(np.float32)
        bands[:, wi, 2, :] = ((dcur >= 0) & (dcur < w)) / cnt - (dcur == 0)
    c["c_bands"] = bands.reshape(128, 4 * 3 * 128)
    half = 8
    inv = 500000.0 ** (-np.arange(0, 16, 2, dtype=np.float32) / 16.0)
    ang = np.arange(S, dtype=np.float32)[None, :] * inv[:, None]
    C = np.ones((64, S), np.float32)
    Sg = np.zeros((64, S), np.float32)
    C[0:8] = np.cos(ang)
    C[8:16] = np.cos(ang)
    Sg[0:8] = -np.sin(ang)
    Sg[8:16] = np.sin(ang)
    c["c_ropec"] = np.concatenate([C, C], 0)
    c["c_ropes"] = np.concatenate([Sg, Sg], 0)
    gm = np.zeros((128, 8), np.float32)
    for g in range(8):
        gm[g * 16:(g + 1) * 16, g] = 1.0
    c["c_gmask"] = gm
    return c


def swap_perm():
    perm = np.arange(2048)
    for blk in range(2048 // 64):
        b = blk * 64
        perm[b:b + 8] = np.arange(b + 8, b + 16)
        perm[b + 8:b + 16] = np.arange(b, b + 8)
    return perm


def build(layers=(0, 1, 2, 3)):
    nc = bass.Bass("TRN2", target_bir_lowering=False)
    dt_ = {}

    def din(name, shape):
        h = nc.dram_tensor(name, list(shape), F32, kind="ExternalInput")
        dt_[name] = h
        return h.ap()

    x_d = din("x", [S, D])
    pre_d = din("pre_norm", [4, D])
    post_d = din("post_norm", [4, D])
    ewin_d = din("even_w_in", [2, D, 5120])
    ewsw_d = din("even_w_sw", [2, D, 2048])
    ewout_d = din("even_w_out", [2, 1536, D])
    are_d = din("ssm_a_re", [2, 32, 64])
    aim_d = din("ssm_a_im", [2, 32, 64])
    ldt_d = din("ssm_log_dt", [2, 32])
    bre_d = din("ssm_b_re", [2, 32, 64, 16])
    bim_d = din("ssm_b_im", [2, 32, 64, 16])
    cre_d = din("ssm_c_re", [2, 32, 16, 64])
    cim_d = din("ssm_c_im", [2, 32, 16, 64])
    sd_d = din("ssm_d", [2, 512])
    glw_d = din("ssm_glu_w", [2, 512, 512])
    glb_d = din("ssm_glu_b", [2, 512])
    owin_d = din("odd_w_in", [2, D, 4096])
    pw_d = din("pool_w", [2, 4, 512, 512])
    psc_d = din("pool_scale", [2, 2048])
    owout_d = din("odd_w_out", [2, 2048, D])
    cid_d = din("c_ident", [128, 128])
    cmask_d = din("c_mask", [128, 19 * 128])
    cband_d = din("c_bands", [128, 12 * 128])
    cropec_d = din("c_ropec", [128, S])
    cropes_d = din("c_ropes", [128, S])
    cgm_d = din("c_gmask", [128, 8])
    out_d = nc.dram_tensor("out", [S, D], F32, kind="ExternalOutput").ap()

    f = FW(nc)
    uid = [0]

    def nk(p):
        uid[0] += 1
        return "%s%d" % (p, uid[0])

    def bcast(name, off, n):
        return bass.AP(dt_[name], off, [[0, 128], [1, n]])

    from contextlib import ExitStack
    with ExitStack() as es:
        def sb(name, shape, dtype):
            return es.enter_context(nc.sbuf_tensor(name, list(shape), dtype))

        def psb(name, shape, dtype):
            return es.enter_context(nc.psum_tensor(name, list(shape), dtype))

        xres = sb("xres", [128, NB, D], F32)
        ps = [psb("ps%d" % i, [128, 512], F32) for i in range(6)]
        pst = [psb("pst%d" % i, [128, 1024], BF16) for i in range(2)]
        PS = ["ps%d" % i for i in range(6)]
        PST = ["pst0", "pst1"]
        ident = sb("ident", [128, 128], BF16)
        identf = sb("identf", [128, 128], F32)
        gain = sb("gain", [128, D], F32)
        junk = sb("junk", [128, D], BF16)
        stat = sb("stat", [128, 64], F32)
        gmask = sb("gmask", [128, 8], F32)
        wbf = [sb("wbf%d" % i, [128, 8, 128], BF16) for i in range(4)]
        wctr = [0, 0]

        f.dma("sp", identf[:], cid_d, writes=["identf"])
        f.op("dve", lambda e: e.tensor_copy(out=ident[:], in_=identf[:]), reads=["identf"], writes=["ident"])
        f.dma("sp", gmask[:], cgm_d, writes=["gmask"])
        for b in range(NB):
            f.dma("sp", xres[:, b, :], x_d[b * 128:(b + 1) * 128, :], writes=[("x", b)])

        cast_rr = [0]

        def cast(out, in_, reads, writes):
            e = "act"
            if e == "pool":
                f.op("pool", lambda g: g.tensor_copy(out=out, in_=in_), reads=reads, writes=writes)
            else:
                f.op("act", lambda g: g.copy(out=out, in_=in_), reads=reads, writes=writes)

        class WS:
            DEPTH = 2

            def __init__(self, srcs):
                self.srcs = srcs
                self.issued = 0
                self.cur = 0
                self.buf = {}

            def get(self):
                while self.issued < min(len(self.srcs), self.cur + 1 + WS.DEPTH):
                    src, kc = self.srcs[self.issued]
                    bi = wctr[1] % 4
                    wctr[1] += 1
                    f.dma("pool", wbf[bi][:, 0:kc, :], src.rearrange("(k p) n -> p k n", p=128),
                          writes=["wbf%d" % bi])
                    self.buf[self.issued] = bi
                    self.issued += 1
                bi = self.buf[self.cur]
                self.cur += 1
                return wbf[bi], "wbf%d" % bi

        def load_big(dst, dkey, src_ap, kc, step=4):
            keys = []
            for k0 in range(0, kc, step):
                k1 = min(kc, k0 + step)
                f.dma("pool", dst[:, k0:k1, :], src_ap[k0 * 128:k1 * 128, :].rearrange("(k p) n -> p k n", p=128),
                      writes=[(dkey, k0)])
                keys.append((dkey, k0))
            return keys

        def load_big_hw(dst, dkey, src_ap, kc, stg, skey):
            keys = []
            for k0 in range(0, kc, 2):
                f.dma("sp", stg[:, :, :], src_ap[k0 * 128:(k0 + 2) * 128, :].rearrange("(k p) n -> p k n", p=128),
                      writes=[skey])
                f.op("act", lambda g: g.copy(out=dst[:, k0:k0 + 2, :], in_=stg[:, :, :]), reads=[skey], writes=[(dkey, k0)])
                keys.append((dkey, k0))
            return keys

        scol = [0]

        def newcol():
            scol[0] = (scol[0] + 1) % 64
            return scol[0]

        def rstd_from_ss(c_ss):
            c1 = newcol()
            c2 = newcol()
            f.op("act", lambda e: e.activation(out=stat[:, c1:c1 + 1], in_=stat[:, c_ss:c_ss + 1], func=AF.Sqrt,
                                               scale=1.0 / D, bias=EPS),
                 reads=[("st", c_ss)], writes=[("st", c1)])
            f.op("dve", lambda e: e.reciprocal(out=stat[:, c2:c2 + 1], in_=stat[:, c1:c1 + 1]),
                 reads=[("st", c1)], writes=[("st", c2)])
            return c2

        def prenorm_block(b, dst, dkey):
            c = newcol()
            f.op("act", lambda e: e.activation(out=junk[:], in_=xres[:, b, :], func=AF.Square,
                                               accum_out=stat[:, c:c + 1]),
                 reads=[("x", b)], writes=["junk", ("st", c)])
            c2 = rstd_from_ss(c)
            f.op("dve", lambda e: e.scalar_tensor_tensor(out=dst, in0=xres[:, b, :], scalar=stat[:, c2:c2 + 1],
                                                         in1=gain[:], op0=ALU.mult, op1=ALU.mult),
                 reads=[("x", b), ("st", c2), "gain"], writes=[dkey])

        def transpose_block(src, skey, dst_ap, dkey, pi):
            for c in range(8):
                f.op("pe", lambda e, c=c: e.transpose(pst[pi][:, c * 128:(c + 1) * 128], src[:, c * 128:(c + 1) * 128],
                                                      ident[:]),
                     reads=[skey, "ident"], writes=[PST[pi]], inc=(c == 7))
            f.op("act", lambda e: e.copy(out=dst_ap, in_=pst[pi][:].rearrange("p (c t) -> p c t", c=8)),
                 reads=[PST[pi]], writes=[dkey])

        def outproj_block(b, mT_fn, mkeys, KC, wout, wkey, l):
            pp = (b % 2) * 2
            for fh in range(2):
                for kc in range(KC):
                    f.op("pe", lambda e, kc=kc, fh=fh: e.matmul(ps[pp + fh][:], lhsT=mT_fn(kc),
                                                                rhs=wout[:, kc, fh * 512:(fh + 1) * 512],
                                                                start=(kc == 0), stop=(kc == KC - 1)),
                         reads=list(mkeys) + list(wkey), writes=[PS[pp + fh]], inc=(kc == KC - 1))
            ca = newcol()
            cb = newcol()
            f.op("act", lambda e: e.activation(out=junk[:, 0:512], in_=ps[pp][:], func=AF.Square,
                                               accum_out=stat[:, ca:ca + 1]),
                 reads=[PS[pp]], writes=["junk", ("st", ca)])
            f.op("act", lambda e: e.activation(out=junk[:, 512:1024], in_=ps[pp + 1][:], func=AF.Square,
                                               accum_out=stat[:, cb:cb + 1]),
                 reads=[PS[pp + 1]], writes=["junk", ("st", cb)])
            cs = newcol()
            f.op("dve", lambda e: e.tensor_tensor(out=stat[:, cs:cs + 1], in0=stat[:, ca:ca + 1],
                                                  in1=stat[:, cb:cb + 1], op=ALU.add),
                 reads=[("st", ca), ("st", cb)], writes=[("st", cs)])
            c2 = rstd_from_ss(cs)
            for fh in range(2):
                tk = "ytmp%d" % fh
                f.op("dve", lambda e, fh=fh: e.scalar_tensor_tensor(out=ytmp_box[0][:, fh, :], in0=ps[pp + fh][:],
                                                                    scalar=stat[:, c2:c2 + 1],
                                                                    in1=gain[:, fh * 512:(fh + 1) * 512],
                                                                    op0=ALU.mult, op1=ALU.mult),
                     reads=[PS[pp + fh], ("st", c2), "gain"], writes=[tk])
                f.op("pool", lambda e, fh=fh: e.tensor_tensor(out=xres[:, b, fh * 512:(fh + 1) * 512],
                                                              in0=xres[:, b, fh * 512:(fh + 1) * 512],
                                                              in1=ytmp_box[0][:, fh, :], op=ALU.add),
                     reads=[("x", b), tk], writes=[("x", b)])

        ytmp_box = [None]

        def odd_layer(l):
            i = l // 2
            with ExitStack() as ls:
                def lsb(name, shape, dtype):
                    return ls.enter_context(nc.sbuf_tensor(nk(name), list(shape), dtype))
                hN = lsb("hN", [128, 5, D], BF16)
                hT = lsb("hTq", [128, 8, 512], BF16)
                phT = lsb("phT", [128, 8, 512], BF16)
                mixed = lsb("mixed", [128, 4, 512], BF16)
                mT = lsb("mTq", [128, 16, 512], BF16)
                sg = lsb("sg", [128, 512], F32)
                bandf = lsb("bandf", [128, 12, 128], F32)
                band = lsb("band", [128, 4, 4, 128], BF16)
                btmp = lsb("btmp", [128, 4, 128], F32)
                pwbf = lsb("pwbf", [128, 4, 512], BF16)
                wout = lsb("wout", [128, 16, D], BF16)
                pscale = lsb("pscale", [128, 16], F32)
                wost = [lsb("wost%d" % k_, [128, 2, D], F32) for k_ in range(2)]
                pwst = lsb("pwst", [128, 4, 512], F32)
                ytmp_box[0] = lsb("ytmp", [128, 2, 512], F32)
                f.dma("sp", bandf[:], cband_d.rearrange("p (a t) -> p a t", t=128), writes=["bandf"])
                bv = bandf[:].rearrange("p (w k) t -> p w k t", k=3)
                f.op("dve", lambda e: e.tensor_copy(out=band[:, :, 0:3, :], in_=bv), reads=["bandf"], writes=["band"])
                f.op("dve", lambda e: e.tensor_copy(out=btmp[:], in_=band[:, :, 2, :]), reads=["band"], writes=["btmp"])
                f.op("dve", lambda e: e.tensor_tensor(out=btmp[:], in0=bv[:, :, 2, :], in1=btmp[:], op=ALU.subtract),
                     reads=["bandf", "btmp"], writes=["btmp"])
                f.op("dve", lambda e: e.tensor_copy(out=band[:, :, 3, :], in_=btmp[:]), reads=["btmp"], writes=["band"])
                f.dma("sp", pscale[:], psc_d[i].rearrange("(c p) -> p c", p=128), writes=["pscale"],
                      allow_slow_non_contiguous=True)
                srcs = []
                for tq in range(4):
                    for g in range(4):
                        for j in range(4):
                            srcs.append((owin_d[i][:, g * 512 + j * 128:g * 512 + (j + 1) * 128], 8))
                        for dj in range(4):
                            col = 2048 + g * 512 + dj * 128
                            srcs.append((owin_d[i][:, col:col + 128], 8))
                ws = WS(srcs)
                for tq in range(4):
                    b0 = tq * 4
                    woutk = [("wout", 2 * k_) for k_ in range(8)]
                    wstep = [0]

                    def wout_step():
                        k_ = wstep[0]
                        wstep[0] += 1
                        if 1 <= k_ <= 8:
                            kk = k_ - 1
                            f.op("act", lambda g_: g_.copy(out=wout[:, 2 * kk:2 * kk + 2, :], in_=wost[kk % 2][:]),
                                 reads=["wost%d" % (kk % 2)], writes=[("wout", 2 * kk)])
                        if k_ < 8:
                            f.dma("sp", wost[k_ % 2][:],
                                  owout_d[i][k_ * 256:(k_ + 1) * 256, :].rearrange("(k p) n -> p k n", p=128),
                                  writes=["wost%d" % (k_ % 2)])
                    wout_step()
                    f.dma("sp", gain[:], bcast("pre_norm", l * D, D), writes=["gain"])
                    if tq > 0:
                        f.op("pool", lambda e: e.tensor_copy(out=hN[:, 0, :], in_=hN[:, 4, :]),
                             reads=[("hN", 4)], writes=[("hN", 0)])
                    for s_, b in enumerate(range(b0 - 1, b0 + 4)):
                        if s_ == 0:
                            continue
                        prenorm_block(b, hN[:, s_, :], ("hN", s_))
                    for s_ in range(1, 5):
                        transpose_block(hN[:, s_, :], ("hN", s_), hT[:, :, (s_ - 1) * 128:s_ * 128], "hT", s_ % 2)
                    for g in range(4):
                        f.dma("sp", pwst[:], pw_d[i, g].rearrange("(k p) n -> p k n", p=128), writes=["pwst"])
                        pwk = [("pwbf", 0), ("pwbf", 2)]
                        for s_ in range(1, 5):
                            b = b0 + s_ - 1
                            for hf in range(2):
                                pk = 4 + hf
                                for cc in range(4):
                                    c = hf * 4 + cc
                                    o = ps[pk][:, cc * 128:(cc + 1) * 128]
                                    lh = hN[:, s_, c * 128:(c + 1) * 128]
                                    if b == 0:
                                        f.op("pe", lambda e, o=o, lh=lh: e.matmul(o, lhsT=lh, rhs=band[:, g, 2, :],
                                                                                  start=True, stop=False),
                                             reads=[("hN", s_), "band"], writes=[PS[pk]], inc=False)
                                        f.op("pe", lambda e, o=o, lh=lh: e.matmul(o, lhsT=lh, rhs=band[:, g, 3, :],
                                                                                  start=False, stop=True),
                                             reads=[("hN", s_), "band"], writes=[PS[pk]], inc=(cc == 3))
                                    else:
                                        lp = hN[:, s_ - 1, c * 128:(c + 1) * 128]
                                        f.op("pe", lambda e, o=o, lh=lh: e.matmul(o, lhsT=lh, rhs=band[:, g, 0, :],
                                                                                  start=True, stop=False),
                                             reads=[("hN", s_), "band"], writes=[PS[pk]], inc=False)
                                        f.op("pe", lambda e, o=o, lp=lp: e.matmul(o, lhsT=lp, rhs=band[:, g, 1, :],
                                                                                  start=False, stop=True),
                                             reads=[("hN", s_ - 1), "band"], writes=[PS[pk]], inc=(cc == 3))
                                f.op("act", lambda e, hf=hf, pk=pk: e.copy(
                                    out=phT[:, hf * 4:(hf + 1) * 4, (s_ - 1) * 128:s_ * 128],
                                    in_=ps[pk][:].rearrange("p (c t) -> p c t", c=4)),
                                    reads=[PS[pk]], writes=["phT"])
                        for j in range(4):
                            wout_step()
                            wb, wk = ws.get()
                            pk = j % 2
                            for kc in range(8):
                                f.op("pe", lambda e, kc=kc: e.matmul(ps[pk][:], lhsT=wb[:, kc, :], rhs=phT[:, kc, :],
                                                                     start=(kc == 0), stop=(kc == 7)),
                                     reads=[wk, "phT"], writes=[PS[pk]], inc=(kc == 7))
                            f.op("act", lambda e, j=j, pk=pk: e.copy(out=mixed[:, j, :], in_=ps[pk][:]),
                                 reads=[PS[pk]], writes=[("mixed", j)])
                        for k0_ in (0, 2):
                            f.op("act", lambda g_, k0_=k0_: g_.copy(out=pwbf[:, k0_:k0_ + 2, :], in_=pwst[:, k0_:k0_ + 2, :]),
                                 reads=["pwst"], writes=[("pwbf", k0_)])
                        for dj in range(4):
                            wb, wk = ws.get()
                            pg = 2
                            for kc in range(8):
                                f.op("pe", lambda e, kc=kc: e.matmul(ps[pg][:], lhsT=wb[:, kc, :], rhs=hT[:, kc, :],
                                                                     start=(kc == 0), stop=(kc == 7)),
                                     reads=[wk, "hT"], writes=[PS[pg]], inc=(kc == 7))
                            f.op("act", lambda e: e.activation(out=sg[:], in_=ps[pg][:], func=AF.Silu),
                                 reads=[PS[pg]], writes=["sg"])
                            py = 3
                            for j in range(4):
                                f.op("pe", lambda e, j=j, dj=dj: e.matmul(ps[py][:], lhsT=pwbf[:, j, dj * 128:(dj + 1) * 128],
                                                                          rhs=mixed[:, j, :], start=(j == 0), stop=(j == 3)),
                                     reads=pwk + [("mixed", j)], writes=[PS[py]], inc=(j == 3))
                            ch = g * 4 + dj
                            f.op("dve", lambda e, ch=ch: e.scalar_tensor_tensor(out=mT[:, ch, :], in0=ps[py][:],
                                                                                scalar=pscale[:, ch:ch + 1], in1=sg[:],
                                                                                op0=ALU.mult, op1=ALU.mult),
                                 reads=[PS[py], "pscale", "sg"], writes=[("mT", ch)])
                    f.dma("sp", gain[:], bcast("post_norm", l * D, D), writes=["gain"])
                    for bb in range(4):
                        outproj_block(b0 + bb, lambda kc, bb=bb: mT[:, kc, bb * 128:(bb + 1) * 128],
                                      [("mT", ch) for ch in range(16)], 16, wout, woutk, l)
                f.barrier()

        def even_layer(l):
            i = l // 2
            with ExitStack() as ls:
                def lsb(name, shape, dtype):
                    return ls.enter_context(nc.sbuf_tensor(nk(name), list(shape), dtype))
                hT = lsb("hT", [128, 8, S], BF16)
                mT = lsb("mT", [128, 12, S], BF16)
                f.dma("sp", gain[:], bcast("pre_norm", l * D, D), writes=["gain"])
                with ExitStack() as hs_:
                    hNb = [hs_.enter_context(nc.sbuf_tensor(nk("hNb"), [128, D], BF16)) for k in range(2)]
                    for b in range(NB):
                        prenorm_block(b, hNb[b % 2][:], "hNb%d" % (b % 2))
                        transpose_block(hNb[b % 2], "hNb%d" % (b % 2), hT[:, :, b * 128:(b + 1) * 128], "hT", b % 2)
                    f.barrier()

                def proj_fm(wb, wk, pk, tg):
                    for kc in range(8):
                        f.op("pe", lambda e, kc=kc: e.matmul(ps[pk][:], lhsT=wb[:, kc, :],
                                                             rhs=hT[:, kc, tg * 512:(tg + 1) * 512],
                                                             start=(kc == 0), stop=(kc == 7)),
                             reads=[wk, "hT"], writes=[PS[pk]], inc=(kc == 7))

                with ExitStack() as as_:
                    def asb(name, shape, dtype):
                        return as_.enter_context(nc.sbuf_tensor(nk(name), list(shape), dtype))
                    qT = asb("qT", [128, S], BF16)
                    kT = asb("kT", [128, 2, S], BF16)
                    V = asb("V", [128, NB, 2, 128], BF16)
                    ropec = asb("ropec", [128, S], BF16)
                    ropes = asb("ropes", [128, S], BF16)
                    maskT = asb("maskT", [128, 19 * 128], BF16)
                    Pt = [asb("Pt%d" % k, [128, 512], BF16) for k in range(4)]
                    rtmp = [asb("rtmp%d" % k, [128, 512], F32) for k in range(2)]
                    rc = rtmp[0]
                    atmp = rtmp[1]
                    f.dma("pool", ropec[:], cropec_d, writes=["ropec"])
                    f.dma("pool", ropes[:], cropes_d, writes=["ropes"])
                    f.dma("pool", maskT[:], cmask_d, writes=["maskT"])
                    srcs = []
                    for hp_ in range(8):
                        c0_ = hp_ * 128
                        for (base_, swb_) in ((0, 0), (1024, 1024)):
                            srcs.append((ewin_d[i][:, base_ + c0_:base_ + c0_ + 128], 8))
                            srcs.append((ewsw_d[i][:, swb_ + c0_:swb_ + c0_ + 128], 8))
                        srcs.append((ewin_d[i][:, 3072 + c0_:3072 + c0_ + 128], 8))
                        srcs.append((ewin_d[i][:, 2048 + c0_:2048 + c0_ + 128], 8))
                    ws = WS(srcs)
                    f.op("pool", lambda e: e.memset(V[:, :, :, 64:128], 1.0), writes=["V"])
                    f.op("pool", lambda e: e.memset(kT[:], 0.0), writes=["kT"])
                    mrr = [0]
                    for hp in range(8):
                        c0 = hp * 128
                        for (dst, dk, base, swb) in ((qT, "qT", 0, 0), (kT, "kT", 1024, 1024)):
                            wb, wk = ws.get()
                            wb2, wk2 = ws.get()
                            for tg in range(4):
                                sl = slice(tg * 512, (tg + 1) * 512)
                                pa0 = 2 * (tg % 2)
                                proj_fm(wb, wk, pa0, tg)
                                proj_fm(wb2, wk2, pa0 + 1, tg)
                                r0 = "rtmp0"
                                r1 = "rtmp1"
                                f.op("dve", lambda e, sl=sl: e.tensor_tensor(out=rtmp[0][:], in0=ps[pa0][:], in1=ropec[:, sl],
                                                                             op=ALU.mult),
                                     reads=[PS[pa0], "ropec"], writes=[r0])
                                f.op("dve", lambda e, sl=sl: e.tensor_tensor(out=rtmp[1][:], in0=ps[pa0 + 1][:], in1=ropes[:, sl],
                                                                             op=ALU.mult),
                                     reads=[PS[pa0 + 1], "ropes"], writes=[r1])
                                if dk == "qT":
                                    f.op("dve", lambda e, sl=sl: e.tensor_tensor(out=qT[:, sl], in0=rtmp[0][:],
                                                                                 in1=rtmp[1][:], op=ALU.add),
                                         reads=[r0, r1], writes=[dk])
                                else:
                                    for a_ in range(2):
                                        pr_ = slice(64 * a_, 64 * a_ + 64)
                                        f.op("dve", lambda e, sl=sl, a_=a_, pr_=pr_: e.tensor_tensor(
                                            out=kT[pr_, a_, sl], in0=rtmp[0][pr_, :], in1=rtmp[1][pr_, :], op=ALU.add),
                                            reads=[r0, r1], writes=[dk])
                        wb, wk = ws.get()
                        for tg in range(4):
                            proj_fm(wb, wk, 2, tg)
                            f.op("act", lambda e, tg=tg: e.activation(out=mT[:, hp, tg * 512:(tg + 1) * 512], in_=ps[2][:],
                                                                      func=AF.Silu),
                                 reads=[PS[2]], writes=[("mT", hp)])
                        wb, wk = ws.get()
                        for b4 in range(4):
                            pk = 3
                            for bb in range(4):
                                b = b4 * 4 + bb
                                for kc in range(8):
                                    f.op("pe", lambda e, kc=kc, b=b, bb=bb: e.matmul(
                                        ps[pk][:, bb * 128:(bb + 1) * 128], lhsT=hT[:, kc, b * 128:(b + 1) * 128],
                                        rhs=wb[:, kc, :], start=(kc == 0), stop=(kc == 7)),
                                        reads=[wk, "hT"], writes=[PS[pk]], inc=(kc == 7 and bb == 3))
                            f.op("act", lambda e, b4=b4: e.copy(
                                out=V[:, b4 * 4:(b4 + 1) * 4, :, 0:64],
                                in_=ps[pk][:].rearrange("p (b a d) -> p b a d", b=4, a=2)),
                                reads=[PS[pk]], writes=["V"])
                        items = []
                        for a in range(2):
                            for qg in range(4):
                                nkb = 4 * qg + 4
                                for kb in range(nkb):
                                    items.append((a, qg, kb, nkb))
                        LAG = 2

                        def stage1(idx):
                            a, qg, kb, nkb = items[idx]
                            pr = slice(64 * a, 64 * a + 64)
                            cq = max(128 * kb, 512 * qg)
                            N = 512 * (qg + 1) - cq
                            sk = idx % 4
                            f.op("pe", lambda e: e.matmul(
                                ps[sk][:, 0:N], lhsT=kT[:, a, kb * 128:(kb + 1) * 128], rhs=qT[:, cq:cq + N],
                                start=True, stop=True),
                                reads=["kT", "qT"], writes=[PS[sk]])
                            f.op("act", lambda e: e.activation(out=Pt[sk][:, 0:N], in_=ps[sk][:, 0:N],
                                                               func=AF.Exp, scale=0.125),
                                 reads=[PS[sk]], writes=["Pt%d" % sk])
                            moff = ((cq - 128 * kb) // 128 + 3) * 128
                            me = "dve"
                            mrr[0] += 1
                            f.op(me, lambda e: e.tensor_tensor(
                                out=Pt[sk][:, 0:N], in0=Pt[sk][:, 0:N], in1=maskT[:, moff:moff + N], op=ALU.mult),
                                reads=["Pt%d" % sk, "maskT"], writes=["Pt%d" % sk])

                        def stage2(idx):
                            a, qg, kb, nkb = items[idx]
                            pr = slice(64 * a, 64 * a + 64)
                            cq = max(128 * kb, 512 * qg)
                            N = 512 * (qg + 1) - cq
                            sk = idx % 4
                            po = 4 + (qg % 2)
                            oc = cq - 512 * qg
                            f.op("pe", lambda e: e.matmul(
                                ps[po][:, oc:oc + N], lhsT=V[:, kb, a, :], rhs=Pt[sk][:, 0:N],
                                start=(kb == 0), stop=(kb == nkb - 1)),
                                reads=["V", "Pt%d" % sk], writes=[PS[po]])
                            if kb == nkb - 1:
                                qs = slice(qg * 512, (qg + 1) * 512)
                                f.op("act", lambda e: e.activation(out=rc[64:128, :], in_=ps[po][64:128, :], func=AF.Ln),
                                     reads=[PS[po]], writes=["rtmp0"])
                                f.op("act", lambda e: e.activation(out=rc[64:128, :], in_=rc[64:128, :], func=AF.Exp,
                                                                   scale=-1.0),
                                     reads=["rtmp0"], writes=["rtmp0"])
                                f.op("dve", lambda e: e.tensor_tensor(out=atmp[pr, :], in0=ps[po][0:64, :],
                                                                      in1=rc[64:128, :], op=ALU.mult),
                                     reads=[PS[po], "rtmp0"], writes=["rtmp1"])
                                f.op("pool", lambda e: e.tensor_tensor(out=mT[pr, hp, qs], in0=atmp[pr, :],
                                                                       in1=mT[pr, hp, qs], op=ALU.mult),
                                     reads=["rtmp1", ("mT", hp)], writes=[("mT", hp)])

                        for idx in range(len(items) + LAG):
                            if idx < len(items):
                                stage1(idx)
                            if idx - LAG >= 0:
                                stage2(idx - LAG)
                    f.barrier()

                with ExitStack() as ss_:
                    def ssb(name, shape, dtype):
                        return ss_.enter_context(nc.sbuf_tensor(nk(name), list(shape), dtype))
                    NPW = 17 + 7
                    PR = ssb("PR", [128, 16, 3 * 24 + 4], F32)
                    pa = ssb("pa", [128, 16, 12], F32)
                    pi32 = ssb("pi32", [128, 16], I32)
                    XR = ssb("XR", [128, S], F32)
                    XI = ssb("XI", [128, S], F32)
                    cs_ = [ssb("cs%d" % k, [128, 2, 192], F32) for k in range(2)]
                    Xb = [ssb("Xb%d" % k, [128, 512], BF16) for k in range(2)]
                    uT = ssb("uT", [128, S], BF16)
                    bnat = ssb("bnat", [128, 2, 16, 16], F32)
                    cnat = ssb("cnat", [128, 2, 64], F32)
                    padT = ssb("padT", [128, 128], BF16)
                    padTf = ssb("padTf", [128, 2, 128], F32)
                    Bpad = ssb("Bpad", [128, 4, 2, 128], BF16)
                    Cpad = ssb("Cpad", [128, 4, 2, 128], BF16)
                    cf = ssb("cf", [128, 4, 128], F32)
                    dvec = ssb("dvec", [128, 4], F32)
                    glub = ssb("glub", [128, 4], F32)
                    gl = [ssb("gl%d" % k, [128, 512], F32) for k in range(2)] + [XR[:, 0:512]]

                    for k_ in range(2):
                        f.op("pool", lambda e, k_=k_: e.memset(cs_[k_][:], 0.0), writes=[("cs", k_, 0), ("cs", k_, 1)])
                    def ld_gp(dst_col, name):
                        for e_ in range(2):
                            f.dma("sp", pa[e_ * 64:(e_ + 1) * 64, :, dst_col],
                                  bass.AP(dt_[name], i * 2048 + e_ * 64, [[1, 64], [128, 16]]),
                                  writes=["pa"], allow_slow_non_contiguous=True)
                    ld_gp(0, "ssm_a_re")
                    ld_gp(1, "ssm_a_im")
                    for e_ in range(2):
                        f.dma("sp", pa[e_ * 64:(e_ + 1) * 64, :, 2],
                              bass.AP(dt_["ssm_log_dt"], i * 32 + e_, [[0, 64], [2, 16]]),
                              writes=["pa"], allow_slow_non_contiguous=True)
                    f.dma("sp", dvec[:], sd_d[i].rearrange("(c p) -> p c", p=128), writes=["dvec"],
                          allow_slow_non_contiguous=True)
                    f.dma("sp", glub[:], glb_d[i].rearrange("(c p) -> p c", p=128), writes=["glub"],
                          allow_slow_non_contiguous=True)

                    def pop(eng, fn, w=("pa",)):
                        f.op(eng, fn, reads=["pa", "PR"], writes=list(w))
                    A = lambda c: pa[:, :, c]
                    pop("act", lambda e: e.activation(out=A(3), in_=A(2), func=AF.Exp))
                    pop("dve", lambda e: e.tensor_tensor(out=A(4), in0=A(0), in1=A(3), op=ALU.mult))
                    pop("dve", lambda e: e.tensor_tensor(out=A(5), in0=A(1), in1=A(3), op=ALU.mult))
                    pop("act", lambda e: e.activation(out=A(4), in_=A(4), func=AF.Exp))

                    def sin_of(dst, shift):
                        pop("dve", lambda e: e.tensor_scalar(out=A(6), in0=A(5), scalar1=shift, scalar2=1.0 / TWO_PI,
                                                             op0=ALU.add, op1=ALU.mult))
                        f.op("dve", lambda e: e.tensor_copy(out=pi32[:], in_=A(6)), reads=["pa"], writes=["pi32"])
                        f.op("dve", lambda e: e.tensor_copy(out=A(7), in_=pi32[:]), reads=["pi32"], writes=["pa"])
                        pop("dve", lambda e: e.tensor_tensor(out=A(6), in0=A(6), in1=A(7), op=ALU.subtract))
                        pop("dve", lambda e: e.tensor_scalar(out=A(6), in0=A(6), scalar1=TWO_PI, scalar2=math.pi,
                                                             op0=ALU.mult, op1=ALU.min))
                        pop("dve", lambda e: e.tensor_scalar(out=A(6), in0=A(6), scalar1=-math.pi, scalar2=None,
                                                             op0=ALU.max))
                        pop("act", lambda e: e.activation(out=dst, in_=A(6), func=AF.Sin))
                    sin_of(A(8), 0.0)
                    sin_of(A(9), math.pi / 2)
                    P3 = lambda k, c: PR[:, :, 3 * k + c]
                    pop("dve", lambda e: e.tensor_tensor(out=P3(0, 0), in0=A(4), in1=A(9), op=ALU.mult), w=("PR",))
                    pop("dve", lambda e: e.tensor_tensor(out=P3(0, 1), in0=A(4), in1=A(8), op=ALU.mult), w=("PR",))

                    def cmul(dst, a_, b_):
                        pop("dve", lambda e: e.tensor_tensor(out=A(6), in0=P3(a_, 0), in1=P3(b_, 0), op=ALU.mult))
                        pop("dve", lambda e: e.tensor_tensor(out=A(7), in0=P3(a_, 1), in1=P3(b_, 1), op=ALU.mult))
                        pop("dve", lambda e: e.tensor_tensor(out=A(10), in0=P3(a_, 0), in1=P3(b_, 1), op=ALU.mult))
                        pop("dve", lambda e: e.tensor_tensor(out=A(11), in0=P3(a_, 1), in1=P3(b_, 0), op=ALU.mult))
                        pop("dve", lambda e: e.tensor_tensor(out=P3(dst, 0), in0=A(6), in1=A(7), op=ALU.subtract), w=("PR",))
                        pop("dve", lambda e: e.tensor_tensor(out=P3(dst, 1), in0=A(10), in1=A(11), op=ALU.add), w=("PR",))
                    for j in range(1, 16):
                        cmul(j, j - 1, 0)
                    for k in range(16, 16 + 7):
                        cmul(k, k - 1, k - 1)
                    for k in range(23):
                        pop("dve", lambda e, k=k: e.tensor_scalar(out=P3(k, 2), in0=P3(k, 1), scalar1=-1.0, scalar2=None,
                                                                  op0=ALU.mult), w=("PR",))
                    FR = PR[:, :, 72]
                    FI = PR[:, :, 73]
                    pop("dve", lambda e: e.tensor_scalar(out=A(6), in0=P3(0, 0), scalar1=-1.0, scalar2=None, op0=ALU.add))
                    pop("dve", lambda e: e.tensor_tensor(out=A(7), in0=A(0), in1=A(0), op=ALU.mult))
                    pop("dve", lambda e: e.tensor_tensor(out=A(10), in0=A(1), in1=A(1), op=ALU.mult))
                    pop("dve", lambda e: e.tensor_tensor(out=A(7), in0=A(7), in1=A(10), op=ALU.add))
                    pop("dve", lambda e: e.reciprocal(out=A(7), in_=A(7)))
                    pop("dve", lambda e: e.tensor_tensor(out=A(10), in0=A(6), in1=A(0), op=ALU.mult))
                    pop("dve", lambda e: e.tensor_tensor(out=A(11), in0=P3(0, 1), in1=A(1), op=ALU.mult))
                    pop("dve", lambda e: e.tensor_tensor(out=A(10), in0=A(10), in1=A(11), op=ALU.add))
                    pop("dve", lambda e: e.tensor_tensor(out=FR, in0=A(10), in1=A(7), op=ALU.mult), w=("PR",))
                    pop("dve", lambda e: e.tensor_tensor(out=A(10), in0=P3(0, 1), in1=A(0), op=ALU.mult))
                    pop("dve", lambda e: e.tensor_tensor(out=A(11), in0=A(6), in1=A(1), op=ALU.mult))
                    pop("dve", lambda e: e.tensor_tensor(out=A(10), in0=A(10), in1=A(11), op=ALU.subtract))
                    pop("dve", lambda e: e.tensor_tensor(out=FI, in0=A(10), in1=A(7), op=ALU.mult), w=("PR",))
                    for ri, name in enumerate(("ssm_b_re", "ssm_b_im")):
                        for e_ in range(2):
                            f.dma("sp", bnat[e_ * 64:(e_ + 1) * 64, ri, :, :],
                                  bass.AP(dt_[name], i * 32 * 1024 + e_ * 1024,
                                          [[16, 64], [2048, 16], [1, 16]]), writes=["bnat"])

                    XALL = [["X0"] + [("XR", s_) for s_ in range(16)], ["X1"] + [("XI", s_) for s_ in range(16)]]

                    srcs = [(ewin_d[i][:, 4096 + j_ * 128:4096 + (j_ + 1) * 128], 8) for j_ in range(4)]
                    for tg_ in range(4):
                        srcs += [(glw_d[i][:, fo_ * 128:(fo_ + 1) * 128], 4) for fo_ in range(4)]
                        srcs += [(ewin_d[i][:, 4608 + fo_ * 128:4608 + (fo_ + 1) * 128], 8) for fo_ in range(4)]
                    ws = WS(srcs)

                    def PSC(q, k, c):
                        return PR[:, q, 3 * k + c:3 * k + c + 1]

                    for j in range(4):
                        for ri, name in enumerate(("ssm_c_re", "ssm_c_im")):
                            f.dma("sp", cnat[:, ri, :],
                                  bass.AP(dt_[name], i * 32 * 1024 + j * 8 * 1024, [[64, 128], [1, 64]]),
                                  writes=["cnat"])
                        for qq in range(4):
                            q = j * 4 + qq
                            for ri in range(2):
                                f.op("pool", lambda e: e.memset(padT[:], 0.0), writes=["padT"])
                                for e_ in range(2):
                                    co = 16 * (2 * qq + e_)
                                    f.op("dve", lambda e, e_=e_, co=co, ri=ri, q=q: e.tensor_copy(
                                        out=padT[e_ * 64:(e_ + 1) * 64, co:co + 16],
                                        in_=bnat[e_ * 64:(e_ + 1) * 64, ri, q, :]),
                                        reads=["bnat"], writes=["padT"])
                                f.op("pe", lambda e: e.transpose(pst[0][:, 0:128], padT[:], ident[:]),
                                     reads=["padT", "ident"], writes=[PST[0]])
                                f.op("act", lambda e, qq=qq, ri=ri: e.copy(out=Bpad[:, qq, ri, :], in_=pst[0][:, 0:128]),
                                     reads=[PST[0]], writes=["Bpad"])
                            for ri in range(2):
                                for e_ in range(2):
                                    g8 = 2 * qq + e_
                                    f.op("dve", lambda e, e_=e_, ri=ri, g8=g8: e.tensor_scalar(
                                        out=padTf[:, ri, e_ * 64:(e_ + 1) * 64], in0=cnat[:, ri, :],
                                        scalar1=gmask[:, g8:g8 + 1], scalar2=None, op0=ALU.mult),
                                        reads=["cnat", "gmask"], writes=["padTf"])
                            for ri in range(2):
                                f.op("pe", lambda e, ri=ri: e.matmul(ps[4][:, ri * 128:(ri + 1) * 128], lhsT=padTf[:, ri, :],
                                                                     rhs=identf[:], start=True, stop=True),
                                     reads=["padTf", "identf"], writes=[PS[4]])
                            fr = PR[:, q, 72:73]
                            fi = PR[:, q, 73:74]
                            f.op("act", lambda e: e.copy(out=cf[:, 0:2, :], in_=ps[4][:, 0:256].rearrange("p (a b) -> p a b", a=2)),
                                 reads=[PS[4]], writes=["cf"])
                            f.op("dve", lambda e, fr=fr: e.tensor_scalar(out=cf[:, 2, :], in0=cf[:, 0, :], scalar1=fr, scalar2=None,
                                                                         op0=ALU.mult), reads=["cf", "PR"], writes=["cf"])
                            f.op("dve", lambda e, fi=fi: e.tensor_scalar(out=cf[:, 3, :], in0=cf[:, 1, :], scalar1=fi, scalar2=None,
                                                                         op0=ALU.mult), reads=["cf", "PR"], writes=["cf"])
                            f.op("dve", lambda e, qq=qq: e.tensor_tensor(out=Cpad[:, qq, 0, :], in0=cf[:, 2, :], in1=cf[:, 3, :],
                                                                         op=ALU.subtract), reads=["cf"], writes=["Cpad"])
                            f.op("dve", lambda e, fi=fi: e.tensor_scalar(out=cf[:, 2, :], in0=cf[:, 0, :], scalar1=fi, scalar2=-1.0,
                                                                         op0=ALU.mult, op1=ALU.mult), reads=["cf", "PR"], writes=["cf"])
                            f.op("dve", lambda e, fr=fr: e.tensor_scalar(out=cf[:, 3, :], in0=cf[:, 1, :], scalar1=fr, scalar2=None,
                                                                         op0=ALU.mult), reads=["cf", "PR"], writes=["cf"])
                            f.op("dve", lambda e, qq=qq: e.tensor_tensor(out=Cpad[:, qq, 1, :], in0=cf[:, 2, :], in1=cf[:, 3, :],
                                                                         op=ALU.subtract), reads=["cf"], writes=["Cpad"])
                        wb, wk = ws.get()
                        for tg in range(4):
                            sl = slice(tg * 512, (tg + 1) * 512)
                            proj_fm(wb, wk, 4, tg)
                            f.op("act", lambda e, sl=sl: e.copy(out=uT[:, sl], in_=ps[4][:]), reads=[PS[4]], writes=["uT"])
                        for qq in range(4):
                            q = j * 4 + qq
                            for ri, X in enumerate((XR, XI)):
                                for tg in range(4):
                                    sl = slice(tg * 512, (tg + 1) * 512)
                                    pk = 4 + (tg % 2)
                                    f.op("pe", lambda e, sl=sl, pk=pk, ri=ri, qq=qq: e.matmul(
                                        ps[pk][:], lhsT=Bpad[:, qq, ri, :], rhs=uT[:, sl], start=True, stop=True),
                                        reads=["Bpad", "uT"], writes=[PS[pk]])
                                    f.op("act", lambda e, sl=sl, pk=pk, X=X: e.copy(out=X[:, sl], in_=ps[pk][:]),
                                         reads=[PS[pk]], writes=XALL[ri])
                            XRv = XR[:].rearrange("p (c s) -> p c s", s=16)
                            XIv = XI[:].rearrange("p (c s) -> p c s", s=16)

                            def cstep(oR, oI, iR, iI, k, kOR, kOI, kIR, kII, bR=None, bI=None, kB=()):
                                for (o_, i_, c_, ko, ki, b_) in ((oR, iR, 0, kOR, kIR, bR), (oI, iR, 1, kOI, kIR, bI),
                                                                 (oR, iI, 2, kOR, kII, None), (oI, iI, 0, kOI, kII, None)):
                                    add_ = o_ if b_ is None else b_
                                    f.op("dve", lambda e, o_=o_, i_=i_, c_=c_, add_=add_: e.scalar_tensor_tensor(
                                        out=o_, in0=i_, scalar=PSC(q, k, c_), in1=add_, op0=ALU.mult, op1=ALU.add),
                                        reads=[ki, "PR"] + list(kB), writes=[ko])
                            for s_ in range(1, 16):
                                cstep(XRv[:, :, s_], XIv[:, :, s_], XRv[:, :, s_ - 1], XIv[:, :, s_ - 1], 0,
                                      ("XR", s_), ("XI", s_), ("XR", s_ - 1), ("XI", s_ - 1))
                            cur = 0
                            f.op("act", lambda e: e.copy(out=cs_[0][:, 0, 64:192], in_=XRv[:, :, 15]),
                                 reads=[("XR", 15)], writes=[("cs", 0, 0)])
                            f.op("act", lambda e: e.copy(out=cs_[0][:, 1, 64:192], in_=XIv[:, :, 15]),
                                 reads=[("XI", 15)], writes=[("cs", 0, 1)])
                            for k in range(7):
                                sh = 1 << k
                                src = cs_[cur]
                                dst = cs_[1 - cur]
                                cstep(dst[:, 0, 64:192], dst[:, 1, 64:192], src[:, 0, 64 - sh:192 - sh], src[:, 1, 64 - sh:192 - sh],
                                      15 + k, ("cs", 1 - cur, 0), ("cs", 1 - cur, 1), ("cs", cur, 0), ("cs", cur, 1),
                                      bR=src[:, 0, 64:192], bI=src[:, 1, 64:192])
                                cur = 1 - cur
                            fin = cs_[cur]
                            for s_ in range(16):
                                cstep(XRv[:, 1:128, s_], XIv[:, 1:128, s_], fin[:, 0, 64:191], fin[:, 1, 64:191], s_,
                                      ("XR", s_), ("XI", s_), ("cs", cur, 0), ("cs", cur, 1))
                            for ri, X in enumerate((XR, XI)):
                                for tg in range(4):
                                    sl = slice(tg * 512, (tg + 1) * 512)
                                    bi = (ri * 4 + tg) % 2
                                    cast(Xb[bi][:], X[:, sl], XALL[ri], ["Xb%d" % bi])
                                    f.op("pe", lambda e, tg=tg, bi=bi, ri=ri, qq=qq: e.matmul(
                                        ps[tg][:], lhsT=Cpad[:, qq, ri, :], rhs=Xb[bi][:],
                                        start=(qq == 0 and ri == 0), stop=(qq == 3 and ri == 1)),
                                        reads=["Cpad", "Xb%d" % bi], writes=[PS[tg]])
                        for tg in range(4):
                            sl = slice(tg * 512, (tg + 1) * 512)
                            f.op("dve", lambda e, sl=sl, tg=tg, j=j: e.scalar_tensor_tensor(out=gl[0][:], in0=uT[:, sl],
                                                                                            scalar=dvec[:, j:j + 1], in1=ps[tg][:],
                                                                                            op0=ALU.mult, op1=ALU.add),
                                 reads=[PS[tg], "uT", "dvec"], writes=["gl0"])
                            f.op("pool", lambda e: e.tensor_tensor(out=gl[1][:], in0=gl[0][:], in1=gl[0][:], op=ALU.mult),
                                 reads=["gl0"], writes=["gl1"])
                            f.op("pool", lambda e: e.tensor_scalar(out=gl[1][:], in0=gl[1][:], scalar1=0.044715, scalar2=1.0,
                                                                   op0=ALU.mult, op1=ALU.add), reads=["gl1"], writes=["gl1"])
                            f.op("pool", lambda e: e.tensor_tensor(out=gl[1][:], in0=gl[1][:], in1=gl[0][:], op=ALU.mult),
                                 reads=["gl1", "gl0"], writes=["gl1"])
                            f.op("act", lambda e: e.activation(out=gl[2][:], in_=gl[1][:], func=AF.Sigmoid,
                                                               scale=2.0 * math.sqrt(2.0 / math.pi)),
                                 reads=["gl1"], writes=["gl2"] + XALL[0])
                            f.op("dve", lambda e, sl=sl, j=j: e.tensor_tensor(out=mT[:, 8 + j, sl], in0=gl[0][:], in1=gl[2][:],
                                                                              op=ALU.mult),
                                 reads=["gl0", "gl2"] + XALL[0], writes=[("mT", 8 + j)])
                    for tg in range(4):
                        sl = slice(tg * 512, (tg + 1) * 512)
                        for fo in range(4):
                            wb, wk = ws.get()
                            for jj in range(4):
                                f.op("pe", lambda e, fo=fo, jj=jj, sl=sl: e.matmul(
                                    ps[fo][:], lhsT=wb[:, jj, :], rhs=mT[:, 8 + jj, sl],
                                    start=(jj == 0), stop=(jj == 3)),
                                    reads=[wk, ("mT", 8 + jj)], writes=[PS[fo]], inc=(jj == 3))
                        for fo in range(4):
                            wb, wk = ws.get()
                            proj_fm(wb, wk, 4, tg)
                            f.op("act", lambda e: e.activation(out=gl[0][:], in_=ps[4][:], func=AF.Silu),
                                 reads=[PS[4]], writes=["gl0"])
                            f.op("act", lambda e, fo=fo: e.activation(out=gl[2][:], in_=ps[fo][:], func=AF.Sigmoid,
                                                                      bias=glub[:, fo:fo + 1]),
                                 reads=[PS[fo], "glub"], writes=["gl2"] + XALL[0])
                            f.op("pool", lambda e: e.tensor_tensor(out=gl[1][:], in0=gl[2][:], in1=gl[0][:], op=ALU.mult),
                                 reads=["gl2", "gl0"] + XALL[0], writes=["gl1"])
                            f.op("dve", lambda e, fo=fo, sl=sl: e.tensor_tensor(out=mT[:, 8 + fo, sl], in0=mT[:, 8 + fo, sl],
                                                                                in1=gl[1][:], op=ALU.mult),
                                 reads=["gl1", ("mT", 8 + fo)] + [PS[k] for k in range(4)], writes=[("mT", 8 + fo)])
                    f.barrier()

                with ExitStack() as os_:
                    wout = os_.enter_context(nc.sbuf_tensor(nk("ewout"), [128, 12, D], BF16))
                    ytmp_box[0] = os_.enter_context(nc.sbuf_tensor(nk("ytmp"), [128, 2, 512], F32))
                    woutk = load_big(wout, "ewout", ewout_d[i], 12)
                    f.dma("sp", gain[:], bcast("post_norm", l * D, D), writes=["gain"])
                    for b in range(NB):
                        outproj_block(b, lambda kc, b=b: mT[:, kc, b * 128:(b + 1) * 128],
                                      [("mT", ch) for ch in range(12)], 12, wout, woutk, l)
                    f.barrier()

        for l in layers:
            if l % 2 == 0:
                even_layer(l)
            else:
                odd_layer(l)

        for b in range(NB):
            f.dma("sp", out_d[b * 128:(b + 1) * 128, :], xres[:, b, :], reads=[("x", b)], key="out")
        nc.sync.wait_ge(f.dsem["out"][0], f.dsem["out"][1])
    return nc


_CACHE = {}


def prep_inputs(inputs):
    w = {k: np.ascontiguousarray(np.asarray(v, dtype=np.float32)) for k, v in inputs.items()}
    perm = swap_perm()
    shared = {k: v for k, v in w.items() if k != "x"}
    shared["even_w_sw"] = np.ascontiguousarray(w["even_w_in"][:, :, 0:2048][:, :, perm])
    shared.update(host_consts())
    return w["x"], shared


def kernel(**inputs):
    x, shared = prep_inputs(inputs)
    if "nc" not in _CACHE:
        _CACHE["nc"] = build()
    nc = _CACHE["nc"]
    in_maps = []
    for c in range(8):
        m = dict(shared)
        m["x"] = np.ascontiguousarray(x[c])
        in_maps.append(m)
    res = run_bass_kernel_spmd(nc, in_maps, core_ids=list(range(8)))
    return np.stack([np.asarray(r["out"], dtype=np.float32) for r in res.results], axis=0)
```

```python
import math
import numpy as np
import concourse.bass as bass
import concourse.mybir as mybir
from concourse.bass_utils import run_bass_kernel_spmd

F32 = mybir.dt.float32
BF16 = mybir.dt.bfloat16
I32 = mybir.dt.int32
ALU = mybir.AluOpType
AF = mybir.ActivationFunctionType
AX = mybir.AxisListType

S = 2048
D = 1024
NB = 16
EPS = 1e-6
TWO_PI = 2.0 * math.pi


class FW:
    def __init__(self, nc):
        self.nc = nc
        self.eng = {"pe": nc.tensor, "act": nc.scalar, "dve": nc.vector,
                    "pool": nc.gpsimd, "sp": nc.sync}
        self.sem = {}
        self.cnt = {}
        for e in self.eng:
            self.sem[e] = nc.alloc_semaphore("s_" + e)
            self.cnt[e] = 0
        self.waited = {}
        self.lastw = {}
        self.rd = {}
        self.dsem = {}

    def _deps(self, reads, writes):
        deps = {}

        def add(t):
            if t is not None and deps.get(t[0], 0) < t[1]:
                deps[t[0]] = t[1]
        for k in reads:
            add(self.lastw.get(k))
        for k in writes:
            add(self.lastw.get(k))
            for sk, v in self.rd.get(k, {}).items():
                add((sk, v))
        return deps

    def _semof(self, sk):
        if isinstance(sk, tuple):
            return self.dsem[sk[1]][0]
        return self.sem[sk]

    def _emit_waits(self, e, deps):
        for sk, v in deps.items():
            if sk == e and v > self.cnt[e]:
                continue
            if self.waited.get((e, sk), 0) < v:
                self.eng[e].wait_ge(self._semof(sk), v)
                self.waited[(e, sk)] = v

    def _record(self, t, reads, writes):
        for k in writes:
            self.lastw[k] = t
            self.rd[k] = {}
        for k in reads:
            d = self.rd.setdefault(k, {})
            if d.get(t[0], 0) < t[1]:
                d[t[0]] = t[1]

    def op(self, e, fn, reads=(), writes=(), inc=True):
        self._emit_waits(e, self._deps(reads, writes))
        ins = fn(self.eng[e])
        if inc:
            ins.then_inc(self.sem[e], 1)
            self.cnt[e] += 1
            t = (e, self.cnt[e])
        else:
            t = (e, self.cnt[e] + 1)
        self._record(t, reads, writes)
        return ins

    def dma(self, q, out, in_, reads=(), writes=(), key=None, **kw):
        if key is None:
            key = writes[0] if writes else reads[0]
        if key not in self.dsem:
            self.dsem[key] = [self.nc.alloc_semaphore("d%d" % len(self.dsem)), 0]
        self._emit_waits(q, self._deps(reads, writes))
        ins = self.eng[q].dma_start(out=out, in_=in_, **kw)
        ins.then_inc(self.dsem[key][0], 16)
        self.dsem[key][1] += 16
        t = (("dma", key), self.dsem[key][1])
        self._record(t, reads, writes)
        return ins

    def barrier(self):
        for e in self.eng:
            deps = {}
            for o in self.eng:
                if o != e and self.cnt[o] > 0:
                    deps[o] = self.cnt[o]
            for key, (s, c) in self.dsem.items():
                if c > 0:
                    deps[("dma", key)] = c
            self._emit_waits(e, deps)


def _mult(d):
    d = np.asarray(d)
    m = ((d >= 0) & (d <= 128)).astype(np.float32)
    m += ((d >= 0) & (d <= 512) & (d % 4 == 0)).astype(np.float32)
    m += ((d >= 0) & (d % 16 == 0)).astype(np.float32)
    return m


def host_consts():
    c = {}
    c["c_ident"] = np.eye(128, dtype=np.float32)
    j = np.arange(128)[:, None]
    t = np.arange(128)[None, :]
    c["c_mask"] = np.concatenate([_mult(128 * dl + t - j) for dl in range(-3, 16)], axis=1).astype(np.float32)
    bands = np.zeros((128, 4, 3, 128), np.float32)
    tp = np.arange(128)[:, None]
    tt = np.arange(128)[None, :]
    for wi, w in enumerate((2, 4, 8, 16)):
        dcur = tt - tp
        bands[:, wi, 0, :] = ((dcur >= 0) & (dcur < w)) / float(w) - (dcur == 0)
        dprev = tt + 128 - tp
        bands[:, wi, 1, :] = ((dprev >= 0) & (dprev < w)) / float(w)
        cnt = np.minimum(tt + 1, w).astype(np.float32)
        bands[:, wi, 2, :] = ((dcur >= 0) & (dcur < w)) / cnt - (dcur == 0)
    c["c_bands"] = bands.reshape(128, 4 * 3 * 128)
    half = 8
    inv = 500000.0 ** (-np.arange(0, 16, 2, dtype=np.float32) / 16.0)
    ang = np.arange(S, dtype=np.float32)[None, :] * inv[:, None]
    C = np.ones((64, S), np.float32)
    Sg = np.zeros((64, S), np.float32)
    C[0:8] = np.cos(ang)
    C[8:16] = np.cos(ang)
    Sg[0:8] = -np.sin(ang)
    Sg[8:16] = np.sin(ang)
    c["c_ropec"] = np.concatenate([C, C], 0)
    c["c_ropes"] = np.concatenate([Sg, Sg], 0)
    gm = np.zeros((128, 8), np.float32)
    for g in range(8):
        gm[g * 16:(g + 1) * 16, g] = 1.0
    c["c_gmask"] = gm
    return c


def swap_perm():
    perm = np.arange(2048)
    for blk in range(2048 // 64):
        b = blk * 64
        perm[b:b + 8] = np.arange(b + 8, b + 16)
        perm[b + 8:b + 16] = np.arange(b, b + 8)
    return perm


def build(layers=(0, 1, 2, 3)):
    nc = bass.Bass("TRN2", target_bir_lowering=False)
    dt_ = {}

    def din(name, shape):
        h = nc.dram_tensor(name, list(shape), F32, kind="ExternalInput")
        dt_[name] = h
        return h.ap()

    x_d = din("x", [S, D])
    pre_d = din("pre_norm", [4, D])
    post_d = din("post_norm", [4, D])
    ewin_d = din("even_w_in", [2, D, 5120])
    ewsw_d = din("even_w_sw", [2, D, 2048])
    ewout_d = din("even_w_out", [2, 1536, D])
    are_d = din("ssm_a_re", [2, 32, 64])
    aim_d = din("ssm_a_im", [2, 32, 64])
    ldt_d = din("ssm_log_dt", [2, 32])
    bre_d = din("ssm_b_re", [2, 32, 64, 16])
    bim_d = din("ssm_b_im", [2, 32, 64, 16])
    cre_d = din("ssm_c_re", [2, 32, 16, 64])
    cim_d = din("ssm_c_im", [2, 32, 16, 64])
    sd_d = din("ssm_d", [2, 512])
    glw_d = din("ssm_glu_w", [2, 512, 512])
    glb_d = din("ssm_glu_b", [2, 512])
    owin_d = din("odd_w_in", [2, D, 4096])
    pw_d = din("pool_w", [2, 4, 512, 512])
    psc_d = din("pool_scale", [2, 2048])
    owout_d = din("odd_w_out", [2, 2048, D])
    cid_d = din("c_ident", [128, 128])
    cmask_d = din("c_mask", [128, 19 * 128])
    cband_d = din("c_bands", [128, 12 * 128])
    cropec_d = din("c_ropec", [128, S])
    cropes_d = din("c_ropes", [128, S])
    cgm_d = din("c_gmask", [128, 8])
    out_d = nc.dram_tensor("out", [S, D], F32, kind="ExternalOutput").ap()

    f = FW(nc)
    uid = [0]

    def nk(p):
        uid[0] += 1
        return "%s%d" % (p, uid[0])

    def bcast(name, off, n):
        return bass.AP(dt_[name], off, [[0, 128], [1, n]])

    from contextlib import ExitStack
    with ExitStack() as es:
        def sb(name, shape, dtype):
            return es.enter_context(nc.sbuf_tensor(name, list(shape), dtype))

        def psb(name, shape, dtype):
            return es.enter_context(nc.psum_tensor(name, list(shape), dtype))

        xres = sb("xres", [128, NB, D], F32)
        ps = [psb("ps%d" % i, [128, 512], F32) for i in range(6)]
        pst = [psb("pst%d" % i, [128, 1024], BF16) for i in range(2)]
        PS = ["ps%d" % i for i in range(6)]
        PST = ["pst0", "pst1"]
        ident = sb("ident", [128, 128], BF16)
        identf = sb("identf", [128, 128], F32)
        gain = sb("gain", [128, D], F32)
        junk = sb("junk", [128, D], BF16)
        stat = sb("stat", [128, 64], F32)
        gmask = sb("gmask", [128, 8], F32)
        wbf = [sb("wbf%d" % i, [128, 8, 128], BF16) for i in range(4)]
        wctr = [0, 0]

        f.dma("sp", identf[:], cid_d, writes=["identf"])
        f.op("dve", lambda e: e.tensor_copy(out=ident[:], in_=identf[:]), reads=["identf"], writes=["ident"])
        f.dma("sp", gmask[:], cgm_d, writes=["gmask"])
        for b in range(NB):
            f.dma("sp", xres[:, b, :], x_d[b * 128:(b + 1) * 128, :], writes=[("x", b)])

        cast_rr = [0]

        def cast(out, in_, reads, writes):
            e = "act"
            if e == "pool":
                f.op("pool", lambda g: g.tensor_copy(out=out, in_=in_), reads=reads, writes=writes)
            else:
                f.op("act", lambda g: g.copy(out=out, in_=in_), reads=reads, writes=writes)

        class WS:
            DEPTH = 2

            def __init__(self, srcs):
                self.srcs = srcs
                self.issued = 0
                self.cur = 0
                self.buf = {}

            def get(self):
                while self.issued < min(len(self.srcs), self.cur + 1 + WS.DEPTH):
                    src, kc = self.srcs[self.issued]
                    bi = wctr[1] % 4
                    wctr[1] += 1
                    f.dma("pool", wbf[bi][:, 0:kc, :], src.rearrange("(k p) n -> p k n", p=128),
                          writes=["wbf%d" % bi])
                    self.buf[self.issued] = bi
                    self.issued += 1
                bi = self.buf[self.cur]
                self.cur += 1
                return wbf[bi], "wbf%d" % bi

        def load_big(dst, dkey, src_ap, kc, step=4):
            keys = []
            for k0 in range(0, kc, step):
                k1 = min(kc, k0 + step)
                f.dma("pool", dst[:, k0:k1, :], src_ap[k0 * 128:k1 * 128, :].rearrange("(k p) n -> p k n", p=128),
                      writes=[(dkey, k0)])
                keys.append((dkey, k0))
            return keys

        def load_big_hw(dst, dkey, src_ap, kc, stg, skey):
            keys = []
            for k0 in range(0, kc, 2):
                f.dma("sp", stg[:, :, :], src_ap[k0 * 128:(k0 + 2) * 128, :].rearrange("(k p) n -> p k n", p=128),
                      writes=[skey])
                f.op("act", lambda g: g.copy(out=dst[:, k0:k0 + 2, :], in_=stg[:, :, :]), reads=[skey], writes=[(dkey, k0)])
                keys.append((dkey, k0))
            return keys

        scol = [0]

        def newcol():
            scol[0] = (scol[0] + 1) % 64
            return scol[0]

        def rstd_from_ss(c_ss):
            c1 = newcol()
            c2 = newcol()
            f.op("act", lambda e: e.activation(out=stat[:, c1:c1 + 1], in_=stat[:, c_ss:c_ss + 1], func=AF.Sqrt,
                                               scale=1.0 / D, bias=EPS),
                 reads=[("st", c_ss)], writes=[("st", c1)])
            f.op("dve", lambda e: e.reciprocal(out=stat[:, c2:c2 + 1], in_=stat[:, c1:c1 + 1]),
                 reads=[("st", c1)], writes=[("st", c2)])
            return c2

        def prenorm_block(b, dst, dkey):
            c = newcol()
            f.op("act", lambda e: e.activation(out=junk[:], in_=xres[:, b, :], func=AF.Square,
                                               accum_out=stat[:, c:c + 1]),
                 reads=[("x", b)], writes=["junk", ("st", c)])
            c2 = rstd_from_ss(c)
            f.op("dve", lambda e: e.scalar_tensor_tensor(out=dst, in0=xres[:, b, :], scalar=stat[:, c2:c2 + 1],
                                                         in1=gain[:], op0=ALU.mult, op1=ALU.mult),
                 reads=[("x", b), ("st", c2), "gain"], writes=[dkey])

        def transpose_block(src, skey, dst_ap, dkey, pi):
            for c in range(8):
                f.op("pe", lambda e, c=c: e.transpose(pst[pi][:, c * 128:(c + 1) * 128], src[:, c * 128:(c + 1) * 128],
                                                      ident[:]),
                     reads=[skey, "ident"], writes=[PST[pi]], inc=(c == 7))
            f.op("act", lambda e: e.copy(out=dst_ap, in_=pst[pi][:].rearrange("p (c t) -> p c t", c=8)),
                 reads=[PST[pi]], writes=[dkey])

        def outproj_block(b, mT_fn, mkeys, KC, wout, wkey, l):
            pp = (b % 2) * 2
            for fh in range(2):
                for kc in range(KC):
                    f.op("pe", lambda e, kc=kc, fh=fh: e.matmul(ps[pp + fh][:], lhsT=mT_fn(kc),
                                                                rhs=wout[:, kc, fh * 512:(fh + 1) * 512],
                                                                start=(kc == 0), stop=(kc == KC - 1)),
                         reads=list(mkeys) + list(wkey), writes=[PS[pp + fh]], inc=(kc == KC - 1))
            ca = newcol()
            cb = newcol()
            f.op("act", lambda e: e.activation(out=junk[:, 0:512], in_=ps[pp][:], func=AF.Square,
                                               accum_out=stat[:, ca:ca + 1]),
                 reads=[PS[pp]], writes=["junk", ("st", ca)])
            f.op("act", lambda e: e.activation(out=junk[:, 512:1024], in_=ps[pp + 1][:], func=AF.Square,
                                               accum_out=stat[:, cb:cb + 1]),
                 reads=[PS[pp + 1]], writes=["junk", ("st", cb)])
            cs = newcol()
            f.op("dve", lambda e: e.tensor_tensor(out=stat[:, cs:cs + 1], in0=stat[:, ca:ca + 1],
                                                  in1=stat[:, cb:cb + 1], op=ALU.add),
                 reads=[("st", ca), ("st", cb)], writes=[("st", cs)])
            c2 = rstd_from_ss(cs)
            for fh in range(2):
                tk = "ytmp%d" % fh
                f.op("dve", lambda e, fh=fh: e.scalar_tensor_tensor(out=ytmp_box[0][:, fh, :], in0=ps[pp + fh][:],
                                                                    scalar=stat[:, c2:c2 + 1],
                                                                    in1=gain[:, fh * 512:(fh + 1) * 512],
                                                                    op0=ALU.mult, op1=ALU.mult),
                     reads=[PS[pp + fh], ("st", c2), "gain"], writes=[tk])
                f.op("pool", lambda e, fh=fh: e.tensor_tensor(out=xres[:, b, fh * 512:(fh + 1) * 512],
                                                              in0=xres[:, b, fh * 512:(fh + 1) * 512],
                                                              in1=ytmp_box[0][:, fh, :], op=ALU.add),
                     reads=[("x", b), tk], writes=[("x", b)])

        ytmp_box = [None]

        def odd_layer(l):
            i = l // 2
            with ExitStack() as ls:
                def lsb(name, shape, dtype):
                    return ls.enter_context(nc.sbuf_tensor(nk(name), list(shape), dtype))
                hN = lsb("hN", [128, 5, D], BF16)
                hT = lsb("hTq", [128, 8, 512], BF16)
                phT = lsb("phT", [128, 8, 512], BF16)
                mixed = lsb("mixed", [128, 4, 512], BF16)
                mT = lsb("mTq", [128, 16, 512], BF16)
                sg = lsb("sg", [128, 512], F32)
                bandf = lsb("bandf", [128, 12, 128], F32)
                band = lsb("band", [128, 4, 4, 128], BF16)
                btmp = lsb("btmp", [128, 4, 128], F32)
                pwbf = lsb("pwbf", [128, 4, 512], BF16)
                wout = lsb("wout", [128, 16, D], BF16)
                pscale = lsb("pscale", [128, 16], F32)
                wost = [lsb("wost%d" % k_, [128, 2, D], F32) for k_ in range(2)]
                pwst = lsb("pwst", [128, 4, 512], F32)
                ytmp_box[0] = lsb("ytmp", [128, 2, 512], F32)
                f.dma("sp", bandf[:], cband_d.rearrange("p (a t) -> p a t", t=128), writes=["bandf"])
                bv = bandf[:].rearrange("p (w k) t -> p w k t", k=3)
                f.op("dve", lambda e: e.tensor_copy(out=band[:, :, 0:3, :], in_=bv), reads=["bandf"], writes=["band"])
                f.op("dve", lambda e: e.tensor_copy(out=btmp[:], in_=band[:, :, 2, :]), reads=["band"], writes=["btmp"])
                f.op("dve", lambda e: e.tensor_tensor(out=btmp[:], in0=bv[:, :, 2, :], in1=btmp[:], op=ALU.subtract),
                     reads=["bandf", "btmp"], writes=["btmp"])
                f.op("dve", lambda e: e.tensor_copy(out=band[:, :, 3, :], in_=btmp[:]), reads=["btmp"], writes=["band"])
                f.dma("sp", pscale[:], psc_d[i].rearrange("(c p) -> p c", p=128), writes=["pscale"],
                      allow_slow_non_contiguous=True)
                srcs = []
                for tq in range(4):
                    for g in range(4):
                        for j in range(4):
                            srcs.append((owin_d[i][:, g * 512 + j * 128:g * 512 + (j + 1) * 128], 8))
                        for dj in range(4):
                            col = 2048 + g * 512 + dj * 128
                            srcs.append((owin_d[i][:, col:col + 128], 8))
                ws = WS(srcs)
                for tq in range(4):
                    b0 = tq * 4
                    woutk = [("wout", 2 * k_) for k_ in range(8)]
                    wstep = [0]

                    def wout_step():
                        k_ = wstep[0]
                        wstep[0] += 1
                        if 1 <= k_ <= 8:
                            kk = k_ - 1
                            f.op("act", lambda g_: g_.copy(out=wout[:, 2 * kk:2 * kk + 2, :], in_=wost[kk % 2][:]),
                                 reads=["wost%d" % (kk % 2)], writes=[("wout", 2 * kk)])
                        if k_ < 8:
                            f.dma("sp", wost[k_ % 2][:],
                                  owout_d[i][k_ * 256:(k_ + 1) * 256, :].rearrange("(k p) n -> p k n", p=128),
                                  writes=["wost%d" % (k_ % 2)])
                    wout_step()
                    f.dma("sp", gain[:], bcast("pre_norm", l * D, D), writes=["gain"])
                    if tq > 0:
                        f.op("pool", lambda e: e.tensor_copy(out=hN[:, 0, :], in_=hN[:, 4, :]),
                             reads=[("hN", 4)], writes=[("hN", 0)])
                    for s_, b in enumerate(range(b0 - 1, b0 + 4)):
                        if s_ == 0:
                            continue
                        prenorm_block(b, hN[:, s_, :], ("hN", s_))
                    for s_ in range(1, 5):
                        transpose_block(hN[:, s_, :], ("hN", s_), hT[:, :, (s_ - 1) * 128:s_ * 128], "hT", s_ % 2)
                    for g in range(4):
                        f.dma("sp", pwst[:], pw_d[i, g].rearrange("(k p) n -> p k n", p=128), writes=["pwst"])
                        pwk = [("pwbf", 0), ("pwbf", 2)]
                        for s_ in range(1, 5):
                            b = b0 + s_ - 1
                            for hf in range(2):
                                pk = 4 + hf
                                for cc in range(4):
                                    c = hf * 4 + cc
                                    o = ps[pk][:, cc * 128:(cc + 1) * 128]
                                    lh = hN[:, s_, c * 128:(c + 1) * 128]
                                    if b == 0:
                                        f.op("pe", lambda e, o=o, lh=lh: e.matmul(o, lhsT=lh, rhs=band[:, g, 2, :],
                                                                                  start=True, stop=False),
                                             reads=[("hN", s_), "band"], writes=[PS[pk]], inc=False)
                                        f.op("pe", lambda e, o=o, lh=lh: e.matmul(o, lhsT=lh, rhs=band[:, g, 3, :],
                                                                                  start=False, stop=True),
                                             reads=[("hN", s_), "band"], writes=[PS[pk]], inc=(cc == 3))
                                    else:
                                        lp = hN[:, s_ - 1, c * 128:(c + 1) * 128]
                                        f.op("pe", lambda e, o=o, lh=lh: e.matmul(o, lhsT=lh, rhs=band[:, g, 0, :],
                                                                                  start=True, stop=False),
                                             reads=[("hN", s_), "band"], writes=[PS[pk]], inc=False)
                                        f.op("pe", lambda e, o=o, lp=lp: e.matmul(o, lhsT=lp, rhs=band[:, g, 1, :],
                                                                                  start=False, stop=True),
                                             reads=[("hN", s_ - 1), "band"], writes=[PS[pk]], inc=(cc == 3))
                                f.op("act", lambda e, hf=hf, pk=pk: e.copy(
                                    out=phT[:, hf * 4:(hf + 1) * 4, (s_ - 1) * 128:s_ * 128],
                                    in_=ps[pk][:].rearrange("p (c t) -> p c t", c=4)),
                                    reads=[PS[pk]], writes=["phT"])
                        for j in range(4):
                            wout_step()
                            wb, wk = ws.get()
                            pk = j % 2
                            for kc in range(8):
                                f.op("pe", lambda e, kc=kc: e.matmul(ps[pk][:], lhsT=wb[:, kc, :], rhs=phT[:, kc, :],
                                                                     start=(kc == 0), stop=(kc == 7)),
                                     reads=[wk, "phT"], writes=[PS[pk]], inc=(kc == 7))
                            f.op("act", lambda e, j=j, pk=pk: e.copy(out=mixed[:, j, :], in_=ps[pk][:]),
                                 reads=[PS[pk]], writes=[("mixed", j)])
                        for k0_ in (0, 2):
                            f.op("act", lambda g_, k0_=k0_: g_.copy(out=pwbf[:, k0_:k0_ + 2, :], in_=pwst[:, k0_:k0_ + 2, :]),
                                 reads=["pwst"], writes=[("pwbf", k0_)])
                        for dj in range(4):
                            wb, wk = ws.get()
                            pg = 2
                            for kc in range(8):
                                f.op("pe", lambda e, kc=kc: e.matmul(ps[pg][:], lhsT=wb[:, kc, :], rhs=hT[:, kc, :],
                                                                     start=(kc == 0), stop=(kc == 7)),
                                     reads=[wk, "hT"], writes=[PS[pg]], inc=(kc == 7))
                            f.op("act", lambda e: e.activation(out=sg[:], in_=ps[pg][:], func=AF.Silu),
                                 reads=[PS[pg]], writes=["sg"])
                            py = 3
                            for j in range(4):
                                f.op("pe", lambda e, j=j, dj=dj: e.matmul(ps[py][:], lhsT=pwbf[:, j, dj * 128:(dj + 1) * 128],
                                                                          rhs=mixed[:, j, :], start=(j == 0), stop=(j == 3)),
                                     reads=pwk + [("mixed", j)], writes=[PS[py]], inc=(j == 3))
                            ch = g * 4 + dj
                            f.op("dve", lambda e, ch=ch: e.scalar_tensor_tensor(out=mT[:, ch, :], in0=ps[py][:],
                                                                                scalar=pscale[:, ch:ch + 1], in1=sg[:],
                                                                                op0=ALU.mult, op1=ALU.mult),
                                 reads=[PS[py], "pscale", "sg"], writes=[("mT", ch)])
                    f.dma("sp", gain[:], bcast("post_norm", l * D, D), writes=["gain"])
                    for bb in range(4):
                        outproj_block(b0 + bb, lambda kc, bb=bb: mT[:, kc, bb * 128:(bb + 1) * 128],
                                      [("mT", ch) for ch in range(16)], 16, wout, woutk, l)
                f.barrier()

        def even_layer(l):
            i = l // 2
            with ExitStack() as ls:
                def lsb(name, shape, dtype):
                    return ls.enter_context(nc.sbuf_tensor(nk(name), list(shape), dtype))
                hT = lsb("hT", [128, 8, S], BF16)
                mT = lsb("mT", [128, 12, S], BF16)
                f.dma("sp", gain[:], bcast("pre_norm", l * D, D), writes=["gain"])
                with ExitStack() as hs_:
                    hNb = [hs_.enter_context(nc.sbuf_tensor(nk("hNb"), [128, D], BF16)) for k in range(2)]
                    for b in range(NB):
                        prenorm_block(b, hNb[b % 2][:], "hNb%d" % (b % 2))
                        transpose_block(hNb[b % 2], "hNb%d" % (b % 2), hT[:, :, b * 128:(b + 1) * 128], "hT", b % 2)
                    f.barrier()

                def proj_fm(wb, wk, pk, tg):
                    for kc in range(8):
                        f.op("pe", lambda e, kc=kc: e.matmul(ps[pk][:], lhsT=wb[:, kc, :],
                                                             rhs=hT[:, kc, tg * 512:(tg + 1) * 512],
                                                             start=(kc == 0), stop=(kc == 7)),
                             reads=[wk, "hT"], writes=[PS[pk]], inc=(kc == 7))

                with ExitStack() as as_:
                    def asb(name, shape, dtype):
                        return as_.enter_context(nc.sbuf_tensor(nk(name), list(shape), dtype))
                    qT = asb("qT", [128, S], BF16)
                    kT = asb("kT", [128, 2, S], BF16)
                    V = asb("V", [128, NB, 2, 128], BF16)
                    ropec = asb("ropec", [128, S], BF16)
                    ropes = asb("ropes", [128, S], BF16)
                    maskT = asb("maskT", [128, 19 * 128], BF16)
                    Pt = [asb("Pt%d" % k, [128, 512], BF16) for k in range(4)]
                    rtmp = [asb("rtmp%d" % k, [128, 512], F32) for k in range(2)]
                    rc = rtmp[0]
                    atmp = rtmp[1]
                    f.dma("pool", ropec[:], cropec_d, writes=["ropec"])
                    f.dma("pool", ropes[:], cropes_d, writes=["ropes"])
                    f.dma("pool", maskT[:], cmask_d, writes=["maskT"])
                    srcs = []
                    for hp_ in range(8):
                        c0_ = hp_ * 128
                        for (base_, swb_) in ((0, 0), (1024, 1024)):
                            srcs.append((ewin_d[i][:, base_ + c0_:base_ + c0_ + 128], 8))
                            srcs.append((ewsw_d[i][:, swb_ + c0_:swb_ + c0_ + 128], 8))
                        srcs.append((ewin_d[i][:, 3072 + c0_:3072 + c0_ + 128], 8))
                        srcs.append((ewin_d[i][:, 2048 + c0_:2048 + c0_ + 128], 8))
                    ws = WS(srcs)
                    f.op("pool", lambda e: e.memset(V[:, :, :, 64:128], 1.0), writes=["V"])
                    f.op("pool", lambda e: e.memset(kT[:], 0.0), writes=["kT"])
                    mrr = [0]
                    for hp in range(8):
                        c0 = hp * 128
                        for (dst, dk, base, swb) in ((qT, "qT", 0, 0), (kT, "kT", 1024, 1024)):
                            wb, wk = ws.get()
                            wb2, wk2 = ws.get()
                            for tg in range(4):
                                sl = slice(tg * 512, (tg + 1) * 512)
                                pa0 = 2 * (tg % 2)
                                proj_fm(wb, wk, pa0, tg)
                                proj_fm(wb2, wk2, pa0 + 1, tg)
                                r0 = "rtmp0"
                                r1 = "rtmp1"
                                f.op("dve", lambda e, sl=sl: e.tensor_tensor(out=rtmp[0][:], in0=ps[pa0][:], in1=ropec[:, sl],
                                                                             op=ALU.mult),
                                     reads=[PS[pa0], "ropec"], writes=[r0])
                                f.op("dve", lambda e, sl=sl: e.tensor_tensor(out=rtmp[1][:], in0=ps[pa0 + 1][:], in1=ropes[:, sl],
                                                                             op=ALU.mult),
                                     reads=[PS[pa0 + 1], "ropes"], writes=[r1])
                                if dk == "qT":
                                    f.op("dve", lambda e, sl=sl: e.tensor_tensor(out=qT[:, sl], in0=rtmp[0][:],
                                                                                 in1=rtmp[1][:], op=ALU.add),
                                         reads=[r0, r1], writes=[dk])
                                else:
                                    for a_ in range(2):
                                        pr_ = slice(64 * a_, 64 * a_ + 64)
                                        f.op("dve", lambda e, sl=sl, a_=a_, pr_=pr_: e.tensor_tensor(
                                            out=kT[pr_, a_, sl], in0=rtmp[0][pr_, :], in1=rtmp[1][pr_, :], op=ALU.add),
                                            reads=[r0, r1], writes=[dk])
                        wb, wk = ws.get()
                        for tg in range(4):
                            proj_fm(wb, wk, 2, tg)
                            f.op("act", lambda e, tg=tg: e.activation(out=mT[:, hp, tg * 512:(tg + 1) * 512], in_=ps[2][:],
                                                                      func=AF.Silu),
                                 reads=[PS[2]], writes=[("mT", hp)])
                        wb, wk = ws.get()
                        for b4 in range(4):
                            pk = 3
                            for bb in range(4):
                                b = b4 * 4 + bb
                                for kc in range(8):
                                    f.op("pe", lambda e, kc=kc, b=b, bb=bb: e.matmul(
                                        ps[pk][:, bb * 128:(bb + 1) * 128], lhsT=hT[:, kc, b * 128:(b + 1) * 128],
                                        rhs=wb[:, kc, :], start=(kc == 0), stop=(kc == 7)),
                                        reads=[wk, "hT"], writes=[PS[pk]], inc=(kc == 7 and bb == 3))
                            f.op("act", lambda e, b4=b4: e.copy(
                                out=V[:, b4 * 4:(b4 + 1) * 4, :, 0:64],
                                in_=ps[pk][:].rearrange("p (b a d) -> p b a d", b=4, a=2)),
                                reads=[PS[pk]], writes=["V"])
                        items = []
                        for a in range(2):
                            for qg in range(4):
                                nkb = 4 * qg + 4
                                for kb in range(nkb):
                                    items.append((a, qg, kb, nkb))
                        LAG = 2

                        def stage1(idx):
                            a, qg, kb, nkb = items[idx]
                            pr = slice(64 * a, 64 * a + 64)
                            cq = max(128 * kb, 512 * qg)
                            N = 512 * (qg + 1) - cq
                            sk = idx % 4
                            f.op("pe", lambda e: e.matmul(
                                ps[sk][:, 0:N], lhsT=kT[:, a, kb * 128:(kb + 1) * 128], rhs=qT[:, cq:cq + N],
                                start=True, stop=True),
                                reads=["kT", "qT"], writes=[PS[sk]])
                            f.op("act", lambda e: e.activation(out=Pt[sk][:, 0:N], in_=ps[sk][:, 0:N],
                                                               func=AF.Exp, scale=0.125),
                                 reads=[PS[sk]], writes=["Pt%d" % sk])
                            moff = ((cq - 128 * kb) // 128 + 3) * 128
                            me = "dve"
                            mrr[0] += 1
                            f.op(me, lambda e: e.tensor_tensor(
                                out=Pt[sk][:, 0:N], in0=Pt[sk][:, 0:N], in1=maskT[:, moff:moff + N], op=ALU.mult),
                                reads=["Pt%d" % sk, "maskT"], writes=["Pt%d" % sk])

                        def stage2(idx):
                            a, qg, kb, nkb = items[idx]
                            pr = slice(64 * a, 64 * a + 64)
                            cq = max(128 * kb, 512 * qg)
                            N = 512 * (qg + 1) - cq
                            sk = idx % 4
                            po = 4 + (qg % 2)
                            oc = cq - 512 * qg
                            f.op("pe", lambda e: e.matmul(
                                ps[po][:, oc:oc + N], lhsT=V[:, kb, a, :], rhs=Pt[sk][:, 0:N],
                                start=(kb == 0), stop=(kb == nkb - 1)),
                                reads=["V", "Pt%d" % sk], writes=[PS[po]])
                            if kb == nkb - 1:
                                qs = slice(qg * 512, (qg + 1) * 512)
                                f.op("act", lambda e: e.activation(out=rc[64:128, :], in_=ps[po][64:128, :], func=AF.Ln),
                                     reads=[PS[po]], writes=["rtmp0"])
                                f.op("act", lambda e: e.activation(out=rc[64:128, :], in_=rc[64:128, :], func=AF.Exp,
                                                                   scale=-1.0),
                                     reads=["rtmp0"], writes=["rtmp0"])
                                f.op("dve", lambda e: e.tensor_tensor(out=atmp[pr, :], in0=ps[po][0:64, :],
                                                                      in1=rc[64:128, :], op=ALU.mult),
                                     reads=[PS[po], "rtmp0"], writes=["rtmp1"])
                                f.op("pool", lambda e: e.tensor_tensor(out=mT[pr, hp, qs], in0=atmp[pr, :],
                                                                       in1=mT[pr, hp, qs], op=ALU.mult),
                                     reads=["rtmp1", ("mT", hp)], writes=[("mT", hp)])

                        for idx in range(len(items) + LAG):
                            if idx < len(items):
                                stage1(idx)
                            if idx - LAG >= 0:
                                stage2(idx - LAG)
                    f.barrier()

                with ExitStack() as ss_:
                    def ssb(name, shape, dtype):
                        return ss_.enter_context(nc.sbuf_tensor(nk(name), list(shape), dtype))
                    NPW = 17 + 7
                    PR = ssb("PR", [128, 16, 3 * 24 + 4], F32)
                    pa = ssb("pa", [128, 16, 12], F32)
                    pi32 = ssb("pi32", [128, 16], I32)
                    XR = ssb("XR", [128, S], F32)
                    XI = ssb("XI", [128, S], F32)
                    cs_ = [ssb("cs%d" % k, [128, 2, 192], F32) for k in range(2)]
                    Xb = [ssb("Xb%d" % k, [128, 512], BF16) for k in range(2)]
                    uT = ssb("uT", [128, S], BF16)
                    bnat = ssb("bnat", [128, 2, 16, 16], F32)
                    cnat = ssb("cnat", [128, 2, 64], F32)
                    padT = ssb("padT", [128, 128], BF16)
                    padTf = ssb("padTf", [128, 2, 128], F32)
                    Bpad = ssb("Bpad", [128, 4, 2, 128], BF16)
                    Cpad = ssb("Cpad", [128, 4, 2, 128], BF16)
                    cf = ssb("cf", [128, 4, 128], F32)
                    dvec = ssb("dvec", [128, 4], F32)
                    glub = ssb("glub", [128, 4], F32)
                    gl = [ssb("gl%d" % k, [128, 512], F32) for k in range(2)] + [XR[:, 0:512]]

                    for k_ in range(2):
                        f.op("pool", lambda e, k_=k_: e.memset(cs_[k_][:], 0.0), writes=[("cs", k_, 0), ("cs", k_, 1)])
                    def ld_gp(dst_col, name):
                        for e_ in range(2):
                            f.dma("sp", pa[e_ * 64:(e_ + 1) * 64, :, dst_col],
                                  bass.AP(dt_[name], i * 2048 + e_ * 64, [[1, 64], [128, 16]]),
                                  writes=["pa"], allow_slow_non_contiguous=True)
                    ld_gp(0, "ssm_a_re")
                    ld_gp(1, "ssm_a_im")
                    for e_ in range(2):
                        f.dma("sp", pa[e_ * 64:(e_ + 1) * 64, :, 2],
                              bass.AP(dt_["ssm_log_dt"], i * 32 + e_, [[0, 64], [2, 16]]),
                              writes=["pa"], allow_slow_non_contiguous=True)
                    f.dma("sp", dvec[:], sd_d[i].rearrange("(c p) -> p c", p=128), writes=["dvec"],
                          allow_slow_non_contiguous=True)
                    f.dma("sp", glub[:], glb_d[i].rearrange("(c p) -> p c", p=128), writes=["glub"],
                          allow_slow_non_contiguous=True)

                    def pop(eng, fn, w=("pa",)):
                        f.op(eng, fn, reads=["pa", "PR"], writes=list(w))
                    A = lambda c: pa[:, :, c]
                    pop("act", lambda e: e.activation(out=A(3), in_=A(2), func=AF.Exp))
                    pop("dve", lambda e: e.tensor_tensor(out=A(4), in0=A(0), in1=A(3), op=ALU.mult))
                    pop("dve", lambda e: e.tensor_tensor(out=A(5), in0=A(1), in1=A(3), op=ALU.mult))
                    pop("act", lambda e: e.activation(out=A(4), in_=A(4), func=AF.Exp))

                    def sin_of(dst, shift):
                        pop("dve", lambda e: e.tensor_scalar(out=A(6), in0=A(5), scalar1=shift, scalar2=1.0 / TWO_PI,
                                                             op0=ALU.add, op1=ALU.mult))
                        f.op("dve", lambda e: e.tensor_copy(out=pi32[:], in_=A(6)), reads=["pa"], writes=["pi32"])
                        f.op("dve", lambda e: e.tensor_copy(out=A(7), in_=pi32[:]), reads=["pi32"], writes=["pa"])
                        pop("dve", lambda e: e.tensor_tensor(out=A(6), in0=A(6), in1=A(7), op=ALU.subtract))
                        pop("dve", lambda e: e.tensor_scalar(out=A(6), in0=A(6), scalar1=TWO_PI, scalar2=math.pi,
                                                             op0=ALU.mult, op1=ALU.min))
                        pop("dve", lambda e: e.tensor_scalar(out=A(6), in0=A(6), scalar1=-math.pi, scalar2=None,
                                                             op0=ALU.max))
                        pop("act", lambda e: e.activation(out=dst, in_=A(6), func=AF.Sin))
                    sin_of(A(8), 0.0)
                    sin_of(A(9), math.pi / 2)
                    P3 = lambda k, c: PR[:, :, 3 * k + c]
                    pop("dve", lambda e: e.tensor_tensor(out=P3(0, 0), in0=A(4), in1=A(9), op=ALU.mult), w=("PR",))
                    pop("dve", lambda e: e.tensor_tensor(out=P3(0, 1), in0=A(4), in1=A(8), op=ALU.mult), w=("PR",))

                    def cmul(dst, a_, b_):
                        pop("dve", lambda e: e.tensor_tensor(out=A(6), in0=P3(a_, 0), in1=P3(b_, 0), op=ALU.mult))
                        pop("dve", lambda e: e.tensor_tensor(out=A(7), in0=P3(a_, 1), in1=P3(b_, 1), op=ALU.mult))
                        pop("dve", lambda e: e.tensor_tensor(out=A(10), in0=P3(a_, 0), in1=P3(b_, 1), op=ALU.mult))
                        pop("dve", lambda e: e.tensor_tensor(out=A(11), in0=P3(a_, 1), in1=P3(b_, 0), op=ALU.mult))
                        pop("dve", lambda e: e.tensor_tensor(out=P3(dst, 0), in0=A(6), in1=A(7), op=ALU.subtract), w=("PR",))
                        pop("dve", lambda e: e.tensor_tensor(out=P3(dst, 1), in0=A(10), in1=A(11), op=ALU.add), w=("PR",))
                    for j in range(1, 16):
                        cmul(j, j - 1, 0)
                    for k in range(16, 16 + 7):
                        cmul(k, k - 1, k - 1)
                    for k in range(23):
                        pop("dve", lambda e, k=k: e.tensor_scalar(out=P3(k, 2), in0=P3(k, 1), scalar1=-1.0, scalar2=None,
                                                                  op0=ALU.mult), w=("PR",))
                    FR = PR[:, :, 72]
                    FI = PR[:, :, 73]
                    pop("dve", lambda e: e.tensor_scalar(out=A(6), in0=P3(0, 0), scalar1=-1.0, scalar2=None, op0=ALU.add))
                    pop("dve", lambda e: e.tensor_tensor(out=A(7), in0=A(0), in1=A(0), op=ALU.mult))
                    pop("dve", lambda e: e.tensor_tensor(out=A(10), in0=A(1), in1=A(1), op=ALU.mult))
                    pop("dve", lambda e: e.tensor_tensor(out=A(7), in0=A(7), in1=A(10), op=ALU.add))
                    pop("dve", lambda e: e.reciprocal(out=A(7), in_=A(7)))
                    pop("dve", lambda e: e.tensor_tensor(out=A(10), in0=A(6), in1=A(0), op=ALU.mult))
                    pop("dve", lambda e: e.tensor_tensor(out=A(11), in0=P3(0, 1), in1=A(1), op=ALU.mult))
                    pop("dve", lambda e: e.tensor_tensor(out=A(10), in0=A(10), in1=A(11), op=ALU.add))
                    pop("dve", lambda e: e.tensor_tensor(out=FR, in0=A(10), in1=A(7), op=ALU.mult), w=("PR",))
                    pop("dve", lambda e: e.tensor_tensor(out=A(10), in0=P3(0, 1), in1=A(0), op=ALU.mult))
                    pop("dve", lambda e: e.tensor_tensor(out=A(11), in0=A(6), in1=A(1), op=ALU.mult))
                    pop("dve", lambda e: e.tensor_tensor(out=A(10), in0=A(10), in1=A(11), op=ALU.subtract))
                    pop("dve", lambda e: e.tensor_tensor(out=FI, in0=A(10), in1=A(7), op=ALU.mult), w=("PR",))
                    for ri, name in enumerate(("ssm_b_re", "ssm_b_im")):
                        for e_ in range(2):
                            f.dma("sp", bnat[e_ * 64:(e_ + 1) * 64, ri, :, :],
                                  bass.AP(dt_[name], i * 32 * 1024 + e_ * 1024,
                                          [[16, 64], [2048, 16], [1, 16]]), writes=["bnat"])

                    XALL = [["X0"] + [("XR", s_) for s_ in range(16)], ["X1"] + [("XI", s_) for s_ in range(16)]]

                    srcs = [(ewin_d[i][:, 4096 + j_ * 128:4096 + (j_ + 1) * 128], 8) for j_ in range(4)]
                    for tg_ in range(4):
                        srcs += [(glw_d[i][:, fo_ * 128:(fo_ + 1) * 128], 4) for fo_ in range(4)]
                        srcs += [(ewin_d[i][:, 4608 + fo_ * 128:4608 + (fo_ + 1) * 128], 8) for fo_ in range(4)]
                    ws = WS(srcs)

                    def PSC(q, k, c):
                        return PR[:, q, 3 * k + c:3 * k + c + 1]

                    for j in range(4):
                        for ri, name in enumerate(("ssm_c_re", "ssm_c_im")):
                            f.dma("sp", cnat[:, ri, :],
                                  bass.AP(dt_[name], i * 32 * 1024 + j * 8 * 1024, [[64, 128], [1, 64]]),
                                  writes=["cnat"])
                        for qq in range(4):
                            q = j * 4 + qq
                            for ri in range(2):
                                f.op("pool", lambda e: e.memset(padT[:], 0.0), writes=["padT"])
                                for e_ in range(2):
                                    co = 16 * (2 * qq + e_)
                                    f.op("dve", lambda e, e_=e_, co=co, ri=ri, q=q: e.tensor_copy(
                                        out=padT[e_ * 64:(e_ + 1) * 64, co:co + 16],
                                        in_=bnat[e_ * 64:(e_ + 1) * 64, ri, q, :]),
                                        reads=["bnat"], writes=["padT"])
                                f.op("pe", lambda e: e.transpose(pst[0][:, 0:128], padT[:], ident[:]),
                                     reads=["padT", "ident"], writes=[PST[0]])
                                f.op("act", lambda e, qq=qq, ri=ri: e.copy(out=Bpad[:, qq, ri, :], in_=pst[0][:, 0:128]),
                                     reads=[PST[0]], writes=["Bpad"])
                            for ri in range(2):
                                for e_ in range(2):
                                    g8 = 2 * qq + e_
                                    f.op("dve", lambda e, e_=e_, ri=ri, g8=g8: e.tensor_scalar(
                                        out=padTf[:, ri, e_ * 64:(e_ + 1) * 64], in0=cnat[:, ri, :],
                                        scalar1=gmask[:, g8:g8 + 1], scalar2=None, op0=ALU.mult),
                                        reads=["cnat", "gmask"], writes=["padTf"])
                            for ri in range(2):
                                f.op("pe", lambda e, ri=ri: e.matmul(ps[4][:, ri * 128:(ri + 1) * 128], lhsT=padTf[:, ri, :],
                                                                     rhs=identf[:], start=True, stop=True),
                                     reads=["padTf", "identf"], writes=[PS[4]])
                            fr = PR[:, q, 72:73]
                            fi = PR[:, q, 73:74]
                            f.op("act", lambda e: e.copy(out=cf[:, 0:2, :], in_=ps[4][:, 0:256].rearrange("p (a b) -> p a b", a=2)),
                                 reads=[PS[4]], writes=["cf"])
                            f.op("dve", lambda e, fr=fr: e.tensor_scalar(out=cf[:, 2, :], in0=cf[:, 0, :], scalar1=fr, scalar2=None,
                                                                         op0=ALU.mult), reads=["cf", "PR"], writes=["cf"])
                            f.op("dve", lambda e, fi=fi: e.tensor_scalar(out=cf[:, 3, :], in0=cf[:, 1, :], scalar1=fi, scalar2=None,
                                                                         op0=ALU.mult), reads=["cf", "PR"], writes=["cf"])
                            f.op("dve", lambda e, qq=qq: e.tensor_tensor(out=Cpad[:, qq, 0, :], in0=cf[:, 2, :], in1=cf[:, 3, :],
                                                                         op=ALU.subtract), reads=["cf"], writes=["Cpad"])
                            f.op("dve", lambda e, fi=fi: e.tensor_scalar(out=cf[:, 2, :], in0=cf[:, 0, :], scalar1=fi, scalar2=-1.0,
                                                                         op0=ALU.mult, op1=ALU.mult), reads=["cf", "PR"], writes=["cf"])
                            f.op("dve", lambda e, fr=fr: e.tensor_scalar(out=cf[:, 3, :], in0=cf[:, 1, :], scalar1=fr, scalar2=None,
                                                                         op0=ALU.mult), reads=["cf", "PR"], writes=["cf"])
                            f.op("dve", lambda e, qq=qq: e.tensor_tensor(out=Cpad[:, qq, 1, :], in0=cf[:, 2, :], in1=cf[:, 3, :],
                                                                         op=ALU.subtract), reads=["cf"], writes=["Cpad"])
                        wb, wk = ws.get()
                        for tg in range(4):
                            sl = slice(tg * 512, (tg + 1) * 512)
                            proj_fm(wb, wk, 4, tg)
                            f.op("act", lambda e, sl=sl: e.copy(out=uT[:, sl], in_=ps[4][:]), reads=[PS[4]], writes=["uT"])
                        def stage_in(qq_, ri):
                            X = (XR, XI)[ri]
                            for tg in range(4):
                                sl = slice(tg * 512, (tg + 1) * 512)
                                pk = 4 + (tg % 2)
                                f.op("pe", lambda e, sl=sl, pk=pk: e.matmul(
                                    ps[pk][:], lhsT=Bpad[:, qq_, ri, :], rhs=uT[:, sl], start=True, stop=True),
                                    reads=["Bpad", "uT"], writes=[PS[pk]])
                                f.op("act", lambda e, sl=sl, pk=pk, X=X: e.copy(out=X[:, sl], in_=ps[pk][:]),
                                     reads=[PS[pk]], writes=XALL[ri])

                        def stage_out(qq_, ri):
                            X = (XR, XI)[ri]
                            for tg in range(4):
                                sl = slice(tg * 512, (tg + 1) * 512)
                                bi = (ri * 4 + tg) % 2
                                cast(Xb[bi][:], X[:, sl], XALL[ri], ["Xb%d" % bi])
                                f.op("pe", lambda e, tg=tg, bi=bi: e.matmul(
                                    ps[tg][:], lhsT=Cpad[:, qq_, ri, :], rhs=Xb[bi][:],
                                    start=(qq_ == 0 and ri == 0), stop=(qq_ == 3 and ri == 1)),
                                    reads=["Cpad", "Xb%d" % bi], writes=[PS[tg]])

                        stage_in(0, 0)
                        stage_in(0, 1)
                        for qq in range(4):
                            q = j * 4 + qq
                            XRv = XR[:].rearrange("p (c s) -> p c s", s=16)
                            XIv = XI[:].rearrange("p (c s) -> p c s", s=16)

                            def cstep(oR, oI, iR, iI, k, kOR, kOI, kIR, kII, bR=None, bI=None, kB=()):
                                for (o_, i_, c_, ko, ki, b_) in ((oR, iR, 0, kOR, kIR, bR), (oI, iR, 1, kOI, kIR, bI),
                                                                 (oR, iI, 2, kOR, kII, None), (oI, iI, 0, kOI, kII, None)):
                                    add_ = o_ if b_ is None else b_
                                    f.op("dve", lambda e, o_=o_, i_=i_, c_=c_, add_=add_: e.scalar_tensor_tensor(
                                        out=o_, in0=i_, scalar=PSC(q, k, c_), in1=add_, op0=ALU.mult, op1=ALU.add),
                                        reads=[ki, "PR"] + list(kB), writes=[ko])
                            for s_ in range(1, 16):
                                cstep(XRv[:, :, s_], XIv[:, :, s_], XRv[:, :, s_ - 1], XIv[:, :, s_ - 1], 0,
                                      ("XR", s_), ("XI", s_), ("XR", s_ - 1), ("XI", s_ - 1))
                            cur = 0
                            f.op("act", lambda e: e.copy(out=cs_[0][:, 0, 64:192], in_=XRv[:, :, 15]),
                                 reads=[("XR", 15)], writes=[("cs", 0, 0)])
                            f.op("act", lambda e: e.copy(out=cs_[0][:, 1, 64:192], in_=XIv[:, :, 15]),
                                 reads=[("XI", 15)], writes=[("cs", 0, 1)])
                            for k in range(7):
                                sh = 1 << k
                                src = cs_[cur]
                                dst = cs_[1 - cur]
                                cstep(dst[:, 0, 64:192], dst[:, 1, 64:192], src[:, 0, 64 - sh:192 - sh], src[:, 1, 64 - sh:192 - sh],
                                      15 + k, ("cs", 1 - cur, 0), ("cs", 1 - cur, 1), ("cs", cur, 0), ("cs", cur, 1),
                                      bR=src[:, 0, 64:192], bI=src[:, 1, 64:192])
                                cur = 1 - cur
                            fin = cs_[cur]
                            def p3(comp):
                                Xv = (XRv, XIv)[comp]
                                kx = ("XR", "XI")[comp]
                                cidx = ((0, 2), (1, 0))[comp]
                                for s0 in range(0, 16, 2):
                                    for part in range(2):
                                        for s_ in (s0, s0 + 1):
                                            f.op("dve", lambda e, s_=s_, part=part: e.scalar_tensor_tensor(
                                                out=Xv[:, 1:128, s_], in0=fin[:, part, 64:191],
                                                scalar=PSC(q, s_, cidx[part]), in1=Xv[:, 1:128, s_],
                                                op0=ALU.mult, op1=ALU.add),
                                                reads=[("cs", cur, part), "PR"], writes=[(kx, s_)])
                            p3(0)
                            stage_out(qq, 0)
                            if qq < 3:
                                stage_in(qq + 1, 0)
                            p3(1)
                            stage_out(qq, 1)
                            if qq < 3:
                                stage_in(qq + 1, 1)
                        for tg in range(4):
                            sl = slice(tg * 512, (tg + 1) * 512)
                            f.op("dve", lambda e, sl=sl, tg=tg, j=j: e.scalar_tensor_tensor(out=gl[0][:], in0=uT[:, sl],
                                                                                            scalar=dvec[:, j:j + 1], in1=ps[tg][:],
                                                                                            op0=ALU.mult, op1=ALU.add),
                                 reads=[PS[tg], "uT", "dvec"], writes=["gl0"])
                            f.op("pool", lambda e: e.tensor_tensor(out=gl[1][:], in0=gl[0][:], in1=gl[0][:], op=ALU.mult),
                                 reads=["gl0"], writes=["gl1"])
                            f.op("pool", lambda e: e.tensor_scalar(out=gl[1][:], in0=gl[1][:], scalar1=0.044715, scalar2=1.0,
                                                                   op0=ALU.mult, op1=ALU.add), reads=["gl1"], writes=["gl1"])
                            f.op("pool", lambda e: e.tensor_tensor(out=gl[1][:], in0=gl[1][:], in1=gl[0][:], op=ALU.mult),
                                 reads=["gl1", "gl0"], writes=["gl1"])
                            f.op("act", lambda e: e.activation(out=gl[2][:], in_=gl[1][:], func=AF.Sigmoid,
                                                               scale=2.0 * math.sqrt(2.0 / math.pi)),
                                 reads=["gl1"], writes=["gl2"] + XALL[0])
                            f.op("dve", lambda e, sl=sl, j=j: e.tensor_tensor(out=mT[:, 8 + j, sl], in0=gl[0][:], in1=gl[2][:],
                                                                              op=ALU.mult),
                                 reads=["gl0", "gl2"] + XALL[0], writes=[("mT", 8 + j)])
                    for tg in range(4):
                        sl = slice(tg * 512, (tg + 1) * 512)
                        for fo in range(4):
                            wb, wk = ws.get()
                            for jj in range(4):
                                f.op("pe", lambda e, fo=fo, jj=jj, sl=sl: e.matmul(
                                    ps[fo][:], lhsT=wb[:, jj, :], rhs=mT[:, 8 + jj, sl],
                                    start=(jj == 0), stop=(jj == 3)),
                                    reads=[wk, ("mT", 8 + jj)], writes=[PS[fo]], inc=(jj == 3))
                        for fo in range(4):
                            wb, wk = ws.get()
                            proj_fm(wb, wk, 4, tg)
                            f.op("act", lambda e: e.activation(out=gl[0][:], in_=ps[4][:], func=AF.Sigmoid),
                                 reads=[PS[4]], writes=["gl0"])
                            f.op("act", lambda e, fo=fo: e.activation(out=gl[2][:], in_=ps[fo][:], func=AF.Sigmoid,
                                                                      bias=glub[:, fo:fo + 1]),
                                 reads=[PS[fo], "glub"], writes=["gl2"] + XALL[0])
                            f.op("pool", lambda e: e.tensor_tensor(out=gl[1][:], in0=gl[2][:], in1=gl[0][:], op=ALU.mult),
                                 reads=["gl2", "gl0"] + XALL[0], writes=["gl1"])
                            f.op("dve", lambda e: e.tensor_tensor(out=gl[1][:], in0=ps[4][:], in1=gl[1][:], op=ALU.mult),
                                 reads=[PS[4], "gl1"], writes=["gl1"])
                            f.op("dve", lambda e, fo=fo, sl=sl: e.tensor_tensor(out=mT[:, 8 + fo, sl], in0=mT[:, 8 + fo, sl],
                                                                                in1=gl[1][:], op=ALU.mult),
                                 reads=["gl1", ("mT", 8 + fo)] + [PS[k] for k in range(4)], writes=[("mT", 8 + fo)])
                    f.barrier()

                with ExitStack() as os_:
                    wout = os_.enter_context(nc.sbuf_tensor(nk("ewout"), [128, 12, D], BF16))
                    ytmp_box[0] = os_.enter_context(nc.sbuf_tensor(nk("ytmp"), [128, 2, 512], F32))
                    woutk = load_big(wout, "ewout", ewout_d[i], 12)
                    f.dma("sp", gain[:], bcast("post_norm", l * D, D), writes=["gain"])
                    for b in range(NB):
                        outproj_block(b, lambda kc, b=b: mT[:, kc, b * 128:(b + 1) * 128],
                                      [("mT", ch) for ch in range(12)], 12, wout, woutk, l)
                    f.barrier()

        for l in layers:
            if l % 2 == 0:
                even_layer(l)
            else:
                odd_layer(l)

        for b in range(NB):
            f.dma("sp", out_d[b * 128:(b + 1) * 128, :], xres[:, b, :], reads=[("x", b)], key="out")
        nc.sync.wait_ge(f.dsem["out"][0], f.dsem["out"][1])
    return nc


_CACHE = {}


def prep_inputs(inputs):
    w = {k: np.ascontiguousarray(np.asarray(v, dtype=np.float32)) for k, v in inputs.items()}
    perm = swap_perm()
    shared = {k: v for k, v in w.items() if k != "x"}
    shared["even_w_sw"] = np.ascontiguousarray(w["even_w_in"][:, :, 0:2048][:, :, perm])
    shared.update(host_consts())
    return w["x"], shared


def kernel(**inputs):
    x, shared = prep_inputs(inputs)
    if "nc" not in _CACHE:
        _CACHE["nc"] = build()
    nc = _CACHE["nc"]
    in_maps = []
    for c in range(8):
        m = dict(shared)
        m["x"] = np.ascontiguousarray(x[c])
        in_maps.append(m)
    res = run_bass_kernel_spmd(nc, in_maps, core_ids=list(range(8)))
    return np.stack([np.asarray(r["out"], dtype=np.float32) for r in res.results], axis=0)
```

```python
import math
import numpy as np
import concourse.bass as bass
import concourse.mybir as mybir
from concourse.bass_utils import run_bass_kernel_spmd

F32 = mybir.dt.float32
BF16 = mybir.dt.bfloat16
I32 = mybir.dt.int32
ALU = mybir.AluOpType
AF = mybir.ActivationFunctionType
AX = mybir.AxisListType

S = 2048
D = 1024
NB = 16
EPS = 1e-6
TWO_PI = 2.0 * math.pi


class FW:
    def __init__(self, nc):
        self.nc = nc
        self.eng = {"pe": nc.tensor, "act": nc.scalar, "dve": nc.vector,
                    "pool": nc.gpsimd, "sp": nc.sync}
        self.sem = {}
        self.cnt = {}
        for e in self.eng:
            self.sem[e] = nc.alloc_semaphore("s_" + e)
            self.cnt[e] = 0
        self.waited = {}
        self.lastw = {}
        self.rd = {}
        self.dsem = {}

    def _deps(self, reads, writes):
        deps = {}

        def add(t):
            if t is not None and deps.get(t[0], 0) < t[1]:
                deps[t[0]] = t[1]
        for k in reads:
            add(self.lastw.get(k))
        for k in writes:
            add(self.lastw.get(k))
            for sk, v in self.rd.get(k, {}).items():
                add((sk, v))
        return deps

    def _semof(self, sk):
        if isinstance(sk, tuple):
            return self.dsem[sk[1]][0]
        return self.sem[sk]

    def _emit_waits(self, e, deps):
        for sk, v in deps.items():
            if sk == e and v > self.cnt[e]:
                continue
            if self.waited.get((e, sk), 0) < v:
                self.eng[e].wait_ge(self._semof(sk), v)
                self.waited[(e, sk)] = v

    def _record(self, t, reads, writes):
        for k in writes:
            self.lastw[k] = t
            self.rd[k] = {}
        for k in reads:
            d = self.rd.setdefault(k, {})
            if d.get(t[0], 0) < t[1]:
                d[t[0]] = t[1]

    def op(self, e, fn, reads=(), writes=(), inc=True):
        self._emit_waits(e, self._deps(reads, writes))
        ins = fn(self.eng[e])
        if inc:
            ins.then_inc(self.sem[e], 1)
            self.cnt[e] += 1
            t = (e, self.cnt[e])
        else:
            t = (e, self.cnt[e] + 1)
        self._record(t, reads, writes)
        return ins

    def dma(self, q, out, in_, reads=(), writes=(), key=None, **kw):
        if key is None:
            key = writes[0] if writes else reads[0]
        if key not in self.dsem:
            self.dsem[key] = [self.nc.alloc_semaphore("d%d" % len(self.dsem)), 0]
        self._emit_waits(q, self._deps(reads, writes))
        ins = self.eng[q].dma_start(out=out, in_=in_, **kw)
        ins.then_inc(self.dsem[key][0], 16)
        self.dsem[key][1] += 16
        t = (("dma", key), self.dsem[key][1])
        self._record(t, reads, writes)
        return ins

    def barrier(self):
        for e in self.eng:
            deps = {}
            for o in self.eng:
                if o != e and self.cnt[o] > 0:
                    deps[o] = self.cnt[o]
            for key, (s, c) in self.dsem.items():
                if c > 0:
                    deps[("dma", key)] = c
            self._emit_waits(e, deps)


def _mult(d):
    d = np.asarray(d)
    m = ((d >= 0) & (d <= 128)).astype(np.float32)
    m += ((d >= 0) & (d <= 512) & (d % 4 == 0)).astype(np.float32)
    m += ((d >= 0) & (d % 16 == 0)).astype(np.float32)
    return m


def host_consts():
    c = {}
    c["c_ident"] = np.eye(128, dtype=np.float32)
    j = np.arange(128)[:, None]
    t = np.arange(128)[None, :]
    c["c_mask"] = np.concatenate([_mult(128 * dl + t - j) for dl in range(-3, 16)], axis=1).astype(np.float32)
    bands = np.zeros((128, 4, 3, 128), np.float32)
    tp = np.arange(128)[:, None]
    tt = np.arange(128)[None, :]
    for wi, w in enumerate((2, 4, 8, 16)):
        dcur = tt - tp
        bands[:, wi, 0, :] = ((dcur >= 0) & (dcur < w)) / float(w) - (dcur == 0)
        dprev = tt + 128 - tp
        bands[:, wi, 1, :] = ((dprev >= 0) & (dprev < w)) / float(w)
        cnt = np.minimum(tt + 1, w).astype(np.float32)
        bands[:, wi, 2, :] = ((dcur >= 0) & (dcur < w)) / cnt - (dcur == 0)
    c["c_bands"] = bands.reshape(128, 4 * 3 * 128)
    half = 8
    inv = 500000.0 ** (-np.arange(0, 16, 2, dtype=np.float32) / 16.0)
    ang = np.arange(S, dtype=np.float32)[None, :] * inv[:, None]
    C = np.ones((64, S), np.float32)
    Sg = np.zeros((64, S), np.float32)
    C[0:8] = np.cos(ang)
    C[8:16] = np.cos(ang)
    Sg[0:8] = -np.sin(ang)
    Sg[8:16] = np.sin(ang)
    c["c_ropec"] = np.concatenate([C, C], 0)
    c["c_ropes"] = np.concatenate([Sg, Sg], 0)
    gm = np.zeros((128, 8), np.float32)
    for g in range(8):
        gm[g * 16:(g + 1) * 16, g] = 1.0
    c["c_gmask"] = gm
    return c


def swap_perm():
    perm = np.arange(2048)
    for blk in range(2048 // 64):
        b = blk * 64
        perm[b:b + 8] = np.arange(b + 8, b + 16)
        perm[b + 8:b + 16] = np.arange(b, b + 8)
    return perm


def build(layers=(0, 1, 2, 3)):
    nc = bass.Bass("TRN2", target_bir_lowering=False)
    dt_ = {}

    def din(name, shape):
        h = nc.dram_tensor(name, list(shape), F32, kind="ExternalInput")
        dt_[name] = h
        return h.ap()

    x_d = din("x", [S, D])
    pre_d = din("pre_norm", [4, D])
    post_d = din("post_norm", [4, D])
    ewin_d = din("even_w_in", [2, D, 5120])
    ewsw_d = din("even_w_sw", [2, D, 2048])
    ewout_d = din("even_w_out", [2, 1536, D])
    are_d = din("ssm_a_re", [2, 32, 64])
    aim_d = din("ssm_a_im", [2, 32, 64])
    ldt_d = din("ssm_log_dt", [2, 32])
    bre_d = din("ssm_b_re", [2, 32, 64, 16])
    bim_d = din("ssm_b_im", [2, 32, 64, 16])
    cre_d = din("ssm_c_re", [2, 32, 16, 64])
    cim_d = din("ssm_c_im", [2, 32, 16, 64])
    sd_d = din("ssm_d", [2, 512])
    glw_d = din("ssm_glu_w", [2, 512, 512])
    glb_d = din("ssm_glu_b", [2, 512])
    owin_d = din("odd_w_in", [2, D, 4096])
    pw_d = din("pool_w", [2, 4, 512, 512])
    psc_d = din("pool_scale", [2, 2048])
    owout_d = din("odd_w_out", [2, 2048, D])
    cid_d = din("c_ident", [128, 128])
    cmask_d = din("c_mask", [128, 19 * 128])
    cband_d = din("c_bands", [128, 12 * 128])
    cropec_d = din("c_ropec", [128, S])
    cropes_d = din("c_ropes", [128, S])
    cgm_d = din("c_gmask", [128, 8])
    out_d = nc.dram_tensor("out", [S, D], F32, kind="ExternalOutput").ap()

    f = FW(nc)
    uid = [0]

    def nk(p):
        uid[0] += 1
        return "%s%d" % (p, uid[0])

    def bcast(name, off, n):
        return bass.AP(dt_[name], off, [[0, 128], [1, n]])

    from contextlib import ExitStack
    with ExitStack() as es:
        def sb(name, shape, dtype):
            return es.enter_context(nc.sbuf_tensor(name, list(shape), dtype))

        def psb(name, shape, dtype):
            return es.enter_context(nc.psum_tensor(name, list(shape), dtype))

        xres = sb("xres", [128, NB, D], F32)
        ps = [psb("ps%d" % i, [128, 512], F32) for i in range(6)]
        pst = [psb("pst%d" % i, [128, 1024], BF16) for i in range(2)]
        PS = ["ps%d" % i for i in range(6)]
        PST = ["pst0", "pst1"]
        ident = sb("ident", [128, 128], BF16)
        identf = sb("identf", [128, 128], F32)
        gain = sb("gain", [128, D], F32)
        junk = sb("junk", [128, D], BF16)
        stat = sb("stat", [128, 64], F32)
        gmask = sb("gmask", [128, 8], F32)
        wbf = [sb("wbf%d" % i, [128, 8, 128], BF16) for i in range(4)]
        wctr = [0, 0]

        f.dma("sp", identf[:], cid_d, writes=["identf"])
        f.op("dve", lambda e: e.tensor_copy(out=ident[:], in_=identf[:]), reads=["identf"], writes=["ident"])
        f.dma("sp", gmask[:], cgm_d, writes=["gmask"])
        for b in range(NB):
            f.dma("sp", xres[:, b, :], x_d[b * 128:(b + 1) * 128, :], writes=[("x", b)])

        cast_rr = [0]

        def cast(out, in_, reads, writes):
            e = "act"
            if e == "pool":
                f.op("pool", lambda g: g.tensor_copy(out=out, in_=in_), reads=reads, writes=writes)
            else:
                f.op("act", lambda g: g.copy(out=out, in_=in_), reads=reads, writes=writes)

        class WS:
            DEPTH = 2

            def __init__(self, srcs):
                self.srcs = srcs
                self.issued = 0
                self.cur = 0
                self.buf = {}

            def get(self):
                while self.issued < min(len(self.srcs), self.cur + 1 + WS.DEPTH):
                    src, kc = self.srcs[self.issued]
                    bi = wctr[1] % 4
                    wctr[1] += 1
                    f.dma("pool", wbf[bi][:, 0:kc, :], src.rearrange("(k p) n -> p k n", p=128),
                          writes=["wbf%d" % bi])
                    self.buf[self.issued] = bi
                    self.issued += 1
                bi = self.buf[self.cur]
                self.cur += 1
                return wbf[bi], "wbf%d" % bi

        def load_big(dst, dkey, src_ap, kc, step=4):
            keys = []
            for k0 in range(0, kc, step):
                k1 = min(kc, k0 + step)
                f.dma("pool", dst[:, k0:k1, :], src_ap[k0 * 128:k1 * 128, :].rearrange("(k p) n -> p k n", p=128),
                      writes=[(dkey, k0)])
                keys.append((dkey, k0))
            return keys

        def load_big_hw(dst, dkey, src_ap, kc, stg, skey):
            keys = []
            for k0 in range(0, kc, 2):
                f.dma("sp", stg[:, :, :], src_ap[k0 * 128:(k0 + 2) * 128, :].rearrange("(k p) n -> p k n", p=128),
                      writes=[skey])
                f.op("act", lambda g: g.copy(out=dst[:, k0:k0 + 2, :], in_=stg[:, :, :]), reads=[skey], writes=[(dkey, k0)])
                keys.append((dkey, k0))
            return keys

        scol = [0]

        def newcol():
            scol[0] = (scol[0] + 1) % 64
            return scol[0]

        def rstd_from_ss(c_ss):
            c1 = newcol()
            c2 = newcol()
            f.op("act", lambda e: e.activation(out=stat[:, c1:c1 + 1], in_=stat[:, c_ss:c_ss + 1], func=AF.Sqrt,
                                               scale=1.0 / D, bias=EPS),
                 reads=[("st", c_ss)], writes=[("st", c1)])
            f.op("dve", lambda e: e.reciprocal(out=stat[:, c2:c2 + 1], in_=stat[:, c1:c1 + 1]),
                 reads=[("st", c1)], writes=[("st", c2)])
            return c2

        def prenorm_block(b, dst, dkey):
            c = newcol()
            f.op("act", lambda e: e.activation(out=junk[:], in_=xres[:, b, :], func=AF.Square,
                                               accum_out=stat[:, c:c + 1]),
                 reads=[("x", b)], writes=["junk", ("st", c)])
            c2 = rstd_from_ss(c)
            f.op("dve", lambda e: e.scalar_tensor_tensor(out=dst, in0=xres[:, b, :], scalar=stat[:, c2:c2 + 1],
                                                         in1=gain[:], op0=ALU.mult, op1=ALU.mult),
                 reads=[("x", b), ("st", c2), "gain"], writes=[dkey])

        def transpose_block(src, skey, dst_ap, dkey, pi):
            for c in range(8):
                f.op("pe", lambda e, c=c: e.transpose(pst[pi][:, c * 128:(c + 1) * 128], src[:, c * 128:(c + 1) * 128],
                                                      ident[:]),
                     reads=[skey, "ident"], writes=[PST[pi]], inc=(c == 7))
            f.op("act", lambda e: e.copy(out=dst_ap, in_=pst[pi][:].rearrange("p (c t) -> p c t", c=8)),
                 reads=[PST[pi]], writes=[dkey])

        def outproj_block(b, mT_fn, mkeys, KC, wout, wkey, l):
            pp = (b % 2) * 2
            for fh in range(2):
                for kc in range(KC):
                    f.op("pe", lambda e, kc=kc, fh=fh: e.matmul(ps[pp + fh][:], lhsT=mT_fn(kc),
                                                                rhs=wout[:, kc, fh * 512:(fh + 1) * 512],
                                                                start=(kc == 0), stop=(kc == KC - 1)),
                         reads=list(mkeys) + list(wkey), writes=[PS[pp + fh]], inc=(kc == KC - 1))
            ca = newcol()
            cb = newcol()
            f.op("act", lambda e: e.activation(out=junk[:, 0:512], in_=ps[pp][:], func=AF.Square,
                                               accum_out=stat[:, ca:ca + 1]),
                 reads=[PS[pp]], writes=["junk", ("st", ca)])
            f.op("act", lambda e: e.activation(out=junk[:, 512:1024], in_=ps[pp + 1][:], func=AF.Square,
                                               accum_out=stat[:, cb:cb + 1]),
                 reads=[PS[pp + 1]], writes=["junk", ("st", cb)])
            cs = newcol()
            f.op("dve", lambda e: e.tensor_tensor(out=stat[:, cs:cs + 1], in0=stat[:, ca:ca + 1],
                                                  in1=stat[:, cb:cb + 1], op=ALU.add),
                 reads=[("st", ca), ("st", cb)], writes=[("st", cs)])
            c2 = rstd_from_ss(cs)
            for fh in range(2):
                tk = "ytmp%d" % fh
                f.op("dve", lambda e, fh=fh: e.scalar_tensor_tensor(out=ytmp_box[0][:, fh, :], in0=ps[pp + fh][:],
                                                                    scalar=stat[:, c2:c2 + 1],
                                                                    in1=gain[:, fh * 512:(fh + 1) * 512],
                                                                    op0=ALU.mult, op1=ALU.mult),
                     reads=[PS[pp + fh], ("st", c2), "gain"], writes=[tk])
                f.op("pool", lambda e, fh=fh: e.tensor_tensor(out=xres[:, b, fh * 512:(fh + 1) * 512],
                                                              in0=xres[:, b, fh * 512:(fh + 1) * 512],
                                                              in1=ytmp_box[0][:, fh, :], op=ALU.add),
                     reads=[("x", b), tk], writes=[("x", b)])

        ytmp_box = [None]

        def odd_layer(l):
            i = l // 2
            with ExitStack() as ls:
                def lsb(name, shape, dtype):
                    return ls.enter_context(nc.sbuf_tensor(nk(name), list(shape), dtype))
                hN = lsb("hN", [128, 5, D], BF16)
                hT = lsb("hTq", [128, 8, 512], BF16)
                phT = lsb("phT", [128, 8, 512], BF16)
                mixed = lsb("mixed", [128, 4, 512], BF16)
                mT = lsb("mTq", [128, 16, 512], BF16)
                sg = lsb("sg", [128, 512], F32)
                bandf = lsb("bandf", [128, 12, 128], F32)
                band = lsb("band", [128, 4, 4, 128], BF16)
                btmp = lsb("btmp", [128, 4, 128], F32)
                pwbf = lsb("pwbf", [128, 4, 512], BF16)
                wout = lsb("wout", [128, 16, D], BF16)
                pscale = lsb("pscale", [128, 16], F32)
                wost = [lsb("wost%d" % k_, [128, 2, D], F32) for k_ in range(2)]
                pwst = lsb("pwst", [128, 4, 512], F32)
                ytmp_box[0] = lsb("ytmp", [128, 2, 512], F32)
                f.dma("sp", bandf[:], cband_d.rearrange("p (a t) -> p a t", t=128), writes=["bandf"])
                bv = bandf[:].rearrange("p (w k) t -> p w k t", k=3)
                f.op("dve", lambda e: e.tensor_copy(out=band[:, :, 0:3, :], in_=bv), reads=["bandf"], writes=["band"])
                f.op("dve", lambda e: e.tensor_copy(out=btmp[:], in_=band[:, :, 2, :]), reads=["band"], writes=["btmp"])
                f.op("dve", lambda e: e.tensor_tensor(out=btmp[:], in0=bv[:, :, 2, :], in1=btmp[:], op=ALU.subtract),
                     reads=["bandf", "btmp"], writes=["btmp"])
                f.op("dve", lambda e: e.tensor_copy(out=band[:, :, 3, :], in_=btmp[:]), reads=["btmp"], writes=["band"])
                f.dma("sp", pscale[:], psc_d[i].rearrange("(c p) -> p c", p=128), writes=["pscale"],
                      allow_slow_non_contiguous=True)
                srcs = []
                for tq in range(4):
                    for g in range(4):
                        for j in range(4):
                            srcs.append((owin_d[i][:, g * 512 + j * 128:g * 512 + (j + 1) * 128], 8))
                        for dj in range(4):
                            col = 2048 + g * 512 + dj * 128
                            srcs.append((owin_d[i][:, col:col + 128], 8))
                ws = WS(srcs)
                for tq in range(4):
                    b0 = tq * 4
                    woutk = [("wout", 2 * k_) for k_ in range(8)]
                    wstep = [0]

                    def wout_step():
                        k_ = wstep[0]
                        wstep[0] += 1
                        if 1 <= k_ <= 8:
                            kk = k_ - 1
                            f.op("act", lambda g_: g_.copy(out=wout[:, 2 * kk:2 * kk + 2, :], in_=wost[kk % 2][:]),
                                 reads=["wost%d" % (kk % 2)], writes=[("wout", 2 * kk)])
                        if k_ < 8:
                            f.dma("sp", wost[k_ % 2][:],
                                  owout_d[i][k_ * 256:(k_ + 1) * 256, :].rearrange("(k p) n -> p k n", p=128),
                                  writes=["wost%d" % (k_ % 2)])
                    wout_step()
                    f.dma("sp", gain[:], bcast("pre_norm", l * D, D), writes=["gain"])
                    if tq > 0:
                        f.op("pool", lambda e: e.tensor_copy(out=hN[:, 0, :], in_=hN[:, 4, :]),
                             reads=[("hN", 4)], writes=[("hN", 0)])
                    for s_, b in enumerate(range(b0 - 1, b0 + 4)):
                        if s_ == 0:
                            continue
                        prenorm_block(b, hN[:, s_, :], ("hN", s_))
                    for s_ in range(1, 5):
                        transpose_block(hN[:, s_, :], ("hN", s_), hT[:, :, (s_ - 1) * 128:s_ * 128], "hT", s_ % 2)
                    for g in range(4):
                        f.dma("sp", pwst[:], pw_d[i, g].rearrange("(k p) n -> p k n", p=128), writes=["pwst"])
                        pwk = [("pwbf", 0), ("pwbf", 2)]
                        for s_ in range(1, 5):
                            b = b0 + s_ - 1
                            for hf in range(2):
                                pk = 4 + hf
                                for cc in range(4):
                                    c = hf * 4 + cc
                                    o = ps[pk][:, cc * 128:(cc + 1) * 128]
                                    lh = hN[:, s_, c * 128:(c + 1) * 128]
                                    if b == 0:
                                        f.op("pe", lambda e, o=o, lh=lh: e.matmul(o, lhsT=lh, rhs=band[:, g, 2, :],
                                                                                  start=True, stop=False),
                                             reads=[("hN", s_), "band"], writes=[PS[pk]], inc=False)
                                        f.op("pe", lambda e, o=o, lh=lh: e.matmul(o, lhsT=lh, rhs=band[:, g, 3, :],
                                                                                  start=False, stop=True),
                                             reads=[("hN", s_), "band"], writes=[PS[pk]], inc=(cc == 3))
                                    else:
                                        lp = hN[:, s_ - 1, c * 128:(c + 1) * 128]
                                        f.op("pe", lambda e, o=o, lh=lh: e.matmul(o, lhsT=lh, rhs=band[:, g, 0, :],
                                                                                  start=True, stop=False),
                                             reads=[("hN", s_), "band"], writes=[PS[pk]], inc=False)
                                        f.op("pe", lambda e, o=o, lp=lp: e.matmul(o, lhsT=lp, rhs=band[:, g, 1, :],
                                                                                  start=False, stop=True),
                                             reads=[("hN", s_ - 1), "band"], writes=[PS[pk]], inc=(cc == 3))
                                f.op("act", lambda e, hf=hf, pk=pk: e.copy(
                                    out=phT[:, hf * 4:(hf + 1) * 4, (s_ - 1) * 128:s_ * 128],
                                    in_=ps[pk][:].rearrange("p (c t) -> p c t", c=4)),
                                    reads=[PS[pk]], writes=["phT"])
                        for j in range(4):
                            wout_step()
                            wb, wk = ws.get()
                            pk = j % 2
                            for kc in range(8):
                                f.op("pe", lambda e, kc=kc: e.matmul(ps[pk][:], lhsT=wb[:, kc, :], rhs=phT[:, kc, :],
                                                                     start=(kc == 0), stop=(kc == 7)),
                                     reads=[wk, "phT"], writes=[PS[pk]], inc=(kc == 7))
                            f.op("act", lambda e, j=j, pk=pk: e.copy(out=mixed[:, j, :], in_=ps[pk][:]),
                                 reads=[PS[pk]], writes=[("mixed", j)])
                        for k0_ in (0, 2):
                            f.op("act", lambda g_, k0_=k0_: g_.copy(out=pwbf[:, k0_:k0_ + 2, :], in_=pwst[:, k0_:k0_ + 2, :]),
                                 reads=["pwst"], writes=[("pwbf", k0_)])
                        for dj in range(4):
                            wb, wk = ws.get()
                            pg = 2
                            for kc in range(8):
                                f.op("pe", lambda e, kc=kc: e.matmul(ps[pg][:], lhsT=wb[:, kc, :], rhs=hT[:, kc, :],
                                                                     start=(kc == 0), stop=(kc == 7)),
                                     reads=[wk, "hT"], writes=[PS[pg]], inc=(kc == 7))
                            f.op("act", lambda e: e.activation(out=sg[:], in_=ps[pg][:], func=AF.Silu),
                                 reads=[PS[pg]], writes=["sg"])
                            py = 3
                            for j in range(4):
                                f.op("pe", lambda e, j=j, dj=dj: e.matmul(ps[py][:], lhsT=pwbf[:, j, dj * 128:(dj + 1) * 128],
                                                                          rhs=mixed[:, j, :], start=(j == 0), stop=(j == 3)),
                                     reads=pwk + [("mixed", j)], writes=[PS[py]], inc=(j == 3))
                            ch = g * 4 + dj
                            f.op("dve", lambda e, ch=ch: e.scalar_tensor_tensor(out=mT[:, ch, :], in0=ps[py][:],
                                                                                scalar=pscale[:, ch:ch + 1], in1=sg[:],
                                                                                op0=ALU.mult, op1=ALU.mult),
                                 reads=[PS[py], "pscale", "sg"], writes=[("mT", ch)])
                    f.dma("sp", gain[:], bcast("post_norm", l * D, D), writes=["gain"])
                    for bb in range(4):
                        outproj_block(b0 + bb, lambda kc, bb=bb: mT[:, kc, bb * 128:(bb + 1) * 128],
                                      [("mT", ch) for ch in range(16)], 16, wout, woutk, l)
                f.barrier()

        def even_layer(l):
            i = l // 2
            with ExitStack() as ls:
                def lsb(name, shape, dtype):
                    return ls.enter_context(nc.sbuf_tensor(nk(name), list(shape), dtype))
                hT = lsb("hT", [128, 8, S], BF16)
                mT = lsb("mT", [128, 12, S], BF16)
                f.dma("sp", gain[:], bcast("pre_norm", l * D, D), writes=["gain"])
                with ExitStack() as hs_:
                    hNb = [hs_.enter_context(nc.sbuf_tensor(nk("hNb"), [128, D], BF16)) for k in range(2)]
                    for b in range(NB):
                        prenorm_block(b, hNb[b % 2][:], "hNb%d" % (b % 2))
                        transpose_block(hNb[b % 2], "hNb%d" % (b % 2), hT[:, :, b * 128:(b + 1) * 128], "hT", b % 2)
                    f.barrier()

                def proj_fm(wb, wk, pk, tg):
                    for kc in range(8):
                        f.op("pe", lambda e, kc=kc: e.matmul(ps[pk][:], lhsT=wb[:, kc, :],
                                                             rhs=hT[:, kc, tg * 512:(tg + 1) * 512],
                                                             start=(kc == 0), stop=(kc == 7)),
                             reads=[wk, "hT"], writes=[PS[pk]], inc=(kc == 7))

                with ExitStack() as as_:
                    def asb(name, shape, dtype):
                        return as_.enter_context(nc.sbuf_tensor(nk(name), list(shape), dtype))
                    qT = asb("qT", [128, S], BF16)
                    kT = asb("kT", [128, 2, S], BF16)
                    V = asb("V", [128, NB, 2, 128], BF16)
                    ropec = asb("ropec", [128, S], BF16)
                    ropes = asb("ropes", [128, S], BF16)
                    maskT = asb("maskT", [128, 19 * 128], BF16)
                    Pt = [asb("Pt%d" % k, [128, 512], BF16) for k in range(4)]
                    rtmp = [asb("rtmp%d" % k, [128, 512], F32) for k in range(2)]
                    rc = rtmp[0]
                    atmp = rtmp[1]
                    f.dma("pool", ropec[:], cropec_d, writes=["ropec"])
                    f.dma("pool", ropes[:], cropes_d, writes=["ropes"])
                    f.dma("pool", maskT[:], cmask_d, writes=["maskT"])
                    srcs = []
                    for hp_ in range(8):
                        c0_ = hp_ * 128
                        for (base_, swb_) in ((0, 0), (1024, 1024)):
                            srcs.append((ewin_d[i][:, base_ + c0_:base_ + c0_ + 128], 8))
                            srcs.append((ewsw_d[i][:, swb_ + c0_:swb_ + c0_ + 128], 8))
                        srcs.append((ewin_d[i][:, 3072 + c0_:3072 + c0_ + 128], 8))
                        srcs.append((ewin_d[i][:, 2048 + c0_:2048 + c0_ + 128], 8))
                    ws = WS(srcs)
                    f.op("pool", lambda e: e.memset(V[:, :, :, 64:128], 1.0), writes=["V"])
                    f.op("pool", lambda e: e.memset(kT[:], 0.0), writes=["kT"])
                    mrr = [0]
                    for hp in range(8):
                        c0 = hp * 128
                        for (dst, dk, base, swb) in ((qT, "qT", 0, 0), (kT, "kT", 1024, 1024)):
                            wb, wk = ws.get()
                            wb2, wk2 = ws.get()
                            for tg in range(4):
                                sl = slice(tg * 512, (tg + 1) * 512)
                                pa0 = 2 * (tg % 2)
                                proj_fm(wb, wk, pa0, tg)
                                proj_fm(wb2, wk2, pa0 + 1, tg)
                                r0 = "rtmp0"
                                r1 = "rtmp1"
                                f.op("dve", lambda e, sl=sl: e.tensor_tensor(out=rtmp[0][:], in0=ps[pa0][:], in1=ropec[:, sl],
                                                                             op=ALU.mult),
                                     reads=[PS[pa0], "ropec"], writes=[r0])
                                f.op("dve", lambda e, sl=sl: e.tensor_tensor(out=rtmp[1][:], in0=ps[pa0 + 1][:], in1=ropes[:, sl],
                                                                             op=ALU.mult),
                                     reads=[PS[pa0 + 1], "ropes"], writes=[r1])
                                if dk == "qT":
                                    f.op("dve", lambda e, sl=sl: e.tensor_tensor(out=qT[:, sl], in0=rtmp[0][:],
                                                                                 in1=rtmp[1][:], op=ALU.add),
                                         reads=[r0, r1], writes=[dk])
                                else:
                                    for a_ in range(2):
                                        pr_ = slice(64 * a_, 64 * a_ + 64)
                                        f.op("dve", lambda e, sl=sl, a_=a_, pr_=pr_: e.tensor_tensor(
                                            out=kT[pr_, a_, sl], in0=rtmp[0][pr_, :], in1=rtmp[1][pr_, :], op=ALU.add),
                                            reads=[r0, r1], writes=[dk])
                        wb, wk = ws.get()
                        for tg in range(4):
                            proj_fm(wb, wk, 2, tg)
                            f.op("act", lambda e, tg=tg: e.activation(out=mT[:, hp, tg * 512:(tg + 1) * 512], in_=ps[2][:],
                                                                      func=AF.Silu),
                                 reads=[PS[2]], writes=[("mT", hp)])
                        wb, wk = ws.get()
                        for b4 in range(4):
                            pk = 3
                            for bb in range(4):
                                b = b4 * 4 + bb
                                for kc in range(8):
                                    f.op("pe", lambda e, kc=kc, b=b, bb=bb: e.matmul(
                                        ps[pk][:, bb * 128:(bb + 1) * 128], lhsT=hT[:, kc, b * 128:(b + 1) * 128],
                                        rhs=wb[:, kc, :], start=(kc == 0), stop=(kc == 7)),
                                        reads=[wk, "hT"], writes=[PS[pk]], inc=(kc == 7 and bb == 3))
                            f.op("act", lambda e, b4=b4: e.copy(
                                out=V[:, b4 * 4:(b4 + 1) * 4, :, 0:64],
                                in_=ps[pk][:].rearrange("p (b a d) -> p b a d", b=4, a=2)),
                                reads=[PS[pk]], writes=["V"])
                        items = []
                        for a in range(2):
                            for qg in range(4):
                                nkb = 4 * qg + 4
                                for kb in range(nkb):
                                    items.append((a, qg, kb, nkb))
                        LAG = 2

                        def stage1(idx):
                            a, qg, kb, nkb = items[idx]
                            pr = slice(64 * a, 64 * a + 64)
                            cq = max(128 * kb, 512 * qg)
                            N = 512 * (qg + 1) - cq
                            sk = idx % 4
                            f.op("pe", lambda e: e.matmul(
                                ps[sk][:, 0:N], lhsT=kT[:, a, kb * 128:(kb + 1) * 128], rhs=qT[:, cq:cq + N],
                                start=True, stop=True),
                                reads=["kT", "qT"], writes=[PS[sk]])
                            f.op("act", lambda e: e.activation(out=Pt[sk][:, 0:N], in_=ps[sk][:, 0:N],
                                                               func=AF.Exp, scale=0.125),
                                 reads=[PS[sk]], writes=["Pt%d" % sk])
                            moff = ((cq - 128 * kb) // 128 + 3) * 128
                            me = "dve"
                            mrr[0] += 1
                            f.op(me, lambda e: e.tensor_tensor(
                                out=Pt[sk][:, 0:N], in0=Pt[sk][:, 0:N], in1=maskT[:, moff:moff + N], op=ALU.mult),
                                reads=["Pt%d" % sk, "maskT"], writes=["Pt%d" % sk])

                        def stage2(idx):
                            a, qg, kb, nkb = items[idx]
                            pr = slice(64 * a, 64 * a + 64)
                            cq = max(128 * kb, 512 * qg)
                            N = 512 * (qg + 1) - cq
                            sk = idx % 4
                            po = 4 + (qg % 2)
                            oc = cq - 512 * qg
                            f.op("pe", lambda e: e.matmul(
                                ps[po][:, oc:oc + N], lhsT=V[:, kb, a, :], rhs=Pt[sk][:, 0:N],
                                start=(kb == 0), stop=(kb == nkb - 1)),
                                reads=["V", "Pt%d" % sk], writes=[PS[po]])
                            if kb == nkb - 1:
                                qs = slice(qg * 512, (qg + 1) * 512)
                                f.op("act", lambda e: e.activation(out=rc[64:128, :], in_=ps[po][64:128, :], func=AF.Ln),
                                     reads=[PS[po]], writes=["rtmp0"])
                                f.op("act", lambda e: e.activation(out=rc[64:128, :], in_=rc[64:128, :], func=AF.Exp,
                                                                   scale=-1.0),
                                     reads=["rtmp0"], writes=["rtmp0"])
                                f.op("dve", lambda e: e.tensor_tensor(out=atmp[pr, :], in0=ps[po][0:64, :],
                                                                      in1=rc[64:128, :], op=ALU.mult),
                                     reads=[PS[po], "rtmp0"], writes=["rtmp1"])
                                f.op("pool", lambda e: e.tensor_tensor(out=mT[pr, hp, qs], in0=atmp[pr, :],
                                                                       in1=mT[pr, hp, qs], op=ALU.mult),
                                     reads=["rtmp1", ("mT", hp)], writes=[("mT", hp)])

                        for idx in range(len(items) + LAG):
                            if idx < len(items):
                                stage1(idx)
                            if idx - LAG >= 0:
                                stage2(idx - LAG)
                    f.barrier()

                with ExitStack() as ss_:
                    def ssb(name, shape, dtype):
                        return ss_.enter_context(nc.sbuf_tensor(nk(name), list(shape), dtype))
                    NPW = 17 + 7
                    PR = ssb("PR", [128, 16, 3 * 24 + 4], F32)
                    pa = ssb("pa", [128, 16, 12], F32)
                    pi32 = ssb("pi32", [128, 16], I32)
                    XR = ssb("XR", [128, S], F32)
                    XI = ssb("XI", [128, S], F32)
                    cs_ = [ssb("cs%d" % k, [128, 2, 192], F32) for k in range(2)]
                    Xb = [ssb("Xb%d" % k, [128, 512], BF16) for k in range(2)]
                    uT = ssb("uT", [128, S], BF16)
                    bnat = ssb("bnat", [128, 2, 16, 16], F32)
                    cnat = ssb("cnat", [128, 2, 64], F32)
                    padT = ssb("padT", [128, 128], BF16)
                    padTf = ssb("padTf", [128, 2, 128], F32)
                    Bpad = ssb("Bpad", [128, 4, 2, 128], BF16)
                    Cpad = ssb("Cpad", [128, 4, 2, 128], BF16)
                    cf = ssb("cf", [128, 4, 128], F32)
                    dvec = ssb("dvec", [128, 4], F32)
                    glub = ssb("glub", [128, 4], F32)
                    gl = [ssb("gl%d" % k, [128, 512], F32) for k in range(2)] + [XR[:, 0:512]]

                    for k_ in range(2):
                        f.op("pool", lambda e, k_=k_: e.memset(cs_[k_][:], 0.0), writes=[("cs", k_, 0), ("cs", k_, 1)])
                    def ld_gp(dst_col, name):
                        for e_ in range(2):
                            f.dma("sp", pa[e_ * 64:(e_ + 1) * 64, :, dst_col],
                                  bass.AP(dt_[name], i * 2048 + e_ * 64, [[1, 64], [128, 16]]),
                                  writes=["pa"], allow_slow_non_contiguous=True)
                    ld_gp(0, "ssm_a_re")
                    ld_gp(1, "ssm_a_im")
                    for e_ in range(2):
                        f.dma("sp", pa[e_ * 64:(e_ + 1) * 64, :, 2],
                              bass.AP(dt_["ssm_log_dt"], i * 32 + e_, [[0, 64], [2, 16]]),
                              writes=["pa"], allow_slow_non_contiguous=True)
                    f.dma("sp", dvec[:], sd_d[i].rearrange("(c p) -> p c", p=128), writes=["dvec"],
                          allow_slow_non_contiguous=True)
                    f.dma("sp", glub[:], glb_d[i].rearrange("(c p) -> p c", p=128), writes=["glub"],
                          allow_slow_non_contiguous=True)

                    def pop(eng, fn, w=("pa",)):
                        f.op(eng, fn, reads=["pa", "PR"], writes=list(w))
                    A = lambda c: pa[:, :, c]
                    pop("act", lambda e: e.activation(out=A(3), in_=A(2), func=AF.Exp))
                    pop("dve", lambda e: e.tensor_tensor(out=A(4), in0=A(0), in1=A(3), op=ALU.mult))
                    pop("dve", lambda e: e.tensor_tensor(out=A(5), in0=A(1), in1=A(3), op=ALU.mult))
                    pop("act", lambda e: e.activation(out=A(4), in_=A(4), func=AF.Exp))

                    def sin_of(dst, shift):
                        pop("dve", lambda e: e.tensor_scalar(out=A(6), in0=A(5), scalar1=shift, scalar2=1.0 / TWO_PI,
                                                             op0=ALU.add, op1=ALU.mult))
                        f.op("dve", lambda e: e.tensor_copy(out=pi32[:], in_=A(6)), reads=["pa"], writes=["pi32"])
                        f.op("dve", lambda e: e.tensor_copy(out=A(7), in_=pi32[:]), reads=["pi32"], writes=["pa"])
                        pop("dve", lambda e: e.tensor_tensor(out=A(6), in0=A(6), in1=A(7), op=ALU.subtract))
                        pop("dve", lambda e: e.tensor_scalar(out=A(6), in0=A(6), scalar1=TWO_PI, scalar2=math.pi,
                                                             op0=ALU.mult, op1=ALU.min))
                        pop("dve", lambda e: e.tensor_scalar(out=A(6), in0=A(6), scalar1=-math.pi, scalar2=None,
                                                             op0=ALU.max))
                        pop("act", lambda e: e.activation(out=dst, in_=A(6), func=AF.Sin))
                    sin_of(A(8), 0.0)
                    sin_of(A(9), math.pi / 2)
                    P3 = lambda k, c: PR[:, :, 3 * k + c]
                    pop("dve", lambda e: e.tensor_tensor(out=P3(0, 0), in0=A(4), in1=A(9), op=ALU.mult), w=("PR",))
                    pop("dve", lambda e: e.tensor_tensor(out=P3(0, 1), in0=A(4), in1=A(8), op=ALU.mult), w=("PR",))

                    def cmul(dst, a_, b_):
                        pop("dve", lambda e: e.tensor_tensor(out=A(6), in0=P3(a_, 0), in1=P3(b_, 0), op=ALU.mult))
                        pop("dve", lambda e: e.tensor_tensor(out=A(7), in0=P3(a_, 1), in1=P3(b_, 1), op=ALU.mult))
                        pop("dve", lambda e: e.tensor_tensor(out=A(10), in0=P3(a_, 0), in1=P3(b_, 1), op=ALU.mult))
                        pop("dve", lambda e: e.tensor_tensor(out=A(11), in0=P3(a_, 1), in1=P3(b_, 0), op=ALU.mult))
                        pop("dve", lambda e: e.tensor_tensor(out=P3(dst, 0), in0=A(6), in1=A(7), op=ALU.subtract), w=("PR",))
                        pop("dve", lambda e: e.tensor_tensor(out=P3(dst, 1), in0=A(10), in1=A(11), op=ALU.add), w=("PR",))
                    for j in range(1, 16):
                        cmul(j, j - 1, 0)
                    for k in range(16, 16 + 7):
                        cmul(k, k - 1, k - 1)
                    for k in range(23):
                        pop("dve", lambda e, k=k: e.tensor_scalar(out=P3(k, 2), in0=P3(k, 1), scalar1=-1.0, scalar2=None,
                                                                  op0=ALU.mult), w=("PR",))
                    FR = PR[:, :, 72]
                    FI = PR[:, :, 73]
                    pop("dve", lambda e: e.tensor_scalar(out=A(6), in0=P3(0, 0), scalar1=-1.0, scalar2=None, op0=ALU.add))
                    pop("dve", lambda e: e.tensor_tensor(out=A(7), in0=A(0), in1=A(0), op=ALU.mult))
                    pop("dve", lambda e: e.tensor_tensor(out=A(10), in0=A(1), in1=A(1), op=ALU.mult))
                    pop("dve", lambda e: e.tensor_tensor(out=A(7), in0=A(7), in1=A(10), op=ALU.add))
                    pop("dve", lambda e: e.reciprocal(out=A(7), in_=A(7)))
                    pop("dve", lambda e: e.tensor_tensor(out=A(10), in0=A(6), in1=A(0), op=ALU.mult))
                    pop("dve", lambda e: e.tensor_tensor(out=A(11), in0=P3(0, 1), in1=A(1), op=ALU.mult))
                    pop("dve", lambda e: e.tensor_tensor(out=A(10), in0=A(10), in1=A(11), op=ALU.add))
                    pop("dve", lambda e: e.tensor_tensor(out=FR, in0=A(10), in1=A(7), op=ALU.mult), w=("PR",))
                    pop("dve", lambda e: e.tensor_tensor(out=A(10), in0=P3(0, 1), in1=A(0), op=ALU.mult))
                    pop("dve", lambda e: e.tensor_tensor(out=A(11), in0=A(6), in1=A(1), op=ALU.mult))
                    pop("dve", lambda e: e.tensor_tensor(out=A(10), in0=A(10), in1=A(11), op=ALU.subtract))
                    pop("dve", lambda e: e.tensor_tensor(out=FI, in0=A(10), in1=A(7), op=ALU.mult), w=("PR",))
                    for ri, name in enumerate(("ssm_b_re", "ssm_b_im")):
                        for e_ in range(2):
                            f.dma("sp", bnat[e_ * 64:(e_ + 1) * 64, ri, :, :],
                                  bass.AP(dt_[name], i * 32 * 1024 + e_ * 1024,
                                          [[16, 64], [2048, 16], [1, 16]]), writes=["bnat"])

                    XALL = [["X0"] + [("XR", s_) for s_ in range(16)], ["X1"] + [("XI", s_) for s_ in range(16)]]

                    srcs = [(ewin_d[i][:, 4096 + j_ * 128:4096 + (j_ + 1) * 128], 8) for j_ in range(4)]
                    for tg_ in range(4):
                        srcs += [(glw_d[i][:, fo_ * 128:(fo_ + 1) * 128], 4) for fo_ in range(4)]
                        srcs += [(ewin_d[i][:, 4608 + fo_ * 128:4608 + (fo_ + 1) * 128], 8) for fo_ in range(4)]
                    ws = WS(srcs)

                    def PSC(q, k, c):
                        return PR[:, q, 3 * k + c:3 * k + c + 1]

                    for j in range(4):
                        for ri, name in enumerate(("ssm_c_re", "ssm_c_im")):
                            f.dma("sp", cnat[:, ri, :],
                                  bass.AP(dt_[name], i * 32 * 1024 + j * 8 * 1024, [[64, 128], [1, 64]]),
                                  writes=["cnat"])
                        for qq in range(4):
                            q = j * 4 + qq
                            for ri in range(2):
                                f.op("pool", lambda e: e.memset(padT[:], 0.0), writes=["padT"])
                                for e_ in range(2):
                                    co = 16 * (2 * qq + e_)
                                    f.op("dve", lambda e, e_=e_, co=co, ri=ri, q=q: e.tensor_copy(
                                        out=padT[e_ * 64:(e_ + 1) * 64, co:co + 16],
                                        in_=bnat[e_ * 64:(e_ + 1) * 64, ri, q, :]),
                                        reads=["bnat"], writes=["padT"])
                                f.op("pe", lambda e: e.transpose(pst[0][:, 0:128], padT[:], ident[:]),
                                     reads=["padT", "ident"], writes=[PST[0]])
                                f.op("act", lambda e, qq=qq, ri=ri: e.copy(out=Bpad[:, qq, ri, :], in_=pst[0][:, 0:128]),
                                     reads=[PST[0]], writes=["Bpad"])
                            for ri in range(2):
                                for e_ in range(2):
                                    g8 = 2 * qq + e_
                                    f.op("dve", lambda e, e_=e_, ri=ri, g8=g8: e.tensor_scalar(
                                        out=padTf[:, ri, e_ * 64:(e_ + 1) * 64], in0=cnat[:, ri, :],
                                        scalar1=gmask[:, g8:g8 + 1], scalar2=None, op0=ALU.mult),
                                        reads=["cnat", "gmask"], writes=["padTf"])
                            for ri in range(2):
                                f.op("pe", lambda e, ri=ri: e.matmul(ps[4][:, ri * 128:(ri + 1) * 128], lhsT=padTf[:, ri, :],
                                                                     rhs=identf[:], start=True, stop=True),
                                     reads=["padTf", "identf"], writes=[PS[4]])
                            fr = PR[:, q, 72:73]
                            fi = PR[:, q, 73:74]
                            f.op("act", lambda e: e.copy(out=cf[:, 0:2, :], in_=ps[4][:, 0:256].rearrange("p (a b) -> p a b", a=2)),
                                 reads=[PS[4]], writes=["cf"])
                            f.op("dve", lambda e, fr=fr: e.tensor_scalar(out=cf[:, 2, :], in0=cf[:, 0, :], scalar1=fr, scalar2=None,
                                                                         op0=ALU.mult), reads=["cf", "PR"], writes=["cf"])
                            f.op("dve", lambda e, fi=fi: e.tensor_scalar(out=cf[:, 3, :], in0=cf[:, 1, :], scalar1=fi, scalar2=None,
                                                                         op0=ALU.mult), reads=["cf", "PR"], writes=["cf"])
                            f.op("dve", lambda e, qq=qq: e.tensor_tensor(out=Cpad[:, qq, 0, :], in0=cf[:, 2, :], in1=cf[:, 3, :],
                                                                         op=ALU.subtract), reads=["cf"], writes=["Cpad"])
                            f.op("dve", lambda e, fi=fi: e.tensor_scalar(out=cf[:, 2, :], in0=cf[:, 0, :], scalar1=fi, scalar2=-1.0,
                                                                         op0=ALU.mult, op1=ALU.mult), reads=["cf", "PR"], writes=["cf"])
                            f.op("dve", lambda e, fr=fr: e.tensor_scalar(out=cf[:, 3, :], in0=cf[:, 1, :], scalar1=fr, scalar2=None,
                                                                         op0=ALU.mult), reads=["cf", "PR"], writes=["cf"])
                            f.op("dve", lambda e, qq=qq: e.tensor_tensor(out=Cpad[:, qq, 1, :], in0=cf[:, 2, :], in1=cf[:, 3, :],
                                                                         op=ALU.subtract), reads=["cf"], writes=["Cpad"])
                        wb, wk = ws.get()
                        for tg in range(4):
                            sl = slice(tg * 512, (tg + 1) * 512)
                            proj_fm(wb, wk, 4, tg)
                            f.op("act", lambda e, sl=sl: e.copy(out=uT[:, sl], in_=ps[4][:]), reads=[PS[4]], writes=["uT"])
                        def stage_in(qq_, ri):
                            X = (XR, XI)[ri]
                            for tg in range(4):
                                sl = slice(tg * 512, (tg + 1) * 512)
                                pk = 4 + (tg % 2)
                                f.op("pe", lambda e, sl=sl, pk=pk: e.matmul(
                                    ps[pk][:], lhsT=Bpad[:, qq_, ri, :], rhs=uT[:, sl], start=True, stop=True),
                                    reads=["Bpad", "uT"], writes=[PS[pk]])
                                f.op("act", lambda e, sl=sl, pk=pk, X=X: e.copy(out=X[:, sl], in_=ps[pk][:]),
                                     reads=[PS[pk]], writes=XALL[ri])

                        def stage_out(qq_, ri):
                            X = (XR, XI)[ri]
                            for tg in range(4):
                                sl = slice(tg * 512, (tg + 1) * 512)
                                bi = (ri * 4 + tg) % 2
                                cast(Xb[bi][:], X[:, sl], XALL[ri], ["Xb%d" % bi])
                                f.op("pe", lambda e, tg=tg, bi=bi: e.matmul(
                                    ps[tg][:], lhsT=Cpad[:, qq_, ri, :], rhs=Xb[bi][:],
                                    start=(qq_ == 0 and ri == 0), stop=(qq_ == 3 and ri == 1)),
                                    reads=["Cpad", "Xb%d" % bi], writes=[PS[tg]])

                        stage_in(0, 0)
                        stage_in(0, 1)
                        for qq in range(4):
                            q = j * 4 + qq
                            XRv = XR[:].rearrange("p (c s) -> p c s", s=16)
                            XIv = XI[:].rearrange("p (c s) -> p c s", s=16)

                            def _kl(k_):
                                return list(k_) if isinstance(k_, list) else [k_]

                            def cstep(oR, oI, iR, iI, k, kOR, kOI, kIR, kII, bR=None, bI=None, kB=()):
                                for (o_, i_, c_, ko, ki, b_) in ((oR, iR, 0, kOR, kIR, bR), (oI, iR, 1, kOI, kIR, bI),
                                                                 (oR, iI, 2, kOR, kII, None), (oI, iI, 0, kOI, kII, None)):
                                    add_ = o_ if b_ is None else b_
                                    f.op("dve", lambda e, o_=o_, i_=i_, c_=c_, add_=add_: e.scalar_tensor_tensor(
                                        out=o_, in0=i_, scalar=PSC(q, k, c_), in1=add_, op0=ALU.mult, op1=ALU.add),
                                        reads=_kl(ki) + ["PR"] + list(kB), writes=_kl(ko))
                            XRc = XR[:].rearrange("p (n r) -> p n r", r=4)
                            XIc = XI[:].rearrange("p (n r) -> p n r", r=4)
                            XR4 = XR[:].rearrange("p (c m r) -> p c m r", m=4, r=4)
                            XI4 = XI[:].rearrange("p (c m r) -> p c m r", m=4, r=4)
                            for r_ in range(1, 4):
                                ko_ = [r_ + 4 * m_ for m_ in range(4)]
                                ki_ = [r_ - 1 + 4 * m_ for m_ in range(4)]
                                cstep(XRc[:, :, r_], XIc[:, :, r_], XRc[:, :, r_ - 1], XIc[:, :, r_ - 1], 0,
                                      [("XR", c_) for c_ in ko_], [("XI", c_) for c_ in ko_],
                                      [("XR", c_) for c_ in ki_], [("XI", c_) for c_ in ki_])
                            for m_ in range(1, 4):
                                so_ = 4 * m_ + 3
                                si_ = 4 * m_ - 1
                                cstep(XRv[:, :, so_], XIv[:, :, so_], XRv[:, :, si_], XIv[:, :, si_], 3,
                                      ("XR", so_), ("XI", so_), ("XR", si_), ("XI", si_))
                            cur = 0
                            f.op("act", lambda e: e.copy(out=cs_[0][:, 0, 64:192], in_=XRv[:, :, 15]),
                                 reads=[("XR", 15)], writes=[("cs", 0, 0)])
                            f.op("act", lambda e: e.copy(out=cs_[0][:, 1, 64:192], in_=XIv[:, :, 15]),
                                 reads=[("XI", 15)], writes=[("cs", 0, 1)])
                            for k in range(7):
                                sh = 1 << k
                                src = cs_[cur]
                                dst = cs_[1 - cur]
                                cstep(dst[:, 0, 64:192], dst[:, 1, 64:192], src[:, 0, 64 - sh:192 - sh], src[:, 1, 64 - sh:192 - sh],
                                      15 + k, ("cs", 1 - cur, 0), ("cs", 1 - cur, 1), ("cs", cur, 0), ("cs", cur, 1),
                                      bR=src[:, 0, 64:192], bI=src[:, 1, 64:192])
                                cur = 1 - cur
                            fin = cs_[cur]
                            for m_ in range(4):
                                s_ = 4 * m_ + 3
                                cstep(XRv[:, 1:128, s_], XIv[:, 1:128, s_], fin[:, 0, 64:191], fin[:, 1, 64:191], s_,
                                      ("XR", s_), ("XI", s_), ("cs", cur, 0), ("cs", cur, 1))
                            ENDK = [[("XR", 3), ("XR", 7), ("XR", 11)], [("XI", 3), ("XI", 7), ("XI", 11)]]

                            def p3_m(comp, part):
                                X4 = (XR4, XI4)[comp]
                                kx = ("XR", "XI")[comp]
                                cidx = ((0, 2), (1, 0))[comp]
                                E4 = (XR4, XI4)[part]
                                for r_ in range(3):
                                    f.op("dve", lambda e, r_=r_: e.scalar_tensor_tensor(
                                        out=X4[:, :, 1:4, r_], in0=E4[:, :, 0:3, 3],
                                        scalar=PSC(q, r_, cidx[part]), in1=X4[:, :, 1:4, r_],
                                        op0=ALU.mult, op1=ALU.add),
                                        reads=ENDK[part] + ["PR"], writes=[(kx, 4 * m_ + r_) for m_ in range(1, 4)])

                            def p3_0(comp):
                                Xv = (XRv, XIv)[comp]
                                kx = ("XR", "XI")[comp]
                                cidx = ((0, 2), (1, 0))[comp]
                                for part in range(2):
                                    for r_ in range(3):
                                        f.op("dve", lambda e, r_=r_, part=part: e.scalar_tensor_tensor(
                                            out=Xv[:, 1:128, r_], in0=fin[:, part, 64:191],
                                            scalar=PSC(q, r_, cidx[part]), in1=Xv[:, 1:128, r_],
                                            op0=ALU.mult, op1=ALU.add),
                                            reads=[("cs", cur, part), "PR"], writes=[(kx, r_)])
                            p3_m(0, 0)
                            p3_m(0, 1)
                            p3_0(0)
                            p3_m(1, 0)
                            stage_out(qq, 0)
                            if qq < 3:
                                stage_in(qq + 1, 0)
                            p3_m(1, 1)
                            p3_0(1)
                            stage_out(qq, 1)
                            if qq < 3:
                                stage_in(qq + 1, 1)
                        for tg in range(4):
                            sl = slice(tg * 512, (tg + 1) * 512)
                            f.op("dve", lambda e, sl=sl, tg=tg, j=j: e.scalar_tensor_tensor(out=gl[0][:], in0=uT[:, sl],
                                                                                            scalar=dvec[:, j:j + 1], in1=ps[tg][:],
                                                                                            op0=ALU.mult, op1=ALU.add),
                                 reads=[PS[tg], "uT", "dvec"], writes=["gl0"])
                            f.op("pool", lambda e: e.tensor_tensor(out=gl[1][:], in0=gl[0][:], in1=gl[0][:], op=ALU.mult),
                                 reads=["gl0"], writes=["gl1"])
                            f.op("pool", lambda e: e.tensor_scalar(out=gl[1][:], in0=gl[1][:], scalar1=0.044715, scalar2=1.0,
                                                                   op0=ALU.mult, op1=ALU.add), reads=["gl1"], writes=["gl1"])
                            f.op("pool", lambda e: e.tensor_tensor(out=gl[1][:], in0=gl[1][:], in1=gl[0][:], op=ALU.mult),
                                 reads=["gl1", "gl0"], writes=["gl1"])
                            f.op("act", lambda e: e.activation(out=gl[2][:], in_=gl[1][:], func=AF.Sigmoid,
                                                               scale=2.0 * math.sqrt(2.0 / math.pi)),
                                 reads=["gl1"], writes=["gl2"] + XALL[0])
                            f.op("dve", lambda e, sl=sl, j=j: e.tensor_tensor(out=mT[:, 8 + j, sl], in0=gl[0][:], in1=gl[2][:],
                                                                              op=ALU.mult),
                                 reads=["gl0", "gl2"] + XALL[0], writes=[("mT", 8 + j)])
                    for tg in range(4):
                        sl = slice(tg * 512, (tg + 1) * 512)
                        for fo in range(4):
                            wb, wk = ws.get()
                            for jj in range(4):
                                f.op("pe", lambda e, fo=fo, jj=jj, sl=sl: e.matmul(
                                    ps[fo][:], lhsT=wb[:, jj, :], rhs=mT[:, 8 + jj, sl],
                                    start=(jj == 0), stop=(jj == 3)),
                                    reads=[wk, ("mT", 8 + jj)], writes=[PS[fo]], inc=(jj == 3))
                        for fo in range(4):
                            wb, wk = ws.get()
                            proj_fm(wb, wk, 4, tg)
                            f.op("act", lambda e: e.activation(out=gl[0][:], in_=ps[4][:], func=AF.Sigmoid),
                                 reads=[PS[4]], writes=["gl0"])
                            f.op("act", lambda e, fo=fo: e.activation(out=gl[2][:], in_=ps[fo][:], func=AF.Sigmoid,
                                                                      bias=glub[:, fo:fo + 1]),
                                 reads=[PS[fo], "glub"], writes=["gl2"] + XALL[0])
                            f.op("pool", lambda e: e.tensor_tensor(out=gl[1][:], in0=gl[2][:], in1=gl[0][:], op=ALU.mult),
                                 reads=["gl2", "gl0"] + XALL[0], writes=["gl1"])
                            f.op("dve", lambda e: e.tensor_tensor(out=gl[1][:], in0=ps[4][:], in1=gl[1][:], op=ALU.mult),
                                 reads=[PS[4], "gl1"], writes=["gl1"])
                            f.op("dve", lambda e, fo=fo, sl=sl: e.tensor_tensor(out=mT[:, 8 + fo, sl], in0=mT[:, 8 + fo, sl],
                                                                                in1=gl[1][:], op=ALU.mult),
                                 reads=["gl1", ("mT", 8 + fo)] + [PS[k] for k in range(4)], writes=[("mT", 8 + fo)])
                    f.barrier()

                with ExitStack() as os_:
                    wout = os_.enter_context(nc.sbuf_tensor(nk("ewout"), [128, 12, D], BF16))
                    ytmp_box[0] = os_.enter_context(nc.sbuf_tensor(nk("ytmp"), [128, 2, 512], F32))
                    woutk = load_big(wout, "ewout", ewout_d[i], 12)
                    f.dma("sp", gain[:], bcast("post_norm", l * D, D), writes=["gain"])
                    for b in range(NB):
                        outproj_block(b, lambda kc, b=b: mT[:, kc, b * 128:(b + 1) * 128],
                                      [("mT", ch) for ch in range(12)], 12, wout, woutk, l)
                    f.barrier()

        for l in layers:
            if l % 2 == 0:
                even_layer(l)
            else:
                odd_layer(l)

        for b in range(NB):
            f.dma("sp", out_d[b * 128:(b + 1) * 128, :], xres[:, b, :], reads=[("x", b)], key="out")
        nc.sync.wait_ge(f.dsem["out"][0], f.dsem["out"][1])
    return nc


_CACHE = {}


def prep_inputs(inputs):
    w = {k: np.ascontiguousarray(np.asarray(v, dtype=np.float32)) for k, v in inputs.items()}
    perm = swap_perm()
    shared = {k: v for k, v in w.items() if k != "x"}
    shared["even_w_sw"] = np.ascontiguousarray(w["even_w_in"][:, :, 0:2048][:, :, perm])
    shared.update(host_consts())
    return w["x"], shared


def kernel(**inputs):
    x, shared = prep_inputs(inputs)
    if "nc" not in _CACHE:
        _CACHE["nc"] = build()
    nc = _CACHE["nc"]
    in_maps = []
    for c in range(8):
        m = dict(shared)
        m["x"] = np.ascontiguousarray(x[c])
        in_maps.append(m)
    res = run_bass_kernel_spmd(nc, in_maps, core_ids=list(range(8)))
    return np.stack([np.asarray(r["out"], dtype=np.float32) for r in res.results], axis=0)
```

```python
import math
import numpy as np
import concourse.bass as bass
import concourse.mybir as mybir
from concourse.bass_utils import run_bass_kernel_spmd

F32 = mybir.dt.float32
BF16 = mybir.dt.bfloat16
I32 = mybir.dt.int32
ALU = mybir.AluOpType
AF = mybir.ActivationFunctionType
AX = mybir.AxisListType

S = 2048
D = 1024
NB = 16
EPS = 1e-6
TWO_PI = 2.0 * math.pi


class FW:
    def __init__(self, nc):
        self.nc = nc
        self.eng = {"pe": nc.tensor, "act": nc.scalar, "dve": nc.vector,
                    "pool": nc.gpsimd, "sp": nc.sync}
        self.sem = {}
        self.cnt = {}
        for e in self.eng:
            self.sem[e] = nc.alloc_semaphore("s_" + e)
            self.cnt[e] = 0
        self.waited = {}
        self.lastw = {}
        self.rd = {}
        self.dsem = {}

    def _deps(self, reads, writes):
        deps = {}

        def add(t):
            if t is not None and deps.get(t[0], 0) < t[1]:
                deps[t[0]] = t[1]
        for k in reads:
            add(self.lastw.get(k))
        for k in writes:
            add(self.lastw.get(k))
            for sk, v in self.rd.get(k, {}).items():
                add((sk, v))
        return deps

    def _semof(self, sk):
        if isinstance(sk, tuple):
            return self.dsem[sk[1]][0]
        return self.sem[sk]

    def _emit_waits(self, e, deps):
        for sk, v in deps.items():
            if sk == e and v > self.cnt[e]:
                continue
            if self.waited.get((e, sk), 0) < v:
                self.eng[e].wait_ge(self._semof(sk), v)
                self.waited[(e, sk)] = v

    def _record(self, t, reads, writes):
        for k in writes:
            self.lastw[k] = t
            self.rd[k] = {}
        for k in reads:
            d = self.rd.setdefault(k, {})
            if d.get(t[0], 0) < t[1]:
                d[t[0]] = t[1]

    def op(self, e, fn, reads=(), writes=(), inc=True):
        self._emit_waits(e, self._deps(reads, writes))
        ins = fn(self.eng[e])
        if inc:
            ins.then_inc(self.sem[e], 1)
            self.cnt[e] += 1
            t = (e, self.cnt[e])
        else:
            t = (e, self.cnt[e] + 1)
        self._record(t, reads, writes)
        return ins

    def dma(self, q, out, in_, reads=(), writes=(), key=None, **kw):
        if key is None:
            key = writes[0] if writes else reads[0]
        if key not in self.dsem:
            self.dsem[key] = [self.nc.alloc_semaphore("d%d" % len(self.dsem)), 0]
        self._emit_waits(q, self._deps(reads, writes))
        ins = self.eng[q].dma_start(out=out, in_=in_, **kw)
        ins.then_inc(self.dsem[key][0], 16)
        self.dsem[key][1] += 16
        t = (("dma", key), self.dsem[key][1])
        self._record(t, reads, writes)
        return ins

    def barrier(self):
        for e in self.eng:
            deps = {}
            for o in self.eng:
                if o != e and self.cnt[o] > 0:
                    deps[o] = self.cnt[o]
            for key, (s, c) in self.dsem.items():
                if c > 0:
                    deps[("dma", key)] = c
            self._emit_waits(e, deps)


def _mult(d):
    d = np.asarray(d)
    m = ((d >= 0) & (d <= 128)).astype(np.float32)
    m += ((d >= 0) & (d <= 512) & (d % 4 == 0)).astype(np.float32)
    m += ((d >= 0) & (d % 16 == 0)).astype(np.float32)
    return m


def host_consts():
    c = {}
    c["c_ident"] = np.eye(128, dtype=np.float32)
    j = np.arange(128)[:, None]
    t = np.arange(128)[None, :]
    c["c_mask"] = np.concatenate([_mult(128 * dl + t - j) for dl in range(-3, 16)], axis=1).astype(np.float32)
    bands = np.zeros((128, 4, 3, 128), np.float32)
    tp = np.arange(128)[:, None]
    tt = np.arange(128)[None, :]
    for wi, w in enumerate((2, 4, 8, 16)):
        dcur = tt - tp
        bands[:, wi, 0, :] = ((dcur >= 0) & (dcur < w)) / float(w) - (dcur == 0)
        dprev = tt + 128 - tp
        bands[:, wi, 1, :] = ((dprev >= 0) & (dprev < w)) / float(w)
        cnt = np.minimum(tt + 1, w).astype(np.float32)
        bands[:, wi, 2, :] = ((dcur >= 0) & (dcur < w)) / cnt - (dcur == 0)
    c["c_bands"] = bands.reshape(128, 4 * 3 * 128)
    half = 8
    inv = 500000.0 ** (-np.arange(0, 16, 2, dtype=np.float32) / 16.0)
    ang = np.arange(S, dtype=np.float32)[None, :] * inv[:, None]
    C = np.ones((64, S), np.float32)
    Sg = np.zeros((64, S), np.float32)
    C[0:8] = np.cos(ang)
    C[8:16] = np.cos(ang)
    Sg[0:8] = -np.sin(ang)
    Sg[8:16] = np.sin(ang)
    c["c_ropec"] = np.concatenate([C, C], 0)
    c["c_ropes"] = np.concatenate([Sg, Sg], 0)
    gm = np.zeros((128, 8), np.float32)
    for g in range(8):
        gm[g * 16:(g + 1) * 16, g] = 1.0
    c["c_gmask"] = gm
    return c


def swap_perm():
    perm = np.arange(2048)
    for blk in range(2048 // 64):
        b = blk * 64
        perm[b:b + 8] = np.arange(b + 8, b + 16)
        perm[b + 8:b + 16] = np.arange(b, b + 8)
    return perm


def build(layers=(0, 1, 2, 3)):
    nc = bass.Bass("TRN2", target_bir_lowering=False)
    dt_ = {}

    def din(name, shape):
        h = nc.dram_tensor(name, list(shape), F32, kind="ExternalInput")
        dt_[name] = h
        return h.ap()

    x_d = din("x", [S, D])
    pre_d = din("pre_norm", [4, D])
    post_d = din("post_norm", [4, D])
    ewin_d = din("even_w_in", [2, D, 5120])
    ewsw_d = din("even_w_sw", [2, D, 2048])
    ewout_d = din("even_w_out", [2, 1536, D])
    are_d = din("ssm_a_re", [2, 32, 64])
    aim_d = din("ssm_a_im", [2, 32, 64])
    ldt_d = din("ssm_log_dt", [2, 32])
    bre_d = din("ssm_b_re", [2, 32, 64, 16])
    bim_d = din("ssm_b_im", [2, 32, 64, 16])
    cre_d = din("ssm_c_re", [2, 32, 16, 64])
    cim_d = din("ssm_c_im", [2, 32, 16, 64])
    sd_d = din("ssm_d", [2, 512])
    glw_d = din("ssm_glu_w", [2, 512, 512])
    glb_d = din("ssm_glu_b", [2, 512])
    owin_d = din("odd_w_in", [2, D, 4096])
    pw_d = din("pool_w", [2, 4, 512, 512])
    psc_d = din("pool_scale", [2, 2048])
    owout_d = din("odd_w_out", [2, 2048, D])
    cid_d = din("c_ident", [128, 128])
    cmask_d = din("c_mask", [128, 19 * 128])
    cband_d = din("c_bands", [128, 12 * 128])
    cropec_d = din("c_ropec", [128, S])
    cropes_d = din("c_ropes", [128, S])
    cgm_d = din("c_gmask", [128, 8])
    out_d = nc.dram_tensor("out", [S, D], F32, kind="ExternalOutput").ap()

    f = FW(nc)
    uid = [0]

    def nk(p):
        uid[0] += 1
        return "%s%d" % (p, uid[0])

    def bcast(name, off, n):
        return bass.AP(dt_[name], off, [[0, 128], [1, n]])

    from contextlib import ExitStack
    with ExitStack() as es:
        def sb(name, shape, dtype):
            return es.enter_context(nc.sbuf_tensor(name, list(shape), dtype))

        def psb(name, shape, dtype):
            return es.enter_context(nc.psum_tensor(name, list(shape), dtype))

        xres = sb("xres", [128, NB, D], F32)
        ps = [psb("ps%d" % i, [128, 512], F32) for i in range(8)]
        PS = ["ps%d" % i for i in range(8)]
        ident = sb("ident", [128, 128], BF16)
        identf = sb("identf", [128, 128], F32)
        gain = sb("gain", [128, D], F32)
        junk = sb("junk", [128, D], BF16)
        stat = sb("stat", [128, 64], F32)
        gmask = sb("gmask", [128, 8], F32)
        wbf = [sb("wbf%d" % i, [128, 8, 128], BF16) for i in range(6)]
        wctr = [0, 0]

        f.dma("sp", identf[:], cid_d, writes=["identf"])
        f.op("dve", lambda e: e.tensor_copy(out=ident[:], in_=identf[:]), reads=["identf"], writes=["ident"])
        f.dma("sp", gmask[:], cgm_d, writes=["gmask"])
        for b in range(NB):
            f.dma("sp", xres[:, b, :], x_d[b * 128:(b + 1) * 128, :], writes=[("x", b)])

        cast_rr = [0]

        def cast(out, in_, reads, writes):
            e = "act"
            if e == "pool":
                f.op("pool", lambda g: g.tensor_copy(out=out, in_=in_), reads=reads, writes=writes)
            else:
                f.op("act", lambda g: g.copy(out=out, in_=in_), reads=reads, writes=writes)

        class WS:
            DEPTH = 2

            def __init__(self, srcs):
                self.srcs = srcs
                self.issued = 0
                self.cur = 0
                self.buf = {}

            def prefetch(self, n):
                self._issue_to(self.cur + n)

            def get(self, issue=True):
                if issue:
                    self._issue_to(self.cur + 1 + WS.DEPTH)
                assert self.cur < self.issued
                bi = self.buf[self.cur]
                self.cur += 1
                return wbf[bi], "wbf%d" % bi

            def _issue_to(self, lim):
                while self.issued < min(len(self.srcs), lim):
                    src, kc = self.srcs[self.issued]
                    bi = wctr[1] % 6
                    wctr[1] += 1
                    f.dma("pool", wbf[bi][:, 0:kc, :], src.rearrange("(k p) n -> p k n", p=128),
                          writes=["wbf%d" % bi])
                    self.buf[self.issued] = bi
                    self.issued += 1

        def load_big(dst, dkey, src_ap, kc, step=4):
            keys = []
            for k0 in range(0, kc, step):
                k1 = min(kc, k0 + step)
                f.dma("pool", dst[:, k0:k1, :], src_ap[k0 * 128:k1 * 128, :].rearrange("(k p) n -> p k n", p=128),
                      writes=[(dkey, k0)])
                keys.append((dkey, k0))
            return keys

        def load_big_hw(dst, dkey, src_ap, kc, stg, skey):
            keys = []
            for k0 in range(0, kc, 2):
                f.dma("sp", stg[:, :, :], src_ap[k0 * 128:(k0 + 2) * 128, :].rearrange("(k p) n -> p k n", p=128),
                      writes=[skey])
                f.op("act", lambda g: g.copy(out=dst[:, k0:k0 + 2, :], in_=stg[:, :, :]), reads=[skey], writes=[(dkey, k0)])
                keys.append((dkey, k0))
            return keys

        scol = [0]

        def newcol():
            scol[0] = (scol[0] + 1) % 64
            return scol[0]

        def rstd_from_ss(c_ss):
            c1 = newcol()
            c2 = newcol()
            f.op("act", lambda e: e.activation(out=stat[:, c1:c1 + 1], in_=stat[:, c_ss:c_ss + 1], func=AF.Sqrt,
                                               scale=1.0 / D, bias=EPS),
                 reads=[("st", c_ss)], writes=[("st", c1)])
            f.op("dve", lambda e: e.reciprocal(out=stat[:, c2:c2 + 1], in_=stat[:, c1:c1 + 1]),
                 reads=[("st", c1)], writes=[("st", c2)])
            return c2

        def prenorm_block(b, dst, dkey):
            c = newcol()
            f.op("act", lambda e: e.activation(out=junk[:], in_=xres[:, b, :], func=AF.Square,
                                               accum_out=stat[:, c:c + 1]),
                 reads=[("x", b)], writes=["junk", ("st", c)])
            c2 = rstd_from_ss(c)
            f.op("dve", lambda e: e.scalar_tensor_tensor(out=dst, in0=xres[:, b, :], scalar=stat[:, c2:c2 + 1],
                                                         in1=gain[:], op0=ALU.mult, op1=ALU.mult),
                 reads=[("x", b), ("st", c2), "gain"], writes=[dkey])

        def transpose_block(src, skey, dst_ap, dkey, pi):
            for hf in range(2):
                bk = 4 + 2 * pi + hf
                for cc in range(4):
                    c = hf * 4 + cc
                    f.op("pe", lambda e, c=c, cc=cc, bk=bk: e.matmul(ps[bk][:, cc * 128:(cc + 1) * 128],
                                                                     lhsT=src[:, c * 128:(c + 1) * 128], rhs=ident[:],
                                                                     start=True, stop=True),
                         reads=[skey, "ident"], writes=[PS[bk]], inc=(cc == 3))
                f.op("act", lambda e, hf=hf, bk=bk: e.copy(out=dst_ap[:, hf * 4:(hf + 1) * 4, :],
                                                           in_=ps[bk][:].rearrange("p (c t) -> p c t", c=4)),
                     reads=[PS[bk]], writes=[dkey])

        def outproj_block(b, mT_fn, mkeys, KC, wout, wkey, l):
            pp = (b % 2) * 2
            for fh in range(2):
                for kc in range(KC):
                    f.op("pe", lambda e, kc=kc, fh=fh: e.matmul(ps[pp + fh][:], lhsT=mT_fn(kc),
                                                                rhs=wout[:, kc, fh * 512:(fh + 1) * 512],
                                                                start=(kc == 0), stop=(kc == KC - 1)),
                         reads=list(mkeys) + list(wkey), writes=[PS[pp + fh]], inc=(kc == KC - 1))
            ca = newcol()
            cb = newcol()
            f.op("act", lambda e: e.activation(out=junk[:, 0:512], in_=ps[pp][:], func=AF.Square,
                                               accum_out=stat[:, ca:ca + 1]),
                 reads=[PS[pp]], writes=["junk", ("st", ca)])
            f.op("act", lambda e: e.activation(out=junk[:, 512:1024], in_=ps[pp + 1][:], func=AF.Square,
                                               accum_out=stat[:, cb:cb + 1]),
                 reads=[PS[pp + 1]], writes=["junk", ("st", cb)])
            cs = newcol()
            f.op("dve", lambda e: e.tensor_tensor(out=stat[:, cs:cs + 1], in0=stat[:, ca:ca + 1],
                                                  in1=stat[:, cb:cb + 1], op=ALU.add),
                 reads=[("st", ca), ("st", cb)], writes=[("st", cs)])
            c2 = rstd_from_ss(cs)
            for fh in range(2):
                tk = "ytmp%d" % fh
                f.op("dve", lambda e, fh=fh: e.scalar_tensor_tensor(out=ytmp_box[0][:, fh, :], in0=ps[pp + fh][:],
                                                                    scalar=stat[:, c2:c2 + 1],
                                                                    in1=gain[:, fh * 512:(fh + 1) * 512],
                                                                    op0=ALU.mult, op1=ALU.mult),
                     reads=[PS[pp + fh], ("st", c2), "gain"], writes=[tk])
                f.op("pool", lambda e, fh=fh: e.tensor_tensor(out=xres[:, b, fh * 512:(fh + 1) * 512],
                                                              in0=xres[:, b, fh * 512:(fh + 1) * 512],
                                                              in1=ytmp_box[0][:, fh, :], op=ALU.add),
                     reads=[("x", b), tk], writes=[("x", b)])

        ytmp_box = [None]

        def odd_layer(l):
            i = l // 2
            with ExitStack() as ls:
                def lsb(name, shape, dtype):
                    return ls.enter_context(nc.sbuf_tensor(nk(name), list(shape), dtype))
                hN = lsb("hN", [128, 5, D], BF16)
                hT = lsb("hTq", [128, 8, 512], BF16)
                phT = lsb("phT", [128, 8, 512], BF16)
                mixed = lsb("mixed", [128, 4, 512], BF16)
                mT = lsb("mTq", [128, 16, 512], BF16)
                sg = lsb("sg", [128, 512], F32)
                bandf = lsb("bandf", [128, 12, 128], F32)
                band = lsb("band", [128, 4, 4, 128], BF16)
                btmp = lsb("btmp", [128, 4, 128], F32)
                pwbf = lsb("pwbf", [128, 4, 512], BF16)
                wout = lsb("wout", [128, 16, D], BF16)
                pscale = lsb("pscale", [128, 16], F32)
                wost = [lsb("wost%d" % k_, [128, 2, D], F32) for k_ in range(2)]
                pwst = lsb("pwst", [128, 4, 512], F32)
                ytmp_box[0] = lsb("ytmp", [128, 2, 512], F32)
                f.dma("sp", bandf[:], cband_d.rearrange("p (a t) -> p a t", t=128), writes=["bandf"])
                bv = bandf[:].rearrange("p (w k) t -> p w k t", k=3)
                f.op("dve", lambda e: e.tensor_copy(out=band[:, :, 0:3, :], in_=bv), reads=["bandf"], writes=["band"])
                f.op("dve", lambda e: e.tensor_copy(out=btmp[:], in_=band[:, :, 2, :]), reads=["band"], writes=["btmp"])
                f.op("dve", lambda e: e.tensor_tensor(out=btmp[:], in0=bv[:, :, 2, :], in1=btmp[:], op=ALU.subtract),
                     reads=["bandf", "btmp"], writes=["btmp"])
                f.op("dve", lambda e: e.tensor_copy(out=band[:, :, 3, :], in_=btmp[:]), reads=["btmp"], writes=["band"])
                f.dma("sp", pscale[:], psc_d[i].rearrange("(c p) -> p c", p=128), writes=["pscale"],
                      allow_slow_non_contiguous=True)
                srcs = []
                for tq in range(4):
                    for g in range(4):
                        for j in range(4):
                            srcs.append((owin_d[i][:, g * 512 + j * 128:g * 512 + (j + 1) * 128], 8))
                        for dj in range(4):
                            col = 2048 + g * 512 + dj * 128
                            srcs.append((owin_d[i][:, col:col + 128], 8))
                ws = WS(srcs)
                for tq in range(4):
                    b0 = tq * 4
                    woutk = [("wout", 2 * k_) for k_ in range(8)]
                    wstep = [0]

                    def wout_step():
                        k_ = wstep[0]
                        wstep[0] += 1
                        if 1 <= k_ <= 8:
                            kk = k_ - 1
                            f.op("act", lambda g_: g_.copy(out=wout[:, 2 * kk:2 * kk + 2, :], in_=wost[kk % 2][:]),
                                 reads=["wost%d" % (kk % 2)], writes=[("wout", 2 * kk)])
                        if k_ < 8:
                            f.dma("sp", wost[k_ % 2][:],
                                  owout_d[i][k_ * 256:(k_ + 1) * 256, :].rearrange("(k p) n -> p k n", p=128),
                                  writes=["wost%d" % (k_ % 2)])
                    wout_step()
                    f.dma("sp", gain[:], bcast("pre_norm", l * D, D), writes=["gain"])
                    if tq > 0:
                        f.op("pool", lambda e: e.tensor_copy(out=hN[:, 0, :], in_=hN[:, 4, :]),
                             reads=[("hN", 4)], writes=[("hN", 0)])
                    for s_, b in enumerate(range(b0 - 1, b0 + 4)):
                        if s_ == 0:
                            continue
                        prenorm_block(b, hN[:, s_, :], ("hN", s_))
                    for s_ in range(1, 5):
                        transpose_block(hN[:, s_, :], ("hN", s_), hT[:, :, (s_ - 1) * 128:s_ * 128], "hT", s_ % 2)
                    for g in range(4):
                        f.dma("sp", pwst[:], pw_d[i, g].rearrange("(k p) n -> p k n", p=128), writes=["pwst"])
                        pwk = [("pwbf", 0), ("pwbf", 2)]
                        for s_ in range(1, 5):
                            b = b0 + s_ - 1
                            for hf in range(2):
                                pk = 4 + hf
                                for cc in range(4):
                                    c = hf * 4 + cc
                                    o = ps[pk][:, cc * 128:(cc + 1) * 128]
                                    lh = hN[:, s_, c * 128:(c + 1) * 128]
                                    if b == 0:
                                        f.op("pe", lambda e, o=o, lh=lh: e.matmul(o, lhsT=lh, rhs=band[:, g, 2, :],
                                                                                  start=True, stop=False),
                                             reads=[("hN", s_), "band"], writes=[PS[pk]], inc=False)
                                        f.op("pe", lambda e, o=o, lh=lh: e.matmul(o, lhsT=lh, rhs=band[:, g, 3, :],
                                                                                  start=False, stop=True),
                                             reads=[("hN", s_), "band"], writes=[PS[pk]], inc=(cc == 3))
                                    else:
                                        lp = hN[:, s_ - 1, c * 128:(c + 1) * 128]
                                        f.op("pe", lambda e, o=o, lh=lh: e.matmul(o, lhsT=lh, rhs=band[:, g, 0, :],
                                                                                  start=True, stop=False),
                                             reads=[("hN", s_), "band"], writes=[PS[pk]], inc=False)
                                        f.op("pe", lambda e, o=o, lp=lp: e.matmul(o, lhsT=lp, rhs=band[:, g, 1, :],
                                                                                  start=False, stop=True),
                                             reads=[("hN", s_ - 1), "band"], writes=[PS[pk]], inc=(cc == 3))
                                f.op("act", lambda e, hf=hf, pk=pk: e.copy(
                                    out=phT[:, hf * 4:(hf + 1) * 4, (s_ - 1) * 128:s_ * 128],
                                    in_=ps[pk][:].rearrange("p (c t) -> p c t", c=4)),
                                    reads=[PS[pk]], writes=["phT"])
                        for j in range(4):
                            wout_step()
                            wb, wk = ws.get()
                            pk = j % 2
                            for kc in range(8):
                                f.op("pe", lambda e, kc=kc: e.matmul(ps[pk][:], lhsT=wb[:, kc, :], rhs=phT[:, kc, :],
                                                                     start=(kc == 0), stop=(kc == 7)),
                                     reads=[wk, "phT"], writes=[PS[pk]], inc=(kc == 7))
                            f.op("act", lambda e, j=j, pk=pk: e.copy(out=mixed[:, j, :], in_=ps[pk][:]),
                                 reads=[PS[pk]], writes=[("mixed", j)])
                        for k0_ in (0, 2):
                            f.op("act", lambda g_, k0_=k0_: g_.copy(out=pwbf[:, k0_:k0_ + 2, :], in_=pwst[:, k0_:k0_ + 2, :]),
                                 reads=["pwst"], writes=[("pwbf", k0_)])
                        for dj in range(4):
                            wb, wk = ws.get()
                            pg = 2
                            for kc in range(8):
                                f.op("pe", lambda e, kc=kc: e.matmul(ps[pg][:], lhsT=wb[:, kc, :], rhs=hT[:, kc, :],
                                                                     start=(kc == 0), stop=(kc == 7)),
                                     reads=[wk, "hT"], writes=[PS[pg]], inc=(kc == 7))
                            f.op("act", lambda e: e.activation(out=sg[:], in_=ps[pg][:], func=AF.Silu),
                                 reads=[PS[pg]], writes=["sg"])
                            py = 3
                            for j in range(4):
                                f.op("pe", lambda e, j=j, dj=dj: e.matmul(ps[py][:], lhsT=pwbf[:, j, dj * 128:(dj + 1) * 128],
                                                                          rhs=mixed[:, j, :], start=(j == 0), stop=(j == 3)),
                                     reads=pwk + [("mixed", j)], writes=[PS[py]], inc=(j == 3))
                            ch = g * 4 + dj
                            f.op("dve", lambda e, ch=ch: e.scalar_tensor_tensor(out=mT[:, ch, :], in0=ps[py][:],
                                                                                scalar=pscale[:, ch:ch + 1], in1=sg[:],
                                                                                op0=ALU.mult, op1=ALU.mult),
                                 reads=[PS[py], "pscale", "sg"], writes=[("mT", ch)])
                    f.dma("sp", gain[:], bcast("post_norm", l * D, D), writes=["gain"])
                    for bb in range(4):
                        outproj_block(b0 + bb, lambda kc, bb=bb: mT[:, kc, bb * 128:(bb + 1) * 128],
                                      [("mT", ch) for ch in range(16)], 16, wout, woutk, l)
                f.barrier()

        def even_layer(l):
            i = l // 2
            with ExitStack() as ls:
                def lsb(name, shape, dtype):
                    return ls.enter_context(nc.sbuf_tensor(nk(name), list(shape), dtype))
                hT = lsb("hT", [128, 8, S], BF16)
                mT = lsb("mT", [128, 12, S], BF16)
                f.dma("sp", gain[:], bcast("pre_norm", l * D, D), writes=["gain"])
                with ExitStack() as hs_:
                    hNb = [hs_.enter_context(nc.sbuf_tensor(nk("hNb"), [128, D], BF16)) for k in range(2)]
                    for b in range(NB):
                        prenorm_block(b, hNb[b % 2][:], "hNb%d" % (b % 2))
                        transpose_block(hNb[b % 2], "hNb%d" % (b % 2), hT[:, :, b * 128:(b + 1) * 128], "hT", b % 2)
                    f.barrier()

                def proj_fm(wb, wk, pk, tg):
                    for kc in range(8):
                        f.op("pe", lambda e, kc=kc: e.matmul(ps[pk][:], lhsT=wb[:, kc, :],
                                                             rhs=hT[:, kc, tg * 512:(tg + 1) * 512],
                                                             start=(kc == 0), stop=(kc == 7)),
                             reads=[wk, "hT"], writes=[PS[pk]], inc=(kc == 7))

                with ExitStack() as as_:
                    def asb(name, shape, dtype):
                        return as_.enter_context(nc.sbuf_tensor(nk(name), list(shape), dtype))
                    qT = asb("qT", [128, S], BF16)
                    kT = asb("kT", [128, 2, S], BF16)
                    V = asb("V", [128, NB, 2, 128], BF16)
                    ropec = asb("ropec", [128, S], BF16)
                    ropes = asb("ropes", [128, S], BF16)
                    maskT = asb("maskT", [128, 19 * 128], BF16)
                    Pt = [asb("Pt%d" % k, [128, 512], BF16) for k in range(4)]
                    rtmp = [asb("rtmp%d" % k, [128, 512], F32) for k in range(2)]
                    rc = rtmp[0]
                    atmp = rtmp[1]
                    f.dma("pool", ropec[:], cropec_d, writes=["ropec"])
                    f.dma("pool", ropes[:], cropes_d, writes=["ropes"])
                    f.dma("pool", maskT[:], cmask_d, writes=["maskT"])
                    srcs = []
                    for hp_ in range(8):
                        c0_ = hp_ * 128
                        for (base_, swb_) in ((0, 0), (1024, 1024)):
                            srcs.append((ewin_d[i][:, base_ + c0_:base_ + c0_ + 128], 8))
                            srcs.append((ewsw_d[i][:, swb_ + c0_:swb_ + c0_ + 128], 8))
                        srcs.append((ewin_d[i][:, 3072 + c0_:3072 + c0_ + 128], 8))
                        srcs.append((ewin_d[i][:, 2048 + c0_:2048 + c0_ + 128], 8))
                    ws = WS(srcs)
                    f.op("pool", lambda e: e.memset(V[:, :, :, 64:128], 1.0), writes=["V"])
                    f.op("pool", lambda e: e.memset(kT[:], 0.0), writes=[("kT", 0)])
                    mrr = [0]
                    qTb = [qT, mT[:, 8, :]]
                    kTb = [kT, mT[:, 9:11, :]]
                    f.op("pool", lambda e: e.memset(mT[:, 9:11, :], 0.0), writes=[("mT", 9), ("mT", 10), ("kT", 1)])
                    QK = [["qT", ("mT", 8)], [("kT", 0), ("kT", 1), ("mT", 9), ("mT", 10)]]

                    def proj_gen(hp_, issue):
                        par = hp_ % 2
                        qd = qTb[par]
                        kd = kTb[par]
                        qk_ = [QK[0][par]]
                        kk_ = [("kT", par)] + ([("mT", 9), ("mT", 10)] if par == 1 else [])
                        for which in ("q", "k"):
                            wb, wk = ws.get(issue)
                            wb2, wk2 = ws.get(issue)
                            for tg in range(4):
                                sl = slice(tg * 512, (tg + 1) * 512)
                                proj_fm(wb, wk, 6, tg)
                                yield
                                proj_fm(wb2, wk2, 7, tg)
                                r0 = "rtmp0"
                                r1 = "rtmp1"
                                f.op("dve", lambda e, sl=sl: e.tensor_tensor(out=rtmp[0][:], in0=ps[6][:], in1=ropec[:, sl],
                                                                             op=ALU.mult),
                                     reads=[PS[6], "ropec"], writes=[r0])
                                f.op("dve", lambda e, sl=sl: e.tensor_tensor(out=rtmp[1][:], in0=ps[7][:], in1=ropes[:, sl],
                                                                             op=ALU.mult),
                                     reads=[PS[7], "ropes"], writes=[r1])
                                if which == "q":
                                    f.op("dve", lambda e, sl=sl: e.tensor_tensor(out=qd[:, sl], in0=rtmp[0][:],
                                                                                 in1=rtmp[1][:], op=ALU.add),
                                         reads=[r0, r1], writes=qk_)
                                else:
                                    for a_ in range(2):
                                        pr_ = slice(64 * a_, 64 * a_ + 64)
                                        f.op("dve", lambda e, sl=sl, a_=a_, pr_=pr_: e.tensor_tensor(
                                            out=kd[pr_, a_, sl], in0=rtmp[0][pr_, :], in1=rtmp[1][pr_, :], op=ALU.add),
                                            reads=[r0, r1], writes=kk_)
                                yield
                        wb, wk = ws.get(issue)
                        for tg in range(4):
                            pg_ = 6 + (tg % 2)
                            proj_fm(wb, wk, pg_, tg)
                            f.op("act", lambda e, tg=tg, pg_=pg_: e.activation(out=mT[:, hp_, tg * 512:(tg + 1) * 512],
                                                                               in_=ps[pg_][:], func=AF.Silu),
                                 reads=[PS[pg_]], writes=[("mT", hp_)])
                        yield

                    gen = proj_gen(0, True)
                    for _ in gen:
                        pass
                    gen = None
                    for hp in range(8):
                        par = hp % 2
                        qT_ = qTb[par]
                        kT_ = kTb[par]
                        qkeys = [QK[0][par]]
                        kkeys = [("kT", par)] + ([("mT", 9), ("mT", 10)] if par == 1 else [])
                        if gen is not None:
                            for _ in gen:
                                pass
                        wb, wk = ws.get(hp == 0)
                        for b4 in range(4):
                            pk = 6 + (b4 % 2)
                            for bb in range(4):
                                b = b4 * 4 + bb
                                for kc in range(8):
                                    f.op("pe", lambda e, kc=kc, b=b, bb=bb: e.matmul(
                                        ps[pk][:, bb * 128:(bb + 1) * 128], lhsT=hT[:, kc, b * 128:(b + 1) * 128],
                                        rhs=wb[:, kc, :], start=(kc == 0), stop=(kc == 7)),
                                        reads=[wk, "hT"], writes=[PS[pk]], inc=(kc == 7 and bb == 3))
                            f.op("act", lambda e, b4=b4: e.copy(
                                out=V[:, b4 * 4:(b4 + 1) * 4, :, 0:64],
                                in_=ps[pk][:].rearrange("p (b a d) -> p b a d", b=4, a=2)),
                                reads=[PS[pk]], writes=["V"])
                        ws.prefetch(6)
                        gen = proj_gen(hp + 1, False) if hp < 7 else None
                        items = []
                        for a in range(2):
                            for qg in range(4):
                                nkb = 4 * qg + 4
                                for kb in range(nkb):
                                    items.append((a, qg, kb, nkb))
                        LAG = 2

                        def stage1(idx):
                            a, qg, kb, nkb = items[idx]
                            pr = slice(64 * a, 64 * a + 64)
                            cq = max(128 * kb, 512 * qg)
                            N = 512 * (qg + 1) - cq
                            sk = idx % 4
                            f.op("pe", lambda e: e.matmul(
                                ps[sk][:, 0:N], lhsT=kT_[:, a, kb * 128:(kb + 1) * 128], rhs=qT_[:, cq:cq + N],
                                start=True, stop=True),
                                reads=kkeys + qkeys, writes=[PS[sk]])
                            f.op("act", lambda e: e.activation(out=Pt[sk][:, 0:N], in_=ps[sk][:, 0:N],
                                                               func=AF.Exp, scale=0.125),
                                 reads=[PS[sk]], writes=["Pt%d" % sk])
                            moff = ((cq - 128 * kb) // 128 + 3) * 128
                            me = "dve"
                            mrr[0] += 1
                            f.op(me, lambda e: e.tensor_tensor(
                                out=Pt[sk][:, 0:N], in0=Pt[sk][:, 0:N], in1=maskT[:, moff:moff + N], op=ALU.mult),
                                reads=["Pt%d" % sk, "maskT"], writes=["Pt%d" % sk])

                        def stage2(idx):
                            a, qg, kb, nkb = items[idx]
                            pr = slice(64 * a, 64 * a + 64)
                            cq = max(128 * kb, 512 * qg)
                            N = 512 * (qg + 1) - cq
                            sk = idx % 4
                            po = 4 + (qg % 2)
                            oc = cq - 512 * qg
                            f.op("pe", lambda e: e.matmul(
                                ps[po][:, oc:oc + N], lhsT=V[:, kb, a, :], rhs=Pt[sk][:, 0:N],
                                start=(kb == 0), stop=(kb == nkb - 1)),
                                reads=["V", "Pt%d" % sk], writes=[PS[po]])
                            if kb == nkb - 1:
                                qs = slice(qg * 512, (qg + 1) * 512)
                                f.op("act", lambda e: e.activation(out=rc[64:128, :], in_=ps[po][64:128, :], func=AF.Ln),
                                     reads=[PS[po]], writes=["rtmp0"])
                                f.op("act", lambda e: e.activation(out=rc[64:128, :], in_=rc[64:128, :], func=AF.Exp,
                                                                   scale=-1.0),
                                     reads=["rtmp0"], writes=["rtmp0"])
                                f.op("dve", lambda e: e.tensor_tensor(out=atmp[pr, :], in0=ps[po][0:64, :],
                                                                      in1=rc[64:128, :], op=ALU.mult),
                                     reads=[PS[po], "rtmp0"], writes=["rtmp1"])
                                f.op("pool", lambda e: e.tensor_tensor(out=mT[pr, hp, qs], in0=atmp[pr, :],
                                                                       in1=mT[pr, hp, qs], op=ALU.mult),
                                     reads=["rtmp1", ("mT", hp)], writes=[("mT", hp)])

                        for idx in range(len(items) + LAG):
                            if idx < len(items):
                                stage1(idx)
                            if idx - LAG >= 0:
                                stage2(idx - LAG)
                            if gen is not None and idx % 4 == 3:
                                if next(gen, "done") == "done":
                                    gen = None
                    f.barrier()

                with ExitStack() as ss_:
                    def ssb(name, shape, dtype):
                        return ss_.enter_context(nc.sbuf_tensor(nk(name), list(shape), dtype))
                    NPW = 17 + 7
                    PR = ssb("PR", [128, 16, 3 * 24 + 4], F32)
                    pa = ssb("pa", [128, 16, 12], F32)
                    pi32 = ssb("pi32", [128, 16], I32)
                    XR = ssb("XR", [128, S], F32)
                    XI = ssb("XI", [128, S], F32)
                    cs_ = [ssb("cs%d" % k, [128, 2, 192], F32) for k in range(2)]
                    Xb = [ssb("Xb%d" % k, [128, 512], BF16) for k in range(2)]
                    uT = ssb("uT", [128, S], BF16)
                    bnat = ssb("bnat", [128, 2, 16, 16], F32)
                    cnat = ssb("cnat", [128, 2, 64], F32)
                    padT = ssb("padT", [128, 128], BF16)
                    padTf = ssb("padTf", [128, 2, 128], F32)
                    Bpad = ssb("Bpad", [128, 4, 2, 128], BF16)
                    Cpad = ssb("Cpad", [128, 4, 2, 128], BF16)
                    cf = ssb("cf", [128, 4, 128], F32)
                    dvec = ssb("dvec", [128, 4], F32)
                    glub = ssb("glub", [128, 4], F32)
                    gl = [ssb("gl%d" % k, [128, 512], F32) for k in range(2)] + [XR[:, 0:512]]

                    for k_ in range(2):
                        f.op("pool", lambda e, k_=k_: e.memset(cs_[k_][:], 0.0), writes=[("cs", k_, 0), ("cs", k_, 1)])
                    def ld_gp(dst_col, name):
                        for e_ in range(2):
                            f.dma("sp", pa[e_ * 64:(e_ + 1) * 64, :, dst_col],
                                  bass.AP(dt_[name], i * 2048 + e_ * 64, [[1, 64], [128, 16]]),
                                  writes=["pa"], allow_slow_non_contiguous=True)
                    ld_gp(0, "ssm_a_re")
                    ld_gp(1, "ssm_a_im")
                    for e_ in range(2):
                        f.dma("sp", pa[e_ * 64:(e_ + 1) * 64, :, 2],
                              bass.AP(dt_["ssm_log_dt"], i * 32 + e_, [[0, 64], [2, 16]]),
                              writes=["pa"], allow_slow_non_contiguous=True)
                    f.dma("sp", dvec[:], sd_d[i].rearrange("(c p) -> p c", p=128), writes=["dvec"],
                          allow_slow_non_contiguous=True)
                    f.dma("sp", glub[:], glb_d[i].rearrange("(c p) -> p c", p=128), writes=["glub"],
                          allow_slow_non_contiguous=True)

                    def pop(eng, fn, w=("pa",)):
                        f.op(eng, fn, reads=["pa", "PR"], writes=list(w))
                    A = lambda c: pa[:, :, c]
                    pop("act", lambda e: e.activation(out=A(3), in_=A(2), func=AF.Exp))
                    pop("dve", lambda e: e.tensor_tensor(out=A(4), in0=A(0), in1=A(3), op=ALU.mult))
                    pop("dve", lambda e: e.tensor_tensor(out=A(5), in0=A(1), in1=A(3), op=ALU.mult))
                    pop("act", lambda e: e.activation(out=A(4), in_=A(4), func=AF.Exp))

                    def sin_of(dst, shift):
                        pop("dve", lambda e: e.tensor_scalar(out=A(6), in0=A(5), scalar1=shift, scalar2=1.0 / TWO_PI,
                                                             op0=ALU.add, op1=ALU.mult))
                        f.op("dve", lambda e: e.tensor_copy(out=pi32[:], in_=A(6)), reads=["pa"], writes=["pi32"])
                        f.op("dve", lambda e: e.tensor_copy(out=A(7), in_=pi32[:]), reads=["pi32"], writes=["pa"])
                        pop("dve", lambda e: e.tensor_tensor(out=A(6), in0=A(6), in1=A(7), op=ALU.subtract))
                        pop("dve", lambda e: e.tensor_scalar(out=A(6), in0=A(6), scalar1=TWO_PI, scalar2=math.pi,
                                                             op0=ALU.mult, op1=ALU.min))
                        pop("dve", lambda e: e.tensor_scalar(out=A(6), in0=A(6), scalar1=-math.pi, scalar2=None,
                                                             op0=ALU.max))
                        pop("act", lambda e: e.activation(out=dst, in_=A(6), func=AF.Sin))
                    sin_of(A(8), 0.0)
                    sin_of(A(9), math.pi / 2)
                    P3 = lambda k, c: PR[:, :, 3 * k + c]
                    pop("dve", lambda e: e.tensor_tensor(out=P3(0, 0), in0=A(4), in1=A(9), op=ALU.mult), w=("PR",))
                    pop("dve", lambda e: e.tensor_tensor(out=P3(0, 1), in0=A(4), in1=A(8), op=ALU.mult), w=("PR",))

                    def cmul(dst, a_, b_):
                        pop("dve", lambda e: e.tensor_tensor(out=A(6), in0=P3(a_, 0), in1=P3(b_, 0), op=ALU.mult))
                        pop("dve", lambda e: e.tensor_tensor(out=A(7), in0=P3(a_, 1), in1=P3(b_, 1), op=ALU.mult))
                        pop("dve", lambda e: e.tensor_tensor(out=A(10), in0=P3(a_, 0), in1=P3(b_, 1), op=ALU.mult))
                        pop("dve", lambda e: e.tensor_tensor(out=A(11), in0=P3(a_, 1), in1=P3(b_, 0), op=ALU.mult))
                        pop("dve", lambda e: e.tensor_tensor(out=P3(dst, 0), in0=A(6), in1=A(7), op=ALU.subtract), w=("PR",))
                        pop("dve", lambda e: e.tensor_tensor(out=P3(dst, 1), in0=A(10), in1=A(11), op=ALU.add), w=("PR",))
                    for j in range(1, 16):
                        cmul(j, j - 1, 0)
                    for k in range(16, 16 + 7):
                        cmul(k, k - 1, k - 1)
                    for k in range(23):
                        pop("dve", lambda e, k=k: e.tensor_scalar(out=P3(k, 2), in0=P3(k, 1), scalar1=-1.0, scalar2=None,
                                                                  op0=ALU.mult), w=("PR",))
                    FR = PR[:, :, 72]
                    FI = PR[:, :, 73]
                    pop("dve", lambda e: e.tensor_scalar(out=A(6), in0=P3(0, 0), scalar1=-1.0, scalar2=None, op0=ALU.add))
                    pop("dve", lambda e: e.tensor_tensor(out=A(7), in0=A(0), in1=A(0), op=ALU.mult))
                    pop("dve", lambda e: e.tensor_tensor(out=A(10), in0=A(1), in1=A(1), op=ALU.mult))
                    pop("dve", lambda e: e.tensor_tensor(out=A(7), in0=A(7), in1=A(10), op=ALU.add))
                    pop("dve", lambda e: e.reciprocal(out=A(7), in_=A(7)))
                    pop("dve", lambda e: e.tensor_tensor(out=A(10), in0=A(6), in1=A(0), op=ALU.mult))
                    pop("dve", lambda e: e.tensor_tensor(out=A(11), in0=P3(0, 1), in1=A(1), op=ALU.mult))
                    pop("dve", lambda e: e.tensor_tensor(out=A(10), in0=A(10), in1=A(11), op=ALU.add))
                    pop("dve", lambda e: e.tensor_tensor(out=FR, in0=A(10), in1=A(7), op=ALU.mult), w=("PR",))
                    pop("dve", lambda e: e.tensor_tensor(out=A(10), in0=P3(0, 1), in1=A(0), op=ALU.mult))
                    pop("dve", lambda e: e.tensor_tensor(out=A(11), in0=A(6), in1=A(1), op=ALU.mult))
                    pop("dve", lambda e: e.tensor_tensor(out=A(10), in0=A(10), in1=A(11), op=ALU.subtract))
                    pop("dve", lambda e: e.tensor_tensor(out=FI, in0=A(10), in1=A(7), op=ALU.mult), w=("PR",))
                    for ri, name in enumerate(("ssm_b_re", "ssm_b_im")):
                        for e_ in range(2):
                            f.dma("sp", bnat[e_ * 64:(e_ + 1) * 64, ri, :, :],
                                  bass.AP(dt_[name], i * 32 * 1024 + e_ * 1024,
                                          [[16, 64], [2048, 16], [1, 16]]), writes=["bnat"])

                    XALL = [["X0"] + [("XR", s_) for s_ in range(16)], ["X1"] + [("XI", s_) for s_ in range(16)]]

                    srcs = [(ewin_d[i][:, 4096 + j_ * 128:4096 + (j_ + 1) * 128], 8) for j_ in range(4)]
                    for tg_ in range(4):
                        srcs += [(glw_d[i][:, fo_ * 128:(fo_ + 1) * 128], 4) for fo_ in range(4)]
                        srcs += [(ewin_d[i][:, 4608 + fo_ * 128:4608 + (fo_ + 1) * 128], 8) for fo_ in range(4)]
                    ws = WS(srcs)

                    def PSC(q, k, c):
                        return PR[:, q, 3 * k + c:3 * k + c + 1]

                    for j in range(4):
                        for ri, name in enumerate(("ssm_c_re", "ssm_c_im")):
                            f.dma("sp", cnat[:, ri, :],
                                  bass.AP(dt_[name], i * 32 * 1024 + j * 8 * 1024, [[64, 128], [1, 64]]),
                                  writes=["cnat"])
                        for qq in range(4):
                            q = j * 4 + qq
                            for ri in range(2):
                                f.op("pool", lambda e: e.memset(padT[:], 0.0), writes=["padT"])
                                for e_ in range(2):
                                    co = 16 * (2 * qq + e_)
                                    f.op("dve", lambda e, e_=e_, co=co, ri=ri, q=q: e.tensor_copy(
                                        out=padT[e_ * 64:(e_ + 1) * 64, co:co + 16],
                                        in_=bnat[e_ * 64:(e_ + 1) * 64, ri, q, :]),
                                        reads=["bnat"], writes=["padT"])
                                f.op("pe", lambda e: e.matmul(ps[6][:, 0:128], lhsT=padT[:], rhs=ident[:], start=True, stop=True),
                                     reads=["padT", "ident"], writes=[PS[6]])
                                f.op("act", lambda e, qq=qq, ri=ri: e.copy(out=Bpad[:, qq, ri, :], in_=ps[6][:, 0:128]),
                                     reads=[PS[6]], writes=["Bpad"])
                            for ri in range(2):
                                for e_ in range(2):
                                    g8 = 2 * qq + e_
                                    f.op("dve", lambda e, e_=e_, ri=ri, g8=g8: e.tensor_scalar(
                                        out=padTf[:, ri, e_ * 64:(e_ + 1) * 64], in0=cnat[:, ri, :],
                                        scalar1=gmask[:, g8:g8 + 1], scalar2=None, op0=ALU.mult),
                                        reads=["cnat", "gmask"], writes=["padTf"])
                            for ri in range(2):
                                f.op("pe", lambda e, ri=ri: e.matmul(ps[4][:, ri * 128:(ri + 1) * 128], lhsT=padTf[:, ri, :],
                                                                     rhs=identf[:], start=True, stop=True),
                                     reads=["padTf", "identf"], writes=[PS[4]])
                            fr = PR[:, q, 72:73]
                            fi = PR[:, q, 73:74]
                            f.op("act", lambda e: e.copy(out=cf[:, 0:2, :], in_=ps[4][:, 0:256].rearrange("p (a b) -> p a b", a=2)),
                                 reads=[PS[4]], writes=["cf"])
                            f.op("dve", lambda e, fr=fr: e.tensor_scalar(out=cf[:, 2, :], in0=cf[:, 0, :], scalar1=fr, scalar2=None,
                                                                         op0=ALU.mult), reads=["cf", "PR"], writes=["cf"])
                            f.op("dve", lambda e, fi=fi: e.tensor_scalar(out=cf[:, 3, :], in0=cf[:, 1, :], scalar1=fi, scalar2=None,
                                                                         op0=ALU.mult), reads=["cf", "PR"], writes=["cf"])
                            f.op("dve", lambda e, qq=qq: e.tensor_tensor(out=Cpad[:, qq, 0, :], in0=cf[:, 2, :], in1=cf[:, 3, :],
                                                                         op=ALU.subtract), reads=["cf"], writes=["Cpad"])
                            f.op("dve", lambda e, fi=fi: e.tensor_scalar(out=cf[:, 2, :], in0=cf[:, 0, :], scalar1=fi, scalar2=-1.0,
                                                                         op0=ALU.mult, op1=ALU.mult), reads=["cf", "PR"], writes=["cf"])
                            f.op("dve", lambda e, fr=fr: e.tensor_scalar(out=cf[:, 3, :], in0=cf[:, 1, :], scalar1=fr, scalar2=None,
                                                                         op0=ALU.mult), reads=["cf", "PR"], writes=["cf"])
                            f.op("dve", lambda e, qq=qq: e.tensor_tensor(out=Cpad[:, qq, 1, :], in0=cf[:, 2, :], in1=cf[:, 3, :],
                                                                         op=ALU.subtract), reads=["cf"], writes=["Cpad"])
                        wb, wk = ws.get()
                        for tg in range(4):
                            sl = slice(tg * 512, (tg + 1) * 512)
                            proj_fm(wb, wk, 4, tg)
                            f.op("act", lambda e, sl=sl: e.copy(out=uT[:, sl], in_=ps[4][:]), reads=[PS[4]], writes=["uT"])
                        def stage_in(qq_, ri):
                            X = (XR, XI)[ri]
                            for tg in range(4):
                                sl = slice(tg * 512, (tg + 1) * 512)
                                pk = 4 + (tg % 2)
                                f.op("pe", lambda e, sl=sl, pk=pk: e.matmul(
                                    ps[pk][:], lhsT=Bpad[:, qq_, ri, :], rhs=uT[:, sl], start=True, stop=True),
                                    reads=["Bpad", "uT"], writes=[PS[pk]])
                                f.op("act", lambda e, sl=sl, pk=pk, X=X: e.copy(out=X[:, sl], in_=ps[pk][:]),
                                     reads=[PS[pk]], writes=XALL[ri])

                        def stage_out(qq_, ri):
                            X = (XR, XI)[ri]
                            for tg in range(4):
                                sl = slice(tg * 512, (tg + 1) * 512)
                                bi = (ri * 4 + tg) % 2
                                cast(Xb[bi][:], X[:, sl], XALL[ri], ["Xb%d" % bi])
                                f.op("pe", lambda e, tg=tg, bi=bi: e.matmul(
                                    ps[tg][:], lhsT=Cpad[:, qq_, ri, :], rhs=Xb[bi][:],
                                    start=(qq_ == 0 and ri == 0), stop=(qq_ == 3 and ri == 1)),
                                    reads=["Cpad", "Xb%d" % bi], writes=[PS[tg]])

                        stage_in(0, 0)
                        stage_in(0, 1)
                        for qq in range(4):
                            q = j * 4 + qq
                            XRv = XR[:].rearrange("p (c s) -> p c s", s=16)
                            XIv = XI[:].rearrange("p (c s) -> p c s", s=16)

                            def _kl(k_):
                                return list(k_) if isinstance(k_, list) else [k_]

                            def cstep(oR, oI, iR, iI, k, kOR, kOI, kIR, kII, bR=None, bI=None, kB=()):
                                for (o_, i_, c_, ko, ki, b_) in ((oR, iR, 0, kOR, kIR, bR), (oI, iR, 1, kOI, kIR, bI),
                                                                 (oR, iI, 2, kOR, kII, None), (oI, iI, 0, kOI, kII, None)):
                                    add_ = o_ if b_ is None else b_
                                    f.op("dve", lambda e, o_=o_, i_=i_, c_=c_, add_=add_: e.scalar_tensor_tensor(
                                        out=o_, in0=i_, scalar=PSC(q, k, c_), in1=add_, op0=ALU.mult, op1=ALU.add),
                                        reads=_kl(ki) + ["PR"] + list(kB), writes=_kl(ko))
                            XRc = XR[:].rearrange("p (n r) -> p n r", r=4)
                            XIc = XI[:].rearrange("p (n r) -> p n r", r=4)
                            XR4 = XR[:].rearrange("p (c m r) -> p c m r", m=4, r=4)
                            XI4 = XI[:].rearrange("p (c m r) -> p c m r", m=4, r=4)
                            for r_ in range(1, 4):
                                ko_ = [r_ + 4 * m_ for m_ in range(4)]
                                ki_ = [r_ - 1 + 4 * m_ for m_ in range(4)]
                                cstep(XRc[:, :, r_], XIc[:, :, r_], XRc[:, :, r_ - 1], XIc[:, :, r_ - 1], 0,
                                      [("XR", c_) for c_ in ko_], [("XI", c_) for c_ in ko_],
                                      [("XR", c_) for c_ in ki_], [("XI", c_) for c_ in ki_])
                            for m_ in range(1, 4):
                                so_ = 4 * m_ + 3
                                si_ = 4 * m_ - 1
                                cstep(XRv[:, :, so_], XIv[:, :, so_], XRv[:, :, si_], XIv[:, :, si_], 3,
                                      ("XR", so_), ("XI", so_), ("XR", si_), ("XI", si_))
                            cur = 0
                            f.op("act", lambda e: e.copy(out=cs_[0][:, 0, 64:192], in_=XRv[:, :, 15]),
                                 reads=[("XR", 15)], writes=[("cs", 0, 0)])
                            f.op("act", lambda e: e.copy(out=cs_[0][:, 1, 64:192], in_=XIv[:, :, 15]),
                                 reads=[("XI", 15)], writes=[("cs", 0, 1)])
                            for k in range(7):
                                sh = 1 << k
                                src = cs_[cur]
                                dst = cs_[1 - cur]
                                cstep(dst[:, 0, 64:192], dst[:, 1, 64:192], src[:, 0, 64 - sh:192 - sh], src[:, 1, 64 - sh:192 - sh],
                                      15 + k, ("cs", 1 - cur, 0), ("cs", 1 - cur, 1), ("cs", cur, 0), ("cs", cur, 1),
                                      bR=src[:, 0, 64:192], bI=src[:, 1, 64:192])
                                cur = 1 - cur
                            fin = cs_[cur]
                            for m_ in range(4):
                                s_ = 4 * m_ + 3
                                cstep(XRv[:, 1:128, s_], XIv[:, 1:128, s_], fin[:, 0, 64:191], fin[:, 1, 64:191], s_,
                                      ("XR", s_), ("XI", s_), ("cs", cur, 0), ("cs", cur, 1))
                            ENDK = [[("XR", 3), ("XR", 7), ("XR", 11)], [("XI", 3), ("XI", 7), ("XI", 11)]]

                            def p3_m(comp, part):
                                X4 = (XR4, XI4)[comp]
                                kx = ("XR", "XI")[comp]
                                cidx = ((0, 2), (1, 0))[comp]
                                E4 = (XR4, XI4)[part]
                                for r_ in range(3):
                                    f.op("dve", lambda e, r_=r_: e.scalar_tensor_tensor(
                                        out=X4[:, :, 1:4, r_], in0=E4[:, :, 0:3, 3],
                                        scalar=PSC(q, r_, cidx[part]), in1=X4[:, :, 1:4, r_],
                                        op0=ALU.mult, op1=ALU.add),
                                        reads=ENDK[part] + ["PR"], writes=[(kx, 4 * m_ + r_) for m_ in range(1, 4)])

                            def p3_0(comp):
                                Xv = (XRv, XIv)[comp]
                                kx = ("XR", "XI")[comp]
                                cidx = ((0, 2), (1, 0))[comp]
                                for part in range(2):
                                    for r_ in range(3):
                                        f.op("dve", lambda e, r_=r_, part=part: e.scalar_tensor_tensor(
                                            out=Xv[:, 1:128, r_], in0=fin[:, part, 64:191],
                                            scalar=PSC(q, r_, cidx[part]), in1=Xv[:, 1:128, r_],
                                            op0=ALU.mult, op1=ALU.add),
                                            reads=[("cs", cur, part), "PR"], writes=[(kx, r_)])
                            p3_m(0, 0)
                            p3_m(0, 1)
                            p3_0(0)
                            p3_m(1, 0)
                            stage_out(qq, 0)
                            if qq < 3:
                                stage_in(qq + 1, 0)
                            p3_m(1, 1)
                            p3_0(1)
                            stage_out(qq, 1)
                            if qq < 3:
                                stage_in(qq + 1, 1)
                        for tg in range(4):
                            sl = slice(tg * 512, (tg + 1) * 512)
                            f.op("dve", lambda e, sl=sl, tg=tg, j=j: e.scalar_tensor_tensor(out=gl[0][:], in0=uT[:, sl],
                                                                                            scalar=dvec[:, j:j + 1], in1=ps[tg][:],
                                                                                            op0=ALU.mult, op1=ALU.add),
                                 reads=[PS[tg], "uT", "dvec"], writes=["gl0"])
                            f.op("pool", lambda e: e.tensor_tensor(out=gl[1][:], in0=gl[0][:], in1=gl[0][:], op=ALU.mult),
                                 reads=["gl0"], writes=["gl1"])
                            f.op("pool", lambda e: e.tensor_scalar(out=gl[1][:], in0=gl[1][:], scalar1=0.044715, scalar2=1.0,
                                                                   op0=ALU.mult, op1=ALU.add), reads=["gl1"], writes=["gl1"])
                            f.op("pool", lambda e: e.tensor_tensor(out=gl[1][:], in0=gl[1][:], in1=gl[0][:], op=ALU.mult),
                                 reads=["gl1", "gl0"], writes=["gl1"])
                            f.op("act", lambda e: e.activation(out=gl[2][:], in_=gl[1][:], func=AF.Sigmoid,
                                                               scale=2.0 * math.sqrt(2.0 / math.pi)),
                                 reads=["gl1"], writes=["gl2"] + XALL[0])
                            f.op("dve", lambda e, sl=sl, j=j: e.tensor_tensor(out=mT[:, 8 + j, sl], in0=gl[0][:], in1=gl[2][:],
                                                                              op=ALU.mult),
                                 reads=["gl0", "gl2"] + XALL[0], writes=[("mT", 8 + j)])
                    for tg in range(4):
                        sl = slice(tg * 512, (tg + 1) * 512)
                        for fo in range(4):
                            wb, wk = ws.get()
                            for jj in range(4):
                                f.op("pe", lambda e, fo=fo, jj=jj, sl=sl: e.matmul(
                                    ps[fo][:], lhsT=wb[:, jj, :], rhs=mT[:, 8 + jj, sl],
                                    start=(jj == 0), stop=(jj == 3)),
                                    reads=[wk, ("mT", 8 + jj)], writes=[PS[fo]], inc=(jj == 3))
                        for fo in range(4):
                            wb, wk = ws.get()
                            proj_fm(wb, wk, 4, tg)
                            f.op("act", lambda e: e.activation(out=gl[0][:], in_=ps[4][:], func=AF.Sigmoid),
                                 reads=[PS[4]], writes=["gl0"])
                            f.op("act", lambda e, fo=fo: e.activation(out=gl[2][:], in_=ps[fo][:], func=AF.Sigmoid,
                                                                      bias=glub[:, fo:fo + 1]),
                                 reads=[PS[fo], "glub"], writes=["gl2"] + XALL[0])
                            f.op("pool", lambda e: e.tensor_tensor(out=gl[1][:], in0=gl[2][:], in1=gl[0][:], op=ALU.mult),
                                 reads=["gl2", "gl0"] + XALL[0], writes=["gl1"])
                            f.op("dve", lambda e: e.tensor_tensor(out=gl[1][:], in0=ps[4][:], in1=gl[1][:], op=ALU.mult),
                                 reads=[PS[4], "gl1"], writes=["gl1"])
                            f.op("dve", lambda e, fo=fo, sl=sl: e.tensor_tensor(out=mT[:, 8 + fo, sl], in0=mT[:, 8 + fo, sl],
                                                                                in1=gl[1][:], op=ALU.mult),
                                 reads=["gl1", ("mT", 8 + fo)] + [PS[k] for k in range(4)], writes=[("mT", 8 + fo)])
                    f.barrier()

                with ExitStack() as os_:
                    wout = os_.enter_context(nc.sbuf_tensor(nk("ewout"), [128, 12, D], BF16))
                    ytmp_box[0] = os_.enter_context(nc.sbuf_tensor(nk("ytmp"), [128, 2, 512], F32))
                    woutk = load_big(wout, "ewout", ewout_d[i], 12)
                    f.dma("sp", gain[:], bcast("post_norm", l * D, D), writes=["gain"])
                    for b in range(NB):
                        outproj_block(b, lambda kc, b=b: mT[:, kc, b * 128:(b + 1) * 128],
                                      [("mT", ch) for ch in range(12)], 12, wout, woutk, l)
                    f.barrier()

        for l in layers:
            if l % 2 == 0:
                even_layer(l)
            else:
                odd_layer(l)

        for b in range(NB):
            f.dma("sp", out_d[b * 128:(b + 1) * 128, :], xres[:, b, :], reads=[("x", b)], key="out")
        nc.sync.wait_ge(f.dsem["out"][0], f.dsem["out"][1])
    return nc


_CACHE = {}


def prep_inputs(inputs):
    w = {k: np.ascontiguousarray(np.asarray(v, dtype=np.float32)) for k, v in inputs.items()}
    perm = swap_perm()
    shared = {k: v for k, v in w.items() if k != "x"}
    shared["even_w_sw"] = np.ascontiguousarray(w["even_w_in"][:, :, 0:2048][:, :, perm])
    shared.update(host_consts())
    return w["x"], shared


def kernel(**inputs):
    x, shared = prep_inputs(inputs)
    if "nc" not in _CACHE:
        _CACHE["nc"] = build()
    nc = _CACHE["nc"]
    in_maps = []
    for c in range(8):
        m = dict(shared)
        m["x"] = np.ascontiguousarray(x[c])
        in_maps.append(m)
    res = run_bass_kernel_spmd(nc, in_maps, core_ids=list(range(8)))
    return np.stack([np.asarray(r["out"], dtype=np.float32) for r in res.results], axis=0)
```

```python
import math
import numpy as np
import concourse.bass as bass
import concourse.mybir as mybir
from concourse.bass_utils import run_bass_kernel_spmd

F32 = mybir.dt.float32
BF16 = mybir.dt.bfloat16
I32 = mybir.dt.int32
ALU = mybir.AluOpType
AF = mybir.ActivationFunctionType
AX = mybir.AxisListType

S = 2048
D = 1024
NB = 16
EPS = 1e-6
TWO_PI = 2.0 * math.pi


class FW:
    def __init__(self, nc):
        self.nc = nc
        self.eng = {"pe": nc.tensor, "act": nc.scalar, "dve": nc.vector,
                    "pool": nc.gpsimd, "sp": nc.sync}
        self.sem = {}
        self.cnt = {}
        for e in self.eng:
            self.sem[e] = nc.alloc_semaphore("s_" + e)
            self.cnt[e] = 0
        self.waited = {}
        self.lastw = {}
        self.rd = {}
        self.dsem = {}

    def _deps(self, reads, writes):
        deps = {}

        def add(t):
            if t is not None and deps.get(t[0], 0) < t[1]:
                deps[t[0]] = t[1]
        for k in reads:
            add(self.lastw.get(k))
        for k in writes:
            add(self.lastw.get(k))
            for sk, v in self.rd.get(k, {}).items():
                add((sk, v))
        return deps

    def _semof(self, sk):
        if isinstance(sk, tuple):
            return self.dsem[sk[1]][0]
        return self.sem[sk]

    def _emit_waits(self, e, deps):
        for sk, v in deps.items():
            if sk == e and v > self.cnt[e]:
                continue
            if self.waited.get((e, sk), 0) < v:
                self.eng[e].wait_ge(self._semof(sk), v)
                self.waited[(e, sk)] = v

    def _record(self, t, reads, writes):
        for k in writes:
            self.lastw[k] = t
            self.rd[k] = {}
        for k in reads:
            d = self.rd.setdefault(k, {})
            if d.get(t[0], 0) < t[1]:
                d[t[0]] = t[1]

    def op(self, e, fn, reads=(), writes=(), inc=True):
        self._emit_waits(e, self._deps(reads, writes))
        ins = fn(self.eng[e])
        if inc:
            ins.then_inc(self.sem[e], 1)
            self.cnt[e] += 1
            t = (e, self.cnt[e])
        else:
            t = (e, self.cnt[e] + 1)
        self._record(t, reads, writes)
        return ins

    def dma(self, q, out, in_, reads=(), writes=(), key=None, **kw):
        if key is None:
            key = writes[0] if writes else reads[0]
        if key not in self.dsem:
            self.dsem[key] = [self.nc.alloc_semaphore("d%d" % len(self.dsem)), 0]
        self._emit_waits(q, self._deps(reads, writes))
        ins = self.eng[q].dma_start(out=out, in_=in_, **kw)
        ins.then_inc(self.dsem[key][0], 16)
        self.dsem[key][1] += 16
        t = (("dma", key), self.dsem[key][1])
        self._record(t, reads, writes)
        return ins

    def barrier(self):
        for e in self.eng:
            deps = {}
            for o in self.eng:
                if o != e and self.cnt[o] > 0:
                    deps[o] = self.cnt[o]
            for key, (s, c) in self.dsem.items():
                if c > 0:
                    deps[("dma", key)] = c
            self._emit_waits(e, deps)


def _mult(d):
    d = np.asarray(d)
    m = ((d >= 0) & (d <= 128)).astype(np.float32)
    m += ((d >= 0) & (d <= 512) & (d % 4 == 0)).astype(np.float32)
    m += ((d >= 0) & (d % 16 == 0)).astype(np.float32)
    return m


def host_consts():
    c = {}
    c["c_ident"] = np.eye(128, dtype=np.float32)
    j = np.arange(128)[:, None]
    t = np.arange(128)[None, :]
    c["c_mask"] = np.concatenate([_mult(128 * dl + t - j) for dl in range(-3, 16)], axis=1).astype(np.float32)
    bands = np.zeros((128, 4, 3, 128), np.float32)
    tp = np.arange(128)[:, None]
    tt = np.arange(128)[None, :]
    for wi, w in enumerate((2, 4, 8, 16)):
        dcur = tt - tp
        bands[:, wi, 0, :] = ((dcur >= 0) & (dcur < w)) / float(w) - (dcur == 0)
        dprev = tt + 128 - tp
        bands[:, wi, 1, :] = ((dprev >= 0) & (dprev < w)) / float(w)
        cnt = np.minimum(tt + 1, w).astype(np.float32)
        bands[:, wi, 2, :] = ((dcur >= 0) & (dcur < w)) / cnt - (dcur == 0)
    c["c_bands"] = bands.reshape(128, 4 * 3 * 128)
    half = 8
    inv = 500000.0 ** (-np.arange(0, 16, 2, dtype=np.float32) / 16.0)
    ang = np.arange(S, dtype=np.float32)[None, :] * inv[:, None]
    C = np.ones((64, S), np.float32)
    Sg = np.zeros((64, S), np.float32)
    C[0:8] = np.cos(ang)
    C[8:16] = np.cos(ang)
    Sg[0:8] = -np.sin(ang)
    Sg[8:16] = np.sin(ang)
    c["c_ropec"] = np.concatenate([C, C], 0)
    c["c_ropes"] = np.concatenate([Sg, Sg], 0)
    gm = np.zeros((128, 8), np.float32)
    for g in range(8):
        gm[g * 16:(g + 1) * 16, g] = 1.0
    c["c_gmask"] = gm
    return c


def swap_perm():
    perm = np.arange(2048)
    for blk in range(2048 // 64):
        b = blk * 64
        perm[b:b + 8] = np.arange(b + 8, b + 16)
        perm[b + 8:b + 16] = np.arange(b, b + 8)
    return perm


def build(layers=(0, 1, 2, 3)):
    nc = bass.Bass("TRN2", target_bir_lowering=False)
    dt_ = {}

    def din(name, shape):
        h = nc.dram_tensor(name, list(shape), F32, kind="ExternalInput")
        dt_[name] = h
        return h.ap()

    x_d = din("x", [S, D])
    pre_d = din("pre_norm", [4, D])
    post_d = din("post_norm", [4, D])
    ewin_d = din("even_w_in", [2, D, 5120])
    ewsw_d = din("even_w_sw", [2, D, 2048])
    ewout_d = din("even_w_out", [2, 1536, D])
    are_d = din("ssm_a_re", [2, 32, 64])
    aim_d = din("ssm_a_im", [2, 32, 64])
    ldt_d = din("ssm_log_dt", [2, 32])
    bre_d = din("ssm_b_re", [2, 32, 64, 16])
    bim_d = din("ssm_b_im", [2, 32, 64, 16])
    cre_d = din("ssm_c_re", [2, 32, 16, 64])
    cim_d = din("ssm_c_im", [2, 32, 16, 64])
    sd_d = din("ssm_d", [2, 512])
    glw_d = din("ssm_glu_w", [2, 512, 512])
    glb_d = din("ssm_glu_b", [2, 512])
    owin_d = din("odd_w_in", [2, D, 4096])
    pw_d = din("pool_w", [2, 4, 512, 512])
    psc_d = din("pool_scale", [2, 2048])
    owout_d = din("odd_w_out", [2, 2048, D])
    cid_d = din("c_ident", [128, 128])
    cmask_d = din("c_mask", [128, 19 * 128])
    cband_d = din("c_bands", [128, 12 * 128])
    cropec_d = din("c_ropec", [128, S])
    cropes_d = din("c_ropes", [128, S])
    cgm_d = din("c_gmask", [128, 8])
    out_d = nc.dram_tensor("out", [S, D], F32, kind="ExternalOutput").ap()

    f = FW(nc)
    uid = [0]

    def nk(p):
        uid[0] += 1
        return "%s%d" % (p, uid[0])

    def bcast(name, off, n):
        return bass.AP(dt_[name], off, [[0, 128], [1, n]])

    from contextlib import ExitStack
    with ExitStack() as es:
        def sb(name, shape, dtype):
            return es.enter_context(nc.sbuf_tensor(name, list(shape), dtype))

        def psb(name, shape, dtype):
            return es.enter_context(nc.psum_tensor(name, list(shape), dtype))

        xres = sb("xres", [128, NB, D], F32)
        ps = [psb("ps%d" % i, [128, 512], F32) for i in range(8)]
        PS = ["ps%d" % i for i in range(8)]
        ident = sb("ident", [128, 128], BF16)
        identf = sb("identf", [128, 128], F32)
        gain = sb("gain", [128, D], F32)
        junk = sb("junk", [128, D], BF16)
        stat = sb("stat", [128, 64], F32)
        gmask = sb("gmask", [128, 8], F32)
        wbf = [sb("wbf%d" % i, [128, 8, 128], BF16) for i in range(6)]
        wctr = [0, 0]

        f.dma("sp", identf[:], cid_d, writes=["identf"])
        f.op("dve", lambda e: e.tensor_copy(out=ident[:], in_=identf[:]), reads=["identf"], writes=["ident"])
        f.dma("sp", gmask[:], cgm_d, writes=["gmask"])
        for b in range(NB):
            f.dma("sp", xres[:, b, :], x_d[b * 128:(b + 1) * 128, :], writes=[("x", b)])

        cast_rr = [0]

        def cast(out, in_, reads, writes):
            e = "act"
            if e == "pool":
                f.op("pool", lambda g: g.tensor_copy(out=out, in_=in_), reads=reads, writes=writes)
            else:
                f.op("act", lambda g: g.copy(out=out, in_=in_), reads=reads, writes=writes)

        class WS:
            DEPTH = 2

            def __init__(self, srcs):
                self.srcs = srcs
                self.issued = 0
                self.cur = 0
                self.buf = {}

            def prefetch(self, n):
                self._issue_to(self.cur + n)

            def get(self, issue=True):
                if issue:
                    self._issue_to(self.cur + 1 + WS.DEPTH)
                assert self.cur < self.issued
                bi = self.buf[self.cur]
                self.cur += 1
                return wbf[bi], "wbf%d" % bi

            def _issue_to(self, lim):
                while self.issued < min(len(self.srcs), lim):
                    src, kc = self.srcs[self.issued]
                    bi = wctr[1] % 6
                    wctr[1] += 1
                    f.dma("pool", wbf[bi][:, 0:kc, :], src.rearrange("(k p) n -> p k n", p=128),
                          writes=["wbf%d" % bi])
                    self.buf[self.issued] = bi
                    self.issued += 1

        def load_big(dst, dkey, src_ap, kc, step=4):
            keys = []
            for k0 in range(0, kc, step):
                k1 = min(kc, k0 + step)
                f.dma("pool", dst[:, k0:k1, :], src_ap[k0 * 128:k1 * 128, :].rearrange("(k p) n -> p k n", p=128),
                      writes=[(dkey, k0)])
                keys.append((dkey, k0))
            return keys

        def load_big_hw(dst, dkey, src_ap, kc, stg, skey):
            keys = []
            for k0 in range(0, kc, 2):
                f.dma("sp", stg[:, :, :], src_ap[k0 * 128:(k0 + 2) * 128, :].rearrange("(k p) n -> p k n", p=128),
                      writes=[skey])
                f.op("act", lambda g: g.copy(out=dst[:, k0:k0 + 2, :], in_=stg[:, :, :]), reads=[skey], writes=[(dkey, k0)])
                keys.append((dkey, k0))
            return keys

        scol = [0]

        def newcol():
            scol[0] = (scol[0] + 1) % 64
            return scol[0]

        def rstd_from_ss(c_ss):
            c1 = newcol()
            c2 = newcol()
            f.op("act", lambda e: e.activation(out=stat[:, c1:c1 + 1], in_=stat[:, c_ss:c_ss + 1], func=AF.Sqrt,
                                               scale=1.0 / D, bias=EPS),
                 reads=[("st", c_ss)], writes=[("st", c1)])
            f.op("dve", lambda e: e.reciprocal(out=stat[:, c2:c2 + 1], in_=stat[:, c1:c1 + 1]),
                 reads=[("st", c1)], writes=[("st", c2)])
            return c2

        def prenorm_block(b, dst, dkey):
            c = newcol()
            f.op("act", lambda e: e.activation(out=junk[:], in_=xres[:, b, :], func=AF.Square,
                                               accum_out=stat[:, c:c + 1]),
                 reads=[("x", b)], writes=["junk", ("st", c)])
            c2 = rstd_from_ss(c)
            f.op("dve", lambda e: e.scalar_tensor_tensor(out=dst, in0=xres[:, b, :], scalar=stat[:, c2:c2 + 1],
                                                         in1=gain[:], op0=ALU.mult, op1=ALU.mult),
                 reads=[("x", b), ("st", c2), "gain"], writes=[dkey])

        def transpose_block(src, skey, dst_ap, dkey, pi):
            for hf in range(2):
                bk = 4 + 2 * pi + hf
                for cc in range(4):
                    c = hf * 4 + cc
                    f.op("pe", lambda e, c=c, cc=cc, bk=bk: e.matmul(ps[bk][:, cc * 128:(cc + 1) * 128],
                                                                     lhsT=src[:, c * 128:(c + 1) * 128], rhs=ident[:],
                                                                     start=True, stop=True),
                         reads=[skey, "ident"], writes=[PS[bk]], inc=(cc == 3))
                f.op("act", lambda e, hf=hf, bk=bk: e.copy(out=dst_ap[:, hf * 4:(hf + 1) * 4, :],
                                                           in_=ps[bk][:].rearrange("p (c t) -> p c t", c=4)),
                     reads=[PS[bk]], writes=[dkey])

        def outproj_block(b, mT_fn, mkeys, KC, wout, wkey, l):
            pp = (b % 2) * 2
            for fh in range(2):
                for kc in range(KC):
                    f.op("pe", lambda e, kc=kc, fh=fh: e.matmul(ps[pp + fh][:], lhsT=mT_fn(kc),
                                                                rhs=wout[:, kc, fh * 512:(fh + 1) * 512],
                                                                start=(kc == 0), stop=(kc == KC - 1)),
                         reads=list(mkeys) + list(wkey), writes=[PS[pp + fh]], inc=(kc == KC - 1))
            ca = newcol()
            cb = newcol()
            f.op("act", lambda e: e.activation(out=junk[:, 0:512], in_=ps[pp][:], func=AF.Square,
                                               accum_out=stat[:, ca:ca + 1]),
                 reads=[PS[pp]], writes=["junk", ("st", ca)])
            f.op("act", lambda e: e.activation(out=junk[:, 512:1024], in_=ps[pp + 1][:], func=AF.Square,
                                               accum_out=stat[:, cb:cb + 1]),
                 reads=[PS[pp + 1]], writes=["junk", ("st", cb)])
            cs = newcol()
            f.op("dve", lambda e: e.tensor_tensor(out=stat[:, cs:cs + 1], in0=stat[:, ca:ca + 1],
                                                  in1=stat[:, cb:cb + 1], op=ALU.add),
                 reads=[("st", ca), ("st", cb)], writes=[("st", cs)])
            c2 = rstd_from_ss(cs)
            for fh in range(2):
                tk = "ytmp%d" % fh
                f.op("dve", lambda e, fh=fh: e.scalar_tensor_tensor(out=ytmp_box[0][:, fh, :], in0=ps[pp + fh][:],
                                                                    scalar=stat[:, c2:c2 + 1],
                                                                    in1=gain[:, fh * 512:(fh + 1) * 512],
                                                                    op0=ALU.mult, op1=ALU.mult),
                     reads=[PS[pp + fh], ("st", c2), "gain"], writes=[tk])
                f.op("pool", lambda e, fh=fh: e.tensor_tensor(out=xres[:, b, fh * 512:(fh + 1) * 512],
                                                              in0=xres[:, b, fh * 512:(fh + 1) * 512],
                                                              in1=ytmp_box[0][:, fh, :], op=ALU.add),
                     reads=[("x", b), tk], writes=[("x", b)])

        ytmp_box = [None]

        def odd_layer(l):
            i = l // 2
            with ExitStack() as ls:
                def lsb(name, shape, dtype):
                    return ls.enter_context(nc.sbuf_tensor(nk(name), list(shape), dtype))
                hN = lsb("hN", [128, 5, D], BF16)
                hT = lsb("hTq", [128, 8, 512], BF16)
                phT = lsb("phT", [128, 8, 512], BF16)
                mixed = lsb("mixed", [128, 4, 512], BF16)
                mT = lsb("mTq", [128, 16, 512], BF16)
                sgb = [lsb("sg0", [128, 512], F32)] * 2
                bandf = lsb("bandf", [128, 12, 128], F32)
                band = lsb("band", [128, 4, 4, 128], BF16)
                btmp = lsb("btmp", [128, 4, 128], F32)
                pwbf = lsb("pwbf", [128, 4, 512], BF16)
                wout = lsb("wout", [128, 16, D], BF16)
                pscale = lsb("pscale", [128, 16], F32)
                wost = [lsb("wost%d" % k_, [128, 2, D], F32) for k_ in range(2)]
                pwst = lsb("pwst", [128, 4, 512], F32)
                ytmp_box[0] = lsb("ytmp", [128, 2, 512], F32)
                f.dma("sp", bandf[:], cband_d.rearrange("p (a t) -> p a t", t=128), writes=["bandf"])
                bv = bandf[:].rearrange("p (w k) t -> p w k t", k=3)
                f.op("dve", lambda e: e.tensor_copy(out=band[:, :, 0:3, :], in_=bv), reads=["bandf"], writes=["band"])
                f.op("dve", lambda e: e.tensor_copy(out=btmp[:], in_=band[:, :, 2, :]), reads=["band"], writes=["btmp"])
                f.op("dve", lambda e: e.tensor_tensor(out=btmp[:], in0=bv[:, :, 2, :], in1=btmp[:], op=ALU.subtract),
                     reads=["bandf", "btmp"], writes=["btmp"])
                f.op("dve", lambda e: e.tensor_copy(out=band[:, :, 3, :], in_=btmp[:]), reads=["btmp"], writes=["band"])
                f.dma("sp", pscale[:], psc_d[i].rearrange("(c p) -> p c", p=128), writes=["pscale"],
                      allow_slow_non_contiguous=True)
                srcs = []
                for tq in range(4):
                    for g in range(4):
                        for j in range(4):
                            srcs.append((owin_d[i][:, g * 512 + j * 128:g * 512 + (j + 1) * 128], 8))
                        for dj in range(4):
                            col = 2048 + g * 512 + dj * 128
                            srcs.append((owin_d[i][:, col:col + 128], 8))
                ws = WS(srcs)
                for tq in range(4):
                    b0 = tq * 4
                    woutk = [("wout", 2 * k_) for k_ in range(8)]
                    wstep = [0]

                    def wout_step():
                        k_ = wstep[0]
                        wstep[0] += 1
                        if 1 <= k_ <= 8:
                            kk = k_ - 1
                            f.op("act", lambda g_: g_.copy(out=wout[:, 2 * kk:2 * kk + 2, :], in_=wost[kk % 2][:]),
                                 reads=["wost%d" % (kk % 2)], writes=[("wout", 2 * kk)])
                        if k_ < 8:
                            f.dma("sp", wost[k_ % 2][:],
                                  owout_d[i][k_ * 256:(k_ + 1) * 256, :].rearrange("(k p) n -> p k n", p=128),
                                  writes=["wost%d" % (k_ % 2)])
                    wout_step()
                    f.dma("sp", gain[:], bcast("pre_norm", l * D, D), writes=["gain"])
                    if tq > 0:
                        f.op("pool", lambda e: e.tensor_copy(out=hN[:, 0, :], in_=hN[:, 4, :]),
                             reads=[("hN", 4)], writes=[("hN", 0)])
                    for s_, b in enumerate(range(b0 - 1, b0 + 4)):
                        if s_ == 0:
                            continue
                        prenorm_block(b, hN[:, s_, :], ("hN", s_))
                    for s_ in range(1, 5):
                        transpose_block(hN[:, s_, :], ("hN", s_), hT[:, :, (s_ - 1) * 128:s_ * 128], "hT", s_ % 2)
                    for g in range(4):
                        f.dma("sp", pwst[:], pw_d[i, g].rearrange("(k p) n -> p k n", p=128), writes=["pwst"])
                        pwk = [("pwbf", 0), ("pwbf", 2)]
                        for s_ in range(1, 5):
                            b = b0 + s_ - 1
                            for hf in range(2):
                                pk = 4 + hf
                                for cc in range(4):
                                    c = hf * 4 + cc
                                    o = ps[pk][:, cc * 128:(cc + 1) * 128]
                                    lh = hN[:, s_, c * 128:(c + 1) * 128]
                                    if b == 0:
                                        f.op("pe", lambda e, o=o, lh=lh: e.matmul(o, lhsT=lh, rhs=band[:, g, 2, :],
                                                                                  start=True, stop=False),
                                             reads=[("hN", s_), "band"], writes=[PS[pk]], inc=False)
                                        f.op("pe", lambda e, o=o, lh=lh: e.matmul(o, lhsT=lh, rhs=band[:, g, 3, :],
                                                                                  start=False, stop=True),
                                             reads=[("hN", s_), "band"], writes=[PS[pk]], inc=(cc == 3))
                                    else:
                                        lp = hN[:, s_ - 1, c * 128:(c + 1) * 128]
                                        f.op("pe", lambda e, o=o, lh=lh: e.matmul(o, lhsT=lh, rhs=band[:, g, 0, :],
                                                                                  start=True, stop=False),
                                             reads=[("hN", s_), "band"], writes=[PS[pk]], inc=False)
                                        f.op("pe", lambda e, o=o, lp=lp: e.matmul(o, lhsT=lp, rhs=band[:, g, 1, :],
                                                                                  start=False, stop=True),
                                             reads=[("hN", s_ - 1), "band"], writes=[PS[pk]], inc=(cc == 3))
                                f.op("act", lambda e, hf=hf, pk=pk: e.copy(
                                    out=phT[:, hf * 4:(hf + 1) * 4, (s_ - 1) * 128:s_ * 128],
                                    in_=ps[pk][:].rearrange("p (c t) -> p c t", c=4)),
                                    reads=[PS[pk]], writes=["phT"])
                        for j in range(4):
                            wout_step()
                            wb, wk = ws.get()
                            pk = j % 2
                            for kc in range(8):
                                f.op("pe", lambda e, kc=kc: e.matmul(ps[pk][:], lhsT=wb[:, kc, :], rhs=phT[:, kc, :],
                                                                     start=(kc == 0), stop=(kc == 7)),
                                     reads=[wk, "phT"], writes=[PS[pk]], inc=(kc == 7))
                            f.op("act", lambda e, j=j, pk=pk: e.copy(out=mixed[:, j, :], in_=ps[pk][:]),
                                 reads=[PS[pk]], writes=[("mixed", j)])
                        for k0_ in (0, 2):
                            f.op("act", lambda g_, k0_=k0_: g_.copy(out=pwbf[:, k0_:k0_ + 2, :], in_=pwst[:, k0_:k0_ + 2, :]),
                                 reads=["pwst"], writes=[("pwbf", k0_)])
                        for dj in range(4):
                            wb, wk = ws.get()
                            pg = 2 if dj % 2 == 0 else 6
                            sg = sgb[dj % 2]
                            sgk = "sg0"
                            for kc in range(8):
                                f.op("pe", lambda e, kc=kc: e.matmul(ps[pg][:], lhsT=wb[:, kc, :], rhs=hT[:, kc, :],
                                                                     start=(kc == 0), stop=(kc == 7)),
                                     reads=[wk, "hT"], writes=[PS[pg]], inc=(kc == 7))
                            f.op("act", lambda e: e.activation(out=sg[:], in_=ps[pg][:], func=AF.Silu),
                                 reads=[PS[pg]], writes=[sgk])
                            py = 3 if dj % 2 == 0 else 7
                            for j in range(4):
                                f.op("pe", lambda e, j=j, dj=dj: e.matmul(ps[py][:], lhsT=pwbf[:, j, dj * 128:(dj + 1) * 128],
                                                                          rhs=mixed[:, j, :], start=(j == 0), stop=(j == 3)),
                                     reads=pwk + [("mixed", j)], writes=[PS[py]], inc=(j == 3))
                            ch = g * 4 + dj
                            f.op("dve", lambda e, ch=ch: e.scalar_tensor_tensor(out=mT[:, ch, :], in0=ps[py][:],
                                                                                scalar=pscale[:, ch:ch + 1], in1=sg[:],
                                                                                op0=ALU.mult, op1=ALU.mult),
                                 reads=[PS[py], "pscale", sgk], writes=[("mT", ch)])
                    f.dma("sp", gain[:], bcast("post_norm", l * D, D), writes=["gain"])
                    for bb in range(4):
                        outproj_block(b0 + bb, lambda kc, bb=bb: mT[:, kc, bb * 128:(bb + 1) * 128],
                                      [("mT", ch) for ch in range(16)], 16, wout, woutk, l)
                f.barrier()

        def even_layer(l):
            i = l // 2
            with ExitStack() as ls:
                def lsb(name, shape, dtype):
                    return ls.enter_context(nc.sbuf_tensor(nk(name), list(shape), dtype))
                hT = lsb("hT", [128, 8, S], BF16)
                mT = lsb("mT", [128, 12, S], BF16)
                f.dma("sp", gain[:], bcast("pre_norm", l * D, D), writes=["gain"])
                with ExitStack() as hs_:
                    hNb = [hs_.enter_context(nc.sbuf_tensor(nk("hNb"), [128, D], BF16)) for k in range(2)]
                    for b in range(NB):
                        prenorm_block(b, hNb[b % 2][:], "hNb%d" % (b % 2))
                        transpose_block(hNb[b % 2], "hNb%d" % (b % 2), hT[:, :, b * 128:(b + 1) * 128], "hT", b % 2)
                    f.barrier()

                def proj_fm(wb, wk, pk, tg):
                    for kc in range(8):
                        f.op("pe", lambda e, kc=kc: e.matmul(ps[pk][:], lhsT=wb[:, kc, :],
                                                             rhs=hT[:, kc, tg * 512:(tg + 1) * 512],
                                                             start=(kc == 0), stop=(kc == 7)),
                             reads=[wk, "hT"], writes=[PS[pk]], inc=(kc == 7))

                with ExitStack() as as_:
                    def asb(name, shape, dtype):
                        return as_.enter_context(nc.sbuf_tensor(nk(name), list(shape), dtype))
                    qT = asb("qT", [128, S], BF16)
                    kT = asb("kT", [128, 2, S], BF16)
                    V = asb("V", [128, NB, 2, 128], BF16)
                    ropec = asb("ropec", [128, S], BF16)
                    ropes = asb("ropes", [128, S], BF16)
                    maskT = asb("maskT", [128, 19 * 128], BF16)
                    Pt = [asb("Pt%d" % k, [128, 512], BF16) for k in range(4)]
                    rtmp = [asb("rtmp%d" % k, [128, 512], F32) for k in range(2)]
                    rc = rtmp[0]
                    atmp = rtmp[1]
                    f.dma("pool", ropec[:], cropec_d, writes=["ropec"])
                    f.dma("pool", ropes[:], cropes_d, writes=["ropes"])
                    f.dma("pool", maskT[:], cmask_d, writes=["maskT"])
                    srcs = []
                    for hp_ in range(8):
                        c0_ = hp_ * 128
                        for (base_, swb_) in ((0, 0), (1024, 1024)):
                            srcs.append((ewin_d[i][:, base_ + c0_:base_ + c0_ + 128], 8))
                            srcs.append((ewsw_d[i][:, swb_ + c0_:swb_ + c0_ + 128], 8))
                        srcs.append((ewin_d[i][:, 3072 + c0_:3072 + c0_ + 128], 8))
                        srcs.append((ewin_d[i][:, 2048 + c0_:2048 + c0_ + 128], 8))
                    ws = WS(srcs)
                    f.op("pool", lambda e: e.memset(V[:, :, :, 64:128], 1.0), writes=["V"])
                    f.op("pool", lambda e: e.memset(kT[:], 0.0), writes=[("kT", 0)])
                    mrr = [0]
                    qTb = [qT, mT[:, 8, :]]
                    kTb = [kT, mT[:, 9:11, :]]
                    f.op("pool", lambda e: e.memset(mT[:, 9:11, :], 0.0), writes=[("mT", 9), ("mT", 10), ("kT", 1)])
                    QK = [["qT", ("mT", 8)], [("kT", 0), ("kT", 1), ("mT", 9), ("mT", 10)]]

                    def proj_gen(hp_, issue):
                        par = hp_ % 2
                        qd = qTb[par]
                        kd = kTb[par]
                        qk_ = [QK[0][par]]
                        kk_ = [("kT", par)] + ([("mT", 9), ("mT", 10)] if par == 1 else [])
                        for which in ("q", "k"):
                            wb, wk = ws.get(issue)
                            wb2, wk2 = ws.get(issue)
                            for tg in range(4):
                                sl = slice(tg * 512, (tg + 1) * 512)
                                proj_fm(wb, wk, 6, tg)
                                yield
                                proj_fm(wb2, wk2, 7, tg)
                                r0 = "rtmp0"
                                r1 = "rtmp1"
                                f.op("dve", lambda e, sl=sl: e.tensor_tensor(out=rtmp[0][:], in0=ps[6][:], in1=ropec[:, sl],
                                                                             op=ALU.mult),
                                     reads=[PS[6], "ropec"], writes=[r0])
                                f.op("dve", lambda e, sl=sl: e.tensor_tensor(out=rtmp[1][:], in0=ps[7][:], in1=ropes[:, sl],
                                                                             op=ALU.mult),
                                     reads=[PS[7], "ropes"], writes=[r1])
                                if which == "q":
                                    f.op("dve", lambda e, sl=sl: e.tensor_tensor(out=qd[:, sl], in0=rtmp[0][:],
                                                                                 in1=rtmp[1][:], op=ALU.add),
                                         reads=[r0, r1], writes=qk_)
                                else:
                                    for a_ in range(2):
                                        pr_ = slice(64 * a_, 64 * a_ + 64)
                                        f.op("dve", lambda e, sl=sl, a_=a_, pr_=pr_: e.tensor_tensor(
                                            out=kd[pr_, a_, sl], in0=rtmp[0][pr_, :], in1=rtmp[1][pr_, :], op=ALU.add),
                                            reads=[r0, r1], writes=kk_)
                                yield
                        wb, wk = ws.get(issue)
                        for tg in range(4):
                            pg_ = 6 + (tg % 2)
                            proj_fm(wb, wk, pg_, tg)
                            f.op("act", lambda e, tg=tg, pg_=pg_: e.activation(out=mT[:, hp_, tg * 512:(tg + 1) * 512],
                                                                               in_=ps[pg_][:], func=AF.Silu),
                                 reads=[PS[pg_]], writes=[("mT", hp_)])
                        yield

                    gen = proj_gen(0, True)
                    for _ in gen:
                        pass
                    gen = None
                    for hp in range(8):
                        par = hp % 2
                        qT_ = qTb[par]
                        kT_ = kTb[par]
                        qkeys = [QK[0][par]]
                        kkeys = [("kT", par)] + ([("mT", 9), ("mT", 10)] if par == 1 else [])
                        if gen is not None:
                            for _ in gen:
                                pass
                        wb, wk = ws.get(hp == 0)
                        for b4 in range(4):
                            pk = 6 + (b4 % 2)
                            for bb in range(4):
                                b = b4 * 4 + bb
                                for kc in range(8):
                                    f.op("pe", lambda e, kc=kc, b=b, bb=bb: e.matmul(
                                        ps[pk][:, bb * 128:(bb + 1) * 128], lhsT=hT[:, kc, b * 128:(b + 1) * 128],
                                        rhs=wb[:, kc, :], start=(kc == 0), stop=(kc == 7)),
                                        reads=[wk, "hT"], writes=[PS[pk]], inc=(kc == 7 and bb == 3))
                            f.op("act", lambda e, b4=b4: e.copy(
                                out=V[:, b4 * 4:(b4 + 1) * 4, :, 0:64],
                                in_=ps[pk][:].rearrange("p (b a d) -> p b a d", b=4, a=2)),
                                reads=[PS[pk]], writes=["V"])
                        ws.prefetch(6)
                        gen = proj_gen(hp + 1, False) if hp < 7 else None
                        items = []
                        for a in range(2):
                            for qg in range(4):
                                nkb = 4 * qg + 4
                                for kb in range(nkb):
                                    items.append((a, qg, kb, nkb))
                        LAG = 2

                        def stage1(idx):
                            a, qg, kb, nkb = items[idx]
                            pr = slice(64 * a, 64 * a + 64)
                            cq = max(128 * kb, 512 * qg)
                            N = 512 * (qg + 1) - cq
                            sk = idx % 4
                            f.op("pe", lambda e: e.matmul(
                                ps[sk][:, 0:N], lhsT=kT_[:, a, kb * 128:(kb + 1) * 128], rhs=qT_[:, cq:cq + N],
                                start=True, stop=True),
                                reads=kkeys + qkeys, writes=[PS[sk]])
                            f.op("act", lambda e: e.activation(out=Pt[sk][:, 0:N], in_=ps[sk][:, 0:N],
                                                               func=AF.Exp, scale=0.125),
                                 reads=[PS[sk]], writes=["Pt%d" % sk])
                            moff = ((cq - 128 * kb) // 128 + 3) * 128
                            me = "dve"
                            mrr[0] += 1
                            f.op(me, lambda e: e.tensor_tensor(
                                out=Pt[sk][:, 0:N], in0=Pt[sk][:, 0:N], in1=maskT[:, moff:moff + N], op=ALU.mult),
                                reads=["Pt%d" % sk, "maskT"], writes=["Pt%d" % sk])

                        def stage2(idx):
                            a, qg, kb, nkb = items[idx]
                            pr = slice(64 * a, 64 * a + 64)
                            cq = max(128 * kb, 512 * qg)
                            N = 512 * (qg + 1) - cq
                            sk = idx % 4
                            po = 4 + (qg % 2)
                            oc = cq - 512 * qg
                            f.op("pe", lambda e: e.matmul(
                                ps[po][:, oc:oc + N], lhsT=V[:, kb, a, :], rhs=Pt[sk][:, 0:N],
                                start=(kb == 0), stop=(kb == nkb - 1)),
                                reads=["V", "Pt%d" % sk], writes=[PS[po]])
                            if kb == nkb - 1:
                                qs = slice(qg * 512, (qg + 1) * 512)
                                f.op("act", lambda e: e.activation(out=rc[64:128, :], in_=ps[po][64:128, :], func=AF.Ln),
                                     reads=[PS[po]], writes=["rtmp0"])
                                f.op("act", lambda e: e.activation(out=rc[64:128, :], in_=rc[64:128, :], func=AF.Exp,
                                                                   scale=-1.0),
                                     reads=["rtmp0"], writes=["rtmp0"])
                                f.op("dve", lambda e: e.tensor_tensor(out=atmp[pr, :], in0=ps[po][0:64, :],
                                                                      in1=rc[64:128, :], op=ALU.mult),
                                     reads=[PS[po], "rtmp0"], writes=["rtmp1"])
                                f.op("pool", lambda e: e.tensor_tensor(out=mT[pr, hp, qs], in0=atmp[pr, :],
                                                                       in1=mT[pr, hp, qs], op=ALU.mult),
                                     reads=["rtmp1", ("mT", hp)], writes=[("mT", hp)])

                        for idx in range(len(items) + LAG):
                            if idx < len(items):
                                stage1(idx)
                            if idx - LAG >= 0:
                                stage2(idx - LAG)
                            if gen is not None and idx % 4 == 3:
                                if next(gen, "done") == "done":
                                    gen = None
                    f.barrier()

                with ExitStack() as ss_:
                    def ssb(name, shape, dtype):
                        return ss_.enter_context(nc.sbuf_tensor(nk(name), list(shape), dtype))
                    NPW = 17 + 7
                    PR = ssb("PR", [128, 16, 3 * 24 + 4], F32)
                    pa = ssb("pa", [128, 16, 12], F32)
                    pi32 = ssb("pi32", [128, 16], I32)
                    XR = ssb("XR", [128, S], F32)
                    XI = ssb("XI", [128, S], F32)
                    cs_ = [ssb("cs%d" % k, [128, 2, 192], F32) for k in range(2)]
                    Xb = [ssb("Xb%d" % k, [128, 512], BF16) for k in range(2)]
                    uT = ssb("uT", [128, S], BF16)
                    bnat = ssb("bnat", [128, 2, 16, 16], F32)
                    cnat = ssb("cnat", [128, 2, 64], F32)
                    padT = ssb("padT", [128, 128], BF16)
                    padTf = ssb("padTf", [128, 2, 128], F32)
                    Bpad = ssb("Bpad", [128, 4, 2, 128], BF16)
                    Cpad = ssb("Cpad", [128, 4, 2, 128], BF16)
                    cf = ssb("cf", [128, 4, 128], F32)
                    dvec = ssb("dvec", [128, 4], F32)
                    glub = ssb("glub", [128, 4], F32)
                    gl = [ssb("gl%d" % k, [128, 512], F32) for k in range(2)] + [XR[:, 0:512]]

                    for k_ in range(2):
                        f.op("pool", lambda e, k_=k_: e.memset(cs_[k_][:], 0.0), writes=[("cs", k_, 0), ("cs", k_, 1)])
                    def ld_gp(dst_col, name):
                        for e_ in range(2):
                            f.dma("sp", pa[e_ * 64:(e_ + 1) * 64, :, dst_col],
                                  bass.AP(dt_[name], i * 2048 + e_ * 64, [[1, 64], [128, 16]]),
                                  writes=["pa"], allow_slow_non_contiguous=True)
                    ld_gp(0, "ssm_a_re")
                    ld_gp(1, "ssm_a_im")
                    for e_ in range(2):
                        f.dma("sp", pa[e_ * 64:(e_ + 1) * 64, :, 2],
                              bass.AP(dt_["ssm_log_dt"], i * 32 + e_, [[0, 64], [2, 16]]),
                              writes=["pa"], allow_slow_non_contiguous=True)
                    f.dma("sp", dvec[:], sd_d[i].rearrange("(c p) -> p c", p=128), writes=["dvec"],
                          allow_slow_non_contiguous=True)
                    f.dma("sp", glub[:], glb_d[i].rearrange("(c p) -> p c", p=128), writes=["glub"],
                          allow_slow_non_contiguous=True)

                    def pop(eng, fn, w=("pa",)):
                        f.op(eng, fn, reads=["pa", "PR"], writes=list(w))
                    A = lambda c: pa[:, :, c]
                    pop("act", lambda e: e.activation(out=A(3), in_=A(2), func=AF.Exp))
                    pop("dve", lambda e: e.tensor_tensor(out=A(4), in0=A(0), in1=A(3), op=ALU.mult))
                    pop("dve", lambda e: e.tensor_tensor(out=A(5), in0=A(1), in1=A(3), op=ALU.mult))
                    pop("act", lambda e: e.activation(out=A(4), in_=A(4), func=AF.Exp))

                    def sin_of(dst, shift):
                        pop("dve", lambda e: e.tensor_scalar(out=A(6), in0=A(5), scalar1=shift, scalar2=1.0 / TWO_PI,
                                                             op0=ALU.add, op1=ALU.mult))
                        f.op("dve", lambda e: e.tensor_copy(out=pi32[:], in_=A(6)), reads=["pa"], writes=["pi32"])
                        f.op("dve", lambda e: e.tensor_copy(out=A(7), in_=pi32[:]), reads=["pi32"], writes=["pa"])
                        pop("dve", lambda e: e.tensor_tensor(out=A(6), in0=A(6), in1=A(7), op=ALU.subtract))
                        pop("dve", lambda e: e.tensor_scalar(out=A(6), in0=A(6), scalar1=TWO_PI, scalar2=math.pi,
                                                             op0=ALU.mult, op1=ALU.min))
                        pop("dve", lambda e: e.tensor_scalar(out=A(6), in0=A(6), scalar1=-math.pi, scalar2=None,
                                                             op0=ALU.max))
                        pop("act", lambda e: e.activation(out=dst, in_=A(6), func=AF.Sin))
                    sin_of(A(8), 0.0)
                    sin_of(A(9), math.pi / 2)
                    P3 = lambda k, c: PR[:, :, 3 * k + c]
                    pop("dve", lambda e: e.tensor_tensor(out=P3(0, 0), in0=A(4), in1=A(9), op=ALU.mult), w=("PR",))
                    pop("dve", lambda e: e.tensor_tensor(out=P3(0, 1), in0=A(4), in1=A(8), op=ALU.mult), w=("PR",))

                    def cmul(dst, a_, b_):
                        pop("dve", lambda e: e.tensor_tensor(out=A(6), in0=P3(a_, 0), in1=P3(b_, 0), op=ALU.mult))
                        pop("dve", lambda e: e.tensor_tensor(out=A(7), in0=P3(a_, 1), in1=P3(b_, 1), op=ALU.mult))
                        pop("dve", lambda e: e.tensor_tensor(out=A(10), in0=P3(a_, 0), in1=P3(b_, 1), op=ALU.mult))
                        pop("dve", lambda e: e.tensor_tensor(out=A(11), in0=P3(a_, 1), in1=P3(b_, 0), op=ALU.mult))
                        pop("dve", lambda e: e.tensor_tensor(out=P3(dst, 0), in0=A(6), in1=A(7), op=ALU.subtract), w=("PR",))
                        pop("dve", lambda e: e.tensor_tensor(out=P3(dst, 1), in0=A(10), in1=A(11), op=ALU.add), w=("PR",))
                    for j in range(1, 16):
                        cmul(j, j - 1, 0)
                    for k in range(16, 16 + 7):
                        cmul(k, k - 1, k - 1)
                    for k in range(23):
                        pop("dve", lambda e, k=k: e.tensor_scalar(out=P3(k, 2), in0=P3(k, 1), scalar1=-1.0, scalar2=None,
                                                                  op0=ALU.mult), w=("PR",))
                    FR = PR[:, :, 72]
                    FI = PR[:, :, 73]
                    pop("dve", lambda e: e.tensor_scalar(out=A(6), in0=P3(0, 0), scalar1=-1.0, scalar2=None, op0=ALU.add))
                    pop("dve", lambda e: e.tensor_tensor(out=A(7), in0=A(0), in1=A(0), op=ALU.mult))
                    pop("dve", lambda e: e.tensor_tensor(out=A(10), in0=A(1), in1=A(1), op=ALU.mult))
                    pop("dve", lambda e: e.tensor_tensor(out=A(7), in0=A(7), in1=A(10), op=ALU.add))
                    pop("dve", lambda e: e.reciprocal(out=A(7), in_=A(7)))
                    pop("dve", lambda e: e.tensor_tensor(out=A(10), in0=A(6), in1=A(0), op=ALU.mult))
                    pop("dve", lambda e: e.tensor_tensor(out=A(11), in0=P3(0, 1), in1=A(1), op=ALU.mult))
                    pop("dve", lambda e: e.tensor_tensor(out=A(10), in0=A(10), in1=A(11), op=ALU.add))
                    pop("dve", lambda e: e.tensor_tensor(out=FR, in0=A(10), in1=A(7), op=ALU.mult), w=("PR",))
                    pop("dve", lambda e: e.tensor_tensor(out=A(10), in0=P3(0, 1), in1=A(0), op=ALU.mult))
                    pop("dve", lambda e: e.tensor_tensor(out=A(11), in0=A(6), in1=A(1), op=ALU.mult))
                    pop("dve", lambda e: e.tensor_tensor(out=A(10), in0=A(10), in1=A(11), op=ALU.subtract))
                    pop("dve", lambda e: e.tensor_tensor(out=FI, in0=A(10), in1=A(7), op=ALU.mult), w=("PR",))
                    for ri, name in enumerate(("ssm_b_re", "ssm_b_im")):
                        for e_ in range(2):
                            f.dma("sp", bnat[e_ * 64:(e_ + 1) * 64, ri, :, :],
                                  bass.AP(dt_[name], i * 32 * 1024 + e_ * 1024,
                                          [[16, 64], [2048, 16], [1, 16]]), writes=["bnat"])

                    XALL = [["X0"] + [("XR", s_) for s_ in range(16)], ["X1"] + [("XI", s_) for s_ in range(16)]]

                    srcs = [(ewin_d[i][:, 4096 + j_ * 128:4096 + (j_ + 1) * 128], 8) for j_ in range(4)]
                    for tg_ in range(4):
                        srcs += [(glw_d[i][:, fo_ * 128:(fo_ + 1) * 128], 4) for fo_ in range(4)]
                        srcs += [(ewin_d[i][:, 4608 + fo_ * 128:4608 + (fo_ + 1) * 128], 8) for fo_ in range(4)]
                    ws = WS(srcs)

                    def PSC(q, k, c):
                        return PR[:, q, 3 * k + c:3 * k + c + 1]

                    for j in range(4):
                        for ri, name in enumerate(("ssm_c_re", "ssm_c_im")):
                            f.dma("sp", cnat[:, ri, :],
                                  bass.AP(dt_[name], i * 32 * 1024 + j * 8 * 1024, [[64, 128], [1, 64]]),
                                  writes=["cnat"])
                        for qq in range(4):
                            q = j * 4 + qq
                            for ri in range(2):
                                f.op("pool", lambda e: e.memset(padT[:], 0.0), writes=["padT"])
                                for e_ in range(2):
                                    co = 16 * (2 * qq + e_)
                                    f.op("dve", lambda e, e_=e_, co=co, ri=ri, q=q: e.tensor_copy(
                                        out=padT[e_ * 64:(e_ + 1) * 64, co:co + 16],
                                        in_=bnat[e_ * 64:(e_ + 1) * 64, ri, q, :]),
                                        reads=["bnat"], writes=["padT"])
                                f.op("pe", lambda e: e.matmul(ps[6][:, 0:128], lhsT=padT[:], rhs=ident[:], start=True, stop=True),
                                     reads=["padT", "ident"], writes=[PS[6]])
                                f.op("act", lambda e, qq=qq, ri=ri: e.copy(out=Bpad[:, qq, ri, :], in_=ps[6][:, 0:128]),
                                     reads=[PS[6]], writes=["Bpad"])
                            for ri in range(2):
                                for e_ in range(2):
                                    g8 = 2 * qq + e_
                                    f.op("dve", lambda e, e_=e_, ri=ri, g8=g8: e.tensor_scalar(
                                        out=padTf[:, ri, e_ * 64:(e_ + 1) * 64], in0=cnat[:, ri, :],
                                        scalar1=gmask[:, g8:g8 + 1], scalar2=None, op0=ALU.mult),
                                        reads=["cnat", "gmask"], writes=["padTf"])
                            for ri in range(2):
                                f.op("pe", lambda e, ri=ri: e.matmul(ps[4][:, ri * 128:(ri + 1) * 128], lhsT=padTf[:, ri, :],
                                                                     rhs=identf[:], start=True, stop=True),
                                     reads=["padTf", "identf"], writes=[PS[4]])
                            fr = PR[:, q, 72:73]
                            fi = PR[:, q, 73:74]
                            f.op("act", lambda e: e.copy(out=cf[:, 0:2, :], in_=ps[4][:, 0:256].rearrange("p (a b) -> p a b", a=2)),
                                 reads=[PS[4]], writes=["cf"])
                            f.op("dve", lambda e, fr=fr: e.tensor_scalar(out=cf[:, 2, :], in0=cf[:, 0, :], scalar1=fr, scalar2=None,
                                                                         op0=ALU.mult), reads=["cf", "PR"], writes=["cf"])
                            f.op("dve", lambda e, fi=fi: e.tensor_scalar(out=cf[:, 3, :], in0=cf[:, 1, :], scalar1=fi, scalar2=None,
                                                                         op0=ALU.mult), reads=["cf", "PR"], writes=["cf"])
                            f.op("dve", lambda e, qq=qq: e.tensor_tensor(out=Cpad[:, qq, 0, :], in0=cf[:, 2, :], in1=cf[:, 3, :],
                                                                         op=ALU.subtract), reads=["cf"], writes=["Cpad"])
                            f.op("dve", lambda e, fi=fi: e.tensor_scalar(out=cf[:, 2, :], in0=cf[:, 0, :], scalar1=fi, scalar2=-1.0,
                                                                         op0=ALU.mult, op1=ALU.mult), reads=["cf", "PR"], writes=["cf"])
                            f.op("dve", lambda e, fr=fr: e.tensor_scalar(out=cf[:, 3, :], in0=cf[:, 1, :], scalar1=fr, scalar2=None,
                                                                         op0=ALU.mult), reads=["cf", "PR"], writes=["cf"])
                            f.op("dve", lambda e, qq=qq: e.tensor_tensor(out=Cpad[:, qq, 1, :], in0=cf[:, 2, :], in1=cf[:, 3, :],
                                                                         op=ALU.subtract), reads=["cf"], writes=["Cpad"])
                        wb, wk = ws.get()
                        for tg in range(4):
                            sl = slice(tg * 512, (tg + 1) * 512)
                            proj_fm(wb, wk, 4, tg)
                            f.op("act", lambda e, sl=sl: e.copy(out=uT[:, sl], in_=ps[4][:]), reads=[PS[4]], writes=["uT"])
                        def stage_in(qq_, ri):
                            X = (XR, XI)[ri]
                            for tg in range(4):
                                sl = slice(tg * 512, (tg + 1) * 512)
                                pk = 4 + (tg % 2)
                                f.op("pe", lambda e, sl=sl, pk=pk: e.matmul(
                                    ps[pk][:], lhsT=Bpad[:, qq_, ri, :], rhs=uT[:, sl], start=True, stop=True),
                                    reads=["Bpad", "uT"], writes=[PS[pk]])
                                f.op("act", lambda e, sl=sl, pk=pk, X=X: e.copy(out=X[:, sl], in_=ps[pk][:]),
                                     reads=[PS[pk]], writes=XALL[ri])

                        def stage_out(qq_, ri):
                            X = (XR, XI)[ri]
                            for tg in range(4):
                                sl = slice(tg * 512, (tg + 1) * 512)
                                bi = (ri * 4 + tg) % 2
                                cast(Xb[bi][:], X[:, sl], XALL[ri], ["Xb%d" % bi])
                                f.op("pe", lambda e, tg=tg, bi=bi: e.matmul(
                                    ps[tg][:], lhsT=Cpad[:, qq_, ri, :], rhs=Xb[bi][:],
                                    start=(qq_ == 0 and ri == 0), stop=(qq_ == 3 and ri == 1)),
                                    reads=["Cpad", "Xb%d" % bi], writes=[PS[tg]])

                        stage_in(0, 0)
                        stage_in(0, 1)
                        for qq in range(4):
                            q = j * 4 + qq
                            XRv = XR[:].rearrange("p (c s) -> p c s", s=16)
                            XIv = XI[:].rearrange("p (c s) -> p c s", s=16)

                            def _kl(k_):
                                return list(k_) if isinstance(k_, list) else [k_]

                            def cstep(oR, oI, iR, iI, k, kOR, kOI, kIR, kII, bR=None, bI=None, kB=()):
                                for (o_, i_, c_, ko, ki, b_) in ((oR, iR, 0, kOR, kIR, bR), (oI, iR, 1, kOI, kIR, bI),
                                                                 (oR, iI, 2, kOR, kII, None), (oI, iI, 0, kOI, kII, None)):
                                    add_ = o_ if b_ is None else b_
                                    f.op("dve", lambda e, o_=o_, i_=i_, c_=c_, add_=add_: e.scalar_tensor_tensor(
                                        out=o_, in0=i_, scalar=PSC(q, k, c_), in1=add_, op0=ALU.mult, op1=ALU.add),
                                        reads=_kl(ki) + ["PR"] + list(kB), writes=_kl(ko))
                            XRc = XR[:].rearrange("p (n r) -> p n r", r=4)
                            XIc = XI[:].rearrange("p (n r) -> p n r", r=4)
                            XR4 = XR[:].rearrange("p (c m r) -> p c m r", m=4, r=4)
                            XI4 = XI[:].rearrange("p (c m r) -> p c m r", m=4, r=4)
                            for r_ in range(1, 4):
                                ko_ = [r_ + 4 * m_ for m_ in range(4)]
                                ki_ = [r_ - 1 + 4 * m_ for m_ in range(4)]
                                cstep(XRc[:, :, r_], XIc[:, :, r_], XRc[:, :, r_ - 1], XIc[:, :, r_ - 1], 0,
                                      [("XR", c_) for c_ in ko_], [("XI", c_) for c_ in ko_],
                                      [("XR", c_) for c_ in ki_], [("XI", c_) for c_ in ki_])
                            for m_ in range(1, 4):
                                so_ = 4 * m_ + 3
                                si_ = 4 * m_ - 1
                                cstep(XRv[:, :, so_], XIv[:, :, so_], XRv[:, :, si_], XIv[:, :, si_], 3,
                                      ("XR", so_), ("XI", so_), ("XR", si_), ("XI", si_))
                            cur = 0
                            f.op("act", lambda e: e.copy(out=cs_[0][:, 0, 64:192], in_=XRv[:, :, 15]),
                                 reads=[("XR", 15)], writes=[("cs", 0, 0)])
                            f.op("act", lambda e: e.copy(out=cs_[0][:, 1, 64:192], in_=XIv[:, :, 15]),
                                 reads=[("XI", 15)], writes=[("cs", 0, 1)])
                            for k in range(7):
                                sh = 1 << k
                                src = cs_[cur]
                                dst = cs_[1 - cur]
                                cstep(dst[:, 0, 64:192], dst[:, 1, 64:192], src[:, 0, 64 - sh:192 - sh], src[:, 1, 64 - sh:192 - sh],
                                      15 + k, ("cs", 1 - cur, 0), ("cs", 1 - cur, 1), ("cs", cur, 0), ("cs", cur, 1),
                                      bR=src[:, 0, 64:192], bI=src[:, 1, 64:192])
                                cur = 1 - cur
                            fin = cs_[cur]
                            for m_ in range(4):
                                s_ = 4 * m_ + 3
                                cstep(XRv[:, 1:128, s_], XIv[:, 1:128, s_], fin[:, 0, 64:191], fin[:, 1, 64:191], s_,
                                      ("XR", s_), ("XI", s_), ("cs", cur, 0), ("cs", cur, 1))
                            ENDK = [[("XR", 3), ("XR", 7), ("XR", 11)], [("XI", 3), ("XI", 7), ("XI", 11)]]

                            def p3_m(comp, part):
                                X4 = (XR4, XI4)[comp]
                                kx = ("XR", "XI")[comp]
                                cidx = ((0, 2), (1, 0))[comp]
                                E4 = (XR4, XI4)[part]
                                for r_ in range(3):
                                    f.op("dve", lambda e, r_=r_: e.scalar_tensor_tensor(
                                        out=X4[:, :, 1:4, r_], in0=E4[:, :, 0:3, 3],
                                        scalar=PSC(q, r_, cidx[part]), in1=X4[:, :, 1:4, r_],
                                        op0=ALU.mult, op1=ALU.add),
                                        reads=ENDK[part] + ["PR"], writes=[(kx, 4 * m_ + r_) for m_ in range(1, 4)])

                            def p3_0(comp):
                                Xv = (XRv, XIv)[comp]
                                kx = ("XR", "XI")[comp]
                                cidx = ((0, 2), (1, 0))[comp]
                                for part in range(2):
                                    for r_ in range(3):
                                        f.op("dve", lambda e, r_=r_, part=part: e.scalar_tensor_tensor(
                                            out=Xv[:, 1:128, r_], in0=fin[:, part, 64:191],
                                            scalar=PSC(q, r_, cidx[part]), in1=Xv[:, 1:128, r_],
                                            op0=ALU.mult, op1=ALU.add),
                                            reads=[("cs", cur, part), "PR"], writes=[(kx, r_)])
                            p3_m(0, 0)
                            p3_m(0, 1)
                            p3_0(0)
                            p3_m(1, 0)
                            stage_out(qq, 0)
                            if qq < 3:
                                stage_in(qq + 1, 0)
                            p3_m(1, 1)
                            p3_0(1)
                            stage_out(qq, 1)
                            if qq < 3:
                                stage_in(qq + 1, 1)
                        for tg in range(4):
                            sl = slice(tg * 512, (tg + 1) * 512)
                            f.op("dve", lambda e, sl=sl, tg=tg, j=j: e.scalar_tensor_tensor(out=gl[0][:], in0=uT[:, sl],
                                                                                            scalar=dvec[:, j:j + 1], in1=ps[tg][:],
                                                                                            op0=ALU.mult, op1=ALU.add),
                                 reads=[PS[tg], "uT", "dvec"], writes=["gl0"])
                            f.op("pool", lambda e: e.tensor_tensor(out=gl[1][:], in0=gl[0][:], in1=gl[0][:], op=ALU.mult),
                                 reads=["gl0"], writes=["gl1"])
                            f.op("pool", lambda e: e.tensor_scalar(out=gl[1][:], in0=gl[1][:], scalar1=0.044715, scalar2=1.0,
                                                                   op0=ALU.mult, op1=ALU.add), reads=["gl1"], writes=["gl1"])
                            f.op("pool", lambda e: e.tensor_tensor(out=gl[1][:], in0=gl[1][:], in1=gl[0][:], op=ALU.mult),
                                 reads=["gl1", "gl0"], writes=["gl1"])
                            f.op("act", lambda e: e.activation(out=gl[2][:], in_=gl[1][:], func=AF.Sigmoid,
                                                               scale=2.0 * math.sqrt(2.0 / math.pi)),
                                 reads=["gl1"], writes=["gl2"] + XALL[0])
                            f.op("dve", lambda e, sl=sl, j=j: e.tensor_tensor(out=mT[:, 8 + j, sl], in0=gl[0][:], in1=gl[2][:],
                                                                              op=ALU.mult),
                                 reads=["gl0", "gl2"] + XALL[0], writes=[("mT", 8 + j)])
                    for tg in range(4):
                        sl = slice(tg * 512, (tg + 1) * 512)
                        for fo in range(4):
                            wb, wk = ws.get()
                            for jj in range(4):
                                f.op("pe", lambda e, fo=fo, jj=jj, sl=sl: e.matmul(
                                    ps[fo][:], lhsT=wb[:, jj, :], rhs=mT[:, 8 + jj, sl],
                                    start=(jj == 0), stop=(jj == 3)),
                                    reads=[wk, ("mT", 8 + jj)], writes=[PS[fo]], inc=(jj == 3))
                        for fo in range(4):
                            wb, wk = ws.get()
                            proj_fm(wb, wk, 4, tg)
                            f.op("act", lambda e: e.activation(out=gl[0][:], in_=ps[4][:], func=AF.Sigmoid),
                                 reads=[PS[4]], writes=["gl0"])
                            f.op("act", lambda e, fo=fo: e.activation(out=gl[2][:], in_=ps[fo][:], func=AF.Sigmoid,
                                                                      bias=glub[:, fo:fo + 1]),
                                 reads=[PS[fo], "glub"], writes=["gl2"] + XALL[0])
                            f.op("pool", lambda e: e.tensor_tensor(out=gl[1][:], in0=gl[2][:], in1=gl[0][:], op=ALU.mult),
                                 reads=["gl2", "gl0"] + XALL[0], writes=["gl1"])
                            f.op("dve", lambda e: e.tensor_tensor(out=gl[1][:], in0=ps[4][:], in1=gl[1][:], op=ALU.mult),
                                 reads=[PS[4], "gl1"], writes=["gl1"])
                            f.op("dve", lambda e, fo=fo, sl=sl: e.tensor_tensor(out=mT[:, 8 + fo, sl], in0=mT[:, 8 + fo, sl],
                                                                                in1=gl[1][:], op=ALU.mult),
                                 reads=["gl1", ("mT", 8 + fo)] + [PS[k] for k in range(4)], writes=[("mT", 8 + fo)])
                    f.barrier()

                with ExitStack() as os_:
                    wout = os_.enter_context(nc.sbuf_tensor(nk("ewout"), [128, 12, D], BF16))
                    ytmp_box[0] = os_.enter_context(nc.sbuf_tensor(nk("ytmp"), [128, 2, 512], F32))
                    woutk = load_big(wout, "ewout", ewout_d[i], 12)
                    f.dma("sp", gain[:], bcast("post_norm", l * D, D), writes=["gain"])
                    for b in range(NB):
                        outproj_block(b, lambda kc, b=b: mT[:, kc, b * 128:(b + 1) * 128],
                                      [("mT", ch) for ch in range(12)], 12, wout, woutk, l)
                    f.barrier()

        for l in layers:
            if l % 2 == 0:
                even_layer(l)
            else:
                odd_layer(l)

        for b in range(NB):
            f.dma("sp", out_d[b * 128:(b + 1) * 128, :], xres[:, b, :], reads=[("x", b)], key="out")
        nc.sync.wait_ge(f.dsem["out"][0], f.dsem["out"][1])
    return nc


_CACHE = {}


def prep_inputs(inputs):
    w = {k: np.ascontiguousarray(np.asarray(v, dtype=np.float32)) for k, v in inputs.items()}
    perm = swap_perm()
    shared = {k: v for k, v in w.items() if k != "x"}
    shared["even_w_sw"] = np.ascontiguousarray(w["even_w_in"][:, :, 0:2048][:, :, perm])
    shared.update(host_consts())
    return w["x"], shared


def kernel(**inputs):
    x, shared = prep_inputs(inputs)
    if "nc" not in _CACHE:
        _CACHE["nc"] = build()
    nc = _CACHE["nc"]
    in_maps = []
    for c in range(8):
        m = dict(shared)
        m["x"] = np.ascontiguousarray(x[c])
        in_maps.append(m)
    res = run_bass_kernel_spmd(nc, in_maps, core_ids=list(range(8)))
    return np.stack([np.asarray(r["out"], dtype=np.float32) for r in res.results], axis=0)
```

```python
import math
import numpy as np
import concourse.bass as bass
import concourse.mybir as mybir
from concourse.bass_utils import run_bass_kernel_spmd

F32 = mybir.dt.float32
BF16 = mybir.dt.bfloat16
I32 = mybir.dt.int32
ALU = mybir.AluOpType
AF = mybir.ActivationFunctionType
AX = mybir.AxisListType

S = 2048
D = 1024
NB = 16
EPS = 1e-6
TWO_PI = 2.0 * math.pi


class FW:
    def __init__(self, nc):
        self.nc = nc
        self.eng = {"pe": nc.tensor, "act": nc.scalar, "dve": nc.vector,
                    "pool": nc.gpsimd, "sp": nc.sync}
        self.sem = {}
        self.cnt = {}
        for e in self.eng:
            self.sem[e] = nc.alloc_semaphore("s_" + e)
            self.cnt[e] = 0
        self.waited = {}
        self.lastw = {}
        self.rd = {}
        self.dsem = {}

    def _deps(self, reads, writes):
        deps = {}

        def add(t):
            if t is not None and deps.get(t[0], 0) < t[1]:
                deps[t[0]] = t[1]
        for k in reads:
            add(self.lastw.get(k))
        for k in writes:
            add(self.lastw.get(k))
            for sk, v in self.rd.get(k, {}).items():
                add((sk, v))
        return deps

    def _semof(self, sk):
        if isinstance(sk, tuple):
            return self.dsem[sk[1]][0]
        return self.sem[sk]

    def _emit_waits(self, e, deps):
        for sk, v in deps.items():
            if sk == e and v > self.cnt[e]:
                continue
            if self.waited.get((e, sk), 0) < v:
                self.eng[e].wait_ge(self._semof(sk), v)
                self.waited[(e, sk)] = v

    def _record(self, t, reads, writes):
        for k in writes:
            self.lastw[k] = t
            self.rd[k] = {}
        for k in reads:
            d = self.rd.setdefault(k, {})
            if d.get(t[0], 0) < t[1]:
                d[t[0]] = t[1]

    def op(self, e, fn, reads=(), writes=(), inc=True):
        self._emit_waits(e, self._deps(reads, writes))
        ins = fn(self.eng[e])
        if inc:
            ins.then_inc(self.sem[e], 1)
            self.cnt[e] += 1
            t = (e, self.cnt[e])
        else:
            t = (e, self.cnt[e] + 1)
        self._record(t, reads, writes)
        return ins

    def dma(self, q, out, in_, reads=(), writes=(), key=None, **kw):
        if key is None:
            key = writes[0] if writes else reads[0]
        if key not in self.dsem:
            self.dsem[key] = [self.nc.alloc_semaphore("d%d" % len(self.dsem)), 0]
        self._emit_waits(q, self._deps(reads, writes))
        ins = self.eng[q].dma_start(out=out, in_=in_, **kw)
        ins.then_inc(self.dsem[key][0], 16)
        self.dsem[key][1] += 16
        t = (("dma", key), self.dsem[key][1])
        self._record(t, reads, writes)
        return ins

    def barrier(self):
        for e in self.eng:
            deps = {}
            for o in self.eng:
                if o != e and self.cnt[o] > 0:
                    deps[o] = self.cnt[o]
            for key, (s, c) in self.dsem.items():
                if c > 0:
                    deps[("dma", key)] = c
            self._emit_waits(e, deps)


def _mult(d):
    d = np.asarray(d)
    m = ((d >= 0) & (d <= 128)).astype(np.float32)
    m += ((d >= 0) & (d <= 512) & (d % 4 == 0)).astype(np.float32)
    m += ((d >= 0) & (d % 16 == 0)).astype(np.float32)
    return m


def host_consts():
    c = {}
    c["c_ident"] = np.eye(128, dtype=np.float32)
    j = np.arange(128)[:, None]
    t = np.arange(128)[None, :]
    c["c_mask"] = np.concatenate([_mult(128 * dl + t - j) for dl in range(-3, 16)], axis=1).astype(np.float32)
    bands = np.zeros((128, 4, 3, 128), np.float32)
    tp = np.arange(128)[:, None]
    tt = np.arange(128)[None, :]
    for wi, w in enumerate((2, 4, 8, 16)):
        dcur = tt - tp
        bands[:, wi, 0, :] = ((dcur >= 0) & (dcur < w)) / float(w) - (dcur == 0)
        dprev = tt + 128 - tp
        bands[:, wi, 1, :] = ((dprev >= 0) & (dprev < w)) / float(w)
        cnt = np.minimum(tt + 1, w).astype(np.float32)
        bands[:, wi, 2, :] = ((dcur >= 0) & (dcur < w)) / cnt - (dcur == 0)
    c["c_bands"] = bands.reshape(128, 4 * 3 * 128)
    half = 8
    inv = 500000.0 ** (-np.arange(0, 16, 2, dtype=np.float32) / 16.0)
    ang = np.arange(S, dtype=np.float32)[None, :] * inv[:, None]
    C = np.ones((64, S), np.float32)
    Sg = np.zeros((64, S), np.float32)
    C[0:8] = np.cos(ang)
    C[8:16] = np.cos(ang)
    Sg[0:8] = -np.sin(ang)
    Sg[8:16] = np.sin(ang)
    c["c_ropec"] = np.concatenate([C, C], 0)
    c["c_ropes"] = np.concatenate([Sg, Sg], 0)
    gm = np.zeros((128, 8), np.float32)
    for g in range(8):
        gm[g * 16:(g + 1) * 16, g] = 1.0
    c["c_gmask"] = gm
    return c


def swap_perm():
    perm = np.arange(2048)
    for blk in range(2048 // 64):
        b = blk * 64
        perm[b:b + 8] = np.arange(b + 8, b + 16)
        perm[b + 8:b + 16] = np.arange(b, b + 8)
    return perm


def build(layers=(0, 1, 2, 3)):
    nc = bass.Bass("TRN2", target_bir_lowering=False)
    dt_ = {}

    def din(name, shape):
        h = nc.dram_tensor(name, list(shape), F32, kind="ExternalInput")
        dt_[name] = h
        return h.ap()

    x_d = din("x", [S, D])
    pre_d = din("pre_norm", [4, D])
    post_d = din("post_norm", [4, D])
    ewin_d = din("even_w_in", [2, D, 5120])
    ewsw_d = din("even_w_sw", [2, D, 2048])
    ewout_d = din("even_w_out", [2, 1536, D])
    are_d = din("ssm_a_re", [2, 32, 64])
    aim_d = din("ssm_a_im", [2, 32, 64])
    ldt_d = din("ssm_log_dt", [2, 32])
    bre_d = din("ssm_b_re", [2, 32, 64, 16])
    bim_d = din("ssm_b_im", [2, 32, 64, 16])
    cre_d = din("ssm_c_re", [2, 32, 16, 64])
    cim_d = din("ssm_c_im", [2, 32, 16, 64])
    sd_d = din("ssm_d", [2, 512])
    glw_d = din("ssm_glu_w", [2, 512, 512])
    glb_d = din("ssm_glu_b", [2, 512])
    owin_d = din("odd_w_in", [2, D, 4096])
    pw_d = din("pool_w", [2, 4, 512, 512])
    psc_d = din("pool_scale", [2, 2048])
    owout_d = din("odd_w_out", [2, 2048, D])
    cid_d = din("c_ident", [128, 128])
    cmask_d = din("c_mask", [128, 19 * 128])
    cband_d = din("c_bands", [128, 12 * 128])
    cropec_d = din("c_ropec", [128, S])
    cropes_d = din("c_ropes", [128, S])
    cgm_d = din("c_gmask", [128, 8])
    out_d = nc.dram_tensor("out", [S, D], F32, kind="ExternalOutput").ap()

    f = FW(nc)
    uid = [0]

    def nk(p):
        uid[0] += 1
        return "%s%d" % (p, uid[0])

    def bcast(name, off, n):
        return bass.AP(dt_[name], off, [[0, 128], [1, n]])

    from contextlib import ExitStack
    with ExitStack() as es:
        def sb(name, shape, dtype):
            return es.enter_context(nc.sbuf_tensor(name, list(shape), dtype))

        def psb(name, shape, dtype):
            return es.enter_context(nc.psum_tensor(name, list(shape), dtype))

        xres = sb("xres", [128, NB, D], F32)
        ps = [psb("ps%d" % i, [128, 512], F32) for i in range(8)]
        PS = ["ps%d" % i for i in range(8)]
        ident = sb("ident", [128, 128], BF16)
        identf = sb("identf", [128, 128], F32)
        gain = sb("gain", [128, D], F32)
        junk = sb("junk", [128, D], BF16)
        stat = sb("stat", [128, 64], F32)
        gmask = sb("gmask", [128, 8], F32)
        wbf = [sb("wbf%d" % i, [128, 8, 128], BF16) for i in range(6)]
        wctr = [0, 0]

        f.dma("sp", identf[:], cid_d, writes=["identf"])
        f.op("dve", lambda e: e.tensor_copy(out=ident[:], in_=identf[:]), reads=["identf"], writes=["ident"])
        f.dma("sp", gmask[:], cgm_d, writes=["gmask"])
        for b in range(NB):
            f.dma("sp", xres[:, b, :], x_d[b * 128:(b + 1) * 128, :], writes=[("x", b)])

        cast_rr = [0]

        def cast(out, in_, reads, writes):
            e = "act"
            if e == "pool":
                f.op("pool", lambda g: g.tensor_copy(out=out, in_=in_), reads=reads, writes=writes)
            else:
                f.op("act", lambda g: g.copy(out=out, in_=in_), reads=reads, writes=writes)

        class WS:
            DEPTH = 2

            def __init__(self, srcs):
                self.srcs = srcs
                self.issued = 0
                self.cur = 0
                self.buf = {}

            def prefetch(self, n):
                self._issue_to(self.cur + n)

            def get(self, issue=True):
                if issue:
                    self._issue_to(self.cur + 1 + WS.DEPTH)
                assert self.cur < self.issued
                bi = self.buf[self.cur]
                self.cur += 1
                return wbf[bi], "wbf%d" % bi

            def _issue_to(self, lim):
                while self.issued < min(len(self.srcs), lim):
                    src, kc = self.srcs[self.issued]
                    bi = wctr[1] % 6
                    wctr[1] += 1
                    f.dma("pool", wbf[bi][:, 0:kc, :], src.rearrange("(k p) n -> p k n", p=128),
                          writes=["wbf%d" % bi])
                    self.buf[self.issued] = bi
                    self.issued += 1

        def load_big(dst, dkey, src_ap, kc, step=4):
            keys = []
            for k0 in range(0, kc, step):
                k1 = min(kc, k0 + step)
                f.dma("pool", dst[:, k0:k1, :], src_ap[k0 * 128:k1 * 128, :].rearrange("(k p) n -> p k n", p=128),
                      writes=[(dkey, k0)])
                keys.append((dkey, k0))
            return keys

        def load_big_hw(dst, dkey, src_ap, kc, stg, skey):
            keys = []
            for k0 in range(0, kc, 2):
                f.dma("sp", stg[:, :, :], src_ap[k0 * 128:(k0 + 2) * 128, :].rearrange("(k p) n -> p k n", p=128),
                      writes=[skey])
                f.op("act", lambda g: g.copy(out=dst[:, k0:k0 + 2, :], in_=stg[:, :, :]), reads=[skey], writes=[(dkey, k0)])
                keys.append((dkey, k0))
            return keys

        scol = [0]

        def newcol():
            scol[0] = (scol[0] + 1) % 64
            return scol[0]

        def rstd_from_ss(c_ss):
            c1 = newcol()
            c2 = newcol()
            f.op("act", lambda e: e.activation(out=stat[:, c1:c1 + 1], in_=stat[:, c_ss:c_ss + 1], func=AF.Sqrt,
                                               scale=1.0 / D, bias=EPS),
                 reads=[("st", c_ss)], writes=[("st", c1)])
            f.op("dve", lambda e: e.reciprocal(out=stat[:, c2:c2 + 1], in_=stat[:, c1:c1 + 1]),
                 reads=[("st", c1)], writes=[("st", c2)])
            return c2

        def prenorm_block(b, dst, dkey):
            c = newcol()
            f.op("act", lambda e: e.activation(out=junk[:], in_=xres[:, b, :], func=AF.Square,
                                               accum_out=stat[:, c:c + 1]),
                 reads=[("x", b)], writes=["junk", ("st", c)])
            c2 = rstd_from_ss(c)
            f.op("dve", lambda e: e.scalar_tensor_tensor(out=dst, in0=xres[:, b, :], scalar=stat[:, c2:c2 + 1],
                                                         in1=gain[:], op0=ALU.mult, op1=ALU.mult),
                 reads=[("x", b), ("st", c2), "gain"], writes=[dkey])

        def transpose_block(src, skey, dst_ap, dkey, pi):
            for hf in range(2):
                bk = 4 + 2 * pi + hf
                for cc in range(4):
                    c = hf * 4 + cc
                    f.op("pe", lambda e, c=c, cc=cc, bk=bk: e.matmul(ps[bk][:, cc * 128:(cc + 1) * 128],
                                                                     lhsT=src[:, c * 128:(c + 1) * 128], rhs=ident[:],
                                                                     start=True, stop=True),
                         reads=[skey, "ident"], writes=[PS[bk]], inc=(cc == 3))
                f.op("act", lambda e, hf=hf, bk=bk: e.copy(out=dst_ap[:, hf * 4:(hf + 1) * 4, :],
                                                           in_=ps[bk][:].rearrange("p (c t) -> p c t", c=4)),
                     reads=[PS[bk]], writes=[dkey])

        def outproj_block(b, mT_fn, mkeys, KC, wout, wkey, l):
            pp = (b % 2) * 2
            for fh in range(2):
                for kc in range(KC):
                    f.op("pe", lambda e, kc=kc, fh=fh: e.matmul(ps[pp + fh][:], lhsT=mT_fn(kc),
                                                                rhs=wout[:, kc, fh * 512:(fh + 1) * 512],
                                                                start=(kc == 0), stop=(kc == KC - 1)),
                         reads=list(mkeys) + list(wkey), writes=[PS[pp + fh]], inc=(kc == KC - 1))
            ca = newcol()
            cb = newcol()
            f.op("act", lambda e: e.activation(out=junk[:, 0:512], in_=ps[pp][:], func=AF.Square,
                                               accum_out=stat[:, ca:ca + 1]),
                 reads=[PS[pp]], writes=["junk", ("st", ca)])
            f.op("act", lambda e: e.activation(out=junk[:, 512:1024], in_=ps[pp + 1][:], func=AF.Square,
                                               accum_out=stat[:, cb:cb + 1]),
                 reads=[PS[pp + 1]], writes=["junk", ("st", cb)])
            cs = newcol()
            f.op("dve", lambda e: e.tensor_tensor(out=stat[:, cs:cs + 1], in0=stat[:, ca:ca + 1],
                                                  in1=stat[:, cb:cb + 1], op=ALU.add),
                 reads=[("st", ca), ("st", cb)], writes=[("st", cs)])
            c2 = rstd_from_ss(cs)
            for fh in range(2):
                tk = "ytmp%d" % fh
                f.op("dve", lambda e, fh=fh: e.scalar_tensor_tensor(out=ytmp_box[0][:, fh, :], in0=ps[pp + fh][:],
                                                                    scalar=stat[:, c2:c2 + 1],
                                                                    in1=gain[:, fh * 512:(fh + 1) * 512],
                                                                    op0=ALU.mult, op1=ALU.mult),
                     reads=[PS[pp + fh], ("st", c2), "gain"], writes=[tk])
                f.op("pool", lambda e, fh=fh: e.tensor_tensor(out=xres[:, b, fh * 512:(fh + 1) * 512],
                                                              in0=xres[:, b, fh * 512:(fh + 1) * 512],
                                                              in1=ytmp_box[0][:, fh, :], op=ALU.add),
                     reads=[("x", b), tk], writes=[("x", b)])

        ytmp_box = [None]

        def odd_layer(l):
            i = l // 2
            with ExitStack() as ls:
                def lsb(name, shape, dtype):
                    return ls.enter_context(nc.sbuf_tensor(nk(name), list(shape), dtype))
                hN = lsb("hN", [128, 5, D], BF16)
                hT = lsb("hTq", [128, 8, 512], BF16)
                phT = lsb("phT", [128, 8, 512], BF16)
                mixed = lsb("mixed", [128, 4, 512], BF16)
                mT = lsb("mTq", [128, 16, 512], BF16)
                sgb = [lsb("sg0", [128, 512], F32)] * 2
                bandf = lsb("bandf", [128, 12, 128], F32)
                band = lsb("band", [128, 4, 4, 128], BF16)
                btmp = lsb("btmp", [128, 4, 128], F32)
                pwbf = lsb("pwbf", [128, 4, 512], BF16)
                wout = lsb("wout", [128, 16, D], BF16)
                pscale = lsb("pscale", [128, 16], F32)
                wost = [lsb("wost%d" % k_, [128, 2, D], F32) for k_ in range(2)]
                pwst = lsb("pwst", [128, 4, 512], F32)
                ytmp_box[0] = lsb("ytmp", [128, 2, 512], F32)
                f.dma("sp", bandf[:], cband_d.rearrange("p (a t) -> p a t", t=128), writes=["bandf"])
                bv = bandf[:].rearrange("p (w k) t -> p w k t", k=3)
                f.op("dve", lambda e: e.tensor_copy(out=band[:, :, 0:3, :], in_=bv), reads=["bandf"], writes=["band"])
                f.op("dve", lambda e: e.tensor_copy(out=btmp[:], in_=band[:, :, 2, :]), reads=["band"], writes=["btmp"])
                f.op("dve", lambda e: e.tensor_tensor(out=btmp[:], in0=bv[:, :, 2, :], in1=btmp[:], op=ALU.subtract),
                     reads=["bandf", "btmp"], writes=["btmp"])
                f.op("dve", lambda e: e.tensor_copy(out=band[:, :, 3, :], in_=btmp[:]), reads=["btmp"], writes=["band"])
                f.dma("sp", pscale[:], psc_d[i].rearrange("(c p) -> p c", p=128), writes=["pscale"],
                      allow_slow_non_contiguous=True)
                srcs = []
                for tq in range(4):
                    for g in range(4):
                        for j in range(4):
                            srcs.append((owin_d[i][:, g * 512 + j * 128:g * 512 + (j + 1) * 128], 8))
                        for dj in range(4):
                            col = 2048 + g * 512 + dj * 128
                            srcs.append((owin_d[i][:, col:col + 128], 8))
                ws = WS(srcs)
                for tq in range(4):
                    b0 = tq * 4
                    woutk = [("wout", 2 * k_) for k_ in range(8)]
                    wstep = [0]

                    def wout_step():
                        k_ = wstep[0]
                        wstep[0] += 1
                        if 1 <= k_ <= 8:
                            kk = k_ - 1
                            f.op("act", lambda g_: g_.copy(out=wout[:, 2 * kk:2 * kk + 2, :], in_=wost[kk % 2][:]),
                                 reads=["wost%d" % (kk % 2)], writes=[("wout", 2 * kk)])
                        if k_ < 8:
                            f.dma("sp", wost[k_ % 2][:],
                                  owout_d[i][k_ * 256:(k_ + 1) * 256, :].rearrange("(k p) n -> p k n", p=128),
                                  writes=["wost%d" % (k_ % 2)])
                    wout_step()
                    f.dma("sp", gain[:], bcast("pre_norm", l * D, D), writes=["gain"])
                    if tq > 0:
                        f.op("pool", lambda e: e.tensor_copy(out=hN[:, 0, :], in_=hN[:, 4, :]),
                             reads=[("hN", 4)], writes=[("hN", 0)])
                    for s_, b in enumerate(range(b0 - 1, b0 + 4)):
                        if s_ == 0:
                            continue
                        prenorm_block(b, hN[:, s_, :], ("hN", s_))
                    for s_ in range(1, 5):
                        transpose_block(hN[:, s_, :], ("hN", s_), hT[:, :, (s_ - 1) * 128:s_ * 128], "hT", s_ % 2)
                    for g in range(4):
                        f.dma("sp", pwst[:], pw_d[i, g].rearrange("(k p) n -> p k n", p=128), writes=["pwst"])
                        pwk = [("pwbf", 0), ("pwbf", 2)]
                        for s_ in range(1, 5):
                            b = b0 + s_ - 1
                            for hf in range(2):
                                pk = 4 + hf
                                for cc in range(4):
                                    c = hf * 4 + cc
                                    o = ps[pk][:, cc * 128:(cc + 1) * 128]
                                    lh = hN[:, s_, c * 128:(c + 1) * 128]
                                    if b == 0:
                                        f.op("pe", lambda e, o=o, lh=lh: e.matmul(o, lhsT=lh, rhs=band[:, g, 2, :],
                                                                                  start=True, stop=False),
                                             reads=[("hN", s_), "band"], writes=[PS[pk]], inc=False)
                                        f.op("pe", lambda e, o=o, lh=lh: e.matmul(o, lhsT=lh, rhs=band[:, g, 3, :],
                                                                                  start=False, stop=True),
                                             reads=[("hN", s_), "band"], writes=[PS[pk]], inc=(cc == 3))
                                    else:
                                        lp = hN[:, s_ - 1, c * 128:(c + 1) * 128]
                                        f.op("pe", lambda e, o=o, lh=lh: e.matmul(o, lhsT=lh, rhs=band[:, g, 0, :],
                                                                                  start=True, stop=False),
                                             reads=[("hN", s_), "band"], writes=[PS[pk]], inc=False)
                                        f.op("pe", lambda e, o=o, lp=lp: e.matmul(o, lhsT=lp, rhs=band[:, g, 1, :],
                                                                                  start=False, stop=True),
                                             reads=[("hN", s_ - 1), "band"], writes=[PS[pk]], inc=(cc == 3))
                                f.op("act", lambda e, hf=hf, pk=pk: e.copy(
                                    out=phT[:, hf * 4:(hf + 1) * 4, (s_ - 1) * 128:s_ * 128],
                                    in_=ps[pk][:].rearrange("p (c t) -> p c t", c=4)),
                                    reads=[PS[pk]], writes=["phT"])
                        for j in range(4):
                            wout_step()
                            wb, wk = ws.get()
                            pk = j % 2
                            for kc in range(8):
                                f.op("pe", lambda e, kc=kc: e.matmul(ps[pk][:], lhsT=wb[:, kc, :], rhs=phT[:, kc, :],
                                                                     start=(kc == 0), stop=(kc == 7)),
                                     reads=[wk, "phT"], writes=[PS[pk]], inc=(kc == 7))
                            f.op("act", lambda e, j=j, pk=pk: e.copy(out=mixed[:, j, :], in_=ps[pk][:]),
                                 reads=[PS[pk]], writes=[("mixed", j)])
                        for k0_ in (0, 2):
                            f.op("act", lambda g_, k0_=k0_: g_.copy(out=pwbf[:, k0_:k0_ + 2, :], in_=pwst[:, k0_:k0_ + 2, :]),
                                 reads=["pwst"], writes=[("pwbf", k0_)])
                        for dj in range(4):
                            wb, wk = ws.get()
                            pg = 2 if dj % 2 == 0 else 6
                            sg = sgb[dj % 2]
                            sgk = "sg0"
                            for kc in range(8):
                                f.op("pe", lambda e, kc=kc: e.matmul(ps[pg][:], lhsT=wb[:, kc, :], rhs=hT[:, kc, :],
                                                                     start=(kc == 0), stop=(kc == 7)),
                                     reads=[wk, "hT"], writes=[PS[pg]], inc=(kc == 7))
                            f.op("act", lambda e: e.activation(out=sg[:], in_=ps[pg][:], func=AF.Silu),
                                 reads=[PS[pg]], writes=[sgk])
                            py = 3 if dj % 2 == 0 else 7
                            for j in range(4):
                                f.op("pe", lambda e, j=j, dj=dj: e.matmul(ps[py][:], lhsT=pwbf[:, j, dj * 128:(dj + 1) * 128],
                                                                          rhs=mixed[:, j, :], start=(j == 0), stop=(j == 3)),
                                     reads=pwk + [("mixed", j)], writes=[PS[py]], inc=(j == 3))
                            ch = g * 4 + dj
                            f.op("dve", lambda e, ch=ch: e.scalar_tensor_tensor(out=mT[:, ch, :], in0=ps[py][:],
                                                                                scalar=pscale[:, ch:ch + 1], in1=sg[:],
                                                                                op0=ALU.mult, op1=ALU.mult),
                                 reads=[PS[py], "pscale", sgk], writes=[("mT", ch)])
                    f.dma("sp", gain[:], bcast("post_norm", l * D, D), writes=["gain"])
                    for bb in range(4):
                        outproj_block(b0 + bb, lambda kc, bb=bb: mT[:, kc, bb * 128:(bb + 1) * 128],
                                      [("mT", ch) for ch in range(16)], 16, wout, woutk, l)
                f.barrier()

        def even_layer(l):
            i = l // 2
            with ExitStack() as ls:
                def lsb(name, shape, dtype):
                    return ls.enter_context(nc.sbuf_tensor(nk(name), list(shape), dtype))
                hT = lsb("hT", [128, 8, S], BF16)
                mT = lsb("mT", [128, 12, S], BF16)
                f.dma("sp", gain[:], bcast("pre_norm", l * D, D), writes=["gain"])
                with ExitStack() as hs_:
                    hNb = [hs_.enter_context(nc.sbuf_tensor(nk("hNb"), [128, D], BF16)) for k in range(2)]
                    for b in range(NB):
                        prenorm_block(b, hNb[b % 2][:], "hNb%d" % (b % 2))
                        transpose_block(hNb[b % 2], "hNb%d" % (b % 2), hT[:, :, b * 128:(b + 1) * 128], "hT", b % 2)
                    f.barrier()

                def proj_fm(wb, wk, pk, tg):
                    for kc in range(8):
                        f.op("pe", lambda e, kc=kc: e.matmul(ps[pk][:], lhsT=wb[:, kc, :],
                                                             rhs=hT[:, kc, tg * 512:(tg + 1) * 512],
                                                             start=(kc == 0), stop=(kc == 7)),
                             reads=[wk, "hT"], writes=[PS[pk]], inc=(kc == 7))

                with ExitStack() as as_:
                    def asb(name, shape, dtype):
                        return as_.enter_context(nc.sbuf_tensor(nk(name), list(shape), dtype))
                    qT = asb("qT", [128, S], BF16)
                    kT = asb("kT", [128, 2, S], BF16)
                    V = asb("V", [128, NB, 2, 128], BF16)
                    ropec = asb("ropec", [128, S], BF16)
                    ropes = asb("ropes", [128, S], BF16)
                    maskT = asb("maskT", [128, 19 * 128], BF16)
                    Pt = [asb("Pt%d" % k, [128, 512], BF16) for k in range(4)]
                    rtmp = [asb("rtmp%d" % k, [128, 512], F32) for k in range(2)]
                    rc = rtmp[0]
                    atmp = rtmp[1]
                    f.dma("pool", ropec[:], cropec_d, writes=["ropec"])
                    f.dma("pool", ropes[:], cropes_d, writes=["ropes"])
                    f.dma("pool", maskT[:], cmask_d, writes=["maskT"])
                    srcs = []
                    for hp_ in range(8):
                        c0_ = hp_ * 128
                        for (base_, swb_) in ((0, 0), (1024, 1024)):
                            srcs.append((ewin_d[i][:, base_ + c0_:base_ + c0_ + 128], 8))
                            srcs.append((ewsw_d[i][:, swb_ + c0_:swb_ + c0_ + 128], 8))
                        srcs.append((ewin_d[i][:, 3072 + c0_:3072 + c0_ + 128], 8))
                        srcs.append((ewin_d[i][:, 2048 + c0_:2048 + c0_ + 128], 8))
                    ws = WS(srcs)
                    f.op("pool", lambda e: e.memset(V[:, :, :, 64:128], 1.0), writes=["V"])
                    f.op("pool", lambda e: e.memset(kT[:], 0.0), writes=[("kT", 0)])
                    mrr = [0]
                    qTb = [qT, mT[:, 8, :]]
                    kTb = [kT, mT[:, 9:11, :]]
                    f.op("pool", lambda e: e.memset(mT[:, 9:11, :], 0.0), writes=[("mT", 9), ("mT", 10), ("kT", 1)])
                    QK = [["qT", ("mT", 8)], [("kT", 0), ("kT", 1), ("mT", 9), ("mT", 10)]]

                    def proj_gen(hp_, issue):
                        par = hp_ % 2
                        qd = qTb[par]
                        kd = kTb[par]
                        qk_ = [QK[0][par]]
                        kk_ = [("kT", par)] + ([("mT", 9), ("mT", 10)] if par == 1 else [])
                        for which in ("q", "k"):
                            wb, wk = ws.get(issue)
                            wb2, wk2 = ws.get(issue)
                            for tg in range(4):
                                sl = slice(tg * 512, (tg + 1) * 512)
                                proj_fm(wb, wk, 6, tg)
                                yield
                                proj_fm(wb2, wk2, 7, tg)
                                r0 = "rtmp0"
                                r1 = "rtmp1"
                                f.op("dve", lambda e, sl=sl: e.tensor_tensor(out=rtmp[0][:], in0=ps[6][:], in1=ropec[:, sl],
                                                                             op=ALU.mult),
                                     reads=[PS[6], "ropec"], writes=[r0])
                                f.op("dve", lambda e, sl=sl: e.tensor_tensor(out=rtmp[1][:], in0=ps[7][:], in1=ropes[:, sl],
                                                                             op=ALU.mult),
                                     reads=[PS[7], "ropes"], writes=[r1])
                                if which == "q":
                                    f.op("dve", lambda e, sl=sl: e.tensor_tensor(out=qd[:, sl], in0=rtmp[0][:],
                                                                                 in1=rtmp[1][:], op=ALU.add),
                                         reads=[r0, r1], writes=qk_)
                                else:
                                    for a_ in range(2):
                                        pr_ = slice(64 * a_, 64 * a_ + 64)
                                        f.op("dve", lambda e, sl=sl, a_=a_, pr_=pr_: e.tensor_tensor(
                                            out=kd[pr_, a_, sl], in0=rtmp[0][pr_, :], in1=rtmp[1][pr_, :], op=ALU.add),
                                            reads=[r0, r1], writes=kk_)
                                yield
                        wb, wk = ws.get(issue)
                        for tg in range(4):
                            pg_ = 6 + (tg % 2)
                            proj_fm(wb, wk, pg_, tg)
                            f.op("act", lambda e, tg=tg, pg_=pg_: e.activation(out=mT[:, hp_, tg * 512:(tg + 1) * 512],
                                                                               in_=ps[pg_][:], func=AF.Silu),
                                 reads=[PS[pg_]], writes=[("mT", hp_)])
                        yield

                    gen = proj_gen(0, True)
                    for _ in gen:
                        pass
                    gen = None
                    for hp in range(8):
                        par = hp % 2
                        qT_ = qTb[par]
                        kT_ = kTb[par]
                        qkeys = [QK[0][par]]
                        kkeys = [("kT", par)] + ([("mT", 9), ("mT", 10)] if par == 1 else [])
                        if gen is not None:
                            for _ in gen:
                                pass
                        wb, wk = ws.get(hp == 0)
                        for b4 in range(4):
                            pk = 6 + (b4 % 2)
                            for bb in range(4):
                                b = b4 * 4 + bb
                                for kc in range(8):
                                    f.op("pe", lambda e, kc=kc, b=b, bb=bb: e.matmul(
                                        ps[pk][:, bb * 128:(bb + 1) * 128], lhsT=hT[:, kc, b * 128:(b + 1) * 128],
                                        rhs=wb[:, kc, :], start=(kc == 0), stop=(kc == 7)),
                                        reads=[wk, "hT"], writes=[PS[pk]], inc=(kc == 7 and bb == 3))
                            f.op("act", lambda e, b4=b4: e.copy(
                                out=V[:, b4 * 4:(b4 + 1) * 4, :, 0:64],
                                in_=ps[pk][:].rearrange("p (b a d) -> p b a d", b=4, a=2)),
                                reads=[PS[pk]], writes=["V"])
                        ws.prefetch(6)
                        gen = proj_gen(hp + 1, False) if hp < 7 else None
                        items = []
                        for a in range(2):
                            for qg in range(4):
                                nkb = 4 * qg + 4
                                for kb in range(nkb):
                                    items.append((a, qg, kb, nkb))
                        LAG = 2

                        def stage1(idx):
                            a, qg, kb, nkb = items[idx]
                            pr = slice(64 * a, 64 * a + 64)
                            cq = max(128 * kb, 512 * qg)
                            N = 512 * (qg + 1) - cq
                            sk = idx % 4
                            f.op("pe", lambda e: e.matmul(
                                ps[sk][:, 0:N], lhsT=kT_[:, a, kb * 128:(kb + 1) * 128], rhs=qT_[:, cq:cq + N],
                                start=True, stop=True),
                                reads=kkeys + qkeys, writes=[PS[sk]])
                            f.op("act", lambda e: e.activation(out=Pt[sk][:, 0:N], in_=ps[sk][:, 0:N],
                                                               func=AF.Exp, scale=0.125),
                                 reads=[PS[sk]], writes=["Pt%d" % sk])
                            moff = ((cq - 128 * kb) // 128 + 3) * 128
                            me = "dve"
                            mrr[0] += 1
                            f.op(me, lambda e: e.tensor_tensor(
                                out=Pt[sk][:, 0:N], in0=Pt[sk][:, 0:N], in1=maskT[:, moff:moff + N], op=ALU.mult),
                                reads=["Pt%d" % sk, "maskT"], writes=["Pt%d" % sk])

                        def stage2(idx):
                            a, qg, kb, nkb = items[idx]
                            pr = slice(64 * a, 64 * a + 64)
                            cq = max(128 * kb, 512 * qg)
                            N = 512 * (qg + 1) - cq
                            sk = idx % 4
                            po = 4 + (qg % 2)
                            oc = cq - 512 * qg
                            f.op("pe", lambda e: e.matmul(
                                ps[po][:, oc:oc + N], lhsT=V[:, kb, a, :], rhs=Pt[sk][:, 0:N],
                                start=(kb == 0), stop=(kb == nkb - 1)),
                                reads=["V", "Pt%d" % sk], writes=[PS[po]])
                            if kb == nkb - 1:
                                qs = slice(qg * 512, (qg + 1) * 512)
                                f.op("act", lambda e: e.activation(out=rc[64:128, :], in_=ps[po][64:128, :], func=AF.Ln),
                                     reads=[PS[po]], writes=["rtmp0"])
                                f.op("act", lambda e: e.activation(out=rc[64:128, :], in_=rc[64:128, :], func=AF.Exp,
                                                                   scale=-1.0),
                                     reads=["rtmp0"], writes=["rtmp0"])
                                f.op("dve", lambda e: e.tensor_tensor(out=atmp[pr, :], in0=ps[po][0:64, :],
                                                                      in1=rc[64:128, :], op=ALU.mult),
                                     reads=[PS[po], "rtmp0"], writes=["rtmp1"])
                                f.op("pool", lambda e: e.tensor_tensor(out=mT[pr, hp, qs], in0=atmp[pr, :],
                                                                       in1=mT[pr, hp, qs], op=ALU.mult),
                                     reads=["rtmp1", ("mT", hp)], writes=[("mT", hp)])

                        for idx in range(len(items) + LAG):
                            if idx < len(items):
                                stage1(idx)
                            if idx - LAG >= 0:
                                stage2(idx - LAG)
                            if gen is not None and idx % 4 == 3:
                                if next(gen, "done") == "done":
                                    gen = None
                    f.barrier()

                with ExitStack() as ss_:
                    def ssb(name, shape, dtype):
                        return ss_.enter_context(nc.sbuf_tensor(nk(name), list(shape), dtype))
                    NPW = 17 + 7
                    PR = ssb("PR", [128, 16, 3 * 24 + 4], F32)
                    pa = ssb("pa", [128, 16, 12], F32)
                    pi32 = ssb("pi32", [128, 16], I32)
                    XR = ssb("XR", [128, S], F32)
                    XI = ssb("XI", [128, S], F32)
                    cs_ = [ssb("cs%d" % k, [128, 2, 192], F32) for k in range(2)]
                    Xb = [ssb("Xb%d" % k, [128, 512], BF16) for k in range(2)]
                    uT = ssb("uT", [128, S], BF16)
                    bnat = ssb("bnat", [128, 2, 16, 16], F32)
                    cnat = ssb("cnat", [128, 2, 64], F32)
                    padT = ssb("padT", [128, 128], BF16)
                    padTf = ssb("padTf", [128, 2, 128], F32)
                    Bpad = ssb("Bpad", [128, 4, 2, 128], BF16)
                    Cpad = ssb("Cpad", [128, 4, 2, 128], BF16)
                    cf = ssb("cf", [128, 4, 128], F32)
                    dvec = ssb("dvec", [128, 4], F32)
                    glub = ssb("glub", [128, 4], F32)
                    gl = [ssb("gl%d" % k, [128, 512], F32) for k in range(2)] + [XR[:, 0:512]]

                    for k_ in range(2):
                        f.op("pool", lambda e, k_=k_: e.memset(cs_[k_][:], 0.0), writes=[("cs", k_, 0), ("cs", k_, 1)])
                    def ld_gp(dst_col, name):
                        for e_ in range(2):
                            f.dma("sp", pa[e_ * 64:(e_ + 1) * 64, :, dst_col],
                                  bass.AP(dt_[name], i * 2048 + e_ * 64, [[1, 64], [128, 16]]),
                                  writes=["pa"], allow_slow_non_contiguous=True)
                    ld_gp(0, "ssm_a_re")
                    ld_gp(1, "ssm_a_im")
                    for e_ in range(2):
                        f.dma("sp", pa[e_ * 64:(e_ + 1) * 64, :, 2],
                              bass.AP(dt_["ssm_log_dt"], i * 32 + e_, [[0, 64], [2, 16]]),
                              writes=["pa"], allow_slow_non_contiguous=True)
                    f.dma("sp", dvec[:], sd_d[i].rearrange("(c p) -> p c", p=128), writes=["dvec"],
                          allow_slow_non_contiguous=True)
                    f.dma("sp", glub[:], glb_d[i].rearrange("(c p) -> p c", p=128), writes=["glub"],
                          allow_slow_non_contiguous=True)

                    def pop(eng, fn, w=("pa",)):
                        f.op(eng, fn, reads=["pa", "PR"], writes=list(w))
                    A = lambda c: pa[:, :, c]
                    pop("act", lambda e: e.activation(out=A(3), in_=A(2), func=AF.Exp))
                    pop("dve", lambda e: e.tensor_tensor(out=A(4), in0=A(0), in1=A(3), op=ALU.mult))
                    pop("dve", lambda e: e.tensor_tensor(out=A(5), in0=A(1), in1=A(3), op=ALU.mult))
                    pop("act", lambda e: e.activation(out=A(4), in_=A(4), func=AF.Exp))

                    def sin_of(dst, shift):
                        pop("dve", lambda e: e.tensor_scalar(out=A(6), in0=A(5), scalar1=shift, scalar2=1.0 / TWO_PI,
                                                             op0=ALU.add, op1=ALU.mult))
                        f.op("dve", lambda e: e.tensor_copy(out=pi32[:], in_=A(6)), reads=["pa"], writes=["pi32"])
                        f.op("dve", lambda e: e.tensor_copy(out=A(7), in_=pi32[:]), reads=["pi32"], writes=["pa"])
                        pop("dve", lambda e: e.tensor_tensor(out=A(6), in0=A(6), in1=A(7), op=ALU.subtract))
                        pop("dve", lambda e: e.tensor_scalar(out=A(6), in0=A(6), scalar1=TWO_PI, scalar2=math.pi,
                                                             op0=ALU.mult, op1=ALU.min))
                        pop("dve", lambda e: e.tensor_scalar(out=A(6), in0=A(6), scalar1=-math.pi, scalar2=None,
                                                             op0=ALU.max))
                        pop("act", lambda e: e.activation(out=dst, in_=A(6), func=AF.Sin))
                    sin_of(A(8), 0.0)
                    sin_of(A(9), math.pi / 2)
                    P3 = lambda k, c: PR[:, :, 3 * k + c]
                    pop("dve", lambda e: e.tensor_tensor(out=P3(0, 0), in0=A(4), in1=A(9), op=ALU.mult), w=("PR",))
                    pop("dve", lambda e: e.tensor_tensor(out=P3(0, 1), in0=A(4), in1=A(8), op=ALU.mult), w=("PR",))

                    def cmul(dst, a_, b_):
                        pop("dve", lambda e: e.tensor_tensor(out=A(6), in0=P3(a_, 0), in1=P3(b_, 0), op=ALU.mult))
                        pop("dve", lambda e: e.tensor_tensor(out=A(7), in0=P3(a_, 1), in1=P3(b_, 1), op=ALU.mult))
                        pop("dve", lambda e: e.tensor_tensor(out=A(10), in0=P3(a_, 0), in1=P3(b_, 1), op=ALU.mult))
                        pop("dve", lambda e: e.tensor_tensor(out=A(11), in0=P3(a_, 1), in1=P3(b_, 0), op=ALU.mult))
                        pop("dve", lambda e: e.tensor_tensor(out=P3(dst, 0), in0=A(6), in1=A(7), op=ALU.subtract), w=("PR",))
                        pop("dve", lambda e: e.tensor_tensor(out=P3(dst, 1), in0=A(10), in1=A(11), op=ALU.add), w=("PR",))
                    for j in range(1, 16):
                        cmul(j, j - 1, 0)
                    for k in range(16, 16 + 7):
                        cmul(k, k - 1, k - 1)
                    for k in range(23):
                        pop("dve", lambda e, k=k: e.tensor_scalar(out=P3(k, 2), in0=P3(k, 1), scalar1=-1.0, scalar2=None,
                                                                  op0=ALU.mult), w=("PR",))
                    FR = PR[:, :, 72]
                    FI = PR[:, :, 73]
                    pop("dve", lambda e: e.tensor_scalar(out=A(6), in0=P3(0, 0), scalar1=-1.0, scalar2=None, op0=ALU.add))
                    pop("dve", lambda e: e.tensor_tensor(out=A(7), in0=A(0), in1=A(0), op=ALU.mult))
                    pop("dve", lambda e: e.tensor_tensor(out=A(10), in0=A(1), in1=A(1), op=ALU.mult))
                    pop("dve", lambda e: e.tensor_tensor(out=A(7), in0=A(7), in1=A(10), op=ALU.add))
                    pop("dve", lambda e: e.reciprocal(out=A(7), in_=A(7)))
                    pop("dve", lambda e: e.tensor_tensor(out=A(10), in0=A(6), in1=A(0), op=ALU.mult))
                    pop("dve", lambda e: e.tensor_tensor(out=A(11), in0=P3(0, 1), in1=A(1), op=ALU.mult))
                    pop("dve", lambda e: e.tensor_tensor(out=A(10), in0=A(10), in1=A(11), op=ALU.add))
                    pop("dve", lambda e: e.tensor_tensor(out=FR, in0=A(10), in1=A(7), op=ALU.mult), w=("PR",))
                    pop("dve", lambda e: e.tensor_tensor(out=A(10), in0=P3(0, 1), in1=A(0), op=ALU.mult))
                    pop("dve", lambda e: e.tensor_tensor(out=A(11), in0=A(6), in1=A(1), op=ALU.mult))
                    pop("dve", lambda e: e.tensor_tensor(out=A(10), in0=A(10), in1=A(11), op=ALU.subtract))
                    pop("dve", lambda e: e.tensor_tensor(out=FI, in0=A(10), in1=A(7), op=ALU.mult), w=("PR",))
                    for ri, name in enumerate(("ssm_b_re", "ssm_b_im")):
                        for e_ in range(2):
                            f.dma("sp", bnat[e_ * 64:(e_ + 1) * 64, ri, :, :],
                                  bass.AP(dt_[name], i * 32 * 1024 + e_ * 1024,
                                          [[16, 64], [2048, 16], [1, 16]]), writes=["bnat"])

                    XALL = [["X0"] + [("XR", s_) for s_ in range(16)], ["X1"] + [("XI", s_) for s_ in range(16)]]

                    srcs = [(ewin_d[i][:, 4096 + j_ * 128:4096 + (j_ + 1) * 128], 8) for j_ in range(4)]
                    for tg_ in range(4):
                        srcs += [(glw_d[i][:, fo_ * 128:(fo_ + 1) * 128], 4) for fo_ in range(4)]
                        srcs += [(ewin_d[i][:, 4608 + fo_ * 128:4608 + (fo_ + 1) * 128], 8) for fo_ in range(4)]
                    ws = WS(srcs)

                    def PSC(q, k, c):
                        return PR[:, q, 3 * k + c:3 * k + c + 1]

                    for j in range(4):
                        for ri, name in enumerate(("ssm_c_re", "ssm_c_im")):
                            f.dma("sp", cnat[:, ri, :],
                                  bass.AP(dt_[name], i * 32 * 1024 + j * 8 * 1024, [[64, 128], [1, 64]]),
                                  writes=["cnat"])
                        for qq in range(4):
                            q = j * 4 + qq
                            for ri in range(2):
                                f.op("pool", lambda e: e.memset(padT[:], 0.0), writes=["padT"])
                                for e_ in range(2):
                                    co = 16 * (2 * qq + e_)
                                    f.op("dve", lambda e, e_=e_, co=co, ri=ri, q=q: e.tensor_copy(
                                        out=padT[e_ * 64:(e_ + 1) * 64, co:co + 16],
                                        in_=bnat[e_ * 64:(e_ + 1) * 64, ri, q, :]),
                                        reads=["bnat"], writes=["padT"])
                                f.op("pe", lambda e: e.matmul(ps[6][:, 0:128], lhsT=padT[:], rhs=ident[:], start=True, stop=True),
                                     reads=["padT", "ident"], writes=[PS[6]])
                                f.op("act", lambda e, qq=qq, ri=ri: e.copy(out=Bpad[:, qq, ri, :], in_=ps[6][:, 0:128]),
                                     reads=[PS[6]], writes=["Bpad"])
                            for ri in range(2):
                                for e_ in range(2):
                                    g8 = 2 * qq + e_
                                    f.op("dve", lambda e, e_=e_, ri=ri, g8=g8: e.tensor_scalar(
                                        out=padTf[:, ri, e_ * 64:(e_ + 1) * 64], in0=cnat[:, ri, :],
                                        scalar1=gmask[:, g8:g8 + 1], scalar2=None, op0=ALU.mult),
                                        reads=["cnat", "gmask"], writes=["padTf"])
                            for ri in range(2):
                                f.op("pe", lambda e, ri=ri: e.matmul(ps[4][:, ri * 128:(ri + 1) * 128], lhsT=padTf[:, ri, :],
                                                                     rhs=identf[:], start=True, stop=True),
                                     reads=["padTf", "identf"], writes=[PS[4]])
                            fr = PR[:, q, 72:73]
                            fi = PR[:, q, 73:74]
                            f.op("act", lambda e: e.copy(out=cf[:, 0:2, :], in_=ps[4][:, 0:256].rearrange("p (a b) -> p a b", a=2)),
                                 reads=[PS[4]], writes=["cf"])
                            f.op("dve", lambda e, fr=fr: e.tensor_scalar(out=cf[:, 2, :], in0=cf[:, 0, :], scalar1=fr, scalar2=None,
                                                                         op0=ALU.mult), reads=["cf", "PR"], writes=["cf"])
                            f.op("dve", lambda e, fi=fi: e.tensor_scalar(out=cf[:, 3, :], in0=cf[:, 1, :], scalar1=fi, scalar2=None,
                                                                         op0=ALU.mult), reads=["cf", "PR"], writes=["cf"])
                            f.op("dve", lambda e, qq=qq: e.tensor_tensor(out=Cpad[:, qq, 0, :], in0=cf[:, 2, :], in1=cf[:, 3, :],
                                                                         op=ALU.subtract), reads=["cf"], writes=["Cpad"])
                            f.op("dve", lambda e, fi=fi: e.tensor_scalar(out=cf[:, 2, :], in0=cf[:, 0, :], scalar1=fi, scalar2=-1.0,
                                                                         op0=ALU.mult, op1=ALU.mult), reads=["cf", "PR"], writes=["cf"])
                            f.op("dve", lambda e, fr=fr: e.tensor_scalar(out=cf[:, 3, :], in0=cf[:, 1, :], scalar1=fr, scalar2=None,
                                                                         op0=ALU.mult), reads=["cf", "PR"], writes=["cf"])
                            f.op("dve", lambda e, qq=qq: e.tensor_tensor(out=Cpad[:, qq, 1, :], in0=cf[:, 2, :], in1=cf[:, 3, :],
                                                                         op=ALU.subtract), reads=["cf"], writes=["Cpad"])
                        wb, wk = ws.get()
                        for tg in range(4):
                            sl = slice(tg * 512, (tg + 1) * 512)
                            proj_fm(wb, wk, 4, tg)
                            f.op("act", lambda e, sl=sl: e.copy(out=uT[:, sl], in_=ps[4][:]), reads=[PS[4]], writes=["uT"])
                        def stage_in(qq_, ri):
                            X = (XR, XI)[ri]
                            for tg in range(4):
                                sl = slice(tg * 512, (tg + 1) * 512)
                                pk = 4 + (tg % 2)
                                f.op("pe", lambda e, sl=sl, pk=pk: e.matmul(
                                    ps[pk][:], lhsT=Bpad[:, qq_, ri, :], rhs=uT[:, sl], start=True, stop=True),
                                    reads=["Bpad", "uT"], writes=[PS[pk]])
                                f.op("act", lambda e, sl=sl, pk=pk, X=X: e.copy(out=X[:, sl], in_=ps[pk][:]),
                                     reads=[PS[pk]], writes=XALL[ri])

                        def stage_out(qq_, ri):
                            X = (XR, XI)[ri]
                            for tg in range(4):
                                sl = slice(tg * 512, (tg + 1) * 512)
                                bi = (ri * 4 + tg) % 2
                                cast(Xb[bi][:], X[:, sl], XALL[ri], ["Xb%d" % bi])
                                f.op("pe", lambda e, tg=tg, bi=bi: e.matmul(
                                    ps[tg][:], lhsT=Cpad[:, qq_, ri, :], rhs=Xb[bi][:],
                                    start=(qq_ == 0 and ri == 0), stop=(qq_ == 3 and ri == 1)),
                                    reads=["Cpad", "Xb%d" % bi], writes=[PS[tg]])

                        stage_in(0, 0)
                        stage_in(0, 1)
                        for qq in range(4):
                            q = j * 4 + qq
                            XRv = XR[:].rearrange("p (c s) -> p c s", s=16)
                            XIv = XI[:].rearrange("p (c s) -> p c s", s=16)

                            def _kl(k_):
                                return list(k_) if isinstance(k_, list) else [k_]

                            def cstep(oR, oI, iR, iI, k, kOR, kOI, kIR, kII, bR=None, bI=None, kB=()):
                                for (o_, i_, c_, ko, ki, b_) in ((oR, iR, 0, kOR, kIR, bR), (oI, iR, 1, kOI, kIR, bI),
                                                                 (oR, iI, 2, kOR, kII, None), (oI, iI, 0, kOI, kII, None)):
                                    add_ = o_ if b_ is None else b_
                                    f.op("dve", lambda e, o_=o_, i_=i_, c_=c_, add_=add_: e.scalar_tensor_tensor(
                                        out=o_, in0=i_, scalar=PSC(q, k, c_), in1=add_, op0=ALU.mult, op1=ALU.add),
                                        reads=_kl(ki) + ["PR"] + list(kB), writes=_kl(ko))
                            XRc = XR[:].rearrange("p (n r) -> p n r", r=4)
                            XIc = XI[:].rearrange("p (n r) -> p n r", r=4)
                            XR4 = XR[:].rearrange("p (c m r) -> p c m r", m=4, r=4)
                            XI4 = XI[:].rearrange("p (c m r) -> p c m r", m=4, r=4)
                            for r_ in range(1, 4):
                                ko_ = [r_ + 4 * m_ for m_ in range(4)]
                                ki_ = [r_ - 1 + 4 * m_ for m_ in range(4)]
                                cstep(XRc[:, :, r_], XIc[:, :, r_], XRc[:, :, r_ - 1], XIc[:, :, r_ - 1], 0,
                                      [("XR", c_) for c_ in ko_], [("XI", c_) for c_ in ko_],
                                      [("XR", c_) for c_ in ki_], [("XI", c_) for c_ in ki_])
                            for m_ in range(1, 4):
                                so_ = 4 * m_ + 3
                                si_ = 4 * m_ - 1
                                cstep(XRv[:, :, so_], XIv[:, :, so_], XRv[:, :, si_], XIv[:, :, si_], 3,
                                      ("XR", so_), ("XI", so_), ("XR", si_), ("XI", si_))
                            cur = 0
                            f.op("act", lambda e: e.copy(out=cs_[0][:, 0, 64:192], in_=XRv[:, :, 15]),
                                 reads=[("XR", 15)], writes=[("cs", 0, 0)])
                            f.op("act", lambda e: e.copy(out=cs_[0][:, 1, 64:192], in_=XIv[:, :, 15]),
                                 reads=[("XI", 15)], writes=[("cs", 0, 1)])
                            for k in range(7):
                                sh = 1 << k
                                src = cs_[cur]
                                dst = cs_[1 - cur]
                                cstep(dst[:, 0, 64:192], dst[:, 1, 64:192], src[:, 0, 64 - sh:192 - sh], src[:, 1, 64 - sh:192 - sh],
                                      15 + k, ("cs", 1 - cur, 0), ("cs", 1 - cur, 1), ("cs", cur, 0), ("cs", cur, 1),
                                      bR=src[:, 0, 64:192], bI=src[:, 1, 64:192])
                                cur = 1 - cur
                            fin = cs_[cur]
                            for m_ in range(4):
                                s_ = 4 * m_ + 3
                                cstep(XRv[:, 1:128, s_], XIv[:, 1:128, s_], fin[:, 0, 64:191], fin[:, 1, 64:191], s_,
                                      ("XR", s_), ("XI", s_), ("cs", cur, 0), ("cs", cur, 1))
                            ENDK = [[("XR", 3), ("XR", 7), ("XR", 11)], [("XI", 3), ("XI", 7), ("XI", 11)]]

                            def p3_m(comp, part):
                                X4 = (XR4, XI4)[comp]
                                kx = ("XR", "XI")[comp]
                                cidx = ((0, 2), (1, 0))[comp]
                                E4 = (XR4, XI4)[part]
                                for r_ in range(3):
                                    f.op("dve", lambda e, r_=r_: e.scalar_tensor_tensor(
                                        out=X4[:, :, 1:4, r_], in0=E4[:, :, 0:3, 3],
                                        scalar=PSC(q, r_, cidx[part]), in1=X4[:, :, 1:4, r_],
                                        op0=ALU.mult, op1=ALU.add),
                                        reads=ENDK[part] + ["PR"], writes=[(kx, 4 * m_ + r_) for m_ in range(1, 4)])

                            def p3_0(comp):
                                Xv = (XRv, XIv)[comp]
                                kx = ("XR", "XI")[comp]
                                cidx = ((0, 2), (1, 0))[comp]
                                for part in range(2):
                                    for r_ in range(3):
                                        f.op("dve", lambda e, r_=r_, part=part: e.scalar_tensor_tensor(
                                            out=Xv[:, 1:128, r_], in0=fin[:, part, 64:191],
                                            scalar=PSC(q, r_, cidx[part]), in1=Xv[:, 1:128, r_],
                                            op0=ALU.mult, op1=ALU.add),
                                            reads=[("cs", cur, part), "PR"], writes=[(kx, r_)])
                            p3_m(0, 0)
                            p3_m(0, 1)
                            p3_0(0)
                            p3_m(1, 0)
                            stage_out(qq, 0)
                            if qq < 3:
                                stage_in(qq + 1, 0)
                            p3_m(1, 1)
                            p3_0(1)
                            stage_out(qq, 1)
                            if qq < 3:
                                stage_in(qq + 1, 1)
                        for tg in range(4):
                            sl = slice(tg * 512, (tg + 1) * 512)
                            f.op("dve", lambda e, sl=sl, tg=tg, j=j: e.scalar_tensor_tensor(out=gl[0][:], in0=uT[:, sl],
                                                                                            scalar=dvec[:, j:j + 1], in1=ps[tg][:],
                                                                                            op0=ALU.mult, op1=ALU.add),
                                 reads=[PS[tg], "uT", "dvec"], writes=["gl0"])
                            f.op("pool", lambda e: e.tensor_tensor(out=gl[1][:], in0=gl[0][:], in1=gl[0][:], op=ALU.mult),
                                 reads=["gl0"], writes=["gl1"])
                            f.op("pool", lambda e: e.tensor_scalar(out=gl[1][:], in0=gl[1][:], scalar1=0.044715, scalar2=1.0,
                                                                   op0=ALU.mult, op1=ALU.add), reads=["gl1"], writes=["gl1"])
                            f.op("pool", lambda e: e.tensor_tensor(out=gl[1][:], in0=gl[1][:], in1=gl[0][:], op=ALU.mult),
                                 reads=["gl1", "gl0"], writes=["gl1"])
                            f.op("act", lambda e: e.activation(out=gl[2][:], in_=gl[1][:], func=AF.Sigmoid,
                                                               scale=2.0 * math.sqrt(2.0 / math.pi)),
                                 reads=["gl1"], writes=["gl2"] + XALL[0])
                            f.op("dve", lambda e, sl=sl, j=j: e.tensor_tensor(out=mT[:, 8 + j, sl], in0=gl[0][:], in1=gl[2][:],
                                                                              op=ALU.mult),
                                 reads=["gl0", "gl2"] + XALL[0], writes=[("mT", 8 + j)])
                    for tg in range(4):
                        sl = slice(tg * 512, (tg + 1) * 512)
                        for fo in range(4):
                            wb, wk = ws.get()
                            for jj in range(4):
                                f.op("pe", lambda e, fo=fo, jj=jj, sl=sl: e.matmul(
                                    ps[fo][:], lhsT=wb[:, jj, :], rhs=mT[:, 8 + jj, sl],
                                    start=(jj == 0), stop=(jj == 3)),
                                    reads=[wk, ("mT", 8 + jj)], writes=[PS[fo]], inc=(jj == 3))
                        for fo in range(4):
                            wb, wk = ws.get()
                            pgz = 4 + (fo % 2)
                            proj_fm(wb, wk, pgz, tg)
                            f.op("act", lambda e: e.activation(out=gl[0][:], in_=ps[pgz][:], func=AF.Sigmoid),
                                 reads=[PS[pgz]], writes=["gl0"])
                            f.op("act", lambda e, fo=fo: e.activation(out=gl[2][:], in_=ps[fo][:], func=AF.Sigmoid,
                                                                      bias=glub[:, fo:fo + 1]),
                                 reads=[PS[fo], "glub"], writes=["gl2"] + XALL[0])
                            f.op("pool", lambda e: e.tensor_tensor(out=gl[1][:], in0=gl[2][:], in1=gl[0][:], op=ALU.mult),
                                 reads=["gl2", "gl0"] + XALL[0], writes=["gl1"])
                            f.op("dve", lambda e: e.tensor_tensor(out=gl[1][:], in0=ps[pgz][:], in1=gl[1][:], op=ALU.mult),
                                 reads=[PS[pgz], "gl1"], writes=["gl1"])
                            f.op("dve", lambda e, fo=fo, sl=sl: e.tensor_tensor(out=mT[:, 8 + fo, sl], in0=mT[:, 8 + fo, sl],
                                                                                in1=gl[1][:], op=ALU.mult),
                                 reads=["gl1", ("mT", 8 + fo)] + [PS[k] for k in range(4)], writes=[("mT", 8 + fo)])
                    f.barrier()

                with ExitStack() as os_:
                    wout = os_.enter_context(nc.sbuf_tensor(nk("ewout"), [128, 12, D], BF16))
                    ytmp_box[0] = os_.enter_context(nc.sbuf_tensor(nk("ytmp"), [128, 2, 512], F32))
                    woutk = load_big(wout, "ewout", ewout_d[i], 12)
                    f.dma("sp", gain[:], bcast("post_norm", l * D, D), writes=["gain"])
                    for b in range(NB):
                        outproj_block(b, lambda kc, b=b: mT[:, kc, b * 128:(b + 1) * 128],
                                      [("mT", ch) for ch in range(12)], 12, wout, woutk, l)
                    f.barrier()

        for l in layers:
            if l % 2 == 0:
                even_layer(l)
            else:
                odd_layer(l)

        for b in range(NB):
            f.dma("sp", out_d[b * 128:(b + 1) * 128, :], xres[:, b, :], reads=[("x", b)], key="out")
        nc.sync.wait_ge(f.dsem["out"][0], f.dsem["out"][1])
    return nc


_CACHE = {}


def prep_inputs(inputs):
    w = {k: np.ascontiguousarray(np.asarray(v, dtype=np.float32)) for k, v in inputs.items()}
    perm = swap_perm()
    shared = {k: v for k, v in w.items() if k != "x"}
    shared["even_w_sw"] = np.ascontiguousarray(w["even_w_in"][:, :, 0:2048][:, :, perm])
    shared.update(host_consts())
    return w["x"], shared


def kernel(**inputs):
    x, shared = prep_inputs(inputs)
    if "nc" not in _CACHE:
        _CACHE["nc"] = build()
    nc = _CACHE["nc"]
    in_maps = []
    for c in range(8):
        m = dict(shared)
        m["x"] = np.ascontiguousarray(x[c])
        in_maps.append(m)
    res = run_bass_kernel_spmd(nc, in_maps, core_ids=list(range(8)))
    return np.stack([np.asarray(r["out"], dtype=np.float32) for r in res.results], axis=0)
```

```python
import math
import numpy as np
import concourse.bass as bass
import concourse.mybir as mybir
from concourse.bass_utils import run_bass_kernel_spmd

F32 = mybir.dt.float32
BF16 = mybir.dt.bfloat16
I32 = mybir.dt.int32
ALU = mybir.AluOpType
AF = mybir.ActivationFunctionType
AX = mybir.AxisListType

S = 2048
D = 1024
NB = 16
EPS = 1e-6
TWO_PI = 2.0 * math.pi


class FW:
    def __init__(self, nc):
        self.nc = nc
        self.eng = {"pe": nc.tensor, "act": nc.scalar, "dve": nc.vector,
                    "pool": nc.gpsimd, "sp": nc.sync}
        self.sem = {}
        self.cnt = {}
        for e in self.eng:
            self.sem[e] = nc.alloc_semaphore("s_" + e)
            self.cnt[e] = 0
        self.waited = {}
        self.lastw = {}
        self.rd = {}
        self.dsem = {}

    def _deps(self, reads, writes):
        deps = {}

        def add(t):
            if t is not None and deps.get(t[0], 0) < t[1]:
                deps[t[0]] = t[1]
        for k in reads:
            add(self.lastw.get(k))
        for k in writes:
            add(self.lastw.get(k))
            for sk, v in self.rd.get(k, {}).items():
                add((sk, v))
        return deps

    def _semof(self, sk):
        if isinstance(sk, tuple):
            return self.dsem[sk[1]][0]
        return self.sem[sk]

    def _emit_waits(self, e, deps):
        for sk, v in deps.items():
            if sk == e and v > self.cnt[e]:
                continue
            if self.waited.get((e, sk), 0) < v:
                self.eng[e].wait_ge(self._semof(sk), v)
                self.waited[(e, sk)] = v

    def _record(self, t, reads, writes):
        for k in writes:
            self.lastw[k] = t
            self.rd[k] = {}
        for k in reads:
            d = self.rd.setdefault(k, {})
            if d.get(t[0], 0) < t[1]:
                d[t[0]] = t[1]

    def op(self, e, fn, reads=(), writes=(), inc=True):
        self._emit_waits(e, self._deps(reads, writes))
        ins = fn(self.eng[e])
        if inc:
            ins.then_inc(self.sem[e], 1)
            self.cnt[e] += 1
            t = (e, self.cnt[e])
        else:
            t = (e, self.cnt[e] + 1)
        self._record(t, reads, writes)
        return ins

    def dma(self, q, out, in_, reads=(), writes=(), key=None, **kw):
        if key is None:
            key = writes[0] if writes else reads[0]
        if key not in self.dsem:
            self.dsem[key] = [self.nc.alloc_semaphore("d%d" % len(self.dsem)), 0]
        self._emit_waits(q, self._deps(reads, writes))
        ins = self.eng[q].dma_start(out=out, in_=in_, **kw)
        ins.then_inc(self.dsem[key][0], 16)
        self.dsem[key][1] += 16
        t = (("dma", key), self.dsem[key][1])
        self._record(t, reads, writes)
        return ins

    def barrier(self):
        for e in self.eng:
            deps = {}
            for o in self.eng:
                if o != e and self.cnt[o] > 0:
                    deps[o] = self.cnt[o]
            for key, (s, c) in self.dsem.items():
                if c > 0:
                    deps[("dma", key)] = c
            self._emit_waits(e, deps)


def _mult(d):
    d = np.asarray(d)
    m = ((d >= 0) & (d <= 128)).astype(np.float32)
    m += ((d >= 0) & (d <= 512) & (d % 4 == 0)).astype(np.float32)
    m += ((d >= 0) & (d % 16 == 0)).astype(np.float32)
    return m


def host_consts():
    c = {}
    c["c_ident"] = np.eye(128, dtype=np.float32)
    j = np.arange(128)[:, None]
    t = np.arange(128)[None, :]
    c["c_mask"] = np.concatenate([_mult(128 * dl + t - j) for dl in range(-3, 16)], axis=1).astype(np.float32)
    bands = np.zeros((128, 4, 3, 128), np.float32)
    tp = np.arange(128)[:, None]
    tt = np.arange(128)[None, :]
    for wi, w in enumerate((2, 4, 8, 16)):
        dcur = tt - tp
        bands[:, wi, 0, :] = ((dcur >= 0) & (dcur < w)) / float(w) - (dcur == 0)
        dprev = tt + 128 - tp
        bands[:, wi, 1, :] = ((dprev >= 0) & (dprev < w)) / float(w)
        cnt = np.minimum(tt + 1, w).astype(np.float32)
        bands[:, wi, 2, :] = ((dcur >= 0) & (dcur < w)) / cnt - (dcur == 0)
    c["c_bands"] = bands.reshape(128, 4 * 3 * 128)
    half = 8
    inv = 500000.0 ** (-np.arange(0, 16, 2, dtype=np.float32) / 16.0)
    ang = np.arange(S, dtype=np.float32)[None, :] * inv[:, None]
    C = np.ones((64, S), np.float32)
    Sg = np.zeros((64, S), np.float32)
    C[0:8] = np.cos(ang)
    C[8:16] = np.cos(ang)
    Sg[0:8] = -np.sin(ang)
    Sg[8:16] = np.sin(ang)
    c["c_ropec"] = np.concatenate([C, C], 0)
    c["c_ropes"] = np.concatenate([Sg, Sg], 0)
    gm = np.zeros((128, 8), np.float32)
    for g in range(8):
        gm[g * 16:(g + 1) * 16, g] = 1.0
    c["c_gmask"] = gm
    return c


def swap_perm():
    perm = np.arange(2048)
    for blk in range(2048 // 64):
        b = blk * 64
        perm[b:b + 8] = np.arange(b + 8, b + 16)
        perm[b + 8:b + 16] = np.arange(b, b + 8)
    return perm


def build(layers=(0, 1, 2, 3)):
    nc = bass.Bass("TRN2", target_bir_lowering=False)
    dt_ = {}

    def din(name, shape):
        h = nc.dram_tensor(name, list(shape), F32, kind="ExternalInput")
        dt_[name] = h
        return h.ap()

    x_d = din("x", [S, D])
    pre_d = din("pre_norm", [4, D])
    post_d = din("post_norm", [4, D])
    ewin_d = din("even_w_in", [2, D, 5120])
    ewsw_d = din("even_w_sw", [2, D, 2048])
    ewout_d = din("even_w_out", [2, 1536, D])
    are_d = din("ssm_a_re", [2, 32, 64])
    aim_d = din("ssm_a_im", [2, 32, 64])
    ldt_d = din("ssm_log_dt", [2, 32])
    bre_d = din("ssm_b_re", [2, 32, 64, 16])
    bim_d = din("ssm_b_im", [2, 32, 64, 16])
    cre_d = din("ssm_c_re", [2, 32, 16, 64])
    cim_d = din("ssm_c_im", [2, 32, 16, 64])
    sd_d = din("ssm_d", [2, 512])
    glw_d = din("ssm_glu_w", [2, 512, 512])
    glb_d = din("ssm_glu_b", [2, 512])
    owin_d = din("odd_w_in", [2, D, 4096])
    pw_d = din("pool_w", [2, 4, 512, 512])
    psc_d = din("pool_scale", [2, 2048])
    owout_d = din("odd_w_out", [2, 2048, D])
    cid_d = din("c_ident", [128, 128])
    cmask_d = din("c_mask", [128, 19 * 128])
    cband_d = din("c_bands", [128, 12 * 128])
    cropec_d = din("c_ropec", [128, S])
    cropes_d = din("c_ropes", [128, S])
    cgm_d = din("c_gmask", [128, 8])
    out_d = nc.dram_tensor("out", [S, D], F32, kind="ExternalOutput").ap()

    f = FW(nc)
    uid = [0]

    def nk(p):
        uid[0] += 1
        return "%s%d" % (p, uid[0])

    def bcast(name, off, n):
        return bass.AP(dt_[name], off, [[0, 128], [1, n]])

    from contextlib import ExitStack
    with ExitStack() as es:
        def sb(name, shape, dtype):
            return es.enter_context(nc.sbuf_tensor(name, list(shape), dtype))

        def psb(name, shape, dtype):
            return es.enter_context(nc.psum_tensor(name, list(shape), dtype))

        xres = sb("xres", [128, NB, D], F32)
        ps = [psb("ps%d" % i, [128, 512], F32) for i in range(8)]
        PS = ["ps%d" % i for i in range(8)]
        ident = sb("ident", [128, 128], BF16)
        identf = sb("identf", [128, 128], F32)
        gain = sb("gain", [128, D], F32)
        junk = sb("junk", [128, D], BF16)
        stat = sb("stat", [128, 64], F32)
        gmask = sb("gmask", [128, 8], F32)
        wbf = [sb("wbf%d" % i, [128, 8, 128], BF16) for i in range(6)]
        wctr = [0, 0]

        f.dma("sp", identf[:], cid_d, writes=["identf"])
        f.op("dve", lambda e: e.tensor_copy(out=ident[:], in_=identf[:]), reads=["identf"], writes=["ident"])
        f.dma("sp", gmask[:], cgm_d, writes=["gmask"])
        for b in range(NB):
            f.dma("sp", xres[:, b, :], x_d[b * 128:(b + 1) * 128, :], writes=[("x", b)])

        cast_rr = [0]

        def cast(out, in_, reads, writes):
            e = "act"
            if e == "pool":
                f.op("pool", lambda g: g.tensor_copy(out=out, in_=in_), reads=reads, writes=writes)
            else:
                f.op("act", lambda g: g.copy(out=out, in_=in_), reads=reads, writes=writes)

        class WS:
            DEPTH = 2

            def __init__(self, srcs):
                self.srcs = srcs
                self.issued = 0
                self.cur = 0
                self.buf = {}

            def prefetch(self, n):
                self._issue_to(self.cur + n)

            def get(self, issue=True):
                if issue:
                    self._issue_to(self.cur + 1 + WS.DEPTH)
                assert self.cur < self.issued
                bi = self.buf[self.cur]
                self.cur += 1
                return wbf[bi], "wbf%d" % bi

            def _issue_to(self, lim):
                while self.issued < min(len(self.srcs), lim):
                    src, kc = self.srcs[self.issued]
                    bi = wctr[1] % 6
                    wctr[1] += 1
                    f.dma("pool", wbf[bi][:, 0:kc, :], src.rearrange("(k p) n -> p k n", p=128),
                          writes=["wbf%d" % bi])
                    self.buf[self.issued] = bi
                    self.issued += 1

        def load_big(dst, dkey, src_ap, kc, step=4):
            keys = []
            for k0 in range(0, kc, step):
                k1 = min(kc, k0 + step)
                f.dma("pool", dst[:, k0:k1, :], src_ap[k0 * 128:k1 * 128, :].rearrange("(k p) n -> p k n", p=128),
                      writes=[(dkey, k0)])
                keys.append((dkey, k0))
            return keys

        def load_big_hw(dst, dkey, src_ap, kc, stg, skey):
            keys = []
            for k0 in range(0, kc, 2):
                f.dma("sp", stg[:, :, :], src_ap[k0 * 128:(k0 + 2) * 128, :].rearrange("(k p) n -> p k n", p=128),
                      writes=[skey])
                f.op("act", lambda g: g.copy(out=dst[:, k0:k0 + 2, :], in_=stg[:, :, :]), reads=[skey], writes=[(dkey, k0)])
                keys.append((dkey, k0))
            return keys

        scol = [0]

        def newcol():
            scol[0] = (scol[0] + 1) % 64
            return scol[0]

        def rstd_from_ss(c_ss):
            c1 = newcol()
            c2 = newcol()
            f.op("act", lambda e: e.activation(out=stat[:, c1:c1 + 1], in_=stat[:, c_ss:c_ss + 1], func=AF.Sqrt,
                                               scale=1.0 / D, bias=EPS),
                 reads=[("st", c_ss)], writes=[("st", c1)])
            f.op("dve", lambda e: e.reciprocal(out=stat[:, c2:c2 + 1], in_=stat[:, c1:c1 + 1]),
                 reads=[("st", c1)], writes=[("st", c2)])
            return c2

        def prenorm_block(b, dst, dkey):
            c = newcol()
            f.op("act", lambda e: e.activation(out=junk[:], in_=xres[:, b, :], func=AF.Square,
                                               accum_out=stat[:, c:c + 1]),
                 reads=[("x", b)], writes=["junk", ("st", c)])
            c2 = rstd_from_ss(c)
            f.op("dve", lambda e: e.scalar_tensor_tensor(out=dst, in0=xres[:, b, :], scalar=stat[:, c2:c2 + 1],
                                                         in1=gain[:], op0=ALU.mult, op1=ALU.mult),
                 reads=[("x", b), ("st", c2), "gain"], writes=[dkey])

        def transpose_block(src, skey, dst_ap, dkey, pi):
            for hf in range(2):
                bk = 4 + 2 * pi + hf
                for cc in range(4):
                    c = hf * 4 + cc
                    f.op("pe", lambda e, c=c, cc=cc, bk=bk: e.matmul(ps[bk][:, cc * 128:(cc + 1) * 128],
                                                                     lhsT=src[:, c * 128:(c + 1) * 128], rhs=ident[:],
                                                                     start=True, stop=True),
                         reads=[skey, "ident"], writes=[PS[bk]], inc=(cc == 3))
                f.op("act", lambda e, hf=hf, bk=bk: e.copy(out=dst_ap[:, hf * 4:(hf + 1) * 4, :],
                                                           in_=ps[bk][:].rearrange("p (c t) -> p c t", c=4)),
                     reads=[PS[bk]], writes=[dkey])

        def outproj_block(b, mT_fn, mkeys, KC, wout, wkey, l):
            pp = (b % 2) * 2
            for fh in range(2):
                for kc in range(KC):
                    f.op("pe", lambda e, kc=kc, fh=fh: e.matmul(ps[pp + fh][:], lhsT=mT_fn(kc),
                                                                rhs=wout[:, kc, fh * 512:(fh + 1) * 512],
                                                                start=(kc == 0), stop=(kc == KC - 1)),
                         reads=list(mkeys) + list(wkey), writes=[PS[pp + fh]], inc=(kc == KC - 1))
            ca = newcol()
            cb = newcol()
            f.op("act", lambda e: e.activation(out=junk[:, 0:512], in_=ps[pp][:], func=AF.Square,
                                               accum_out=stat[:, ca:ca + 1]),
                 reads=[PS[pp]], writes=["junk", ("st", ca)])
            f.op("act", lambda e: e.activation(out=junk[:, 512:1024], in_=ps[pp + 1][:], func=AF.Square,
                                               accum_out=stat[:, cb:cb + 1]),
                 reads=[PS[pp + 1]], writes=["junk", ("st", cb)])
            cs = newcol()
            f.op("dve", lambda e: e.tensor_tensor(out=stat[:, cs:cs + 1], in0=stat[:, ca:ca + 1],
                                                  in1=stat[:, cb:cb + 1], op=ALU.add),
                 reads=[("st", ca), ("st", cb)], writes=[("st", cs)])
            c2 = rstd_from_ss(cs)
            for fh in range(2):
                tk = "ytmp%d" % fh
                f.op("dve", lambda e, fh=fh: e.scalar_tensor_tensor(out=ytmp_box[0][:, fh, :], in0=ps[pp + fh][:],
                                                                    scalar=stat[:, c2:c2 + 1],
                                                                    in1=gain[:, fh * 512:(fh + 1) * 512],
                                                                    op0=ALU.mult, op1=ALU.mult),
                     reads=[PS[pp + fh], ("st", c2), "gain"], writes=[tk])
                f.op("pool", lambda e, fh=fh: e.tensor_tensor(out=xres[:, b, fh * 512:(fh + 1) * 512],
                                                              in0=xres[:, b, fh * 512:(fh + 1) * 512],
                                                              in1=ytmp_box[0][:, fh, :], op=ALU.add),
                     reads=[("x", b), tk], writes=[("x", b)])

        ytmp_box = [None]

        def odd_layer(l):
            i = l // 2
            with ExitStack() as ls:
                def lsb(name, shape, dtype):
                    return ls.enter_context(nc.sbuf_tensor(nk(name), list(shape), dtype))
                hN = lsb("hN", [128, 5, D], BF16)
                hT = lsb("hTq", [128, 8, 512], BF16)
                phT = lsb("phT", [128, 8, 512], BF16)
                mixed = lsb("mixed", [128, 4, 512], BF16)
                mT = lsb("mTq", [128, 16, 512], BF16)
                sgb = [lsb("sg0", [128, 512], F32)] * 2
                bandf = lsb("bandf", [128, 12, 128], F32)
                band = lsb("band", [128, 4, 4, 128], BF16)
                btmp = lsb("btmp", [128, 4, 128], F32)
                pwbf = lsb("pwbf", [128, 4, 512], BF16)
                wout = lsb("wout", [128, 16, D], BF16)
                pscale = lsb("pscale", [128, 16], F32)
                wost = [lsb("wost%d" % k_, [128, 2, D], F32) for k_ in range(2)]
                pwst = lsb("pwst", [128, 4, 512], F32)
                ytmp_box[0] = lsb("ytmp", [128, 2, 512], F32)
                f.dma("sp", bandf[:], cband_d.rearrange("p (a t) -> p a t", t=128), writes=["bandf"])
                bv = bandf[:].rearrange("p (w k) t -> p w k t", k=3)
                f.op("dve", lambda e: e.tensor_copy(out=band[:, :, 0:3, :], in_=bv), reads=["bandf"], writes=["band"])
                f.op("dve", lambda e: e.tensor_copy(out=btmp[:], in_=band[:, :, 2, :]), reads=["band"], writes=["btmp"])
                f.op("dve", lambda e: e.tensor_tensor(out=btmp[:], in0=bv[:, :, 2, :], in1=btmp[:], op=ALU.subtract),
                     reads=["bandf", "btmp"], writes=["btmp"])
                f.op("dve", lambda e: e.tensor_copy(out=band[:, :, 3, :], in_=btmp[:]), reads=["btmp"], writes=["band"])
                f.dma("sp", pscale[:], psc_d[i].rearrange("(c p) -> p c", p=128), writes=["pscale"],
                      allow_slow_non_contiguous=True)
                srcs = []
                for tq in range(4):
                    for g in range(4):
                        for j in range(4):
                            srcs.append((owin_d[i][:, g * 512 + j * 128:g * 512 + (j + 1) * 128], 8))
                        for dj in range(4):
                            col = 2048 + g * 512 + dj * 128
                            srcs.append((owin_d[i][:, col:col + 128], 8))
                ws = WS(srcs)
                for tq in range(4):
                    b0 = tq * 4
                    woutk = [("wout", 2 * k_) for k_ in range(8)]
                    wstep = [0]

                    def wout_step():
                        k_ = wstep[0]
                        wstep[0] += 1
                        if 1 <= k_ <= 8:
                            kk = k_ - 1
                            f.op("act", lambda g_: g_.copy(out=wout[:, 2 * kk:2 * kk + 2, :], in_=wost[kk % 2][:]),
                                 reads=["wost%d" % (kk % 2)], writes=[("wout", 2 * kk)])
                        if k_ < 8:
                            f.dma("sp", wost[k_ % 2][:],
                                  owout_d[i][k_ * 256:(k_ + 1) * 256, :].rearrange("(k p) n -> p k n", p=128),
                                  writes=["wost%d" % (k_ % 2)])
                    wout_step()
                    f.dma("sp", gain[:], bcast("pre_norm", l * D, D), writes=["gain"])
                    if tq > 0:
                        f.op("pool", lambda e: e.tensor_copy(out=hN[:, 0, :], in_=hN[:, 4, :]),
                             reads=[("hN", 4)], writes=[("hN", 0)])
                    for s_, b in enumerate(range(b0 - 1, b0 + 4)):
                        if s_ == 0:
                            continue
                        prenorm_block(b, hN[:, s_, :], ("hN", s_))
                    for s_ in range(1, 5):
                        transpose_block(hN[:, s_, :], ("hN", s_), hT[:, :, (s_ - 1) * 128:s_ * 128], "hT", s_ % 2)
                    for g in range(4):
                        f.dma("sp", pwst[:], pw_d[i, g].rearrange("(k p) n -> p k n", p=128), writes=["pwst"])
                        pwk = [("pwbf", 0), ("pwbf", 2)]
                        for s_ in range(1, 5):
                            b = b0 + s_ - 1
                            for hf in range(2):
                                pk = 4 + hf
                                for cc in range(4):
                                    c = hf * 4 + cc
                                    o = ps[pk][:, cc * 128:(cc + 1) * 128]
                                    lh = hN[:, s_, c * 128:(c + 1) * 128]
                                    if b == 0:
                                        f.op("pe", lambda e, o=o, lh=lh: e.matmul(o, lhsT=lh, rhs=band[:, g, 2, :],
                                                                                  start=True, stop=False),
                                             reads=[("hN", s_), "band"], writes=[PS[pk]], inc=False)
                                        f.op("pe", lambda e, o=o, lh=lh: e.matmul(o, lhsT=lh, rhs=band[:, g, 3, :],
                                                                                  start=False, stop=True),
                                             reads=[("hN", s_), "band"], writes=[PS[pk]], inc=(cc == 3))
                                    else:
                                        lp = hN[:, s_ - 1, c * 128:(c + 1) * 128]
                                        f.op("pe", lambda e, o=o, lh=lh: e.matmul(o, lhsT=lh, rhs=band[:, g, 0, :],
                                                                                  start=True, stop=False),
                                             reads=[("hN", s_), "band"], writes=[PS[pk]], inc=False)
                                        f.op("pe", lambda e, o=o, lp=lp: e.matmul(o, lhsT=lp, rhs=band[:, g, 1, :],
                                                                                  start=False, stop=True),
                                             reads=[("hN", s_ - 1), "band"], writes=[PS[pk]], inc=(cc == 3))
                                f.op("act", lambda e, hf=hf, pk=pk: e.copy(
                                    out=phT[:, hf * 4:(hf + 1) * 4, (s_ - 1) * 128:s_ * 128],
                                    in_=ps[pk][:].rearrange("p (c t) -> p c t", c=4)),
                                    reads=[PS[pk]], writes=["phT"])
                        for j in range(4):
                            wout_step()
                            wb, wk = ws.get()
                            pk = j % 2
                            for kc in range(8):
                                f.op("pe", lambda e, kc=kc: e.matmul(ps[pk][:], lhsT=wb[:, kc, :], rhs=phT[:, kc, :],
                                                                     start=(kc == 0), stop=(kc == 7)),
                                     reads=[wk, "phT"], writes=[PS[pk]], inc=(kc == 7))
                            f.op("act", lambda e, j=j, pk=pk: e.copy(out=mixed[:, j, :], in_=ps[pk][:]),
                                 reads=[PS[pk]], writes=[("mixed", j)])
                        for k0_ in (0, 2):
                            f.op("act", lambda g_, k0_=k0_: g_.copy(out=pwbf[:, k0_:k0_ + 2, :], in_=pwst[:, k0_:k0_ + 2, :]),
                                 reads=["pwst"], writes=[("pwbf", k0_)])
                        for dj in range(4):
                            wb, wk = ws.get()
                            pg = 2 if dj % 2 == 0 else 6
                            sg = sgb[dj % 2]
                            sgk = "sg0"
                            for kc in range(8):
                                f.op("pe", lambda e, kc=kc: e.matmul(ps[pg][:], lhsT=wb[:, kc, :], rhs=hT[:, kc, :],
                                                                     start=(kc == 0), stop=(kc == 7)),
                                     reads=[wk, "hT"], writes=[PS[pg]], inc=(kc == 7))
                            f.op("act", lambda e: e.activation(out=sg[:], in_=ps[pg][:], func=AF.Silu),
                                 reads=[PS[pg]], writes=[sgk])
                            py = 3 if dj % 2 == 0 else 7
                            for j in range(4):
                                f.op("pe", lambda e, j=j, dj=dj: e.matmul(ps[py][:], lhsT=pwbf[:, j, dj * 128:(dj + 1) * 128],
                                                                          rhs=mixed[:, j, :], start=(j == 0), stop=(j == 3)),
                                     reads=pwk + [("mixed", j)], writes=[PS[py]], inc=(j == 3))
                            ch = g * 4 + dj
                            f.op("dve", lambda e, ch=ch: e.scalar_tensor_tensor(out=mT[:, ch, :], in0=ps[py][:],
                                                                                scalar=pscale[:, ch:ch + 1], in1=sg[:],
                                                                                op0=ALU.mult, op1=ALU.mult),
                                 reads=[PS[py], "pscale", sgk], writes=[("mT", ch)])
                    f.dma("sp", gain[:], bcast("post_norm", l * D, D), writes=["gain"])
                    for bb in range(4):
                        outproj_block(b0 + bb, lambda kc, bb=bb: mT[:, kc, bb * 128:(bb + 1) * 128],
                                      [("mT", ch) for ch in range(16)], 16, wout, woutk, l)
                f.barrier()

        def even_layer(l):
            i = l // 2
            with ExitStack() as ls:
                def lsb(name, shape, dtype):
                    return ls.enter_context(nc.sbuf_tensor(nk(name), list(shape), dtype))
                hT = lsb("hT", [128, 8, S], BF16)
                mT = lsb("mT", [128, 12, S], BF16)
                f.dma("sp", gain[:], bcast("pre_norm", l * D, D), writes=["gain"])
                with ExitStack() as hs_:
                    hNb = [hs_.enter_context(nc.sbuf_tensor(nk("hNb"), [128, D], BF16)) for k in range(2)]
                    for b in range(NB):
                        prenorm_block(b, hNb[b % 2][:], "hNb%d" % (b % 2))
                        transpose_block(hNb[b % 2], "hNb%d" % (b % 2), hT[:, :, b * 128:(b + 1) * 128], "hT", b % 2)
                    f.barrier()

                def proj_fm(wb, wk, pk, tg):
                    for kc in range(8):
                        f.op("pe", lambda e, kc=kc: e.matmul(ps[pk][:], lhsT=wb[:, kc, :],
                                                             rhs=hT[:, kc, tg * 512:(tg + 1) * 512],
                                                             start=(kc == 0), stop=(kc == 7)),
                             reads=[wk, "hT"], writes=[PS[pk]], inc=(kc == 7))

                with ExitStack() as as_:
                    def asb(name, shape, dtype):
                        return as_.enter_context(nc.sbuf_tensor(nk(name), list(shape), dtype))
                    qT = asb("qT", [128, S], BF16)
                    kT = asb("kT", [128, 2, S], BF16)
                    V = asb("V", [128, NB, 2, 128], BF16)
                    ropec = asb("ropec", [128, S], BF16)
                    ropes = asb("ropes", [128, S], BF16)
                    maskT = asb("maskT", [128, 19 * 128], BF16)
                    Pt = [asb("Pt%d" % k, [128, 512], BF16) for k in range(4)]
                    rtmp = [asb("rtmp%d" % k, [128, 512], F32) for k in range(2)]
                    rc = rtmp[0]
                    atmp = rtmp[1]
                    f.dma("pool", ropec[:], cropec_d, writes=["ropec"])
                    f.dma("pool", ropes[:], cropes_d, writes=["ropes"])
                    f.dma("pool", maskT[:], cmask_d, writes=["maskT"])
                    srcs = []
                    for hp_ in range(8):
                        c0_ = hp_ * 128
                        for (base_, swb_) in ((0, 0), (1024, 1024)):
                            srcs.append((ewin_d[i][:, base_ + c0_:base_ + c0_ + 128], 8))
                            srcs.append((ewsw_d[i][:, swb_ + c0_:swb_ + c0_ + 128], 8))
                        srcs.append((ewin_d[i][:, 3072 + c0_:3072 + c0_ + 128], 8))
                        srcs.append((ewin_d[i][:, 2048 + c0_:2048 + c0_ + 128], 8))
                    ws = WS(srcs)
                    f.op("pool", lambda e: e.memset(V[:, :, :, 64:128], 1.0), writes=["V"])
                    f.op("pool", lambda e: e.memset(kT[:], 0.0), writes=[("kT", 0)])
                    mrr = [0]
                    qTb = [qT, mT[:, 8, :]]
                    kTb = [kT, mT[:, 9:11, :]]
                    f.op("pool", lambda e: e.memset(mT[:, 9:11, :], 0.0), writes=[("mT", 9), ("mT", 10), ("kT", 1)])
                    QK = [["qT", ("mT", 8)], [("kT", 0), ("kT", 1), ("mT", 9), ("mT", 10)]]

                    def proj_gen(hp_, issue):
                        par = hp_ % 2
                        qd = qTb[par]
                        kd = kTb[par]
                        qk_ = [QK[0][par]]
                        kk_ = [("kT", par)] + ([("mT", 9), ("mT", 10)] if par == 1 else [])
                        for which in ("q", "k"):
                            wb, wk = ws.get(issue)
                            wb2, wk2 = ws.get(issue)
                            for tg in range(4):
                                sl = slice(tg * 512, (tg + 1) * 512)
                                proj_fm(wb, wk, 6, tg)
                                yield
                                proj_fm(wb2, wk2, 7, tg)
                                r0 = "rtmp0"
                                r1 = "rtmp1"
                                f.op("dve", lambda e, sl=sl: e.tensor_tensor(out=rtmp[0][:], in0=ps[6][:], in1=ropec[:, sl],
                                                                             op=ALU.mult),
                                     reads=[PS[6], "ropec"], writes=[r0])
                                f.op("dve", lambda e, sl=sl: e.tensor_tensor(out=rtmp[1][:], in0=ps[7][:], in1=ropes[:, sl],
                                                                             op=ALU.mult),
                                     reads=[PS[7], "ropes"], writes=[r1])
                                if which == "q":
                                    f.op("dve", lambda e, sl=sl: e.tensor_tensor(out=qd[:, sl], in0=rtmp[0][:],
                                                                                 in1=rtmp[1][:], op=ALU.add),
                                         reads=[r0, r1], writes=qk_)
                                else:
                                    for a_ in range(2):
                                        pr_ = slice(64 * a_, 64 * a_ + 64)
                                        f.op("dve", lambda e, sl=sl, a_=a_, pr_=pr_: e.tensor_tensor(
                                            out=kd[pr_, a_, sl], in0=rtmp[0][pr_, :], in1=rtmp[1][pr_, :], op=ALU.add),
                                            reads=[r0, r1], writes=kk_)
                                yield
                        wb, wk = ws.get(issue)
                        for tg in range(4):
                            pg_ = 6 + (tg % 2)
                            proj_fm(wb, wk, pg_, tg)
                            f.op("act", lambda e, tg=tg, pg_=pg_: e.activation(out=mT[:, hp_, tg * 512:(tg + 1) * 512],
                                                                               in_=ps[pg_][:], func=AF.Silu),
                                 reads=[PS[pg_]], writes=[("mT", hp_)])
                        yield

                    gen = proj_gen(0, True)
                    for _ in gen:
                        pass
                    gen = None
                    for hp in range(8):
                        par = hp % 2
                        qT_ = qTb[par]
                        kT_ = kTb[par]
                        qkeys = [QK[0][par]]
                        kkeys = [("kT", par)] + ([("mT", 9), ("mT", 10)] if par == 1 else [])
                        if gen is not None:
                            for _ in gen:
                                pass
                        wb, wk = ws.get(hp == 0)
                        for b4 in range(4):
                            pk = 6 + (b4 % 2)
                            for bb in range(4):
                                b = b4 * 4 + bb
                                for kc in range(8):
                                    f.op("pe", lambda e, kc=kc, b=b, bb=bb: e.matmul(
                                        ps[pk][:, bb * 128:(bb + 1) * 128], lhsT=hT[:, kc, b * 128:(b + 1) * 128],
                                        rhs=wb[:, kc, :], start=(kc == 0), stop=(kc == 7)),
                                        reads=[wk, "hT"], writes=[PS[pk]], inc=(kc == 7 and bb == 3))
                            f.op("act", lambda e, b4=b4: e.copy(
                                out=V[:, b4 * 4:(b4 + 1) * 4, :, 0:64],
                                in_=ps[pk][:].rearrange("p (b a d) -> p b a d", b=4, a=2)),
                                reads=[PS[pk]], writes=["V"])
                        ws.prefetch(6)
                        gen = proj_gen(hp + 1, False) if hp < 7 else None
                        items = []
                        for a in range(2):
                            for qg in range(4):
                                nkb = 4 * qg + 4
                                for kb in range(nkb):
                                    items.append((a, qg, kb, nkb))
                        LAG = 3

                        def stage1(idx):
                            a, qg, kb, nkb = items[idx]
                            pr = slice(64 * a, 64 * a + 64)
                            cq = max(128 * kb, 512 * qg)
                            N = 512 * (qg + 1) - cq
                            sk = idx % 4
                            f.op("pe", lambda e: e.matmul(
                                ps[sk][:, 0:N], lhsT=kT_[:, a, kb * 128:(kb + 1) * 128], rhs=qT_[:, cq:cq + N],
                                start=True, stop=True),
                                reads=kkeys + qkeys, writes=[PS[sk]])
                            f.op("act", lambda e: e.activation(out=Pt[sk][:, 0:N], in_=ps[sk][:, 0:N],
                                                               func=AF.Exp, scale=0.125),
                                 reads=[PS[sk]], writes=["Pt%d" % sk])
                            moff = ((cq - 128 * kb) // 128 + 3) * 128
                            me = "dve"
                            mrr[0] += 1
                            f.op(me, lambda e: e.tensor_tensor(
                                out=Pt[sk][:, 0:N], in0=Pt[sk][:, 0:N], in1=maskT[:, moff:moff + N], op=ALU.mult),
                                reads=["Pt%d" % sk, "maskT"], writes=["Pt%d" % sk])

                        def stage2(idx):
                            a, qg, kb, nkb = items[idx]
                            pr = slice(64 * a, 64 * a + 64)
                            cq = max(128 * kb, 512 * qg)
                            N = 512 * (qg + 1) - cq
                            sk = idx % 4
                            po = 4 + (qg % 2)
                            oc = cq - 512 * qg
                            f.op("pe", lambda e: e.matmul(
                                ps[po][:, oc:oc + N], lhsT=V[:, kb, a, :], rhs=Pt[sk][:, 0:N],
                                start=(kb == 0), stop=(kb == nkb - 1)),
                                reads=["V", "Pt%d" % sk], writes=[PS[po]])
                            if kb == nkb - 1:
                                qs = slice(qg * 512, (qg + 1) * 512)
                                f.op("act", lambda e: e.activation(out=rc[64:128, :], in_=ps[po][64:128, :], func=AF.Ln),
                                     reads=[PS[po]], writes=["rtmp0"])
                                f.op("act", lambda e: e.activation(out=rc[64:128, :], in_=rc[64:128, :], func=AF.Exp,
                                                                   scale=-1.0),
                                     reads=["rtmp0"], writes=["rtmp0"])
                                f.op("dve", lambda e: e.tensor_tensor(out=atmp[pr, :], in0=ps[po][0:64, :],
                                                                      in1=rc[64:128, :], op=ALU.mult),
                                     reads=[PS[po], "rtmp0"], writes=["rtmp1"])
                                f.op("pool", lambda e: e.tensor_tensor(out=mT[pr, hp, qs], in0=atmp[pr, :],
                                                                       in1=mT[pr, hp, qs], op=ALU.mult),
                                     reads=["rtmp1", ("mT", hp)], writes=[("mT", hp)])

                        for idx in range(len(items) + LAG):
                            if idx < len(items):
                                stage1(idx)
                            if idx - LAG >= 0:
                                stage2(idx - LAG)
                            if gen is not None and idx % 4 == 3:
                                if next(gen, "done") == "done":
                                    gen = None
                    f.barrier()

                with ExitStack() as ss_:
                    def ssb(name, shape, dtype):
                        return ss_.enter_context(nc.sbuf_tensor(nk(name), list(shape), dtype))
                    NPW = 17 + 7
                    PR = ssb("PR", [128, 16, 3 * 24 + 4], F32)
                    pa = ssb("pa", [128, 16, 12], F32)
                    pi32 = ssb("pi32", [128, 16], I32)
                    XR = ssb("XR", [128, S], F32)
                    XI = ssb("XI", [128, S], F32)
                    cs_ = [ssb("cs%d" % k, [128, 2, 192], F32) for k in range(2)]
                    Xb = [ssb("Xb%d" % k, [128, 512], BF16) for k in range(2)]
                    uT = ssb("uT", [128, S], BF16)
                    bnat = ssb("bnat", [128, 2, 16, 16], F32)
                    cnat = ssb("cnat", [128, 2, 64], F32)
                    padT = ssb("padT", [128, 128], BF16)
                    padTf = ssb("padTf", [128, 2, 128], F32)
                    Bpad = ssb("Bpad", [128, 4, 2, 128], BF16)
                    Cpad = ssb("Cpad", [128, 4, 2, 128], BF16)
                    cf = ssb("cf", [128, 4, 128], F32)
                    dvec = ssb("dvec", [128, 4], F32)
                    glub = ssb("glub", [128, 4], F32)
                    gl = [ssb("gl%d" % k, [128, 512], F32) for k in range(2)] + [XR[:, 0:512]]

                    for k_ in range(2):
                        f.op("pool", lambda e, k_=k_: e.memset(cs_[k_][:], 0.0), writes=[("cs", k_, 0), ("cs", k_, 1)])
                    def ld_gp(dst_col, name):
                        for e_ in range(2):
                            f.dma("sp", pa[e_ * 64:(e_ + 1) * 64, :, dst_col],
                                  bass.AP(dt_[name], i * 2048 + e_ * 64, [[1, 64], [128, 16]]),
                                  writes=["pa"], allow_slow_non_contiguous=True)
                    ld_gp(0, "ssm_a_re")
                    ld_gp(1, "ssm_a_im")
                    for e_ in range(2):
                        f.dma("sp", pa[e_ * 64:(e_ + 1) * 64, :, 2],
                              bass.AP(dt_["ssm_log_dt"], i * 32 + e_, [[0, 64], [2, 16]]),
                              writes=["pa"], allow_slow_non_contiguous=True)
                    f.dma("sp", dvec[:], sd_d[i].rearrange("(c p) -> p c", p=128), writes=["dvec"],
                          allow_slow_non_contiguous=True)
                    f.dma("sp", glub[:], glb_d[i].rearrange("(c p) -> p c", p=128), writes=["glub"],
                          allow_slow_non_contiguous=True)

                    def pop(eng, fn, w=("pa",)):
                        f.op(eng, fn, reads=["pa", "PR"], writes=list(w))
                    A = lambda c: pa[:, :, c]
                    pop("act", lambda e: e.activation(out=A(3), in_=A(2), func=AF.Exp))
                    pop("dve", lambda e: e.tensor_tensor(out=A(4), in0=A(0), in1=A(3), op=ALU.mult))
                    pop("dve", lambda e: e.tensor_tensor(out=A(5), in0=A(1), in1=A(3), op=ALU.mult))
                    pop("act", lambda e: e.activation(out=A(4), in_=A(4), func=AF.Exp))

                    def sin_of(dst, shift):
                        pop("dve", lambda e: e.tensor_scalar(out=A(6), in0=A(5), scalar1=shift, scalar2=1.0 / TWO_PI,
                                                             op0=ALU.add, op1=ALU.mult))
                        f.op("dve", lambda e: e.tensor_copy(out=pi32[:], in_=A(6)), reads=["pa"], writes=["pi32"])
                        f.op("dve", lambda e: e.tensor_copy(out=A(7), in_=pi32[:]), reads=["pi32"], writes=["pa"])
                        pop("dve", lambda e: e.tensor_tensor(out=A(6), in0=A(6), in1=A(7), op=ALU.subtract))
                        pop("dve", lambda e: e.tensor_scalar(out=A(6), in0=A(6), scalar1=TWO_PI, scalar2=math.pi,
                                                             op0=ALU.mult, op1=ALU.min))
                        pop("dve", lambda e: e.tensor_scalar(out=A(6), in0=A(6), scalar1=-math.pi, scalar2=None,
                                                             op0=ALU.max))
                        pop("act", lambda e: e.activation(out=dst, in_=A(6), func=AF.Sin))
                    sin_of(A(8), 0.0)
                    sin_of(A(9), math.pi / 2)
                    P3 = lambda k, c: PR[:, :, 3 * k + c]
                    pop("dve", lambda e: e.tensor_tensor(out=P3(0, 0), in0=A(4), in1=A(9), op=ALU.mult), w=("PR",))
                    pop("dve", lambda e: e.tensor_tensor(out=P3(0, 1), in0=A(4), in1=A(8), op=ALU.mult), w=("PR",))

                    def cmul(dst, a_, b_):
                        pop("dve", lambda e: e.tensor_tensor(out=A(6), in0=P3(a_, 0), in1=P3(b_, 0), op=ALU.mult))
                        pop("dve", lambda e: e.tensor_tensor(out=A(7), in0=P3(a_, 1), in1=P3(b_, 1), op=ALU.mult))
                        pop("dve", lambda e: e.tensor_tensor(out=A(10), in0=P3(a_, 0), in1=P3(b_, 1), op=ALU.mult))
                        pop("dve", lambda e: e.tensor_tensor(out=A(11), in0=P3(a_, 1), in1=P3(b_, 0), op=ALU.mult))
                        pop("dve", lambda e: e.tensor_tensor(out=P3(dst, 0), in0=A(6), in1=A(7), op=ALU.subtract), w=("PR",))
                        pop("dve", lambda e: e.tensor_tensor(out=P3(dst, 1), in0=A(10), in1=A(11), op=ALU.add), w=("PR",))
                    for j in range(1, 16):
                        cmul(j, j - 1, 0)
                    for k in range(16, 16 + 7):
                        cmul(k, k - 1, k - 1)
                    for k in range(23):
                        pop("dve", lambda e, k=k: e.tensor_scalar(out=P3(k, 2), in0=P3(k, 1), scalar1=-1.0, scalar2=None,
                                                                  op0=ALU.mult), w=("PR",))
                    FR = PR[:, :, 72]
                    FI = PR[:, :, 73]
                    pop("dve", lambda e: e.tensor_scalar(out=A(6), in0=P3(0, 0), scalar1=-1.0, scalar2=None, op0=ALU.add))
                    pop("dve", lambda e: e.tensor_tensor(out=A(7), in0=A(0), in1=A(0), op=ALU.mult))
                    pop("dve", lambda e: e.tensor_tensor(out=A(10), in0=A(1), in1=A(1), op=ALU.mult))
                    pop("dve", lambda e: e.tensor_tensor(out=A(7), in0=A(7), in1=A(10), op=ALU.add))
                    pop("dve", lambda e: e.reciprocal(out=A(7), in_=A(7)))
                    pop("dve", lambda e: e.tensor_tensor(out=A(10), in0=A(6), in1=A(0), op=ALU.mult))
                    pop("dve", lambda e: e.tensor_tensor(out=A(11), in0=P3(0, 1), in1=A(1), op=ALU.mult))
                    pop("dve", lambda e: e.tensor_tensor(out=A(10), in0=A(10), in1=A(11), op=ALU.add))
                    pop("dve", lambda e: e.tensor_tensor(out=FR, in0=A(10), in1=A(7), op=ALU.mult), w=("PR",))
                    pop("dve", lambda e: e.tensor_tensor(out=A(10), in0=P3(0, 1), in1=A(0), op=ALU.mult))
                    pop("dve", lambda e: e.tensor_tensor(out=A(11), in0=A(6), in1=A(1), op=ALU.mult))
                    pop("dve", lambda e: e.tensor_tensor(out=A(10), in0=A(10), in1=A(11), op=ALU.subtract))
                    pop("dve", lambda e: e.tensor_tensor(out=FI, in0=A(10), in1=A(7), op=ALU.mult), w=("PR",))
                    for ri, name in enumerate(("ssm_b_re", "ssm_b_im")):
                        for e_ in range(2):
                            f.dma("sp", bnat[e_ * 64:(e_ + 1) * 64, ri, :, :],
                                  bass.AP(dt_[name], i * 32 * 1024 + e_ * 1024,
                                          [[16, 64], [2048, 16], [1, 16]]), writes=["bnat"])

                    XALL = [["X0"] + [("XR", s_) for s_ in range(16)], ["X1"] + [("XI", s_) for s_ in range(16)]]

                    srcs = [(ewin_d[i][:, 4096 + j_ * 128:4096 + (j_ + 1) * 128], 8) for j_ in range(4)]
                    for tg_ in range(4):
                        srcs += [(glw_d[i][:, fo_ * 128:(fo_ + 1) * 128], 4) for fo_ in range(4)]
                        srcs += [(ewin_d[i][:, 4608 + fo_ * 128:4608 + (fo_ + 1) * 128], 8) for fo_ in range(4)]
                    ws = WS(srcs)

                    def PSC(q, k, c):
                        return PR[:, q, 3 * k + c:3 * k + c + 1]

                    for j in range(4):
                        for ri, name in enumerate(("ssm_c_re", "ssm_c_im")):
                            f.dma("sp", cnat[:, ri, :],
                                  bass.AP(dt_[name], i * 32 * 1024 + j * 8 * 1024, [[64, 128], [1, 64]]),
                                  writes=["cnat"])
                        for qq in range(4):
                            q = j * 4 + qq
                            for ri in range(2):
                                f.op("pool", lambda e: e.memset(padT[:], 0.0), writes=["padT"])
                                for e_ in range(2):
                                    co = 16 * (2 * qq + e_)
                                    f.op("dve", lambda e, e_=e_, co=co, ri=ri, q=q: e.tensor_copy(
                                        out=padT[e_ * 64:(e_ + 1) * 64, co:co + 16],
                                        in_=bnat[e_ * 64:(e_ + 1) * 64, ri, q, :]),
                                        reads=["bnat"], writes=["padT"])
                                f.op("pe", lambda e: e.matmul(ps[6][:, 0:128], lhsT=padT[:], rhs=ident[:], start=True, stop=True),
                                     reads=["padT", "ident"], writes=[PS[6]])
                                f.op("act", lambda e, qq=qq, ri=ri: e.copy(out=Bpad[:, qq, ri, :], in_=ps[6][:, 0:128]),
                                     reads=[PS[6]], writes=["Bpad"])
                            for ri in range(2):
                                for e_ in range(2):
                                    g8 = 2 * qq + e_
                                    f.op("dve", lambda e, e_=e_, ri=ri, g8=g8: e.tensor_scalar(
                                        out=padTf[:, ri, e_ * 64:(e_ + 1) * 64], in0=cnat[:, ri, :],
                                        scalar1=gmask[:, g8:g8 + 1], scalar2=None, op0=ALU.mult),
                                        reads=["cnat", "gmask"], writes=["padTf"])
                            for ri in range(2):
                                f.op("pe", lambda e, ri=ri: e.matmul(ps[4][:, ri * 128:(ri + 1) * 128], lhsT=padTf[:, ri, :],
                                                                     rhs=identf[:], start=True, stop=True),
                                     reads=["padTf", "identf"], writes=[PS[4]])
                            fr = PR[:, q, 72:73]
                            fi = PR[:, q, 73:74]
                            f.op("act", lambda e: e.copy(out=cf[:, 0:2, :], in_=ps[4][:, 0:256].rearrange("p (a b) -> p a b", a=2)),
                                 reads=[PS[4]], writes=["cf"])
                            f.op("dve", lambda e, fr=fr: e.tensor_scalar(out=cf[:, 2, :], in0=cf[:, 0, :], scalar1=fr, scalar2=None,
                                                                         op0=ALU.mult), reads=["cf", "PR"], writes=["cf"])
                            f.op("dve", lambda e, fi=fi: e.tensor_scalar(out=cf[:, 3, :], in0=cf[:, 1, :], scalar1=fi, scalar2=None,
                                                                         op0=ALU.mult), reads=["cf", "PR"], writes=["cf"])
                            f.op("dve", lambda e, qq=qq: e.tensor_tensor(out=Cpad[:, qq, 0, :], in0=cf[:, 2, :], in1=cf[:, 3, :],
                                                                         op=ALU.subtract), reads=["cf"], writes=["Cpad"])
                            f.op("dve", lambda e, fi=fi: e.tensor_scalar(out=cf[:, 2, :], in0=cf[:, 0, :], scalar1=fi, scalar2=-1.0,
                                                                         op0=ALU.mult, op1=ALU.mult), reads=["cf", "PR"], writes=["cf"])
                            f.op("dve", lambda e, fr=fr: e.tensor_scalar(out=cf[:, 3, :], in0=cf[:, 1, :], scalar1=fr, scalar2=None,
                                                                         op0=ALU.mult), reads=["cf", "PR"], writes=["cf"])
                            f.op("dve", lambda e, qq=qq: e.tensor_tensor(out=Cpad[:, qq, 1, :], in0=cf[:, 2, :], in1=cf[:, 3, :],
                                                                         op=ALU.subtract), reads=["cf"], writes=["Cpad"])
                        wb, wk = ws.get()
                        for tg in range(4):
                            sl = slice(tg * 512, (tg + 1) * 512)
                            proj_fm(wb, wk, 4, tg)
                            f.op("act", lambda e, sl=sl: e.copy(out=uT[:, sl], in_=ps[4][:]), reads=[PS[4]], writes=["uT"])
                        def stage_in(qq_, ri):
                            X = (XR, XI)[ri]
                            for tg in range(4):
                                sl = slice(tg * 512, (tg + 1) * 512)
                                pk = 4 + (tg % 2)
                                f.op("pe", lambda e, sl=sl, pk=pk: e.matmul(
                                    ps[pk][:], lhsT=Bpad[:, qq_, ri, :], rhs=uT[:, sl], start=True, stop=True),
                                    reads=["Bpad", "uT"], writes=[PS[pk]])
                                f.op("act", lambda e, sl=sl, pk=pk, X=X: e.copy(out=X[:, sl], in_=ps[pk][:]),
                                     reads=[PS[pk]], writes=XALL[ri])

                        def stage_out(qq_, ri):
                            X = (XR, XI)[ri]
                            for tg in range(4):
                                sl = slice(tg * 512, (tg + 1) * 512)
                                bi = (ri * 4 + tg) % 2
                                cast(Xb[bi][:], X[:, sl], XALL[ri], ["Xb%d" % bi])
                                f.op("pe", lambda e, tg=tg, bi=bi: e.matmul(
                                    ps[tg][:], lhsT=Cpad[:, qq_, ri, :], rhs=Xb[bi][:],
                                    start=(qq_ == 0 and ri == 0), stop=(qq_ == 3 and ri == 1)),
                                    reads=["Cpad", "Xb%d" % bi], writes=[PS[tg]])

                        stage_in(0, 0)
                        stage_in(0, 1)
                        for qq in range(4):
                            q = j * 4 + qq
                            XRv = XR[:].rearrange("p (c s) -> p c s", s=16)
                            XIv = XI[:].rearrange("p (c s) -> p c s", s=16)

                            def _kl(k_):
                                return list(k_) if isinstance(k_, list) else [k_]

                            def cstep(oR, oI, iR, iI, k, kOR, kOI, kIR, kII, bR=None, bI=None, kB=()):
                                for (o_, i_, c_, ko, ki, b_) in ((oR, iR, 0, kOR, kIR, bR), (oI, iR, 1, kOI, kIR, bI),
                                                                 (oR, iI, 2, kOR, kII, None), (oI, iI, 0, kOI, kII, None)):
                                    add_ = o_ if b_ is None else b_
                                    f.op("dve", lambda e, o_=o_, i_=i_, c_=c_, add_=add_: e.scalar_tensor_tensor(
                                        out=o_, in0=i_, scalar=PSC(q, k, c_), in1=add_, op0=ALU.mult, op1=ALU.add),
                                        reads=_kl(ki) + ["PR"] + list(kB), writes=_kl(ko))
                            XRc = XR[:].rearrange("p (n r) -> p n r", r=4)
                            XIc = XI[:].rearrange("p (n r) -> p n r", r=4)
                            XR4 = XR[:].rearrange("p (c m r) -> p c m r", m=4, r=4)
                            XI4 = XI[:].rearrange("p (c m r) -> p c m r", m=4, r=4)
                            for r_ in range(1, 4):
                                ko_ = [r_ + 4 * m_ for m_ in range(4)]
                                ki_ = [r_ - 1 + 4 * m_ for m_ in range(4)]
                                cstep(XRc[:, :, r_], XIc[:, :, r_], XRc[:, :, r_ - 1], XIc[:, :, r_ - 1], 0,
                                      [("XR", c_) for c_ in ko_], [("XI", c_) for c_ in ko_],
                                      [("XR", c_) for c_ in ki_], [("XI", c_) for c_ in ki_])
                            for m_ in range(1, 4):
                                so_ = 4 * m_ + 3
                                si_ = 4 * m_ - 1
                                cstep(XRv[:, :, so_], XIv[:, :, so_], XRv[:, :, si_], XIv[:, :, si_], 3,
                                      ("XR", so_), ("XI", so_), ("XR", si_), ("XI", si_))
                            cur = 0
                            f.op("act", lambda e: e.copy(out=cs_[0][:, 0, 64:192], in_=XRv[:, :, 15]),
                                 reads=[("XR", 15)], writes=[("cs", 0, 0)])
                            f.op("act", lambda e: e.copy(out=cs_[0][:, 1, 64:192], in_=XIv[:, :, 15]),
                                 reads=[("XI", 15)], writes=[("cs", 0, 1)])
                            for k in range(7):
                                sh = 1 << k
                                src = cs_[cur]
                                dst = cs_[1 - cur]
                                cstep(dst[:, 0, 64:192], dst[:, 1, 64:192], src[:, 0, 64 - sh:192 - sh], src[:, 1, 64 - sh:192 - sh],
                                      15 + k, ("cs", 1 - cur, 0), ("cs", 1 - cur, 1), ("cs", cur, 0), ("cs", cur, 1),
                                      bR=src[:, 0, 64:192], bI=src[:, 1, 64:192])
                                cur = 1 - cur
                            fin = cs_[cur]
                            for m_ in range(4):
                                s_ = 4 * m_ + 3
                                cstep(XRv[:, 1:128, s_], XIv[:, 1:128, s_], fin[:, 0, 64:191], fin[:, 1, 64:191], s_,
                                      ("XR", s_), ("XI", s_), ("cs", cur, 0), ("cs", cur, 1))
                            ENDK = [[("XR", 3), ("XR", 7), ("XR", 11)], [("XI", 3), ("XI", 7), ("XI", 11)]]

                            def p3_m(comp, part):
                                X4 = (XR4, XI4)[comp]
                                kx = ("XR", "XI")[comp]
                                cidx = ((0, 2), (1, 0))[comp]
                                E4 = (XR4, XI4)[part]
                                for r_ in range(3):
                                    f.op("dve", lambda e, r_=r_: e.scalar_tensor_tensor(
                                        out=X4[:, :, 1:4, r_], in0=E4[:, :, 0:3, 3],
                                        scalar=PSC(q, r_, cidx[part]), in1=X4[:, :, 1:4, r_],
                                        op0=ALU.mult, op1=ALU.add),
                                        reads=ENDK[part] + ["PR"], writes=[(kx, 4 * m_ + r_) for m_ in range(1, 4)])

                            def p3_0(comp):
                                Xv = (XRv, XIv)[comp]
                                kx = ("XR", "XI")[comp]
                                cidx = ((0, 2), (1, 0))[comp]
                                for part in range(2):
                                    for r_ in range(3):
                                        f.op("dve", lambda e, r_=r_, part=part: e.scalar_tensor_tensor(
                                            out=Xv[:, 1:128, r_], in0=fin[:, part, 64:191],
                                            scalar=PSC(q, r_, cidx[part]), in1=Xv[:, 1:128, r_],
                                            op0=ALU.mult, op1=ALU.add),
                                            reads=[("cs", cur, part), "PR"], writes=[(kx, r_)])
                            p3_m(0, 0)
                            p3_m(0, 1)
                            p3_0(0)
                            p3_m(1, 0)
                            stage_out(qq, 0)
                            if qq < 3:
                                stage_in(qq + 1, 0)
                            p3_m(1, 1)
                            p3_0(1)
                            stage_out(qq, 1)
                            if qq < 3:
                                stage_in(qq + 1, 1)
                        for tg in range(4):
                            sl = slice(tg * 512, (tg + 1) * 512)
                            f.op("dve", lambda e, sl=sl, tg=tg, j=j: e.scalar_tensor_tensor(out=gl[0][:], in0=uT[:, sl],
                                                                                            scalar=dvec[:, j:j + 1], in1=ps[tg][:],
                                                                                            op0=ALU.mult, op1=ALU.add),
                                 reads=[PS[tg], "uT", "dvec"], writes=["gl0"])
                            f.op("pool", lambda e: e.tensor_tensor(out=gl[1][:], in0=gl[0][:], in1=gl[0][:], op=ALU.mult),
                                 reads=["gl0"], writes=["gl1"])
                            f.op("pool", lambda e: e.tensor_scalar(out=gl[1][:], in0=gl[1][:], scalar1=0.044715, scalar2=1.0,
                                                                   op0=ALU.mult, op1=ALU.add), reads=["gl1"], writes=["gl1"])
                            f.op("pool", lambda e: e.tensor_tensor(out=gl[1][:], in0=gl[1][:], in1=gl[0][:], op=ALU.mult),
                                 reads=["gl1", "gl0"], writes=["gl1"])
                            f.op("act", lambda e: e.activation(out=gl[2][:], in_=gl[1][:], func=AF.Sigmoid,
                                                               scale=2.0 * math.sqrt(2.0 / math.pi)),
                                 reads=["gl1"], writes=["gl2"] + XALL[0])
                            f.op("dve", lambda e, sl=sl, j=j: e.tensor_tensor(out=mT[:, 8 + j, sl], in0=gl[0][:], in1=gl[2][:],
                                                                              op=ALU.mult),
                                 reads=["gl0", "gl2"] + XALL[0], writes=[("mT", 8 + j)])
                    for tg in range(4):
                        sl = slice(tg * 512, (tg + 1) * 512)
                        for fo in range(4):
                            wb, wk = ws.get()
                            for jj in range(4):
                                f.op("pe", lambda e, fo=fo, jj=jj, sl=sl: e.matmul(
                                    ps[fo][:], lhsT=wb[:, jj, :], rhs=mT[:, 8 + jj, sl],
                                    start=(jj == 0), stop=(jj == 3)),
                                    reads=[wk, ("mT", 8 + jj)], writes=[PS[fo]], inc=(jj == 3))
                        for fo in range(4):
                            wb, wk = ws.get()
                            pgz = 4 + (fo % 2)
                            proj_fm(wb, wk, pgz, tg)
                            f.op("act", lambda e: e.activation(out=gl[0][:], in_=ps[pgz][:], func=AF.Sigmoid),
                                 reads=[PS[pgz]], writes=["gl0"])
                            f.op("act", lambda e, fo=fo: e.activation(out=gl[2][:], in_=ps[fo][:], func=AF.Sigmoid,
                                                                      bias=glub[:, fo:fo + 1]),
                                 reads=[PS[fo], "glub"], writes=["gl2"] + XALL[0])
                            f.op("pool", lambda e: e.tensor_tensor(out=gl[1][:], in0=gl[2][:], in1=gl[0][:], op=ALU.mult),
                                 reads=["gl2", "gl0"] + XALL[0], writes=["gl1"])
                            f.op("dve", lambda e: e.tensor_tensor(out=gl[1][:], in0=ps[pgz][:], in1=gl[1][:], op=ALU.mult),
                                 reads=[PS[pgz], "gl1"], writes=["gl1"])
                            f.op("dve", lambda e, fo=fo, sl=sl: e.tensor_tensor(out=mT[:, 8 + fo, sl], in0=mT[:, 8 + fo, sl],
                                                                                in1=gl[1][:], op=ALU.mult),
                                 reads=["gl1", ("mT", 8 + fo)] + [PS[k] for k in range(4)], writes=[("mT", 8 + fo)])
                    f.barrier()

                with ExitStack() as os_:
                    wout = os_.enter_context(nc.sbuf_tensor(nk("ewout"), [128, 12, D], BF16))
                    ytmp_box[0] = os_.enter_context(nc.sbuf_tensor(nk("ytmp"), [128, 2, 512], F32))
                    woutk = load_big(wout, "ewout", ewout_d[i], 12)
                    f.dma("sp", gain[:], bcast("post_norm", l * D, D), writes=["gain"])
                    for b in range(NB):
                        outproj_block(b, lambda kc, b=b: mT[:, kc, b * 128:(b + 1) * 128],
                                      [("mT", ch) for ch in range(12)], 12, wout, woutk, l)
                    f.barrier()

        for l in layers:
            if l % 2 == 0:
                even_layer(l)
            else:
                odd_layer(l)

        for b in range(NB):
            f.dma("sp", out_d[b * 128:(b + 1) * 128, :], xres[:, b, :], reads=[("x", b)], key="out")
        nc.sync.wait_ge(f.dsem["out"][0], f.dsem["out"][1])
    return nc


_CACHE = {}


def prep_inputs(inputs):
    w = {k: np.ascontiguousarray(np.asarray(v, dtype=np.float32)) for k, v in inputs.items()}
    perm = swap_perm()
    shared = {k: v for k, v in w.items() if k != "x"}
    shared["even_w_sw"] = np.ascontiguousarray(w["even_w_in"][:, :, 0:2048][:, :, perm])
    shared.update(host_consts())
    return w["x"], shared


def kernel(**inputs):
    x, shared = prep_inputs(inputs)
    if "nc" not in _CACHE:
        _CACHE["nc"] = build()
    nc = _CACHE["nc"]
    in_maps = []
    for c in range(8):
        m = dict(shared)
        m["x"] = np.ascontiguousarray(x[c])
        in_maps.append(m)
    res = run_bass_kernel_spmd(nc, in_maps, core_ids=list(range(8)))
    return np.stack([np.asarray(r["out"], dtype=np.float32) for r in res.results], axis=0)
```

```python
import math
import numpy as np
import concourse.bass as bass
import concourse.mybir as mybir
from concourse.bass_utils import run_bass_kernel_spmd

F32 = mybir.dt.float32
BF16 = mybir.dt.bfloat16
I32 = mybir.dt.int32
ALU = mybir.AluOpType
AF = mybir.ActivationFunctionType
AX = mybir.AxisListType

S = 2048
D = 1024
NB = 16
EPS = 1e-6
TWO_PI = 2.0 * math.pi


class FW:
    def __init__(self, nc):
        self.nc = nc
        self.eng = {"pe": nc.tensor, "act": nc.scalar, "dve": nc.vector,
                    "pool": nc.gpsimd, "sp": nc.sync}
        self.sem = {}
        self.cnt = {}
        for e in self.eng:
            self.sem[e] = nc.alloc_semaphore("s_" + e)
            self.cnt[e] = 0
        self.waited = {}
        self.lastw = {}
        self.rd = {}
        self.dsem = {}

    def _deps(self, reads, writes):
        deps = {}

        def add(t):
            if t is not None and deps.get(t[0], 0) < t[1]:
                deps[t[0]] = t[1]
        for k in reads:
            add(self.lastw.get(k))
        for k in writes:
            add(self.lastw.get(k))
            for sk, v in self.rd.get(k, {}).items():
                add((sk, v))
        return deps

    def _semof(self, sk):
        if isinstance(sk, tuple):
            return self.dsem[sk[1]][0]
        return self.sem[sk]

    def _emit_waits(self, e, deps):
        for sk, v in deps.items():
            if sk == e and v > self.cnt[e]:
                continue
            if self.waited.get((e, sk), 0) < v:
                self.eng[e].wait_ge(self._semof(sk), v)
                self.waited[(e, sk)] = v

    def _record(self, t, reads, writes):
        for k in writes:
            self.lastw[k] = t
            self.rd[k] = {}
        for k in reads:
            d = self.rd.setdefault(k, {})
            if d.get(t[0], 0) < t[1]:
                d[t[0]] = t[1]

    def op(self, e, fn, reads=(), writes=(), inc=True):
        self._emit_waits(e, self._deps(reads, writes))
        ins = fn(self.eng[e])
        if inc:
            ins.then_inc(self.sem[e], 1)
            self.cnt[e] += 1
            t = (e, self.cnt[e])
        else:
            t = (e, self.cnt[e] + 1)
        self._record(t, reads, writes)
        return ins

    def dma(self, q, out, in_, reads=(), writes=(), key=None, **kw):
        if key is None:
            key = writes[0] if writes else reads[0]
        if key not in self.dsem:
            self.dsem[key] = [self.nc.alloc_semaphore("d%d" % len(self.dsem)), 0]
        self._emit_waits(q, self._deps(reads, writes))
        ins = self.eng[q].dma_start(out=out, in_=in_, **kw)
        ins.then_inc(self.dsem[key][0], 16)
        self.dsem[key][1] += 16
        t = (("dma", key), self.dsem[key][1])
        self._record(t, reads, writes)
        return ins

    def barrier(self):
        for e in self.eng:
            deps = {}
            for o in self.eng:
                if o != e and self.cnt[o] > 0:
                    deps[o] = self.cnt[o]
            for key, (s, c) in self.dsem.items():
                if c > 0:
                    deps[("dma", key)] = c
            self._emit_waits(e, deps)


def _mult(d):
    d = np.asarray(d)
    m = ((d >= 0) & (d <= 128)).astype(np.float32)
    m += ((d >= 0) & (d <= 512) & (d % 4 == 0)).astype(np.float32)
    m += ((d >= 0) & (d % 16 == 0)).astype(np.float32)
    return m


def host_consts():
    c = {}
    c["c_ident"] = np.eye(128, dtype=np.float32)
    j = np.arange(128)[:, None]
    t = np.arange(128)[None, :]
    c["c_mask"] = np.concatenate([_mult(128 * dl + t - j) for dl in range(-3, 16)], axis=1).astype(np.float32)
    bands = np.zeros((128, 4, 3, 128), np.float32)
    tp = np.arange(128)[:, None]
    tt = np.arange(128)[None, :]
    for wi, w in enumerate((2, 4, 8, 16)):
        dcur = tt - tp
        bands[:, wi, 0, :] = ((dcur >= 0) & (dcur < w)) / float(w) - (dcur == 0)
        dprev = tt + 128 - tp
        bands[:, wi, 1, :] = ((dprev >= 0) & (dprev < w)) / float(w)
        cnt = np.minimum(tt + 1, w).astype(np.float32)
        bands[:, wi, 2, :] = ((dcur >= 0) & (dcur < w)) / cnt - (dcur == 0)
    c["c_bands"] = bands.reshape(128, 4 * 3 * 128)
    half = 8
    inv = 500000.0 ** (-np.arange(0, 16, 2, dtype=np.float32) / 16.0)
    ang = np.arange(S, dtype=np.float32)[None, :] * inv[:, None]
    C = np.ones((64, S), np.float32)
    Sg = np.zeros((64, S), np.float32)
    C[0:8] = np.cos(ang)
    C[8:16] = np.cos(ang)
    Sg[0:8] = -np.sin(ang)
    Sg[8:16] = np.sin(ang)
    c["c_ropec"] = np.concatenate([C, C], 0)
    c["c_ropes"] = np.concatenate([Sg, Sg], 0)
    gm = np.zeros((128, 8), np.float32)
    for g in range(8):
        gm[g * 16:(g + 1) * 16, g] = 1.0
    c["c_gmask"] = gm
    return c


def swap_perm():
    perm = np.arange(2048)
    for blk in range(2048 // 64):
        b = blk * 64
        perm[b:b + 8] = np.arange(b + 8, b + 16)
        perm[b + 8:b + 16] = np.arange(b, b + 8)
    return perm


def build(layers=(0, 1, 2, 3)):
    nc = bass.Bass("TRN2", target_bir_lowering=False)
    dt_ = {}

    def din(name, shape):
        h = nc.dram_tensor(name, list(shape), F32, kind="ExternalInput")
        dt_[name] = h
        return h.ap()

    x_d = din("x", [S, D])
    pre_d = din("pre_norm", [4, D])
    post_d = din("post_norm", [4, D])
    ewin_d = din("even_w_in", [2, D, 5120])
    ewsw_d = din("even_w_sw", [2, D, 2048])
    ewout_d = din("even_w_out", [2, 1536, D])
    are_d = din("ssm_a_re", [2, 32, 64])
    aim_d = din("ssm_a_im", [2, 32, 64])
    ldt_d = din("ssm_log_dt", [2, 32])
    bre_d = din("ssm_b_re", [2, 32, 64, 16])
    bim_d = din("ssm_b_im", [2, 32, 64, 16])
    cre_d = din("ssm_c_re", [2, 32, 16, 64])
    cim_d = din("ssm_c_im", [2, 32, 16, 64])
    sd_d = din("ssm_d", [2, 512])
    glw_d = din("ssm_glu_w", [2, 512, 512])
    glb_d = din("ssm_glu_b", [2, 512])
    owin_d = din("odd_w_in", [2, D, 4096])
    pw_d = din("pool_w", [2, 4, 512, 512])
    psc_d = din("pool_scale", [2, 2048])
    owout_d = din("odd_w_out", [2, 2048, D])
    cid_d = din("c_ident", [128, 128])
    cmask_d = din("c_mask", [128, 19 * 128])
    cband_d = din("c_bands", [128, 12 * 128])
    cropec_d = din("c_ropec", [128, S])
    cropes_d = din("c_ropes", [128, S])
    cgm_d = din("c_gmask", [128, 8])
    out_d = nc.dram_tensor("out", [S, D], F32, kind="ExternalOutput").ap()

    f = FW(nc)
    uid = [0]

    def nk(p):
        uid[0] += 1
        return "%s%d" % (p, uid[0])

    def bcast(name, off, n):
        return bass.AP(dt_[name], off, [[0, 128], [1, n]])

    from contextlib import ExitStack
    with ExitStack() as es:
        def sb(name, shape, dtype):
            return es.enter_context(nc.sbuf_tensor(name, list(shape), dtype))

        def psb(name, shape, dtype):
            return es.enter_context(nc.psum_tensor(name, list(shape), dtype))

        xres = sb("xres", [128, NB, D], F32)
        ps = [psb("ps%d" % i, [128, 512], F32) for i in range(8)]
        PS = ["ps%d" % i for i in range(8)]
        ident = sb("ident", [128, 128], BF16)
        identf = sb("identf", [128, 128], F32)
        gain = sb("gain", [128, D], F32)
        junk = sb("junk", [128, D], BF16)
        stat = sb("stat", [128, 64], F32)
        gmask = sb("gmask", [128, 8], F32)
        wbf = [sb("wbf%d" % i, [128, 8, 128], BF16) for i in range(6)]
        wctr = [0, 0]

        f.dma("sp", identf[:], cid_d, writes=["identf"])
        f.op("dve", lambda e: e.tensor_copy(out=ident[:], in_=identf[:]), reads=["identf"], writes=["ident"])
        f.dma("sp", gmask[:], cgm_d, writes=["gmask"])
        for b in range(NB):
            f.dma("sp", xres[:, b, :], x_d[b * 128:(b + 1) * 128, :], writes=[("x", b)])

        cast_rr = [0]

        def cast(out, in_, reads, writes):
            e = "act"
            if e == "pool":
                f.op("pool", lambda g: g.tensor_copy(out=out, in_=in_), reads=reads, writes=writes)
            else:
                f.op("act", lambda g: g.copy(out=out, in_=in_), reads=reads, writes=writes)

        class WS:
            DEPTH = 2

            def __init__(self, srcs):
                self.srcs = srcs
                self.issued = 0
                self.cur = 0
                self.buf = {}

            def prefetch(self, n):
                self._issue_to(self.cur + n)

            def get(self, issue=True):
                if issue:
                    self._issue_to(self.cur + 1 + WS.DEPTH)
                assert self.cur < self.issued
                bi = self.buf[self.cur]
                self.cur += 1
                return wbf[bi], "wbf%d" % bi

            def _issue_to(self, lim):
                while self.issued < min(len(self.srcs), lim):
                    src, kc = self.srcs[self.issued]
                    bi = wctr[1] % 6
                    wctr[1] += 1
                    f.dma("pool", wbf[bi][:, 0:kc, :], src.rearrange("(k p) n -> p k n", p=128),
                          writes=["wbf%d" % bi])
                    self.buf[self.issued] = bi
                    self.issued += 1

        def load_big(dst, dkey, src_ap, kc, step=4):
            keys = []
            for k0 in range(0, kc, step):
                k1 = min(kc, k0 + step)
                f.dma("pool", dst[:, k0:k1, :], src_ap[k0 * 128:k1 * 128, :].rearrange("(k p) n -> p k n", p=128),
                      writes=[(dkey, k0)])
                keys.append((dkey, k0))
            return keys

        def load_big_hw(dst, dkey, src_ap, kc, stg, skey):
            keys = []
            for k0 in range(0, kc, 2):
                f.dma("sp", stg[:, :, :], src_ap[k0 * 128:(k0 + 2) * 128, :].rearrange("(k p) n -> p k n", p=128),
                      writes=[skey])
                f.op("act", lambda g: g.copy(out=dst[:, k0:k0 + 2, :], in_=stg[:, :, :]), reads=[skey], writes=[(dkey, k0)])
                keys.append((dkey, k0))
            return keys

        scol = [0]

        def newcol():
            scol[0] = (scol[0] + 1) % 64
            return scol[0]

        def rstd_from_ss(c_ss):
            c1 = newcol()
            c2 = newcol()
            f.op("act", lambda e: e.activation(out=stat[:, c1:c1 + 1], in_=stat[:, c_ss:c_ss + 1], func=AF.Sqrt,
                                               scale=1.0 / D, bias=EPS),
                 reads=[("st", c_ss)], writes=[("st", c1)])
            f.op("dve", lambda e: e.reciprocal(out=stat[:, c2:c2 + 1], in_=stat[:, c1:c1 + 1]),
                 reads=[("st", c1)], writes=[("st", c2)])
            return c2

        def prenorm_block(b, dst, dkey):
            c = newcol()
            f.op("act", lambda e: e.activation(out=junk[:], in_=xres[:, b, :], func=AF.Square,
                                               accum_out=stat[:, c:c + 1]),
                 reads=[("x", b)], writes=["junk", ("st", c)])
            c2 = rstd_from_ss(c)
            f.op("dve", lambda e: e.scalar_tensor_tensor(out=dst, in0=xres[:, b, :], scalar=stat[:, c2:c2 + 1],
                                                         in1=gain[:], op0=ALU.mult, op1=ALU.mult),
                 reads=[("x", b), ("st", c2), "gain"], writes=[dkey])

        def transpose_block(src, skey, dst_ap, dkey, pi):
            for hf in range(2):
                bk = 4 + 2 * pi + hf
                for cc in range(4):
                    c = hf * 4 + cc
                    f.op("pe", lambda e, c=c, cc=cc, bk=bk: e.matmul(ps[bk][:, cc * 128:(cc + 1) * 128],
                                                                     lhsT=src[:, c * 128:(c + 1) * 128], rhs=ident[:],
                                                                     start=True, stop=True),
                         reads=[skey, "ident"], writes=[PS[bk]], inc=(cc == 3))
                f.op("act", lambda e, hf=hf, bk=bk: e.copy(out=dst_ap[:, hf * 4:(hf + 1) * 4, :],
                                                           in_=ps[bk][:].rearrange("p (c t) -> p c t", c=4)),
                     reads=[PS[bk]], writes=[dkey])

        def outproj_block(b, mT_fn, mkeys, KC, wout, wkey, l):
            pp = (b % 2) * 2
            for fh in range(2):
                for kc in range(KC):
                    f.op("pe", lambda e, kc=kc, fh=fh: e.matmul(ps[pp + fh][:], lhsT=mT_fn(kc),
                                                                rhs=wout[:, kc, fh * 512:(fh + 1) * 512],
                                                                start=(kc == 0), stop=(kc == KC - 1)),
                         reads=list(mkeys) + list(wkey), writes=[PS[pp + fh]], inc=(kc == KC - 1))
            ca = newcol()
            cb = newcol()
            f.op("act", lambda e: e.activation(out=junk[:, 0:512], in_=ps[pp][:], func=AF.Square,
                                               accum_out=stat[:, ca:ca + 1]),
                 reads=[PS[pp]], writes=["junk", ("st", ca)])
            f.op("act", lambda e: e.activation(out=junk[:, 512:1024], in_=ps[pp + 1][:], func=AF.Square,
                                               accum_out=stat[:, cb:cb + 1]),
                 reads=[PS[pp + 1]], writes=["junk", ("st", cb)])
            cs = newcol()
            f.op("dve", lambda e: e.tensor_tensor(out=stat[:, cs:cs + 1], in0=stat[:, ca:ca + 1],
                                                  in1=stat[:, cb:cb + 1], op=ALU.add),
                 reads=[("st", ca), ("st", cb)], writes=[("st", cs)])
            c2 = rstd_from_ss(cs)
            for fh in range(2):
                tk = "ytmp%d" % fh
                f.op("dve", lambda e, fh=fh: e.scalar_tensor_tensor(out=ytmp_box[0][:, fh, :], in0=ps[pp + fh][:],
                                                                    scalar=stat[:, c2:c2 + 1],
                                                                    in1=gain[:, fh * 512:(fh + 1) * 512],
                                                                    op0=ALU.mult, op1=ALU.mult),
                     reads=[PS[pp + fh], ("st", c2), "gain"], writes=[tk])
                f.op("pool", lambda e, fh=fh: e.tensor_tensor(out=xres[:, b, fh * 512:(fh + 1) * 512],
                                                              in0=xres[:, b, fh * 512:(fh + 1) * 512],
                                                              in1=ytmp_box[0][:, fh, :], op=ALU.add),
                     reads=[("x", b), tk], writes=[("x", b)])

        ytmp_box = [None]

        def odd_layer(l):
            i = l // 2
            with ExitStack() as ls:
                def lsb(name, shape, dtype):
                    return ls.enter_context(nc.sbuf_tensor(nk(name), list(shape), dtype))
                hN = lsb("hN", [128, 5, D], BF16)
                hT = lsb("hTq", [128, 8, 512], BF16)
                phT = lsb("phT", [128, 8, 512], BF16)
                mixed = lsb("mixed", [128, 4, 512], BF16)
                mT = lsb("mTq", [128, 16, 512], BF16)
                sgb = [lsb("sg0", [128, 512], F32)] * 2
                bandf = lsb("bandf", [128, 12, 128], F32)
                band = lsb("band", [128, 4, 4, 128], BF16)
                btmp = lsb("btmp", [128, 4, 128], F32)
                pwbf = lsb("pwbf", [128, 4, 512], BF16)
                wout = lsb("wout", [128, 16, D], BF16)
                pscale = lsb("pscale", [128, 16], F32)
                wost = [lsb("wost%d" % k_, [128, 2, D], F32) for k_ in range(2)]
                pwst = lsb("pwst", [128, 4, 512], F32)
                ytmp_box[0] = lsb("ytmp", [128, 2, 512], F32)
                f.dma("sp", bandf[:], cband_d.rearrange("p (a t) -> p a t", t=128), writes=["bandf"])
                bv = bandf[:].rearrange("p (w k) t -> p w k t", k=3)
                f.op("dve", lambda e: e.tensor_copy(out=band[:, :, 0:3, :], in_=bv), reads=["bandf"], writes=["band"])
                f.op("dve", lambda e: e.tensor_copy(out=btmp[:], in_=band[:, :, 2, :]), reads=["band"], writes=["btmp"])
                f.op("dve", lambda e: e.tensor_tensor(out=btmp[:], in0=bv[:, :, 2, :], in1=btmp[:], op=ALU.subtract),
                     reads=["bandf", "btmp"], writes=["btmp"])
                f.op("dve", lambda e: e.tensor_copy(out=band[:, :, 3, :], in_=btmp[:]), reads=["btmp"], writes=["band"])
                f.dma("sp", pscale[:], psc_d[i].rearrange("(c p) -> p c", p=128), writes=["pscale"],
                      allow_slow_non_contiguous=True)
                srcs = []
                for tq in range(4):
                    for g in range(4):
                        for j in range(4):
                            srcs.append((owin_d[i][:, g * 512 + j * 128:g * 512 + (j + 1) * 128], 8))
                        for dj in range(4):
                            col = 2048 + g * 512 + dj * 128
                            srcs.append((owin_d[i][:, col:col + 128], 8))
                ws = WS(srcs)
                for tq in range(4):
                    b0 = tq * 4
                    woutk = [("wout", 2 * k_) for k_ in range(8)]
                    wstep = [0]

                    def wout_step():
                        k_ = wstep[0]
                        wstep[0] += 1
                        if 1 <= k_ <= 8:
                            kk = k_ - 1
                            f.op("act", lambda g_: g_.copy(out=wout[:, 2 * kk:2 * kk + 2, :], in_=wost[kk % 2][:]),
                                 reads=["wost%d" % (kk % 2)], writes=[("wout", 2 * kk)])
                        if k_ < 8:
                            f.dma("sp", wost[k_ % 2][:],
                                  owout_d[i][k_ * 256:(k_ + 1) * 256, :].rearrange("(k p) n -> p k n", p=128),
                                  writes=["wost%d" % (k_ % 2)])
                    wout_step()
                    f.dma("sp", gain[:], bcast("pre_norm", l * D, D), writes=["gain"])
                    if tq > 0:
                        f.op("pool", lambda e: e.tensor_copy(out=hN[:, 0, :], in_=hN[:, 4, :]),
                             reads=[("hN", 4)], writes=[("hN", 0)])
                    for s_, b in enumerate(range(b0 - 1, b0 + 4)):
                        if s_ == 0:
                            continue
                        prenorm_block(b, hN[:, s_, :], ("hN", s_))
                    for s_ in range(1, 5):
                        transpose_block(hN[:, s_, :], ("hN", s_), hT[:, :, (s_ - 1) * 128:s_ * 128], "hT", s_ % 2)
                    for g in range(4):
                        f.dma("sp", pwst[:], pw_d[i, g].rearrange("(k p) n -> p k n", p=128), writes=["pwst"])
                        pwk = [("pwbf", 0), ("pwbf", 2)]
                        for s_ in range(1, 5):
                            b = b0 + s_ - 1
                            for hf in range(2):
                                pk = 4 + hf
                                for cc in range(4):
                                    c = hf * 4 + cc
                                    o = ps[pk][:, cc * 128:(cc + 1) * 128]
                                    lh = hN[:, s_, c * 128:(c + 1) * 128]
                                    if b == 0:
                                        f.op("pe", lambda e, o=o, lh=lh: e.matmul(o, lhsT=lh, rhs=band[:, g, 2, :],
                                                                                  start=True, stop=False),
                                             reads=[("hN", s_), "band"], writes=[PS[pk]], inc=False)
                                        f.op("pe", lambda e, o=o, lh=lh: e.matmul(o, lhsT=lh, rhs=band[:, g, 3, :],
                                                                                  start=False, stop=True),
                                             reads=[("hN", s_), "band"], writes=[PS[pk]], inc=(cc == 3))
                                    else:
                                        lp = hN[:, s_ - 1, c * 128:(c + 1) * 128]
                                        f.op("pe", lambda e, o=o, lh=lh: e.matmul(o, lhsT=lh, rhs=band[:, g, 0, :],
                                                                                  start=True, stop=False),
                                             reads=[("hN", s_), "band"], writes=[PS[pk]], inc=False)
                                        f.op("pe", lambda e, o=o, lp=lp: e.matmul(o, lhsT=lp, rhs=band[:, g, 1, :],
                                                                                  start=False, stop=True),
                                             reads=[("hN", s_ - 1), "band"], writes=[PS[pk]], inc=(cc == 3))
                                f.op("act", lambda e, hf=hf, pk=pk: e.copy(
                                    out=phT[:, hf * 4:(hf + 1) * 4, (s_ - 1) * 128:s_ * 128],
                                    in_=ps[pk][:].rearrange("p (c t) -> p c t", c=4)),
                                    reads=[PS[pk]], writes=["phT"])
                        for j in range(4):
                            wout_step()
                            wb, wk = ws.get()
                            pk = j % 2
                            for kc in range(8):
                                f.op("pe", lambda e, kc=kc: e.matmul(ps[pk][:], lhsT=wb[:, kc, :], rhs=phT[:, kc, :],
                                                                     start=(kc == 0), stop=(kc == 7)),
                                     reads=[wk, "phT"], writes=[PS[pk]], inc=(kc == 7))
                            f.op("act", lambda e, j=j, pk=pk: e.copy(out=mixed[:, j, :], in_=ps[pk][:]),
                                 reads=[PS[pk]], writes=[("mixed", j)])
                        for k0_ in (0, 2):
                            f.op("act", lambda g_, k0_=k0_: g_.copy(out=pwbf[:, k0_:k0_ + 2, :], in_=pwst[:, k0_:k0_ + 2, :]),
                                 reads=["pwst"], writes=[("pwbf", k0_)])
                        for dj in range(4):
                            wb, wk = ws.get()
                            pg = 2 if dj % 2 == 0 else 6
                            sg = sgb[dj % 2]
                            sgk = "sg0"
                            for kc in range(8):
                                f.op("pe", lambda e, kc=kc: e.matmul(ps[pg][:], lhsT=wb[:, kc, :], rhs=hT[:, kc, :],
                                                                     start=(kc == 0), stop=(kc == 7)),
                                     reads=[wk, "hT"], writes=[PS[pg]], inc=(kc == 7))
                            f.op("act", lambda e: e.activation(out=sg[:], in_=ps[pg][:], func=AF.Silu),
                                 reads=[PS[pg]], writes=[sgk])
                            py = 3 if dj % 2 == 0 else 7
                            for j in range(4):
                                f.op("pe", lambda e, j=j, dj=dj: e.matmul(ps[py][:], lhsT=pwbf[:, j, dj * 128:(dj + 1) * 128],
                                                                          rhs=mixed[:, j, :], start=(j == 0), stop=(j == 3)),
                                     reads=pwk + [("mixed", j)], writes=[PS[py]], inc=(j == 3))
                            ch = g * 4 + dj
                            f.op("dve", lambda e, ch=ch: e.scalar_tensor_tensor(out=mT[:, ch, :], in0=ps[py][:],
                                                                                scalar=pscale[:, ch:ch + 1], in1=sg[:],
                                                                                op0=ALU.mult, op1=ALU.mult),
                                 reads=[PS[py], "pscale", sgk], writes=[("mT", ch)])
                    f.dma("sp", gain[:], bcast("post_norm", l * D, D), writes=["gain"])
                    for bb in range(4):
                        outproj_block(b0 + bb, lambda kc, bb=bb: mT[:, kc, bb * 128:(bb + 1) * 128],
                                      [("mT", ch) for ch in range(16)], 16, wout, woutk, l)
                f.barrier()

        def even_layer(l):
            i = l // 2
            with ExitStack() as ls:
                def lsb(name, shape, dtype):
                    return ls.enter_context(nc.sbuf_tensor(nk(name), list(shape), dtype))
                hT = lsb("hT", [128, 8, S], BF16)
                mT = lsb("mT", [128, 12, S], BF16)
                f.dma("sp", gain[:], bcast("pre_norm", l * D, D), writes=["gain"])
                with ExitStack() as hs_:
                    hNb = [hs_.enter_context(nc.sbuf_tensor(nk("hNb"), [128, D], BF16)) for k in range(2)]
                    for b in range(NB):
                        prenorm_block(b, hNb[b % 2][:], "hNb%d" % (b % 2))
                        transpose_block(hNb[b % 2], "hNb%d" % (b % 2), hT[:, :, b * 128:(b + 1) * 128], "hT", b % 2)
                    f.barrier()

                def proj_fm(wb, wk, pk, tg):
                    for kc in range(8):
                        f.op("pe", lambda e, kc=kc: e.matmul(ps[pk][:], lhsT=wb[:, kc, :],
                                                             rhs=hT[:, kc, tg * 512:(tg + 1) * 512],
                                                             start=(kc == 0), stop=(kc == 7)),
                             reads=[wk, "hT"], writes=[PS[pk]], inc=(kc == 7))

                with ExitStack() as as_:
                    def asb(name, shape, dtype):
                        return as_.enter_context(nc.sbuf_tensor(nk(name), list(shape), dtype))
                    qT = asb("qT", [128, S], BF16)
                    kT = asb("kT", [128, 2, S], BF16)
                    V = asb("V", [128, NB, 2, 128], BF16)
                    ropec = asb("ropec", [128, S], BF16)
                    ropes = asb("ropes", [128, S], BF16)
                    maskT = asb("maskT", [128, 19 * 128], BF16)
                    Pt = [asb("Pt%d" % k, [128, 512], BF16) for k in range(5)]
                    rtmp = [asb("rtmp%d" % k, [128, 512], F32) for k in range(2)]
                    rc = rtmp[0]
                    atmp = rtmp[1]
                    f.dma("pool", ropec[:], cropec_d, writes=["ropec"])
                    f.dma("pool", ropes[:], cropes_d, writes=["ropes"])
                    f.dma("pool", maskT[:], cmask_d, writes=["maskT"])
                    srcs = []
                    for hp_ in range(8):
                        c0_ = hp_ * 128
                        for (base_, swb_) in ((0, 0), (1024, 1024)):
                            srcs.append((ewin_d[i][:, base_ + c0_:base_ + c0_ + 128], 8))
                            srcs.append((ewsw_d[i][:, swb_ + c0_:swb_ + c0_ + 128], 8))
                        srcs.append((ewin_d[i][:, 3072 + c0_:3072 + c0_ + 128], 8))
                        srcs.append((ewin_d[i][:, 2048 + c0_:2048 + c0_ + 128], 8))
                    ws = WS(srcs)
                    f.op("pool", lambda e: e.memset(V[:, :, :, 64:128], 1.0), writes=["V"])
                    f.op("pool", lambda e: e.memset(kT[:], 0.0), writes=[("kT", 0)])
                    mrr = [0]
                    qTb = [qT, mT[:, 8, :]]
                    kTb = [kT, mT[:, 9:11, :]]
                    f.op("pool", lambda e: e.memset(mT[:, 9:11, :], 0.0), writes=[("mT", 9), ("mT", 10), ("kT", 1)])
                    QK = [["qT", ("mT", 8)], [("kT", 0), ("kT", 1), ("mT", 9), ("mT", 10)]]

                    def proj_gen(hp_, issue):
                        par = hp_ % 2
                        qd = qTb[par]
                        kd = kTb[par]
                        qk_ = [QK[0][par]]
                        kk_ = [("kT", par)] + ([("mT", 9), ("mT", 10)] if par == 1 else [])
                        for which in ("q", "k"):
                            wb, wk = ws.get(issue)
                            wb2, wk2 = ws.get(issue)
                            for tg in range(4):
                                sl = slice(tg * 512, (tg + 1) * 512)
                                proj_fm(wb, wk, 6, tg)
                                yield
                                proj_fm(wb2, wk2, 7, tg)
                                r0 = "rtmp0"
                                r1 = "rtmp1"
                                f.op("dve", lambda e, sl=sl: e.tensor_tensor(out=rtmp[0][:], in0=ps[6][:], in1=ropec[:, sl],
                                                                             op=ALU.mult),
                                     reads=[PS[6], "ropec"], writes=[r0])
                                f.op("dve", lambda e, sl=sl: e.tensor_tensor(out=rtmp[1][:], in0=ps[7][:], in1=ropes[:, sl],
                                                                             op=ALU.mult),
                                     reads=[PS[7], "ropes"], writes=[r1])
                                if which == "q":
                                    f.op("dve", lambda e, sl=sl: e.tensor_tensor(out=qd[:, sl], in0=rtmp[0][:],
                                                                                 in1=rtmp[1][:], op=ALU.add),
                                         reads=[r0, r1], writes=qk_)
                                else:
                                    for a_ in range(2):
                                        pr_ = slice(64 * a_, 64 * a_ + 64)
                                        f.op("dve", lambda e, sl=sl, a_=a_, pr_=pr_: e.tensor_tensor(
                                            out=kd[pr_, a_, sl], in0=rtmp[0][pr_, :], in1=rtmp[1][pr_, :], op=ALU.add),
                                            reads=[r0, r1], writes=kk_)
                                yield
                        wb, wk = ws.get(issue)
                        for tg in range(4):
                            pg_ = 6 + (tg % 2)
                            proj_fm(wb, wk, pg_, tg)
                            f.op("act", lambda e, tg=tg, pg_=pg_: e.activation(out=mT[:, hp_, tg * 512:(tg + 1) * 512],
                                                                               in_=ps[pg_][:], func=AF.Silu),
                                 reads=[PS[pg_]], writes=[("mT", hp_)])
                        yield

                    gen = proj_gen(0, True)
                    for _ in gen:
                        pass
                    gen = None
                    for hp in range(8):
                        par = hp % 2
                        qT_ = qTb[par]
                        kT_ = kTb[par]
                        qkeys = [QK[0][par]]
                        kkeys = [("kT", par)] + ([("mT", 9), ("mT", 10)] if par == 1 else [])
                        if gen is not None:
                            for _ in gen:
                                pass
                        wb, wk = ws.get(hp == 0)
                        for b4 in range(4):
                            pk = 6 + (b4 % 2)
                            for bb in range(4):
                                b = b4 * 4 + bb
                                for kc in range(8):
                                    f.op("pe", lambda e, kc=kc, b=b, bb=bb: e.matmul(
                                        ps[pk][:, bb * 128:(bb + 1) * 128], lhsT=hT[:, kc, b * 128:(b + 1) * 128],
                                        rhs=wb[:, kc, :], start=(kc == 0), stop=(kc == 7)),
                                        reads=[wk, "hT"], writes=[PS[pk]], inc=(kc == 7 and bb == 3))
                            f.op("act", lambda e, b4=b4: e.copy(
                                out=V[:, b4 * 4:(b4 + 1) * 4, :, 0:64],
                                in_=ps[pk][:].rearrange("p (b a d) -> p b a d", b=4, a=2)),
                                reads=[PS[pk]], writes=["V"])
                        ws.prefetch(6)
                        gen = proj_gen(hp + 1, False) if hp < 7 else None
                        items = []
                        for a in range(2):
                            for qg in range(4):
                                nkb = 4 * qg + 4
                                for kb in range(nkb):
                                    items.append((a, qg, kb, nkb))
                        LAG = 4

                        def stage1(idx):
                            a, qg, kb, nkb = items[idx]
                            pr = slice(64 * a, 64 * a + 64)
                            cq = max(128 * kb, 512 * qg)
                            N = 512 * (qg + 1) - cq
                            sk = idx % 4
                            pt = idx % 5
                            f.op("pe", lambda e: e.matmul(
                                ps[sk][:, 0:N], lhsT=kT_[:, a, kb * 128:(kb + 1) * 128], rhs=qT_[:, cq:cq + N],
                                start=True, stop=True),
                                reads=kkeys + qkeys, writes=[PS[sk]])
                            f.op("act", lambda e: e.activation(out=Pt[pt][:, 0:N], in_=ps[sk][:, 0:N],
                                                               func=AF.Exp, scale=0.125),
                                 reads=[PS[sk]], writes=["Pt%d" % pt])
                            moff = ((cq - 128 * kb) // 128 + 3) * 128
                            me = "dve"
                            mrr[0] += 1
                            f.op(me, lambda e: e.tensor_tensor(
                                out=Pt[pt][:, 0:N], in0=Pt[pt][:, 0:N], in1=maskT[:, moff:moff + N], op=ALU.mult),
                                reads=["Pt%d" % pt, "maskT"], writes=["Pt%d" % pt])

                        def stage2(idx):
                            a, qg, kb, nkb = items[idx]
                            pr = slice(64 * a, 64 * a + 64)
                            cq = max(128 * kb, 512 * qg)
                            N = 512 * (qg + 1) - cq
                            sk = idx % 4
                            pt = idx % 5
                            po = 4 + (qg % 2)
                            oc = cq - 512 * qg
                            f.op("pe", lambda e: e.matmul(
                                ps[po][:, oc:oc + N], lhsT=V[:, kb, a, :], rhs=Pt[pt][:, 0:N],
                                start=(kb == 0), stop=(kb == nkb - 1)),
                                reads=["V", "Pt%d" % pt], writes=[PS[po]])
                            if kb == nkb - 1:
                                qs = slice(qg * 512, (qg + 1) * 512)
                                f.op("act", lambda e: e.activation(out=rc[64:128, :], in_=ps[po][64:128, :], func=AF.Ln),
                                     reads=[PS[po]], writes=["rtmp0"])
                                f.op("act", lambda e: e.activation(out=rc[64:128, :], in_=rc[64:128, :], func=AF.Exp,
                                                                   scale=-1.0),
                                     reads=["rtmp0"], writes=["rtmp0"])
                                f.op("dve", lambda e: e.tensor_tensor(out=atmp[pr, :], in0=ps[po][0:64, :],
                                                                      in1=rc[64:128, :], op=ALU.mult),
                                     reads=[PS[po], "rtmp0"], writes=["rtmp1"])
                                f.op("pool", lambda e: e.tensor_tensor(out=mT[pr, hp, qs], in0=atmp[pr, :],
                                                                       in1=mT[pr, hp, qs], op=ALU.mult),
                                     reads=["rtmp1", ("mT", hp)], writes=[("mT", hp)])

                        for idx in range(len(items) + LAG):
                            if idx < len(items):
                                stage1(idx)
                            if idx - LAG >= 0:
                                stage2(idx - LAG)
                            if gen is not None and idx % 4 == 3:
                                if next(gen, "done") == "done":
                                    gen = None
                    f.barrier()

                with ExitStack() as ss_:
                    def ssb(name, shape, dtype):
                        return ss_.enter_context(nc.sbuf_tensor(nk(name), list(shape), dtype))
                    NPW = 17 + 7
                    PR = ssb("PR", [128, 16, 3 * 24 + 4], F32)
                    pa = ssb("pa", [128, 16, 12], F32)
                    pi32 = ssb("pi32", [128, 16], I32)
                    XR = ssb("XR", [128, S], F32)
                    XI = ssb("XI", [128, S], F32)
                    cs_ = [ssb("cs%d" % k, [128, 2, 192], F32) for k in range(2)]
                    Xb = [ssb("Xb%d" % k, [128, 512], BF16) for k in range(2)]
                    uT = ssb("uT", [128, S], BF16)
                    bnat = ssb("bnat", [128, 2, 16, 16], F32)
                    cnat = ssb("cnat", [128, 2, 64], F32)
                    padT = ssb("padT", [128, 128], BF16)
                    padTf = ssb("padTf", [128, 2, 128], F32)
                    Bpad = ssb("Bpad", [128, 4, 2, 128], BF16)
                    Cpad = ssb("Cpad", [128, 4, 2, 128], BF16)
                    cf = ssb("cf", [128, 4, 128], F32)
                    dvec = ssb("dvec", [128, 4], F32)
                    glub = ssb("glub", [128, 4], F32)
                    gl = [ssb("gl%d" % k, [128, 512], F32) for k in range(2)] + [XR[:, 0:512]]

                    for k_ in range(2):
                        f.op("pool", lambda e, k_=k_: e.memset(cs_[k_][:], 0.0), writes=[("cs", k_, 0), ("cs", k_, 1)])
                    def ld_gp(dst_col, name):
                        for e_ in range(2):
                            f.dma("sp", pa[e_ * 64:(e_ + 1) * 64, :, dst_col],
                                  bass.AP(dt_[name], i * 2048 + e_ * 64, [[1, 64], [128, 16]]),
                                  writes=["pa"], allow_slow_non_contiguous=True)
                    ld_gp(0, "ssm_a_re")
                    ld_gp(1, "ssm_a_im")
                    for e_ in range(2):
                        f.dma("sp", pa[e_ * 64:(e_ + 1) * 64, :, 2],
                              bass.AP(dt_["ssm_log_dt"], i * 32 + e_, [[0, 64], [2, 16]]),
                              writes=["pa"], allow_slow_non_contiguous=True)
                    f.dma("sp", dvec[:], sd_d[i].rearrange("(c p) -> p c", p=128), writes=["dvec"],
                          allow_slow_non_contiguous=True)
                    f.dma("sp", glub[:], glb_d[i].rearrange("(c p) -> p c", p=128), writes=["glub"],
                          allow_slow_non_contiguous=True)

                    def pop(eng, fn, w=("pa",)):
                        f.op(eng, fn, reads=["pa", "PR"], writes=list(w))
                    A = lambda c: pa[:, :, c]
                    pop("act", lambda e: e.activation(out=A(3), in_=A(2), func=AF.Exp))
                    pop("dve", lambda e: e.tensor_tensor(out=A(4), in0=A(0), in1=A(3), op=ALU.mult))
                    pop("dve", lambda e: e.tensor_tensor(out=A(5), in0=A(1), in1=A(3), op=ALU.mult))
                    pop("act", lambda e: e.activation(out=A(4), in_=A(4), func=AF.Exp))

                    def sin_of(dst, shift):
                        pop("dve", lambda e: e.tensor_scalar(out=A(6), in0=A(5), scalar1=shift, scalar2=1.0 / TWO_PI,
                                                             op0=ALU.add, op1=ALU.mult))
                        f.op("dve", lambda e: e.tensor_copy(out=pi32[:], in_=A(6)), reads=["pa"], writes=["pi32"])
                        f.op("dve", lambda e: e.tensor_copy(out=A(7), in_=pi32[:]), reads=["pi32"], writes=["pa"])
                        pop("dve", lambda e: e.tensor_tensor(out=A(6), in0=A(6), in1=A(7), op=ALU.subtract))
                        pop("dve", lambda e: e.tensor_scalar(out=A(6), in0=A(6), scalar1=TWO_PI, scalar2=math.pi,
                                                             op0=ALU.mult, op1=ALU.min))
                        pop("dve", lambda e: e.tensor_scalar(out=A(6), in0=A(6), scalar1=-math.pi, scalar2=None,
                                                             op0=ALU.max))
                        pop("act", lambda e: e.activation(out=dst, in_=A(6), func=AF.Sin))
                    sin_of(A(8), 0.0)
                    sin_of(A(9), math.pi / 2)
                    P3 = lambda k, c: PR[:, :, 3 * k + c]
                    pop("dve", lambda e: e.tensor_tensor(out=P3(0, 0), in0=A(4), in1=A(9), op=ALU.mult), w=("PR",))
                    pop("dve", lambda e: e.tensor_tensor(out=P3(0, 1), in0=A(4), in1=A(8), op=ALU.mult), w=("PR",))

                    def cmul(dst, a_, b_):
                        pop("dve", lambda e: e.tensor_tensor(out=A(6), in0=P3(a_, 0), in1=P3(b_, 0), op=ALU.mult))
                        pop("dve", lambda e: e.tensor_tensor(out=A(7), in0=P3(a_, 1), in1=P3(b_, 1), op=ALU.mult))
                        pop("dve", lambda e: e.tensor_tensor(out=A(10), in0=P3(a_, 0), in1=P3(b_, 1), op=ALU.mult))
                        pop("dve", lambda e: e.tensor_tensor(out=A(11), in0=P3(a_, 1), in1=P3(b_, 0), op=ALU.mult))
                        pop("dve", lambda e: e.tensor_tensor(out=P3(dst, 0), in0=A(6), in1=A(7), op=ALU.subtract), w=("PR",))
                        pop("dve", lambda e: e.tensor_tensor(out=P3(dst, 1), in0=A(10), in1=A(11), op=ALU.add), w=("PR",))
                    for j in range(1, 16):
                        cmul(j, j - 1, 0)
                    for k in range(16, 16 + 7):
                        cmul(k, k - 1, k - 1)
                    for k in range(23):
                        pop("dve", lambda e, k=k: e.tensor_scalar(out=P3(k, 2), in0=P3(k, 1), scalar1=-1.0, scalar2=None,
                                                                  op0=ALU.mult), w=("PR",))
                    FR = PR[:, :, 72]
                    FI = PR[:, :, 73]
                    pop("dve", lambda e: e.tensor_scalar(out=A(6), in0=P3(0, 0), scalar1=-1.0, scalar2=None, op0=ALU.add))
                    pop("dve", lambda e: e.tensor_tensor(out=A(7), in0=A(0), in1=A(0), op=ALU.mult))
                    pop("dve", lambda e: e.tensor_tensor(out=A(10), in0=A(1), in1=A(1), op=ALU.mult))
                    pop("dve", lambda e: e.tensor_tensor(out=A(7), in0=A(7), in1=A(10), op=ALU.add))
                    pop("dve", lambda e: e.reciprocal(out=A(7), in_=A(7)))
                    pop("dve", lambda e: e.tensor_tensor(out=A(10), in0=A(6), in1=A(0), op=ALU.mult))
                    pop("dve", lambda e: e.tensor_tensor(out=A(11), in0=P3(0, 1), in1=A(1), op=ALU.mult))
                    pop("dve", lambda e: e.tensor_tensor(out=A(10), in0=A(10), in1=A(11), op=ALU.add))
                    pop("dve", lambda e: e.tensor_tensor(out=FR, in0=A(10), in1=A(7), op=ALU.mult), w=("PR",))
                    pop("dve", lambda e: e.tensor_tensor(out=A(10), in0=P3(0, 1), in1=A(0), op=ALU.mult))
                    pop("dve", lambda e: e.tensor_tensor(out=A(11), in0=A(6), in1=A(1), op=ALU.mult))
                    pop("dve", lambda e: e.tensor_tensor(out=A(10), in0=A(10), in1=A(11), op=ALU.subtract))
                    pop("dve", lambda e: e.tensor_tensor(out=FI, in0=A(10), in1=A(7), op=ALU.mult), w=("PR",))
                    for ri, name in enumerate(("ssm_b_re", "ssm_b_im")):
                        for e_ in range(2):
                            f.dma("sp", bnat[e_ * 64:(e_ + 1) * 64, ri, :, :],
                                  bass.AP(dt_[name], i * 32 * 1024 + e_ * 1024,
                                          [[16, 64], [2048, 16], [1, 16]]), writes=["bnat"])

                    XALL = [["X0"] + [("XR", s_) for s_ in range(16)], ["X1"] + [("XI", s_) for s_ in range(16)]]

                    srcs = [(ewin_d[i][:, 4096 + j_ * 128:4096 + (j_ + 1) * 128], 8) for j_ in range(4)]
                    for tg_ in range(4):
                        srcs += [(glw_d[i][:, fo_ * 128:(fo_ + 1) * 128], 4) for fo_ in range(4)]
                        srcs += [(ewin_d[i][:, 4608 + fo_ * 128:4608 + (fo_ + 1) * 128], 8) for fo_ in range(4)]
                    ws = WS(srcs)

                    def PSC(q, k, c):
                        return PR[:, q, 3 * k + c:3 * k + c + 1]

                    for j in range(4):
                        for ri, name in enumerate(("ssm_c_re", "ssm_c_im")):
                            f.dma("sp", cnat[:, ri, :],
                                  bass.AP(dt_[name], i * 32 * 1024 + j * 8 * 1024, [[64, 128], [1, 64]]),
                                  writes=["cnat"])
                        for qq in range(4):
                            q = j * 4 + qq
                            for ri in range(2):
                                f.op("pool", lambda e: e.memset(padT[:], 0.0), writes=["padT"])
                                for e_ in range(2):
                                    co = 16 * (2 * qq + e_)
                                    f.op("dve", lambda e, e_=e_, co=co, ri=ri, q=q: e.tensor_copy(
                                        out=padT[e_ * 64:(e_ + 1) * 64, co:co + 16],
                                        in_=bnat[e_ * 64:(e_ + 1) * 64, ri, q, :]),
                                        reads=["bnat"], writes=["padT"])
                                f.op("pe", lambda e: e.matmul(ps[6][:, 0:128], lhsT=padT[:], rhs=ident[:], start=True, stop=True),
                                     reads=["padT", "ident"], writes=[PS[6]])
                                f.op("act", lambda e, qq=qq, ri=ri: e.copy(out=Bpad[:, qq, ri, :], in_=ps[6][:, 0:128]),
                                     reads=[PS[6]], writes=["Bpad"])
                            for ri in range(2):
                                for e_ in range(2):
                                    g8 = 2 * qq + e_
                                    f.op("dve", lambda e, e_=e_, ri=ri, g8=g8: e.tensor_scalar(
                                        out=padTf[:, ri, e_ * 64:(e_ + 1) * 64], in0=cnat[:, ri, :],
                                        scalar1=gmask[:, g8:g8 + 1], scalar2=None, op0=ALU.mult),
                                        reads=["cnat", "gmask"], writes=["padTf"])
                            for ri in range(2):
                                f.op("pe", lambda e, ri=ri: e.matmul(ps[4][:, ri * 128:(ri + 1) * 128], lhsT=padTf[:, ri, :],
                                                                     rhs=identf[:], start=True, stop=True),
                                     reads=["padTf", "identf"], writes=[PS[4]])
                            fr = PR[:, q, 72:73]
                            fi = PR[:, q, 73:74]
                            f.op("act", lambda e: e.copy(out=cf[:, 0:2, :], in_=ps[4][:, 0:256].rearrange("p (a b) -> p a b", a=2)),
                                 reads=[PS[4]], writes=["cf"])
                            f.op("dve", lambda e, fr=fr: e.tensor_scalar(out=cf[:, 2, :], in0=cf[:, 0, :], scalar1=fr, scalar2=None,
                                                                         op0=ALU.mult), reads=["cf", "PR"], writes=["cf"])
                            f.op("dve", lambda e, fi=fi: e.tensor_scalar(out=cf[:, 3, :], in0=cf[:, 1, :], scalar1=fi, scalar2=None,
                                                                         op0=ALU.mult), reads=["cf", "PR"], writes=["cf"])
                            f.op("dve", lambda e, qq=qq: e.tensor_tensor(out=Cpad[:, qq, 0, :], in0=cf[:, 2, :], in1=cf[:, 3, :],
                                                                         op=ALU.subtract), reads=["cf"], writes=["Cpad"])
                            f.op("dve", lambda e, fi=fi: e.tensor_scalar(out=cf[:, 2, :], in0=cf[:, 0, :], scalar1=fi, scalar2=-1.0,
                                                                         op0=ALU.mult, op1=ALU.mult), reads=["cf", "PR"], writes=["cf"])
                            f.op("dve", lambda e, fr=fr: e.tensor_scalar(out=cf[:, 3, :], in0=cf[:, 1, :], scalar1=fr, scalar2=None,
                                                                         op0=ALU.mult), reads=["cf", "PR"], writes=["cf"])
                            f.op("dve", lambda e, qq=qq: e.tensor_tensor(out=Cpad[:, qq, 1, :], in0=cf[:, 2, :], in1=cf[:, 3, :],
                                                                         op=ALU.subtract), reads=["cf"], writes=["Cpad"])
                        wb, wk = ws.get()
                        for tg in range(4):
                            sl = slice(tg * 512, (tg + 1) * 512)
                            proj_fm(wb, wk, 4, tg)
                            f.op("act", lambda e, sl=sl: e.copy(out=uT[:, sl], in_=ps[4][:]), reads=[PS[4]], writes=["uT"])
                        def stage_in(qq_, ri):
                            X = (XR, XI)[ri]
                            for tg in range(4):
                                sl = slice(tg * 512, (tg + 1) * 512)
                                pk = 4 + (tg % 2)
                                f.op("pe", lambda e, sl=sl, pk=pk: e.matmul(
                                    ps[pk][:], lhsT=Bpad[:, qq_, ri, :], rhs=uT[:, sl], start=True, stop=True),
                                    reads=["Bpad", "uT"], writes=[PS[pk]])
                                f.op("act", lambda e, sl=sl, pk=pk, X=X: e.copy(out=X[:, sl], in_=ps[pk][:]),
                                     reads=[PS[pk]], writes=XALL[ri])

                        def stage_out(qq_, ri):
                            X = (XR, XI)[ri]
                            for tg in range(4):
                                sl = slice(tg * 512, (tg + 1) * 512)
                                bi = (ri * 4 + tg) % 2
                                cast(Xb[bi][:], X[:, sl], XALL[ri], ["Xb%d" % bi])
                                f.op("pe", lambda e, tg=tg, bi=bi: e.matmul(
                                    ps[tg][:], lhsT=Cpad[:, qq_, ri, :], rhs=Xb[bi][:],
                                    start=(qq_ == 0 and ri == 0), stop=(qq_ == 3 and ri == 1)),
                                    reads=["Cpad", "Xb%d" % bi], writes=[PS[tg]])

                        stage_in(0, 0)
                        stage_in(0, 1)
                        for qq in range(4):
                            q = j * 4 + qq
                            XRv = XR[:].rearrange("p (c s) -> p c s", s=16)
                            XIv = XI[:].rearrange("p (c s) -> p c s", s=16)

                            def _kl(k_):
                                return list(k_) if isinstance(k_, list) else [k_]

                            def cstep(oR, oI, iR, iI, k, kOR, kOI, kIR, kII, bR=None, bI=None, kB=()):
                                for (o_, i_, c_, ko, ki, b_) in ((oR, iR, 0, kOR, kIR, bR), (oI, iR, 1, kOI, kIR, bI),
                                                                 (oR, iI, 2, kOR, kII, None), (oI, iI, 0, kOI, kII, None)):
                                    add_ = o_ if b_ is None else b_
                                    f.op("dve", lambda e, o_=o_, i_=i_, c_=c_, add_=add_: e.scalar_tensor_tensor(
                                        out=o_, in0=i_, scalar=PSC(q, k, c_), in1=add_, op0=ALU.mult, op1=ALU.add),
                                        reads=_kl(ki) + ["PR"] + list(kB), writes=_kl(ko))
                            XRc = XR[:].rearrange("p (n r) -> p n r", r=4)
                            XIc = XI[:].rearrange("p (n r) -> p n r", r=4)
                            XR4 = XR[:].rearrange("p (c m r) -> p c m r", m=4, r=4)
                            XI4 = XI[:].rearrange("p (c m r) -> p c m r", m=4, r=4)
                            for r_ in range(1, 4):
                                ko_ = [r_ + 4 * m_ for m_ in range(4)]
                                ki_ = [r_ - 1 + 4 * m_ for m_ in range(4)]
                                cstep(XRc[:, :, r_], XIc[:, :, r_], XRc[:, :, r_ - 1], XIc[:, :, r_ - 1], 0,
                                      [("XR", c_) for c_ in ko_], [("XI", c_) for c_ in ko_],
                                      [("XR", c_) for c_ in ki_], [("XI", c_) for c_ in ki_])
                            for m_ in range(1, 4):
                                so_ = 4 * m_ + 3
                                si_ = 4 * m_ - 1
                                cstep(XRv[:, :, so_], XIv[:, :, so_], XRv[:, :, si_], XIv[:, :, si_], 3,
                                      ("XR", so_), ("XI", so_), ("XR", si_), ("XI", si_))
                            cur = 0
                            f.op("act", lambda e: e.copy(out=cs_[0][:, 0, 64:192], in_=XRv[:, :, 15]),
                                 reads=[("XR", 15)], writes=[("cs", 0, 0)])
                            f.op("act", lambda e: e.copy(out=cs_[0][:, 1, 64:192], in_=XIv[:, :, 15]),
                                 reads=[("XI", 15)], writes=[("cs", 0, 1)])
                            for k in range(7):
                                sh = 1 << k
                                src = cs_[cur]
                                dst = cs_[1 - cur]
                                cstep(dst[:, 0, 64:192], dst[:, 1, 64:192], src[:, 0, 64 - sh:192 - sh], src[:, 1, 64 - sh:192 - sh],
                                      15 + k, ("cs", 1 - cur, 0), ("cs", 1 - cur, 1), ("cs", cur, 0), ("cs", cur, 1),
                                      bR=src[:, 0, 64:192], bI=src[:, 1, 64:192])
                                cur = 1 - cur
                            fin = cs_[cur]
                            for m_ in range(4):
                                s_ = 4 * m_ + 3
                                cstep(XRv[:, 1:128, s_], XIv[:, 1:128, s_], fin[:, 0, 64:191], fin[:, 1, 64:191], s_,
                                      ("XR", s_), ("XI", s_), ("cs", cur, 0), ("cs", cur, 1))
                            ENDK = [[("XR", 3), ("XR", 7), ("XR", 11)], [("XI", 3), ("XI", 7), ("XI", 11)]]

                            def p3_m(comp, part):
                                X4 = (XR4, XI4)[comp]
                                kx = ("XR", "XI")[comp]
                                cidx = ((0, 2), (1, 0))[comp]
                                E4 = (XR4, XI4)[part]
                                for r_ in range(3):
                                    f.op("dve", lambda e, r_=r_: e.scalar_tensor_tensor(
                                        out=X4[:, :, 1:4, r_], in0=E4[:, :, 0:3, 3],
                                        scalar=PSC(q, r_, cidx[part]), in1=X4[:, :, 1:4, r_],
                                        op0=ALU.mult, op1=ALU.add),
                                        reads=ENDK[part] + ["PR"], writes=[(kx, 4 * m_ + r_) for m_ in range(1, 4)])

                            def p3_0(comp):
                                Xv = (XRv, XIv)[comp]
                                kx = ("XR", "XI")[comp]
                                cidx = ((0, 2), (1, 0))[comp]
                                for part in range(2):
                                    for r_ in range(3):
                                        f.op("dve", lambda e, r_=r_, part=part: e.scalar_tensor_tensor(
                                            out=Xv[:, 1:128, r_], in0=fin[:, part, 64:191],
                                            scalar=PSC(q, r_, cidx[part]), in1=Xv[:, 1:128, r_],
                                            op0=ALU.mult, op1=ALU.add),
                                            reads=[("cs", cur, part), "PR"], writes=[(kx, r_)])
                            p3_m(0, 0)
                            p3_m(0, 1)
                            p3_0(0)
                            p3_m(1, 0)
                            stage_out(qq, 0)
                            if qq < 3:
                                stage_in(qq + 1, 0)
                            p3_m(1, 1)
                            p3_0(1)
                            stage_out(qq, 1)
                            if qq < 3:
                                stage_in(qq + 1, 1)
                        for tg in range(4):
                            sl = slice(tg * 512, (tg + 1) * 512)
                            f.op("dve", lambda e, sl=sl, tg=tg, j=j: e.scalar_tensor_tensor(out=gl[0][:], in0=uT[:, sl],
                                                                                            scalar=dvec[:, j:j + 1], in1=ps[tg][:],
                                                                                            op0=ALU.mult, op1=ALU.add),
                                 reads=[PS[tg], "uT", "dvec"], writes=["gl0"])
                            f.op("pool", lambda e: e.tensor_tensor(out=gl[1][:], in0=gl[0][:], in1=gl[0][:], op=ALU.mult),
                                 reads=["gl0"], writes=["gl1"])
                            f.op("pool", lambda e: e.tensor_scalar(out=gl[1][:], in0=gl[1][:], scalar1=0.044715, scalar2=1.0,
                                                                   op0=ALU.mult, op1=ALU.add), reads=["gl1"], writes=["gl1"])
                            f.op("pool", lambda e: e.tensor_tensor(out=gl[1][:], in0=gl[1][:], in1=gl[0][:], op=ALU.mult),
                                 reads=["gl1", "gl0"], writes=["gl1"])
                            f.op("act", lambda e: e.activation(out=gl[2][:], in_=gl[1][:], func=AF.Sigmoid,
                                                               scale=2.0 * math.sqrt(2.0 / math.pi)),
                                 reads=["gl1"], writes=["gl2"] + XALL[0])
                            f.op("dve", lambda e, sl=sl, j=j: e.tensor_tensor(out=mT[:, 8 + j, sl], in0=gl[0][:], in1=gl[2][:],
                                                                              op=ALU.mult),
                                 reads=["gl0", "gl2"] + XALL[0], writes=[("mT", 8 + j)])
                    for tg in range(4):
                        sl = slice(tg * 512, (tg + 1) * 512)
                        for fo in range(4):
                            wb, wk = ws.get()
                            for jj in range(4):
                                f.op("pe", lambda e, fo=fo, jj=jj, sl=sl: e.matmul(
                                    ps[fo][:], lhsT=wb[:, jj, :], rhs=mT[:, 8 + jj, sl],
                                    start=(jj == 0), stop=(jj == 3)),
                                    reads=[wk, ("mT", 8 + jj)], writes=[PS[fo]], inc=(jj == 3))
                        for fo in range(4):
                            wb, wk = ws.get()
                            pgz = 4 + (fo % 2)
                            proj_fm(wb, wk, pgz, tg)
                            f.op("act", lambda e: e.activation(out=gl[0][:], in_=ps[pgz][:], func=AF.Sigmoid),
                                 reads=[PS[pgz]], writes=["gl0"])
                            f.op("act", lambda e, fo=fo: e.activation(out=gl[2][:], in_=ps[fo][:], func=AF.Sigmoid,
                                                                      bias=glub[:, fo:fo + 1]),
                                 reads=[PS[fo], "glub"], writes=["gl2"] + XALL[0])
                            f.op("pool", lambda e: e.tensor_tensor(out=gl[1][:], in0=gl[2][:], in1=gl[0][:], op=ALU.mult),
                                 reads=["gl2", "gl0"] + XALL[0], writes=["gl1"])
                            f.op("dve", lambda e: e.tensor_tensor(out=gl[1][:], in0=ps[pgz][:], in1=gl[1][:], op=ALU.mult),
                                 reads=[PS[pgz], "gl1"], writes=["gl1"])
                            f.op("dve", lambda e, fo=fo, sl=sl: e.tensor_tensor(out=mT[:, 8 + fo, sl], in0=mT[:, 8 + fo, sl],
                                                                                in1=gl[1][:], op=ALU.mult),
                                 reads=["gl1", ("mT", 8 + fo)] + [PS[k] for k in range(4)], writes=[("mT", 8 + fo)])
                    f.barrier()

                with ExitStack() as os_:
                    wout = os_.enter_context(nc.sbuf_tensor(nk("ewout"), [128, 12, D], BF16))
                    ytmp_box[0] = os_.enter_context(nc.sbuf_tensor(nk("ytmp"), [128, 2, 512], F32))
                    woutk = load_big(wout, "ewout", ewout_d[i], 12)
                    f.dma("sp", gain[:], bcast("post_norm", l * D, D), writes=["gain"])
                    for b in range(NB):
                        outproj_block(b, lambda kc, b=b: mT[:, kc, b * 128:(b + 1) * 128],
                                      [("mT", ch) for ch in range(12)], 12, wout, woutk, l)
                    f.barrier()

        for l in layers:
            if l % 2 == 0:
                even_layer(l)
            else:
                odd_layer(l)

        for b in range(NB):
            f.dma("sp", out_d[b * 128:(b + 1) * 128, :], xres[:, b, :], reads=[("x", b)], key="out")
        nc.sync.wait_ge(f.dsem["out"][0], f.dsem["out"][1])
    return nc


_CACHE = {}


def prep_inputs(inputs):
    w = {k: np.ascontiguousarray(np.asarray(v, dtype=np.float32)) for k, v in inputs.items()}
    perm = swap_perm()
    shared = {k: v for k, v in w.items() if k != "x"}
    shared["even_w_sw"] = np.ascontiguousarray(w["even_w_in"][:, :, 0:2048][:, :, perm])
    shared.update(host_consts())
    return w["x"], shared


def kernel(**inputs):
    x, shared = prep_inputs(inputs)
    if "nc" not in _CACHE:
        _CACHE["nc"] = build()
    nc = _CACHE["nc"]
    in_maps = []
    for c in range(8):
        m = dict(shared)
        m["x"] = np.ascontiguousarray(x[c])
        in_maps.append(m)
    res = run_bass_kernel_spmd(nc, in_maps, core_ids=list(range(8)))
    return np.stack([np.asarray(r["out"], dtype=np.float32) for r in res.results], axis=0)
```
